# Optimizing a Trainium2 kernel written in Bass

```python
import math
import jax
import jax.numpy as jnp
from jax import lax
import numpy as np

D_MODEL = 1024
BATCH = 2
SEQ = 8192
DEPTH = 2

GRID_W = 64
ROPE_THETA = 10000.0
Q_BLOCK = 128
NORM_EPS = 1e-6

BRANCH = D_MODEL // 2
D_MIX = 4 * BRANCH

SSM_HEAD = 64
SSM_HEADS = BRANCH // SSM_HEAD
SSM_GROUPS = 2
SSM_STATE = 128
SSM_CHUNK = 128
D_CONV = 5
SSM_CONV_CH = BRANCH + 2 * SSM_GROUPS * SSM_STATE

RWKV_HEAD = 64
RWKV_HEADS = BRANCH // RWKV_HEAD
RWKV_RANK = 64
RWKV_SHIFT_CH = 3 * BRANCH + 2 * RWKV_RANK
RWKV_GN_EPS = 64e-5

DIFF_HEADS = 4
DIFF_HEAD = BRANCH // (2 * DIFF_HEADS)

GQA_HEAD = 128
GQA_Q_HEADS = BRANCH // GQA_HEAD
GQA_KV_HEADS = 2

PROJ_SIZES = (
    BRANCH, SSM_CONV_CH, 2 * SSM_HEADS,
    RWKV_SHIFT_CH, BRANCH,
    BRANCH, BRANCH, BRANCH, BRANCH,
    BRANCH, GQA_KV_HEADS * GQA_HEAD, GQA_KV_HEADS * GQA_HEAD, BRANCH,
)
D_IN_PROJ = sum(PROJ_SIZES)

kernel_name = "hymba_style_bidir_ssd_rwkv7_diffattn_axialgqa"


def _split(u, sizes):
    idx, acc = [], 0
    for s in sizes[:-1]:
        acc += s
        idx.append(acc)
    return jnp.split(u, idx, axis=-1)


def _rms_norm(x, w, eps=NORM_EPS):
    xf = x.astype(jnp.float32)
    y = xf * lax.rsqrt(jnp.mean(xf * xf, axis=-1, keepdims=True) + eps)
    return (y * w.astype(jnp.float32)).astype(x.dtype)


def _rope(x, pos):
    d = x.shape[-1]
    half = d // 2
    inv = ROPE_THETA ** (-jnp.arange(half, dtype=jnp.float32) / half)
    ang = pos.astype(jnp.float32)[:, None] * inv[None, :]
    shape = (1, x.shape[1]) + (1,) * (x.ndim - 3) + (half,)
    cos, sin = jnp.cos(ang).reshape(shape), jnp.sin(ang).reshape(shape)
    xf = x.astype(jnp.float32)
    x1, x2 = xf[..., :half], xf[..., half:]
    return jnp.concatenate([x1 * cos - x2 * sin, x2 * cos + x1 * sin], axis=-1).astype(x.dtype)


def _axial_rope(x, row, col):
    half = x.shape[-1] // 2
    return jnp.concatenate([_rope(x[..., :half], row), _rope(x[..., half:], col)], axis=-1)


def _to_blocks(q):
    b, L = q.shape[:2]
    return jnp.moveaxis(q.reshape((b, L // Q_BLOCK, Q_BLOCK) + q.shape[2:]), 1, 0)


def _from_blocks(o):
    o = jnp.moveaxis(o, 0, 1)
    return o.reshape((o.shape[0], -1) + o.shape[3:])


def _centred_dwconv(u, w, bias):
    pad = D_CONV // 2
    out = lax.conv_general_dilated(
        u, w[:, None, :].astype(u.dtype), window_strides=(1,), padding=[(pad, pad)],
        dimension_numbers=("NWC", "WIO", "NWC"), feature_group_count=u.shape[-1])
    return out + bias.astype(u.dtype)


def _segsum(a):
    T = a.shape[-1]
    aa = jnp.broadcast_to(a[..., :, None], a.shape + (T,))
    aa = jnp.where(jnp.tril(jnp.ones((T, T), bool), -1), aa, 0.0)
    cs = jnp.cumsum(aa, axis=-2)
    return jnp.where(jnp.tril(jnp.ones((T, T), bool)), cs, -jnp.inf)


def _ssd(x, dt, A, bm, cm):
    b, L, h, p = x.shape
    nc = L // SSM_CHUNK
    chunk = lambda t: t.reshape((b, nc, SSM_CHUNK) + t.shape[2:])
    xdt = chunk(x * dt[..., None])
    bm, cm = chunk(bm), chunk(cm)
    a = jnp.moveaxis(chunk(dt * A), -1, 1)
    a_cum = jnp.cumsum(a, axis=-1)
    scores = jnp.einsum("bclhn,bcshn->bhcls", cm, bm) * jnp.exp(_segsum(a))
    y_diag = jnp.einsum("bhcls,bcshp->bclhp", scores, xdt)
    decay_states = jnp.moveaxis(jnp.exp(a_cum[..., -1:] - a_cum), 1, -1)
    states = jnp.einsum("bclhn,bclhp->bchpn", bm, xdt * decay_states[..., None])
    states = jnp.concatenate([jnp.zeros_like(states[:, :1]), states], axis=1)
    decay_chunk = jnp.exp(_segsum(jnp.pad(a_cum[..., -1], ((0, 0), (0, 0), (1, 0)))))
    states = jnp.einsum("bhzc,bchpn->bzhpn", decay_chunk, states)[:, :-1]
    state_decay_out = jnp.moveaxis(jnp.exp(a_cum), 1, -1)
    y_off = jnp.einsum("bclhn,bchpn->bclhp", cm, states) * state_decay_out[..., None]
    return (y_diag + y_off).reshape(b, L, h, p)


def _mamba_mixer(z, xbc, dt_raw, conv_w, conv_b, a_log, dt_bias, d_skip, norm_w):
    b, L, _ = z.shape
    xbc = jax.nn.silu(_centred_dwconv(xbc, conv_w, conv_b))
    xs, bm, cm = _split(xbc, (BRANCH, SSM_GROUPS * SSM_STATE, SSM_GROUPS * SSM_STATE))
    rep = SSM_HEADS // SSM_GROUPS
    xs = xs.reshape(b, L, SSM_HEADS, SSM_HEAD).astype(jnp.float32)
    bm = jnp.repeat(bm.reshape(b, L, SSM_GROUPS, SSM_STATE), rep, axis=2).astype(jnp.float32)
    cm = jnp.repeat(cm.reshape(b, L, SSM_GROUPS, SSM_STATE), rep, axis=2).astype(jnp.float32)
    dt = jax.nn.softplus(dt_raw.astype(jnp.float32).reshape(b, L, 2, SSM_HEADS) + dt_bias.astype(jnp.float32))
    A = -jnp.exp(a_log.astype(jnp.float32))
    flip = lambda t: jnp.flip(t, axis=1)
    y_f = _ssd(xs, dt[:, :, 0], A[0], bm, cm)
    y_b = flip(_ssd(flip(xs), flip(dt[:, :, 1]), A[1], flip(bm), flip(cm)))
    y = y_f + y_b + xs * d_skip.astype(jnp.float32)[:, None]
    y = y.reshape(b, L, BRANCH).astype(z.dtype)
    yg = (y * jax.nn.silu(z)).reshape(b, L, SSM_GROUPS, BRANCH // SSM_GROUPS)
    return _rms_norm(yg, norm_w.reshape(SSM_GROUPS, -1)).reshape(b, L, BRANCH)


def _token_shift_lerp(u, mu):
    prev = jnp.pad(u, ((0, 0), (1, 0), (0, 0)))[:, :-1]
    return u + (prev - u) * mu


def _wkv7_scan(r, w, k, v, a, bb):
    bsz, L, h, dk = r.shape
    tm = lambda t: jnp.moveaxis(t, 1, 0)

    def step(S, inp):
        r_t, w_t, k_t, v_t, a_t, b_t = inp
        sa = jnp.einsum("bhvk,bhk->bhv", S, a_t)
        S = S * w_t[:, :, None, :] + sa[..., None] * b_t[:, :, None, :] + v_t[..., None] * k_t[:, :, None, :]
        return S, jnp.einsum("bhvk,bhk->bhv", S, r_t)

    S0 = jnp.zeros((bsz, h, dk, dk), jnp.float32)
    _, y = lax.scan(step, S0, (tm(r), tm(w), tm(k), tm(v), tm(a), tm(bb)))
    return jnp.moveaxis(y, 0, 1)


def _rwkv_direction(u, mu, w0, w2, a0, a2, k_k, k_a, r_k):
    b, L, _ = u.shape
    u = _token_shift_lerp(u, mu)
    r, k, v, wd, ad = _split(u, (BRANCH, BRANCH, BRANCH, RWKV_RANK, RWKV_RANK))
    w = -jax.nn.softplus(-(w0 + jnp.tanh(wd) @ w2)) - 0.5
    decay = jnp.exp(-jnp.exp(w))
    a = jax.nn.sigmoid(a0 + ad @ a2)
    heads = lambda t: t.reshape(b, L, RWKV_HEADS, RWKV_HEAD)
    kk = heads(k * k_k)
    kk = kk / jnp.maximum(jnp.sqrt(jnp.sum(kk * kk, axis=-1, keepdims=True)), 1e-12)
    k = k * (1.0 + (a - 1.0) * k_a)
    r, k, v, decay, a = heads(r), heads(k), heads(v), heads(decay), heads(a)
    wkv = _wkv7_scan(r, decay, k, v, -kk, kk * a)
    bonus = jnp.sum(r * k * r_k, axis=-1, keepdims=True) * v
    return wkv, bonus


def _rwkv_mixer(u, mu, w0, w2, a0, a2, k_k, k_a, r_k, ln_w, ln_b):
    b, L, _ = u.shape
    f = lambda t: t.astype(jnp.float32)
    uf = f(u)
    shared = (f(a0), f(a2), f(k_k), f(k_a), f(r_k))
    flip = lambda t: jnp.flip(t, axis=1)
    wkv_f, bonus_f = _rwkv_direction(uf, f(mu[0]), f(w0[0]), f(w2[0]), *shared)
    wkv_b, bonus_b = _rwkv_direction(flip(uf), f(mu[1]), f(w0[1]), f(w2[1]), *shared)
    wkv = wkv_f + flip(wkv_b)
    m = jnp.mean(wkv, axis=-1, keepdims=True)
    var = jnp.mean(jnp.square(wkv - m), axis=-1, keepdims=True)
    gn = ((wkv - m) * lax.rsqrt(var + RWKV_GN_EPS)).reshape(b, L, BRANCH) * f(ln_w) + f(ln_b)
    y = gn + (bonus_f + flip(bonus_b)).reshape(b, L, BRANCH)
    return y.astype(u.dtype)


def _diff_mixer(q, k, v, lam_params, norm_w, pos, lambda_init):
    b, L, _ = q.shape
    q = _rope(q.reshape(b, L, DIFF_HEADS, 2, DIFF_HEAD), pos)
    k = _rope(k.reshape(b, L, DIFF_HEADS, 2, DIFF_HEAD), pos)
    v = v.reshape(b, L, DIFF_HEADS, 2 * DIFF_HEAD)
    lp = lam_params.astype(jnp.float32)
    lam = jnp.exp(jnp.sum(lp[0] * lp[1])) - jnp.exp(jnp.sum(lp[2] * lp[3])) + lambda_init
    scale = DIFF_HEAD ** -0.5

    def attend(qb):
        s = jnp.einsum("bqhcd,bkhcd->bhcqk", qb, k).astype(jnp.float32) * scale
        p = jax.nn.softmax(s, axis=-1)
        amap = (p[:, :, 0] - lam * p[:, :, 1]).astype(v.dtype)
        return jnp.einsum("bhqk,bkhe->bqhe", amap, v)

    o = _from_blocks(lax.map(attend, _to_blocks(q)))
    o = _rms_norm(o, norm_w) * (1.0 - lambda_init)
    return o.reshape(b, L, BRANCH)


def _gqa_mixer(q, k, v, q_norm_w, k_norm_w, row, col):
    b, L, _ = q.shape
    q = _axial_rope(_rms_norm(q.reshape(b, L, GQA_Q_HEADS, GQA_HEAD), q_norm_w), row, col)
    k = _axial_rope(_rms_norm(k.reshape(b, L, GQA_KV_HEADS, GQA_HEAD), k_norm_w), row, col)
    v = v.reshape(b, L, GQA_KV_HEADS, GQA_HEAD)
    q = q.reshape(b, L, GQA_KV_HEADS, GQA_Q_HEADS // GQA_KV_HEADS, GQA_HEAD)
    scale = GQA_HEAD ** -0.5

    def attend(qb):
        s = jnp.einsum("bqgrd,bkgd->bgrqk", qb, k).astype(jnp.float32) * scale
        p = jax.nn.softmax(s, axis=-1).astype(v.dtype)
        return jnp.einsum("bgrqk,bkgd->bqgrd", p, v)

    o = _from_blocks(lax.map(attend, _to_blocks(q)))
    return o.reshape(b, L, BRANCH)


def setup_inputs(seed: int = 0) -> dict:
    key = jax.random.key(seed)
    ks = iter(jax.random.split(key, 32))
    nrm = lambda shape, s: jax.random.normal(next(ks), shape, jnp.float32) * s
    uni = lambda shape, lo, hi: jax.random.uniform(next(ks), shape, jnp.float32, lo, hi)
    n = DEPTH
    x = nrm((BATCH, SEQ, D_MODEL), 1.0)
    pre_norm_w = 1.0 + nrm((n, D_MODEL), 0.02)
    post_norm_w = 1.0 + nrm((n, D_MODEL), 0.02)
    w_in = nrm((n, D_MODEL, D_IN_PROJ), D_MODEL ** -0.5)
    w_out = nrm((n, D_MIX, D_MODEL), D_MIX ** -0.5)
    conv_w = nrm((n, D_CONV, SSM_CONV_CH), D_CONV ** -0.5)
    conv_b = nrm((n, SSM_CONV_CH), 0.01)
    ssm_a_log = jnp.log(uni((n, 2, SSM_HEADS), 1.0, 16.0))
    dt0 = jnp.exp(uni((n, 2, SSM_HEADS), math.log(1e-3), math.log(1e-1)))
    ssm_dt_bias = dt0 + jnp.log(-jnp.expm1(-dt0))
    ssm_d = 1.0 + nrm((n, SSM_HEADS), 0.02)
    ssm_norm_w = 1.0 + nrm((n, BRANCH), 0.02)
    rwkv_mu = uni((n, 2, RWKV_SHIFT_CH), 0.0, 1.0)
    rwkv_w0 = uni((n, 2, BRANCH), -5.0, 0.0)
    rwkv_w2 = nrm((n, 2, RWKV_RANK, BRANCH), 0.1)
    rwkv_a0 = nrm((n, BRANCH), 0.1)
    rwkv_a2 = nrm((n, RWKV_RANK, BRANCH), 0.1)
    rwkv_k_k = 0.85 + nrm((n, BRANCH), 0.02)
    rwkv_k_a = 1.0 + nrm((n, BRANCH), 0.02)
    rwkv_r_k = nrm((n, RWKV_HEADS, RWKV_HEAD), 0.1)
    rwkv_ln_w = 1.0 + nrm((n, BRANCH), 0.02)
    rwkv_ln_b = nrm((n, BRANCH), 0.01)
    diff_lambda = nrm((n, 4, DIFF_HEAD), 0.1)
    diff_norm_w = 1.0 + nrm((n, 2 * DIFF_HEAD), 0.02)
    gqa_q_norm_w = 1.0 + nrm((n, GQA_HEAD), 0.02)
    gqa_k_norm_w = 1.0 + nrm((n, GQA_HEAD), 0.02)
    return {"x": x, "pre_norm_w": pre_norm_w, "post_norm_w": post_norm_w, "w_in": w_in, "w_out": w_out,
            "conv_w": conv_w, "conv_b": conv_b, "ssm_a_log": ssm_a_log, "ssm_dt_bias": ssm_dt_bias,
            "ssm_d": ssm_d, "ssm_norm_w": ssm_norm_w, "rwkv_mu": rwkv_mu, "rwkv_w0": rwkv_w0,
            "rwkv_w2": rwkv_w2, "rwkv_a0": rwkv_a0, "rwkv_a2": rwkv_a2, "rwkv_k_k": rwkv_k_k,
            "rwkv_k_a": rwkv_k_a, "rwkv_r_k": rwkv_r_k, "rwkv_ln_w": rwkv_ln_w, "rwkv_ln_b": rwkv_ln_b,
            "diff_lambda": diff_lambda, "diff_norm_w": diff_norm_w, "gqa_q_norm_w": gqa_q_norm_w,
            "gqa_k_norm_w": gqa_k_norm_w}


def reference(x, pre_norm_w, post_norm_w, w_in, w_out, conv_w, conv_b, ssm_a_log, ssm_dt_bias, ssm_d,
              ssm_norm_w, rwkv_mu, rwkv_w0, rwkv_w2, rwkv_a0, rwkv_a2, rwkv_k_k, rwkv_k_a, rwkv_r_k,
              rwkv_ln_w, rwkv_ln_b, diff_lambda, diff_norm_w, gqa_q_norm_w, gqa_k_norm_w):
    L = x.shape[1]
    rows = L // GRID_W
    pos = jnp.arange(L, dtype=jnp.int32)
    row = jnp.repeat(jnp.arange(rows, dtype=jnp.int32), GRID_W)
    col = jnp.tile(jnp.arange(GRID_W, dtype=jnp.int32), rows)
    for i in range(DEPTH):
        lambda_init = 0.8 - 0.6 * math.exp(-0.3 * i)
        h = _rms_norm(x, pre_norm_w[i])
        proj = jnp.einsum("bld,de->ble", h, w_in[i])
        (m_z, m_xbc, m_dt, r_u, r_g, d_q, d_k, d_v, d_g, g_q, g_k, g_v, g_g) = _split(proj, PROJ_SIZES)
        y_a = _mamba_mixer(m_z, m_xbc, m_dt, conv_w[i], conv_b[i], ssm_a_log[i], ssm_dt_bias[i],
                           ssm_d[i], ssm_norm_w[i])
        y_b = _rwkv_mixer(r_u, rwkv_mu[i], rwkv_w0[i], rwkv_w2[i], rwkv_a0[i], rwkv_a2[i], rwkv_k_k[i],
                          rwkv_k_a[i], rwkv_r_k[i], rwkv_ln_w[i], rwkv_ln_b[i]) * jax.nn.silu(r_g)
        y_c = _diff_mixer(d_q, d_k, d_v, diff_lambda[i], diff_norm_w[i], pos, lambda_init) * jax.nn.silu(d_g)
        y_d = _gqa_mixer(g_q, g_k, g_v, gqa_q_norm_w[i], gqa_k_norm_w[i], row, col) * jax.nn.silu(g_g)
        mix = jnp.einsum("ble,ed->bld", jnp.concatenate([y_a, y_b, y_c, y_d], axis=-1), w_out[i])
        x = x + _rms_norm(mix, post_norm_w[i])
    return x
```

```python
import math
import numpy as np
import concourse.bass as bass
import concourse.mybir as mybir

F32 = mybir.dt.float32
BF16 = mybir.dt.bfloat16
ALU = mybir.AluOpType
AF = mybir.ActivationFunctionType
AX = mybir.AxisListType

SEM_CHUNK = 30000


class Buf:
    __slots__ = ("name", "last_w", "readers", "dma_sem", "dma_cnt", "is_dram", "psum", "wlist", "inc_val")

    def __init__(self, name, is_dram=False, psum=False):
        self.psum = psum
        self.wlist = {}
        self.inc_val = 16
        self.name = name
        self.last_w = None
        self.readers = []
        self.dma_sem = None
        self.dma_cnt = 0
        self.is_dram = is_dram


class T:
    __slots__ = ("buf", "ap")

    def __init__(self, buf, ap):
        self.buf = buf
        self.ap = ap

    def __getitem__(self, idx):
        return T(self.buf, self.ap[idx])

    def re(self, pattern, **kw):
        return T(self.buf, self.ap.rearrange(pattern, **kw))

    def sub(self, buf, idx=None):
        return T(buf, self.ap if idx is None else self.ap[idx])


class Op:
    __slots__ = ("eng", "fn", "reads", "writes", "is_dma", "deps", "inc_idx", "dma_buf",
                 "dma_wait", "gi")

    def __init__(self, eng, fn, reads, writes, is_dma=False, dma_buf=None):
        self.eng = eng
        self.fn = fn
        self.reads = reads
        self.writes = writes
        self.is_dma = is_dma
        self.deps = []
        self.inc_idx = None
        self.dma_buf = dma_buf
        self.dma_wait = []
        self.gi = None


class KB:
    ENGS = ("pe", "act", "dve", "pool", "sp")

    ARENA_WORDS = 53100

    def __init__(self, nc, arena=False):
        self.nc = nc
        self.arena = None
        if arena:
            self.arena = nc.alloc_sbuf_tensor("arena", [128, self.ARENA_WORDS], F32).ap()
            self.a_off = 0
            self.a_peak = 0
            self.banks = [T(Buf("bank%d" % i, psum=True),
                            nc.alloc_psum_tensor("gbank%d" % i, [128, 512], F32).ap()) for i in range(8)]
            self.b_next = 0
        self.ops = []
        self.e = {"pe": nc.tensor, "act": nc.scalar, "dve": nc.vector, "pool": nc.gpsimd,
                  "sp": nc.sync}
        self._n = 0

    def sb(self, name, shape, dtype=F32):
        if self.arena is None:
            h = self.nc.alloc_sbuf_tensor(name, list(shape), dtype)
            return T(Buf(name), h.ap())
        shape = list(shape)
        esz = 2 if dtype == BF16 else 4
        n = 1
        for d in shape[1:]:
            n *= d
        nbytes = (n * esz + 31) // 32 * 32
        nw = nbytes // 4
        off = self.a_off
        assert off + nw <= self.ARENA_WORDS, "arena overflow at %s: %d + %d" % (name, off, nw)
        self.a_off = off + nw
        self.a_peak = max(self.a_peak, self.a_off)
        ap = self.arena[0:shape[0], off:off + (n * esz + 3) // 4]
        if dtype == BF16:
            ap = ap.bitcast(BF16)
            if (n * esz) % 4:
                ap = ap[:, 0:n]
        if len(shape) == 3:
            ap = ap.rearrange("p (a b) -> p a b", a=shape[1])
        elif len(shape) == 4:
            ap = ap.rearrange("p (a b c) -> p a b c", a=shape[1], b=shape[2])
        return T(Buf(name), ap)

    def ps(self, name, shape, dtype=F32):
        if self.arena is None:
            h = self.nc.alloc_psum_tensor(name, list(shape), dtype)
            return T(Buf(name, psum=True), h.ap())
        bk = self.banks[self.b_next]
        self.b_next += 1
        if dtype == BF16:
            return T(bk.buf, bk.ap.bitcast(BF16))
        return bk

    def phase_reset(self):
        self._rec("bar", None, [], [])
        self.a_off = 0
        self.b_next = 0

    def dram_in(self, name, shape, dtype=F32):
        h = self.nc.dram_tensor(name, list(shape), dtype, kind="ExternalInput")
        return T(Buf(name, True), h.ap())

    def dram_out(self, name, shape, dtype=F32):
        h = self.nc.dram_tensor(name, list(shape), dtype, kind="ExternalOutput")
        return T(Buf(name, True), h.ap())

    def dram_scratch(self, name, shape, dtype=F32):
        h = self.nc.dram_tensor(name, list(shape), dtype)
        return T(Buf(name, True), h.ap())

    def buf(self, name):
        self._n += 1
        return Buf("%s_%d" % (name, self._n))

    def _rec(self, eng, fn, reads, writes, is_dma=False, dma_buf=None):
        rb = []
        for t in reads:
            if t is None or isinstance(t, (int, float)):
                continue
            b = t.buf if isinstance(t, T) else t
            if b not in rb:
                rb.append(b)
        wb = []
        for t in writes:
            b = t.buf if isinstance(t, T) else t
            if b not in wb:
                wb.append(b)
        op = Op(eng, fn, rb, wb, is_dma, dma_buf)
        if getattr(self, "_defer", None) is not None:
            self._defer.append(op)
            return op
        op.gi = len(self.ops)
        self.ops.append(op)
        return op

    def begin_defer(self):
        self._defer = []

    def end_defer(self):
        lst = self._defer
        self._defer = None
        return lst

    def splice(self, lst, n):
        for _ in range(min(n, len(lst))):
            op = lst.pop(0)
            op.gi = len(self.ops)
            self.ops.append(op)

    @staticmethod
    def _a(x):
        return x.ap if isinstance(x, T) else x

    def dma(self, out, in_, eng="sp", **kw):
        o, i = self._a(out), self._a(in_)
        sbuf_side = out.buf if not out.buf.is_dram else in_.buf
        return self._rec(eng, lambda E: E.dma_start(out=o, in_=i, **kw), [in_], [out],
                         is_dma=True, dma_buf=sbuf_side)

    def matmul(self, out, lhsT, rhs, start=True, stop=True, extra_reads=(), **kw):
        o, l, r = self._a(out), self._a(lhsT), self._a(rhs)
        reads = [lhsT, rhs] + list(extra_reads)
        if not start:
            reads.append(out)
        return self._rec("pe", lambda E: E.matmul(o, l, r, start=start, stop=stop, **kw),
                         reads, [out])

    def transpose(self, out, in_, ident):
        o, i, d = self._a(out), self._a(in_), self._a(ident)
        return self._rec("pe", lambda E: E.transpose(o, i, d), [in_, ident], [out])

    def act(self, out, in_, func, bias=None, scale=None, accum_out=None, eng="act"):
        o, i = self._a(out), self._a(in_)
        kw = {}
        reads = [in_]
        writes = [out]
        if bias is not None:
            kw["bias"] = self._a(bias)
            reads.append(bias)
        if scale is not None:
            kw["scale"] = self._a(scale)
            reads.append(scale)
        if accum_out is not None:
            kw["accum_out"] = self._a(accum_out)
            writes.append(accum_out)
        return self._rec(eng, lambda E: E.activation(o, i, func, **kw), reads, writes)

    def tt(self, out, in0, in1, op, eng="dve"):
        o, a, b = self._a(out), self._a(in0), self._a(in1)
        return self._rec(eng, lambda E: E.tensor_tensor(o, a, b, op), [in0, in1], [out])

    def ts(self, out, in0, s1, op0, s2=None, op1=None, accum_out=None, eng="dve"):
        o, a = self._a(out), self._a(in0)
        s1a, s2a = self._a(s1), self._a(s2)
        kw = {}
        writes = [out]
        if op1 is not None:
            kw["op1"] = op1
        if accum_out is not None:
            kw["accum_out"] = self._a(accum_out)
            writes.append(accum_out)
        return self._rec(eng, lambda E: E.tensor_scalar(o, a, s1a, s2a, op0, **kw),
                         [in0, s1, s2], writes)

    def stt(self, out, in0, scalar, in1, op0, op1, eng="dve"):
        o, a, s, b = self._a(out), self._a(in0), self._a(scalar), self._a(in1)
        return self._rec(eng, lambda E: E.scalar_tensor_tensor(o, a, s, b, op0, op1),
                         [in0, scalar, in1], [out])

    def copy(self, out, in_, eng="dve"):
        o, i = self._a(out), self._a(in_)
        if eng == "act":
            return self._rec(eng, lambda E: E.copy(o, i), [in_], [out])
        return self._rec(eng, lambda E: E.tensor_copy(o, i), [in_], [out])

    def memset(self, out, val, eng="pool"):
        o = self._a(out)
        return self._rec(eng, lambda E: E.memset(o, val), [], [out])

    def recip(self, out, in_):
        o, i = self._a(out), self._a(in_)
        return self._rec("dve", lambda E: E.reciprocal(o, i), [in_], [out])

    def reduce(self, out, in_, op, axis=AX.X, eng="dve"):
        o, i = self._a(out), self._a(in_)
        return self._rec(eng, lambda E: E.tensor_reduce(o, i, axis, op), [in_], [out])

    def affine_select(self, out, in_, pattern, compare_op, fill, base=0, channel_multiplier=0):
        o, i = self._a(out), self._a(in_)
        return self._rec("pool", lambda E: E.affine_select(
            o, i, pattern, compare_op, fill, base=base, channel_multiplier=channel_multiplier),
            [in_], [out])

    def iota(self, out, pattern, base=0, channel_multiplier=0, **kw):
        o = self._a(out)
        return self._rec("pool", lambda E: E.iota(o, pattern, base=base,
                                                   channel_multiplier=channel_multiplier, **kw),
                         [], [out])

    def collective(self, kind, in_, out, groups, op=None):
        i, o = self._a(in_), self._a(out)
        alu = ALU.bypass if op is None else op
        semb = Buf("cc_%d" % len(self.ops))
        semb.inc_val = 1
        return self._rec("pool", lambda E: E.collective_compute(kind, alu, groups, [i], [o]),
                         [in_], [out], is_dma=True, dma_buf=semb)

    def generic(self, eng, fn, reads, writes):
        return self._rec(eng, fn, reads, writes)

    def finish(self, final_wait_outputs=True):
        nc = self.nc
        ops = self.ops
        last_on = {}
        last_dma = {}
        bar_deps = {e: None for e in self.ENGS}
        for op in ops:
            if op.eng == "bar":
                allp = set(last_on.values()) | set(last_dma.values())
                for e in self.ENGS:
                    bar_deps[e] = set(allp) | (bar_deps[e] or set())
                continue
            deps = set()
            for b in op.reads:
                if b.last_w is not None:
                    deps.add(b.last_w)
                if b.is_dram:
                    for w_ in b.wlist.values():
                        deps.add(w_)
                if b.psum:
                    for r in b.readers:
                        if ops[r].eng != op.eng:
                            deps.add(r)
            for b in op.writes:
                if b.last_w is not None:
                    deps.add(b.last_w)
                for r in b.readers:
                    deps.add(r)
            deps.discard(op.gi)
            keep = []
            if bar_deps[op.eng] is not None:
                keep.extend(sorted(bar_deps[op.eng]))
                bar_deps[op.eng] = None
            last_on[op.eng] = op.gi
            if op.is_dma:
                last_dma[id(op.dma_buf)] = op.gi
            for d in deps:
                dop = ops[d]
                if dop.is_dma:
                    keep.append(d)
                    continue
                if dop.eng == op.eng:
                    if op.eng in ("pe", "sp"):
                        continue
                    if op.is_dma:
                        keep.append(d)
                        continue
                    raw = any(b.last_w == d for b in op.reads)
                    if not raw:
                        continue
                keep.append(d)
            op.deps = keep
            for b in op.reads:
                b.readers.append(op.gi)
            for b in op.writes:
                b.last_w = op.gi
                b.readers = []
                if b.is_dram and op.is_dma:
                    b.wlist[id(op.dma_buf)] = op.gi
        needed = set()
        for op in ops:
            for d in op.deps:
                needed.add(d)
        final_dma = [op for op in ops if op.is_dma and any(b.is_dram for b in op.writes)]
        ecount = {e: 0 for e in self.ENGS}
        phase = 0
        nslot = 0
        slot_total = []
        slot_of = {}
        ncc = 0
        abs_cnt = {}
        sem_key = {}
        for op in ops:
            if op.eng == "bar":
                phase += 1
                nslot = 0
                continue
            if op.is_dma:
                b = op.dma_buf
                if b.inc_val == 1:
                    if id(b) not in sem_key:
                        sem_key[id(b)] = ("c", ncc)
                        ncc += 1
                        b.dma_cnt = 0
                    b.dma_cnt += 1
                    abs_cnt[op.gi] = b.dma_cnt
                else:
                    ps_ = slot_of.get(id(b))
                    if ps_ is None or ps_[0] != phase:
                        slot_of[id(b)] = (phase, nslot)
                        if nslot >= len(slot_total):
                            slot_total.append(0)
                        sem_key[id(b)] = ("p", nslot)
                        nslot += 1
                    sl = slot_of[id(b)][1]
                    slot_total[sl] += 1
                    abs_cnt[op.gi] = slot_total[sl]
                op.inc_idx = abs_cnt[op.gi]
            elif op.gi in needed:
                ecount[op.eng] += 1
                op.inc_idx = ecount[op.eng]
        import contextlib
        self._stack = contextlib.ExitStack()
        self._sems = {}
        nsem = 0
        for e in self.ENGS:
            n = max((ecount[e] + SEM_CHUNK - 1) // SEM_CHUNK, 1)
            self._sems[e] = [self._stack.enter_context(nc.semaphore("s_%s_%d" % (e, k))) for k in range(n)]
            nsem += n
        dsem = {}
        for i in range(len(slot_total)):
            dsem[("p", i)] = self._stack.enter_context(nc.semaphore("d_%d" % i))
        for i in range(ncc):
            dsem[("c", i)] = self._stack.enter_context(nc.semaphore("c_%d" % i))
        nsem += len(dsem)
        self.nsem = nsem
        waited = {}
        last_abs = {}
        cur_key = {}
        op_key = {}
        phase = 0
        nslot = 0
        slot_of2 = {}
        for op in ops:
            if op.eng == "bar":
                phase += 1
                nslot = 0
                continue
            if op.is_dma:
                b = op.dma_buf
                if b.inc_val == 1:
                    op_key[op.gi] = sem_key[id(b)]
                else:
                    ps_ = slot_of2.get(id(b))
                    if ps_ is None or ps_[0] != phase:
                        slot_of2[id(b)] = (phase, nslot)
                        nslot += 1
                    op_key[op.gi] = ("p", slot_of2[id(b)][1])
        plan = {e: [] for e in self.ENGS}
        for op in ops:
            if op.eng == "bar":
                continue
            waits = {}
            for d in op.deps:
                dop = ops[d]
                if dop.is_dma:
                    b = dop.dma_buf
                    key = ("d",) + cur_key[id(b)]
                    sem = dsem[cur_key[id(b)]]
                    val = b.inc_val * last_abs[id(b)]
                else:
                    idx = dop.inc_idx - 1
                    ch = idx // SEM_CHUNK
                    val = idx % SEM_CHUNK + 1
                    key = ("e", dop.eng, ch)
                    sem = self._sems[dop.eng][ch]
                    for c2 in range(ch):
                        waited[(op.eng, ("e", dop.eng, c2))] = SEM_CHUNK
                cur = waited.get((op.eng, key), 0)
                if val > cur:
                    waited[(op.eng, key)] = val
                    if key not in waits or waits[key][1] < val:
                        waits[key] = (sem, val)
            plan[op.eng].append((op, list(waits.values())))
            if op.is_dma:
                last_abs[id(op.dma_buf)] = abs_cnt[op.gi]
                cur_key[id(op.dma_buf)] = op_key[op.gi]
        self.stats = {e: len(plan[e]) for e in self.ENGS}
        self.stats["nsem"] = nsem
        self.stats["waits"] = sum(len(w) for e in self.ENGS for _, w in plan[e])
        if self.arena is not None:
            self.stats["arena_peak_words"] = self.a_peak
        for e in self.ENGS:
            E = self.e[e]
            for op, waits in plan[e]:
                for sem, val in waits:
                    E.wait_ge(sem, val)
                ins = op.fn(E)
                if op.is_dma:
                    ins.then_inc(dsem[op_key[op.gi]], op.dma_buf.inc_val)
                elif op.inc_idx is not None:
                    idx = op.inc_idx - 1
                    ins.then_inc(self._sems[e][idx // SEM_CHUNK], 1)
        if final_wait_outputs:
            fin = {}
            for op in final_dma:
                kk = op_key[op.gi]
                v = op.dma_buf.inc_val * abs_cnt[op.gi]
                if kk not in fin or fin[kk] < v:
                    fin[kk] = v
            for kk, v in fin.items():
                nc.sync.wait_ge(dsem[kk], v)
        return self.stats


L = 8192
NT = L // 512
NCH = L // 128
NORM_EPS = 1e-6
GN_EPS = 64e-5
C0 = math.exp(-0.5)


def load_hT(k, hT_tile, hsrc, j):
    v = hT_tile.re("p (g q) t -> p g q t", q=2)
    for r in range(4):
        k.dma(v[(r % 2) * 64:(r % 2) * 64 + 64, :, r // 2, :],
              hsrc[r * 256:(r + 1) * 256, j * 512:(j + 1) * 512].re("(g i) t -> i g t", i=64),
              eng="sp" if r % 2 == 0 else "pool")


L = 8192
NT = L // 512
NORM_EPS = 1e-6


def emit_prologue(k, io):
    xT, pw, hdst = io["xT"], io["pw"], io["hdst"]
    ones_bf = k.sb("ones_bf", [128, 128], BF16)
    k.memset(ones_bf, 1.0)
    eps_t = k.sb("eps_t", [128, 1])
    k.memset(eps_t, NORM_EPS)
    pw_sb = k.sb("pw_sb", [128, 8])
    k.dma(pw_sb, pw)
    xbuf = [k.sb("xbuf%d" % i, [128, 8, 512]) for i in range(2)]
    hT = [k.sb("hT%d" % i, [128, 8, 512], BF16) for i in range(2)]
    sq = [k.sb("sq%d" % i, [128, 512], BF16) for i in range(3)]
    rstd = [k.sb("rstd%d" % i, [128, 512]) for i in range(2)]
    bank = [k.ps("bank%d" % i, [128, 512]) for i in range(2)]
    for j in range(NT):
        xt = xbuf[j % 2]
        k.dma(xt, xT[:, j * 512:(j + 1) * 512].re("(c p) t -> p c t", p=128), eng="sp")
        ps = bank[j % 2]
        for c in range(8):
            s = sq[c % 3]
            k.act(s, xt[:, c, :], AF.Square)
            k.matmul(ps, ones_bf, s, start=(c == 0), stop=(c == 7))
        r = rstd[j % 2]
        k.act(r, ps, AF.Sqrt, scale=1.0 / 1024, bias=eps_t)
        k.recip(r, r)
        h = hT[j % 2]
        for c in range(8):
            k.stt(h[:, c, :], xt[:, c, :], pw_sb[:, c:c + 1], r, ALU.mult, ALU.mult)
        v = h.re("p (g q) t -> p g q t", q=2)
        for rr in range(4):
            k.dma(hdst[rr * 256:(rr + 1) * 256, j * 512:(j + 1) * 512].re("(g i) t -> i g t", i=64),
                  v[(rr % 2) * 64:(rr % 2) * 64 + 64, :, rr // 2, :], eng="sp" if rr % 2 == 0 else "pool")


ATT_KNOBS = {}


def emit_attn(k, io, lambda_init):
    xT, hsrc, pw, wfm, wtm, tabs, vecs, lam = (io.get("xT"), io.get("hT"), io["pw"], io["wfm"], io["wtm"],
                                                io["tabs"], io["vecs"], io["lam"])
    ydst, ydt = io["y"], io["ydt"]

    ones_bf = k.sb("ones_bf", [128, 128], BF16)
    k.memset(ones_bf, 1.0)
    eps_t = k.sb("eps_t", [128, 1])
    k.memset(eps_t, NORM_EPS)
    pw_sb = k.sb("pw_sb", [128, 8])
    k.dma(pw_sb, pw)
    vec_sb = k.sb("vec_sb", [128, 8])
    k.dma(vec_sb, vecs)
    lam_sb = k.sb("lam_sb", [128, 256])
    k.dma(lam_sb, lam)
    w_fm = k.sb("w_fm", [128, 8, 1280], BF16)
    w_tm = k.sb("w_tm", [128, 8, 256], BF16)
    wst = [k.sb("wst%d" % i, [128, 8, 256]) for i in range(2)]
    for i in range(6):
        st = wst[i % 2]
        if i < 5:
            k.dma(st, wfm[:, i * 256:(i + 1) * 256].re("(c p) m -> p c m", p=128),
                  eng="pool" if i % 2 else "sp")
            k.copy(w_fm[:, :, i * 256:(i + 1) * 256], st, eng="pool")
        else:
            k.dma(st, wtm.re("(c p) m -> p c m", p=128), eng="pool")
            k.copy(w_tm, st, eng="pool")

    lt = k.sb("lam_t", [128, 128])
    s12 = k.sb("lam_s", [128, 2])
    k.tt(lt[:, 0:64], lam_sb[:, 0:64], lam_sb[:, 64:128], ALU.mult)
    k.tt(lt[:, 64:128], lam_sb[:, 128:192], lam_sb[:, 192:256], ALU.mult)
    k.reduce(s12[:, 0:1], lt[:, 0:64], ALU.add)
    k.reduce(s12[:, 1:2], lt[:, 64:128], ALU.add)
    e12 = k.sb("lam_e", [128, 2])
    k.act(e12, s12, AF.Exp)
    neg_lam = k.sb("neg_lam", [128, 1])
    k.stt(neg_lam, e12[:, 1:2], -float(lambda_init), e12[:, 0:1], ALU.add, ALU.subtract)
    wn_s = k.sb("wn_s", [128, 1])
    k.ts(wn_s, vec_sb[:, 4:5], 1.0 - float(lambda_init), ALU.mult)

    xbuf = [k.sb("xbuf%d" % i, [128, 8, 512]) for i in range(2)]
    hT = [k.sb("hT%d" % i, [128, 8, 512], BF16) for i in range(2)]
    sq = [k.sb("sq%d" % i, [128, 512], BF16) for i in range(3)]
    rstd = [k.sb("rstd%d" % i, [128, 512]) for i in range(2)]
    tab = [k.sb("tab%d" % i, [128, 2, 512]) for i in range(2)]
    tmp = [k.sb("tmp%d" % i, [128, 512]) for i in range(4)]
    qT = k.sb("qT", [128, L], BF16)
    kT = k.sb("kT", [128, L], BF16)
    gT = k.sb("gT", [128, L], BF16)
    v_tm = k.sb("v_tm", [128, 64, 128], BF16)
    Pb = [[k.sb("Pb%d_%d" % (m, i), [128, 512], BF16) for i in range(3)] for m in range(2)]
    accD = [k.sb("accD%d" % m, [128, 512]) for m in range(2)]
    accP = [k.sb("accP%d" % m, [128, 512]) for m in range(2)]
    ones_f = k.sb("ones_f", [128, 128])
    k.memset(ones_f, 1.0)
    ost = [k.sb("ost%d" % i, [128, 512], ydt) for i in range(2)]
    fin = [k.sb("fin%d" % i, [128, 512]) for i in range(5)]
    bank = [k.ps("bank%d" % i, [128, 512]) for i in range(8)]

    for kind in range(2):
        wc = kind * 5
        def stage1(j):
            k.dma(tab[j % 2], tabs[2 * kind:2 * kind + 2, :, j * 512:(j + 1) * 512]
                  .re("a p t -> p a t"), eng="pool")
            if hsrc is not None:
                load_hT(k, hT[j % 2], hsrc, j)
                return
            xt = xbuf[j % 2]
            k.dma(xt, xT[:, j * 512:(j + 1) * 512].re("(c p) t -> p c t", p=128),
                  eng="sp")
            for c in range(8):
                s = sq[c % 3]
                k.act(s, xt[:, c, :], AF.Square)
                k.matmul(bank[0], ones_bf, s, start=(c == 0), stop=(c == 7))
            r = rstd[j % 2]
            k.act(r, bank[0], AF.Sqrt, scale=1.0 / 1024, bias=eps_t)
            k.recip(r, r)
            for c in range(8):
                k.stt(hT[j % 2][:, c, :], xt[:, c, :], pw_sb[:, c:c + 1], r, ALU.mult, ALU.mult)

        def proj_fm(ps, grp, j):
            h = hT[j % 2]
            for c in range(8):
                k.matmul(ps, w_fm[:, c, grp * 128:(grp + 1) * 128], h[:, c, :],
                         start=(c == 0), stop=(c == 7))

        def stage2(j):
            h = hT[j % 2]
            tb = tab[j % 2]
            ts_ = slice(j * 512, (j + 1) * 512)
            for which, dst in ((0, qT), (1, kT)):
                pa, pb = bank[1 + 2 * which], bank[2 + 2 * which]
                proj_fm(pa, wc + 2 * which, j)
                proj_fm(pb, wc + 2 * which + 1, j)
                t1, t2 = tmp[2 * which], tmp[2 * which + 1]
                if kind == 0:
                    k.tt(t1, pa, tb[:, 0, :], ALU.mult)
                    k.tt(t2, pb, tb[:, 1, :], ALU.mult)
                    k.tt(dst[:, ts_], t1, t2, ALU.add, eng="pool")
                else:
                    s = sq[which]
                    k.act(s, pa, AF.Square)
                    k.matmul(bank[7], ones_bf, s)
                    rr = fin[which]
                    k.act(rr, bank[7], AF.Sqrt, scale=1.0 / 128, bias=eps_t)
                    k.recip(rr, rr)
                    k.stt(t1, pa, vec_sb[:, 2 * which:2 * which + 1], tb[:, 0, :], ALU.mult, ALU.mult)
                    k.stt(t2, pb, vec_sb[:, 2 * which + 1:2 * which + 2], tb[:, 1, :], ALU.mult, ALU.mult)
                    k.tt(t1, t1, t2, ALU.add, eng="pool")
                    k.tt(dst[:, ts_], t1, rr, ALU.mult, eng="pool")
            proj_fm(bank[5], wc + 4, j)
            k.act(gT[:, ts_], bank[5], AF.Silu)
            pv = bank[6]
            for s4 in range(4):
                for c in range(8):
                    k.matmul(pv[:, s4 * 128:(s4 + 1) * 128], h[:, c, s4 * 128:(s4 + 1) * 128],
                             w_tm[:, c, kind * 128:(kind + 1) * 128],
                             start=(c == 0), stop=(c == 7))
            k.copy(v_tm[:, j * 4:(j + 1) * 4, :], pv.re("p (s e) -> p s e", s=4), eng="dve")

        for j in range(NT + 1):
            if j < NT:
                stage1(j)
            if j >= 1:
                stage2(j - 1)

        nm = 2 if kind == 0 else 1
        dk = 64 if kind == 0 else 128
        scale = dk ** -0.5
        O = [bank[0], bank[1]][:nm]
        Sb = [[bank[2], bank[3]], [bank[4], bank[5]]]
        misc = bank[7]
        pairs = [(qb, kc) for qb in range(16) for kc in range(64)]
        LA = 2
        npair = len(pairs)

        def finalize(qb):
            qs = slice(qb * 512, (qb + 1) * 512)
            o = []
            for m in range(nm):
                k.matmul(misc, ones_f, accD[m], start=True, stop=False)
                k.matmul(misc, ones_f, accP[m], start=False, stop=True)
                r = fin[m]
                k.recip(r, misc)
                om = fin[2 + m]
                k.tt(om, O[m], r, ALU.mult)
                o.append(om)
            y = ost[qb % 2]
            if kind == 0:
                od = fin[4]
                k.stt(od, o[1], neg_lam, o[0], ALU.mult, ALU.add)
                s = sq[0]
                k.act(s, od, AF.Square)
                k.matmul(misc, ones_bf, s)
                rr = fin[0]
                k.act(rr, misc, AF.Sqrt, scale=1.0 / 128, bias=eps_t)
                k.recip(rr, rr)
                k.stt(od, od, wn_s, rr, ALU.mult, ALU.mult)
                k.tt(y, od, gT[:, qs], ALU.mult, eng="pool")
            else:
                k.tt(y, o[0], gT[:, qs], ALU.mult, eng="pool")
            k.dma(ydst[kind][:, qs], y, eng="sp")

        for idx in range(npair + LA):
            if idx < npair:
                qb, kc = pairs[idx]
                for m in range(nm):
                    k.matmul(Sb[m][idx % 2], kT[m * dk:(m + 1) * dk, kc * 128:(kc + 1) * 128],
                             qT[m * dk:(m + 1) * dk, qb * 512:(qb + 1) * 512])
                for m in range(nm):
                    k.act(Pb[m][idx % 3], Sb[m][idx % 2], AF.Exp, scale=scale)
            i2 = idx - LA
            if i2 >= 0:
                qb, kc = pairs[i2]
                for m in range(nm):
                    P = Pb[m][i2 % 3]
                    k.matmul(O[m], v_tm[:, kc, :], P, start=(kc == 0), stop=(kc == 63))
                    PMOD = ATT_KNOBS.get('pmod', 1 << 30)
                    on_pool = (kc % PMOD == PMOD - 1)
                    acc, eng = (accP[m], "pool") if on_pool else (accD[m], "dve")
                    first = (kc == PMOD - 1) if on_pool else (kc == 0)
                    if first:
                        k.copy(acc, P, eng=eng)
                    else:
                        k.tt(acc, acc, P, ALU.add, eng=eng)
                if kc == 63:
                    finalize(qb)
        if kind == 0 and io.get("after_kind0") is not None:
            io["after_kind0"]()


def emit_ssd(k, io):
    stop = 99
    xT, hsrc, pw, wfm, wdt, cw, cb, ssmv, cst = (io.get("xT"), io.get("hT"), io["pw"], io["wfm"], io["wdt"],
                                                 io["cw"], io["cb"], io["ssmv"], io["cst"])
    yT, ydt = io["y"], io["ydt"]

    ones_bf = k.sb("ones_bf", [128, 128], BF16)
    k.memset(ones_bf, 1.0)
    ones_f = k.sb("ones_f", [128, 128])
    k.memset(ones_f, 1.0)
    eps_t = k.sb("eps_t", [128, 1])
    k.memset(eps_t, NORM_EPS)
    one_t = k.sb("one_t", [128, 1])
    k.memset(one_t, 1.0)
    cst_sb = k.sb("cst_sb", [128, 5, 128])
    k.dma(cst_sb, cst)
    tri = [cst_sb[:, 0, :], cst_sb[:, 1, :]]
    smask = [cst_sb[:, 2, :], cst_sb[:, 3, :]]
    ident_f = cst_sb[:, 4, :]
    ident_bf = k.sb("ident_bf", [128, 128], BF16)
    k.copy(ident_bf, ident_f, eng="dve")
    pw_sb = k.sb("pw_sb", [128, 8])
    k.dma(pw_sb, pw)
    cw_sb = k.sb("cw_sb", [128, 3, 5])
    k.dma(cw_sb, cw)
    cb_sb = k.sb("cb_sb", [128, 3])
    k.dma(cb_sb, cb)
    sv = k.sb("sv", [128, 16])
    k.dma(sv, ssmv)
    w_fm = k.sb("w_fm", [128, 8, 512], BF16)
    w_dt = k.sb("w_dt", [128, 8, 4], BF16)
    wst2 = k.sb("wst2", [128, 8, 4])
    k.dma(wst2, wdt.re("(c p) m -> p c m", p=128), eng="pool")
    k.copy(w_dt, wst2, eng="pool")

    xbuf = [k.sb("xbuf%d" % i, [128, 8, 512]) for i in range(1)]
    k.dma(xbuf[0], wfm.re("(c p) m -> p c m", p=128))
    k.copy(w_fm, xbuf[0], eng="pool")
    hT = [k.sb("hT%d" % i, [128, 8, 512], BF16) for i in range(2)]
    sq = [k.sb("sq%d" % i, [128, 512], BF16) for i in range(3)]
    rstd = [k.sb("rstd%d" % i, [128, 512]) for i in range(1)] * 2
    upad = [k.sb("upad%d" % i, [128, 3, 516]) for i in range(3)]
    acc = [k.sb("acc%d" % i, [128, 512]) for i in range(3)]
    zsT = k.sb("zsT", [128, L], BF16)
    xcT = k.sb("xcT", [128, L], BF16)
    BT = k.sb("BT", [128, L], BF16)
    CT = k.sb("CT", [128, L], BF16)
    dst3 = [xcT, BT, CT]
    dtraw = k.sb("dtraw", [128, 4, NCH])
    bank = [k.ps("bank%d" % i, [128, 512]) for i in range(7)]
    tbank = k.ps("tbank", [128, 1024], BF16)

    k.memset(upad[0][:, :, 0:2], 0.0)

    def stage1(j):
        if hsrc is not None:
            load_hT(k, hT[j % 2], hsrc, j)
            return
        xt = xbuf[0]
        k.dma(xt, xT[:, j * 512:(j + 1) * 512].re("(c p) t -> p c t", p=128), eng="sp")
        for c in range(8):
            s = sq[c % 3]
            k.act(s, xt[:, c, :], AF.Square)
            k.matmul(bank[0], ones_bf, s, start=(c == 0), stop=(c == 7))
        r = rstd[j % 2]
        k.act(r, bank[0], AF.Sqrt, scale=1.0 / 1024, bias=eps_t)
        k.recip(r, r)
        for c in range(8):
            k.stt(hT[j % 2][:, c, :], xt[:, c, :], pw_sb[:, c:c + 1], r, ALU.mult, ALU.mult)

    def stage2(j):
        h = hT[j % 2]
        ts_ = slice(j * 512, (j + 1) * 512)
        up = upad[j % 3]
        for grp in range(4):
            ps = bank[1 + grp]
            for c in range(8):
                k.matmul(ps, w_fm[:, c, grp * 128:(grp + 1) * 128], h[:, c, :],
                         start=(c == 0), stop=(c == 7))
            if grp == 0:
                k.act(zsT[:, ts_], ps, AF.Silu)
            else:
                k.copy(up[:, grp - 1, 2:514], ps, eng="act")
        pdt = bank[5]
        for s4 in range(4):
            for c in range(8):
                k.matmul(pdt[:, s4 * 4:(s4 + 1) * 4], h[:, c, s4 * 128:(s4 + 1) * 128], w_dt[:, c, :],
                         start=(c == 0), stop=(c == 7))
        k.copy(dtraw[:, :, j * 4:(j + 1) * 4].re("p h s -> p s h"),
               pdt[:, 0:16].re("p (s h) -> p s h", s=4), eng="dve")

    def conv(j):
        up = upad[j % 3]
        if j > 0:
            k.copy(up[:, :, 0:2], upad[(j - 1) % 3][:, :, 512:514], eng="pool")
        if j < NT - 1:
            k.copy(up[:, :, 514:516], upad[(j + 1) % 3][:, :, 2:4], eng="pool")
        else:
            k.memset(up[:, :, 514:516], 0.0)
        ts_ = slice(j * 512, (j + 1) * 512)
        for ch in range(3):
            a = acc[ch]
            k.ts(a, up[:, ch, 0:512], cw_sb[:, ch, 0:1], ALU.mult)
            for o in range(1, 5):
                k.stt(a, up[:, ch, o:o + 512], cw_sb[:, ch, o:o + 1], a, ALU.mult, ALU.add)
            k.act(dst3[ch][:, ts_], a, AF.Silu, bias=cb_sb[:, ch:ch + 1])

    for j in range(NT + 2):
        if j < NT:
            stage1(j)
        if 1 <= j <= NT:
            stage2(j - 1)
        if j >= 2:
            conv(j - 2)


    dt = k.sb("dt", [128, 4, NCH])
    a_ = k.sb("a_", [128, 4, NCH])
    cum = k.sb("cum", [128, 4, NCH])
    dtd = k.sb("dtd", [128, 4, NCH])
    etot = k.sb("etot", [128, 4, NCH])
    aneg = k.sb("aneg", [128, 4])
    k.act(aneg, sv[:, 4:8], AF.Exp)
    k.ts(aneg, aneg, -1.0, ALU.mult)
    for hd in range(4):
        k.act(dt[:, hd, :], dtraw[:, hd, :], AF.Exp, bias=sv[:, hd:hd + 1])
    k.act(dt, dt, AF.Ln, bias=one_t)
    for hd in range(4):
        k.ts(a_[:, hd, :], dt[:, hd, :], aneg[:, hd:hd + 1], ALU.mult)
    pc = bank[0]
    k.matmul(pc[:, 0:128], tri[0], a_[:, 0:2, :].re("p h c -> p (h c)"))
    k.matmul(pc[:, 128:256], tri[1], a_[:, 2:4, :].re("p h c -> p (h c)"))
    k.matmul(pc[:, 256:512], ones_f, a_.re("p h c -> p (h c)"))
    k.copy(cum.re("p h c -> p (h c)"), pc[:, 0:256], eng="dve")
    k.act(etot.re("p h c -> p (h c)"), pc[:, 256:512], AF.Exp)
    k.tt(dtd.re("p h c -> p (h c)"), pc[:, 256:512], cum.re("p h c -> p (h c)"), ALU.subtract)
    k.act(dtd, dtd, AF.Exp)
    k.tt(dtd, dtd, dt, ALU.mult)

    Sf_all = k.sb("Sf_all", [128, NCH, 128], BF16)
    Sb_all = k.sb("Sb_all", [128, NCH, 128], BF16)
    Srun = [k.sb("Srun%d" % i, [128, 128]) for i in range(2)]
    btm = [k.sb("btm%d" % i, [128, 128], BF16) for i in range(2)]
    xdtd = [k.sb("xdtd%d" % i, [128, 128], BF16) for i in range(2)]
    for d in range(2):
        k.memset(Srun[d], 0.0)
        order = range(NCH) if d == 0 else range(NCH - 1, -1, -1)
        S_all = Sf_all if d == 0 else Sb_all
        for i, c in enumerate(order):
            cs = slice(c * 128, (c + 1) * 128)
            tb = tbank[:, (i % 2) * 256:(i % 2) * 256 + 256]
            k.transpose(tb[:, 0:128], xcT[:, cs], ident_bf)
            k.transpose(tb[:, 128:256], BT[:, cs], ident_bf)
            bt = btm[i % 2]
            k.copy(bt, tb[:, 128:256], eng="act")
            xd = xdtd[i % 2]
            for h in range(2):
                hd = d * 2 + h
                k.ts(xd[:, h * 64:(h + 1) * 64], tb[:, h * 64:(h + 1) * 64], dtd[:, hd, c:c + 1], ALU.mult)
            st = bank[1 + i % 2]
            k.matmul(st[:, 0:128], bt, xd)
            k.copy(S_all[:, c, :], Srun[d], eng="act")
            for h in range(2):
                hd = d * 2 + h
                hb = slice(h * 64, (h + 1) * 64)
                k.stt(Srun[d][:, hb], Srun[d][:, hb], etot[:, hd, c:c + 1], st[:, hb], ALU.mult, ALU.add)

    lhsD = [k.sb("lhsD%d" % i, [128, 4, 128]) for i in range(1)] * 2
    abc = [k.sb("abc%d" % i, [128, 4, 128]) for i in range(1)] * 2
    Lexp = [k.sb("Lexp%d" % i, [128, 512]) for i in range(2)]
    Ebc = [k.sb("Ebc%d" % i, [128, 512]) for i in range(1)] * 2
    Gm = [k.sb("Gm%d" % i, [128, 2, 128]) for i in range(2)]
    MT = [k.sb("MT%d" % i, [128, 4, 128], BF16) for i in range(2)]
    Ct = [k.sb("Ct%d" % i, [128, 4, 128], BF16) for i in range(2)]
    xtm = [k.sb("xtm%d" % i, [128, 128]) for i in range(2)]
    xdt = [k.sb("xdt%d" % i, [128, 4, 64], BF16) for i in range(2)]
    y_sb = [k.sb("y_sb%d" % i, [128, 128]) for i in range(2)]
    ost = [k.sb("ost%d" % i, [128, 512], ydt) for i in range(2)]
    for c in range(NCH):
        cs = slice(c * 128, (c + 1) * 128)
        p = c % 2
        tb = tbank[:, p * 256:p * 256 + 128]
        k.transpose(tb, xcT[:, cs], ident_bf)
        k.copy(xtm[p], tb, eng="act")
        for hd in range(4):
            k.ts(xdt[p][:, hd, :], xtm[p][:, (hd % 2) * 64:(hd % 2) * 64 + 64], dt[:, hd, c:c + 1], ALU.mult)
        for hd in range(4):
            d = hd // 2
            k.ts(lhsD[p][:, hd, :], smask[d], a_[:, hd, c:c + 1], ALU.mult)
            k.ts(abc[p][:, hd, :], ones_f, a_[:, hd, c:c + 1], ALU.mult)
        Dps, Cps = bank[3], bank[4]
        for hd in range(4):
            d = hd // 2
            k.matmul(Dps[:, hd * 128:(hd + 1) * 128], lhsD[p][:, hd, :], tri[d])
        for hd in range(4):
            d = hd // 2
            k.matmul(Cps[:, hd * 128:(hd + 1) * 128], abc[p][:, hd, :], tri[d])
        k.act(Lexp[p], Dps, AF.Exp)
        k.act(Ebc[p], Cps, AF.Exp)
        Gps = bank[5]
        k.matmul(Gps[:, 0:128], BT[:, cs], CT[:, cs])
        for d in range(2):
            k.tt(Gm[p][:, d, :], Gps[:, 0:128], tri[d], ALU.mult)
        for hd in range(4):
            d = hd // 2
            k.tt(MT[p][:, hd, :], Lexp[p][:, hd * 128:(hd + 1) * 128], Gm[p][:, d, :], ALU.mult)
            k.tt(Ct[p][:, hd, :], Ebc[p][:, hd * 128:(hd + 1) * 128], CT[:, cs], ALU.mult)
        Yps = bank[6]
        for h in range(2):
            hb = slice(h * 64, (h + 1) * 64)
            k.matmul(Yps[:, hb], MT[p][:, h, :], xdt[p][:, h, :], start=True, stop=False)
            k.matmul(Yps[:, hb], MT[p][:, 2 + h, :], xdt[p][:, 2 + h, :], start=False, stop=False)
            k.matmul(Yps[:, hb], Ct[p][:, h, :], Sf_all[:, c, hb], start=False, stop=False)
            k.matmul(Yps[:, hb], Ct[p][:, 2 + h, :], Sb_all[:, c, hb], start=False, stop=True)
        for h in range(2):
            hb = slice(h * 64, (h + 1) * 64)
            k.stt(y_sb[p][:, hb], xtm[p][:, hb], sv[:, 8 + h:9 + h], Yps[:, hb], ALU.mult, ALU.add)
        yTp = bank[1 + p]
        k.transpose(yTp[:, 0:128], y_sb[p], ident_f)
        o = ost[(c // 4) % 2]
        k.tt(o[:, (c % 4) * 128:(c % 4 + 1) * 128], yTp[:, 0:128], zsT[:, cs], ALU.mult)
        if c % 4 == 3:
            k.dma(yT[:, (c - 3) * 128:(c + 1) * 128], o, eng="sp")


L = 8192
NT = L // 512
NORM_EPS = 1e-6
GN_EPS = 64e-5
C0 = math.exp(-0.5)


def emit_rwkv(k, io):
    stop, nq_lim, do_chain, do_pre, pre_lim = 99, None, True, True, 9
    xT, hsrc, pw, wfm, mu, w2a2, pvd, cm, c128 = (io.get("xT"), io.get("hT"), io["pw"], io["wfm"], io["mu"],
                                                  io["w2a2"], io["pvd"], io["cm"], io["c128"])
    wkv_scr, bon_scr, yT, ydt = io["wkv_scr"], io["bon_scr"], io["y"], io["ydt"]

    ones_bf = k.sb("ones_bf", [128, 128], BF16)
    k.memset(ones_bf, 1.0)
    eps_t = k.sb("eps_t", [128, 1])
    k.memset(eps_t, NORM_EPS)
    epsg_t = k.sb("epsg_t", [128, 1])
    k.memset(epsg_t, GN_EPS)
    pw_sb = k.sb("pw_sb", [128, 8])
    k.dma(pw_sb, pw)
    mu_sb = k.sb("mu_sb", [128, 2, 4])
    k.dma(mu_sb, mu)
    w2a2_sb = k.sb("w2a2_sb", [128, 2, 128])
    k.dma(w2a2_sb, w2a2)
    pv = k.sb("pv", [128, 12])
    k.dma(pv, pvd)
    omk = k.sb("omk", [128, 1])
    k.ts(omk, pv[:, 4:5], -1.0, ALU.mult, 1.0, ALU.add)
    cm_sb = k.sb("cm_sb", [64, 2, 704])
    k.dma(cm_sb, cm)
    c128_sb = k.sb("c128_sb", [128, 2, 128])
    k.dma(c128_sb, c128)
    blk_bf = k.sb("blk_bf", [128, 128], BF16)
    k.copy(blk_bf, c128_sb[:, 0, :], eng="dve")
    ident_f = c128_sb[:, 1, :]
    ident_bf = k.sb("ident_bf", [128, 128], BF16)
    k.copy(ident_bf, ident_f, eng="dve")
    id64_bf = k.sb("id64_bf", [64, 4, 64], BF16)
    for h in range(4):
        k.copy(id64_bf[:, h, :], cm_sb[:, 0, 640:704], eng="dve")
    maskSC = [cm_sb[:, d, 0:512] for d in range(2)]
    maskN = [cm_sb[:, d, 512:640] for d in range(2)]

    xbuf = k.sb("xbuf", [128, 8, 512])
    w_fm = k.sb("w_fm", [128, 8, 640], BF16)
    k.dma(xbuf, wfm[:, 0:512].re("(c p) m -> p c m", p=128))
    k.copy(w_fm[:, :, 0:512], xbuf, eng="pool")
    k.dma(xbuf[:, :, 0:128], wfm[:, 512:640].re("(c p) m -> p c m", p=128))
    k.copy(w_fm[:, :, 512:640], xbuf[:, :, 0:128], eng="pool")

    hT = k.sb("hT", [128, 8, 512], BF16)
    sq = [k.sb("sq%d" % i, [128, 512], BF16) for i in range(2)]
    rstd = k.sb("rstd", [128, 512])
    upad = [k.sb("upad%d" % d, [128, 4, 514]) for d in range(2)]
    ul = k.sb("ul", [128, 4, 512])
    tmpd = [k.sb("tmpd%d" % i, [128, 512]) for i in range(2)]
    gT = k.sb("gT", [128, L], BF16)
    twd = k.sb("twd", [64, 512])
    sg = k.sb("sg", [128, 512])
    aic = k.sb("aic", [128, 512])
    kk = k.sb("kk", [128, 512])
    kkn = k.sb("kkn", [128, 512])
    rs = k.sb("rs", [128, 512])
    tka = k.sb("tka", [128, 512])
    k2 = k.sb("k2", [128, 512])
    bvec = k.sb("bvec", [128, 512])
    rk = k.sb("rk", [128, 512], BF16)
    bon = [k.sb("bon%d" % i, [128, 512]) for i in range(2)]
    cs = [k.sb("cs%d" % i, [128, 512]) for i in range(2)]
    csm = k.sb("csm", [128, 512])
    Epos = k.sb("Epos", [128, 512])
    Eneg = k.sb("Eneg", [128, 512])
    Eprev = k.sb("Eprev", [128, 512])
    RopT = [[k.sb("RopT%d%d" % (d, s), [128, 8, 2, 64], BF16) for s in range(2)] for d in range(2)]
    LopT = [[k.sb("LopT%d%d" % (d, s), [128, 8, 2, 64], BF16) for s in range(2)] for d in range(2)]
    vTb = [[k.sb("vTb%d%d" % (d, s), [128, 512], BF16) for s in range(2)] for d in range(2)]
    eC = [[k.sb("eC%d%d" % (d, s), [128, 8]) for s in range(2)] for d in range(2)]
    NR = 4
    tm = [[k.sb("tm%d%d" % (d, i), [64, 384], BF16) for i in range(NR)] for d in range(2)]
    scmB = [k.sb("scmB%d" % i, [64, 2, 4, 128], BF16) for i in range(NR)]
    TtB = [k.sb("TtB%d" % i, [64, 4, 64], BF16) for i in range(NR)]
    NMt2 = [[k.sb("NMt%d_%d" % (a, i), [64, 2, 4, 64], BF16) for i in range(2)] for a in range(2)]
    Pt2 = [[k.sb("Pt%d_%d" % (a, i), [64, 4, 64], BF16) for i in range(2)] for a in range(2)]
    S32 = [k.sb("S32_%d" % d, [128, 128]) for d in range(2)]
    t1 = [k.sb("t1_%d" % d, [128, 128]) for d in range(2)]
    Sbf = [[k.sb("Sbf%d%d" % (d, i), [128, 128], BF16) for i in range(2)] for d in range(2)]
    Zb = [k.sb("Zb%d" % d, [64, 128], BF16) for d in range(2)]
    Ub = [k.sb("Ub%d" % d, [64, 128], BF16) for d in range(2)]
    Yo = [[k.sb("Yo%d%d" % (d, i), [64, 128]) for i in range(2)] for d in range(2)]
    bankA = [k.ps("bankA%d" % i, [128, 512]) for i in range(2)]
    bSC = k.ps("bSC", [128, 512])
    bNM = k.ps("bNM", [128, 512])
    bP = k.ps("bP", [128, 512])
    bCH = [k.ps("bCH%d" % d, [128, 512]) for d in range(2)]
    tb32 = k.ps("tbank", [128, 512])
    tbank = T(tb32.buf, tb32.ap.bitcast(BF16))

    for d in range(2):
        k.memset(S32[d], 0.0)
        k.memset(Sbf[d][0], 0.0)
    k.memset(upad[0][:, :, 0:1], 0.0)
    k.memset(upad[1][:, :, 513:514], 0.0)

    state = {"a": 0}

    def nbank():
        return bankA[0]

    def phaseA(d, j, slot):
        first = (j == 0) if d == 0 else (j == NT - 1)
        up = upad[d]
        if not first:
            if d == 0:
                k.copy(up[:, :, 0:1], up[:, :, 512:513], eng="pool")
            else:
                k.copy(up[:, :, 513:514], up[:, :, 1:2], eng="pool")
        if hsrc is not None:
            load_hT(k, hT, hsrc, j)
        else:
            k.dma(xbuf, xT[:, j * 512:(j + 1) * 512].re("(c p) t -> p c t", p=128), eng="sp")
            ps = nbank()
            for c in range(8):
                s = sq[c % 2]
                k.act(s, xbuf[:, c, :], AF.Square)
                k.matmul(ps, ones_bf, s, start=(c == 0), stop=(c == 7))
            k.act(rstd, ps, AF.Sqrt, scale=1.0 / 1024, bias=eps_t)
            k.recip(rstd, rstd)
            for c in range(8):
                k.stt(hT[:, c, :], xbuf[:, c, :], pw_sb[:, c:c + 1], rstd, ALU.mult, ALU.mult)
        ts_ = slice(j * 512, (j + 1) * 512)
        for grp in range(5 if d == 0 else 4):
            ps = nbank()
            for c in range(8):
                k.matmul(ps, w_fm[:, c, grp * 128:(grp + 1) * 128], hT[:, c, :],
                         start=(c == 0), stop=(c == 7))
            if grp < 4:
                k.copy(up[:, grp, 1:513], ps, eng="act")
            else:
                k.act(gT[:, ts_], ps, AF.Silu)
        sh = slice(0, 512) if d == 0 else slice(2, 514)
        for grp in range(4):
            td = tmpd[grp % 2]
            k.tt(td, up[:, grp, sh], up[:, grp, 1:513], ALU.subtract, eng="pool")
            k.stt(ul[:, grp, :], td, mu_sb[:, d, grp:grp + 1], up[:, grp, 1:513], ALU.mult, ALU.add)
        r, kx, v, wa = ul[:, 0, :], ul[:, 1, :], ul[:, 2, :], ul[:, 3, :]
        k.copy(vTb[d][slot], v, eng="pool")
        k.act(twd, wa[0:64, :], AF.Tanh)
        pxw = nbank()
        k.matmul(pxw, w2a2_sb[0:64, d, :], twd)
        k.act(sg, pxw, AF.Sigmoid, bias=pv[:, d:d + 1])
        pxa = nbank()
        k.matmul(pxa, w2a2_sb[64:128, d, :], wa[64:128, :])
        k.act(aic, pxa, AF.Sigmoid, bias=pv[:, 2:3])
        k.ts(kk, kx, pv[:, 3:4], ALU.mult)
        k.act(sq[0], kk, AF.Square)
        pss = nbank()
        k.matmul(pss, blk_bf, sq[0])
        k.act(rs, pss, AF.Sqrt)
        k.ts(rs, rs, 1e-12, ALU.max)
        k.recip(rs, rs)
        k.tt(kkn, kk, rs, ALU.mult)
        k.ts(tka, aic, pv[:, 4:5], ALU.mult, omk, ALU.add)
        k.tt(k2, kx, tka, ALU.mult)
        k.tt(bvec, kkn, aic, ALU.mult, eng="pool")
        k.stt(rk, r, pv[:, 6:7], k2, ALU.mult, ALU.mult)
        pbs = nbank()
        k.matmul(pbs, blk_bf, rk)
        bo = bon[d]
        k.tt(bo, pbs, v, ALU.mult)
        k.dma(bon_scr[d, :, ts_], bo, eng="sp")
        src = sg
        i = 0
        for s in (1, 2, 4, 8, 16, 32):
            dst = cs[i % 2]
            sv_, dv_ = src.re("p (c t) -> p c t", t=64), dst.re("p (c t) -> p c t", t=64)
            if d == 0:
                k.tt(dv_[:, :, s:64], sv_[:, :, s:64], sv_[:, :, 0:64 - s], ALU.add)
                k.copy(dv_[:, :, 0:s], sv_[:, :, 0:s], eng="pool")
            else:
                k.tt(dv_[:, :, 0:64 - s], sv_[:, :, 0:64 - s], sv_[:, :, s:64], ALU.add)
                k.copy(dv_[:, :, 64 - s:64], sv_[:, :, 64 - s:64], eng="pool")
            src = dst
            i += 1
        csf = src
        k.tt(csm, csf, sg, ALU.subtract, eng="pool")
        k.act(Epos, csf, AF.Exp, scale=-C0)
        k.act(Eneg, csf, AF.Exp, scale=C0)
        k.act(Eprev, csm, AF.Exp, scale=-C0)
        last = 63 if d == 0 else 0
        k.copy(eC[d][slot], Epos.re("p (c t) -> p c t", t=64)[:, :, last], eng="pool")
        R, Lo = RopT[d][slot], LopT[d][slot]
        v3 = lambda t: t.re("p (c t) -> p c t", t=64)
        k.stt(R[:, :, 0, :], v3(kkn), -1.0, v3(Eprev), ALU.mult, ALU.mult)
        k.tt(R[:, :, 1, :], v3(r), v3(Epos), ALU.mult)
        k.tt(Lo[:, :, 0, :], v3(bvec), v3(Eneg), ALU.mult)
        k.tt(Lo[:, :, 1, :], v3(k2), v3(Eneg), ALU.mult, eng="pool")

    def pre_stages(q):
        items = items_for(q)
        i3 = q % NR
        R = [RopT[d][slot][:, cc] for (d, slot, cc, _, _) in items]
        Lo = [LopT[d][slot][:, cc] for (d, slot, cc, _, _) in items]
        sc = scmB[i3]
        NMt, Pt = NMt2[q % 2], Pt2[q % 2]
        st = []

        def s_tr():
            for (d, slot, cc, _, _) in items:
                tps = tbank[0:64, (d * 384):(d * 384) + 384]
                k.transpose(tps[:, 0:128], Lo[d][:, 0, :], ident_bf)
                k.transpose(tps[:, 128:256], Lo[d][:, 1, :], ident_bf)
                k.transpose(tps[:, 256:384], vTb[d][slot][:, cc * 64:(cc + 1) * 64], ident_bf)
                k.copy(tm[d][i3], tps, eng="act")
        st.append(s_tr)

        def s_sc():
            for h in range(2):
                hs = slice(64 * h, 64 * h + 64)
                bk = bSC if h == 0 else bankA[1]
                nk = bP[0:64, 256:384] if h == 0 else tb32[0:64, 384:512]
                for d in range(2):
                    Rh = R[d][hs].re("p a t -> p (a t)")
                    k.matmul(bk[0:64, d * 256:d * 256 + 128], Lo[d][hs, 0, :], Rh)
                    k.matmul(bk[0:64, d * 256 + 128:d * 256 + 256], Lo[d][hs, 1, :], Rh)
                for d in range(2):
                    k.matmul(nk[:, d * 64:(d + 1) * 64], R[d][hs, 0, :], Lo[d][hs, 0, :])
            nm = NMt[0]
            for h in range(2):
                bk = bSC if h == 0 else bankA[1]
                nk = bP[0:64, 256:384] if h == 0 else tb32[0:64, 384:512]
                k.tt(sc[:, :, 2 * h:2 * h + 2, :].re("p d a t -> p d (a t)"),
                     bk[0:64, :].re("p (d x) -> p d x", d=2), cm_sb[:, :, 0:256], ALU.mult)
                k.tt(nm[:, 0].re("p (d h) t -> p d h t", h=2)[:, :, h, :], nk.re("p (d t) -> p d t", d=2),
                     cm_sb[:, :, 512:576], ALU.mult)
            k.copy(nm[:, 1].re("p (d h) t -> p d h t", h=2),
                   sc.re("p d (h two) t -> p d h two t", two=2)[:, :, :, 0, 0:64], eng="pool")
            k.tt(Pt[0], nm[:, 1], id64_bf, ALU.add, eng="pool")
        st.append(s_sc)

        for lv in range(5):
            cur = lv % 2
            nm_c, nm_n = NMt[cur], NMt[1 - cur]
            P_c = Pt[cur]
            P_n = Pt[1 - cur] if lv < 4 else TtB[i3]

            def s_nm(lv=lv, nm_c=nm_c, nm_n=nm_n):
                pn = bNM[0:64, :]
                for dh in range(4):
                    k.matmul(pn[:, dh * 64:(dh + 1) * 64], nm_c[:, 1, dh, :], nm_c[:, 0, dh, :])
                if lv < 4:
                    for dh in range(4):
                        k.matmul(pn[:, 256 + dh * 64:256 + (dh + 1) * 64], nm_c[:, 0, dh, :], nm_c[:, 1, dh, :])
                    k.copy(nm_n.re("p a h t -> p (a h t)"), pn, eng="act")
                else:
                    k.copy(nm_n[:, 0].re("p h t -> p (h t)"), pn[:, 0:256], eng="act")
            st.append(s_nm)

            def s_p(nm_n=nm_n, P_c=P_c, P_n=P_n):
                pp = bP[0:64, 0:256]
                for dh in range(4):
                    k.matmul(pp[:, dh * 64:(dh + 1) * 64], nm_n[:, 0, dh, :], P_c[:, dh, :], start=True, stop=False)
                    k.matmul(pp[:, dh * 64:(dh + 1) * 64], id64_bf[:, 0, :], P_c[:, dh, :], start=False, stop=True)
                k.copy(P_n.re("p h t -> p (h t)"), pp, eng="dve")
            st.append(s_p)
        return st

    par = [0, 0]

    def chain_stages(q):
        items = items_for(q)
        i3 = q % NR
        ctx = []
        for (d, slot, cc, _, cg) in items:
            CB = bCH[d]
            ctx.append(dict(d=d, R=RopT[d][slot][:, cc], sc=scmB[i3][:, d], tmv=tm[d][i3], T=TtB[i3][:, 2 * d:2 * d + 2, :],
                            X=CB[0:64, 0:128], U=CB[0:64, 128:256], DS=CB[:, 256:384], Y=CB[0:64, 384:512],
                            Sb=Sbf[d][par[d]], Sn=Sbf[d][1 - par[d]], e=eC[d][slot][:, cc:cc + 1],
                            cg=cg, q=q))
            par[d] = 1 - par[d]
        st = []

        def c_x():
            for c in ctx:
                k.matmul(c["X"], c["R"][:, 0, :], c["Sb"], start=True, stop=False)
                for h in range(2):
                    hb = slice(64 * h, 64 * h + 64)
                    k.matmul(c["X"][:, hb], c["sc"][:, 2 * h + 1, 0:64], c["tmv"][:, 256 + 64 * h:256 + 64 * h + 64],
                             start=False, stop=(h == 1))
            for c in ctx:
                k.copy(Zb[c["d"]], c["X"], eng="act")
        st.append(c_x)

        def c_u():
            for c in ctx:
                for h in range(2):
                    hb = slice(64 * h, 64 * h + 64)
                    k.matmul(c["U"][:, hb], c["T"][:, h, :], Zb[c["d"]][:, hb])
            for c in ctx:
                k.copy(Ub[c["d"]], c["U"], eng="dve")
        st.append(c_u)

        def c_y():
            for c in ctx:
                d = c["d"]
                k.matmul(c["DS"], c["tmv"][:, 0:128], Ub[d], start=True, stop=False)
                k.matmul(c["DS"], c["tmv"][:, 128:256], c["tmv"][:, 256:384], start=False, stop=True)
                k.ts(t1[d], S32[d], c["e"], ALU.mult)
            for c in ctx:
                d = c["d"]
                k.matmul(c["Y"], c["R"][:, 1, :], c["Sb"], start=True, stop=False)
                for h in range(2):
                    hb = slice(64 * h, 64 * h + 64)
                    k.matmul(c["Y"][:, hb], c["sc"][:, 2 * h, 64:128], Ub[d][:, hb], start=False, stop=False)
                    k.matmul(c["Y"][:, hb], c["sc"][:, 2 * h + 1, 64:128], c["tmv"][:, 256 + 64 * h:256 + 64 * h + 64],
                             start=False, stop=(h == 1))
        st.append(c_y)

        def c_s():
            for c in ctx:
                d = c["d"]
                for h in range(2):
                    hs = slice(64 * h, 64 * h + 64)
                    k.stt(S32[d][hs, hs], c["DS"][hs, hs], c["e"][hs], t1[d][hs, hs], ALU.mult, ALU.add)
            for c in ctx:
                d = c["d"]
                k.copy(c["Sn"], S32[d], eng="act")
                yo = Yo[d][c["q"] % 2]
                k.copy(yo, c["Y"], eng="dve")
                k.dma(wkv_scr[d, c["cg"] * 64:(c["cg"] + 1) * 64, :], yo, eng="pool")
        st.append(c_s)
        return st

    def items_for(q):
        s, ci = q // 8, q % 8
        return [(0, s % 2, ci, q, s * 8 + ci), (1, s % 2, 7 - ci, q, (NT - 1 - s) * 8 + (7 - ci))]

    nq = NT * 8
    phaseA(0, 0, 0)
    phaseA(1, NT - 1, 0)
    pendA = []
    perA = 1
    for q in range(0, nq + 2, 2):
        PA, PB = [], []
        if q < nq:
            s, ci = q // 8, q % 8
            if ci == 2 and s + 1 < NT:
                k.begin_defer()
                phaseA(0, s + 1, (s + 1) % 2)
                phaseA(1, NT - 2 - s, (s + 1) % 2)
                pendA = k.end_defer()
                perA = (len(pendA) + 3 * 30 - 1) // (3 * 30)
            PA = pre_stages(q)
            PB = pre_stages(q + 1)
        cs_ = []
        if q >= 2:
            cs_ = chain_stages(q - 2) + chain_stages(q - 1)
        np_, nc_ = len(PA), len(cs_)
        ci_ = 0
        for i in range(np_):
            PA[i]()
            if pendA:
                k.splice(pendA, perA)
            PB[i]()
            if pendA:
                k.splice(pendA, perA)
            want = (i + 1) * nc_ // np_
            while ci_ < want:
                cs_[ci_]()
                if pendA:
                    k.splice(pendA, perA)
                ci_ += 1
        while ci_ < nc_:
            cs_[ci_]()
            ci_ += 1
        if q < nq and q % 8 == 6 and pendA:
            k.splice(pendA, len(pendA))
    assert not pendA

    wf = [k.sb("wf%d" % i, [128, 128]) for i in range(2)]
    wb = [k.sb("wb%d" % i, [128, 128]) for i in range(2)]
    ww = [k.sb("ww%d" % i, [128, 128]) for i in range(2)]
    sqw = k.sb("sqw", [128, 128])
    st1 = k.sb("st1", [128, 2])
    st2 = k.sb("st2", [128, 2])
    mean = k.sb("mean", [128, 2])
    msq = k.sb("msq", [128, 2])
    var = k.sb("var", [128, 2])
    gn = [k.sb("gn%d" % i, [128, 128]) for i in range(2)]
    ob = [k.sb("ob%d" % i, [128, 512]) for i in range(2)]
    obo = [k.sb("obo%d" % i, [128, 512], ydt) for i in range(2)]
    bf_ = [k.sb("bf_%d" % i, [128, 512]) for i in range(2)]
    bb_ = [k.sb("bb_%d" % i, [128, 512]) for i in range(2)]
    for i in range(L // 128):
        p = i % 2
        k.dma(wf[p], wkv_scr[0, i * 128:(i + 1) * 128, :], eng="sp")
        k.dma(wb[p], wkv_scr[1, i * 128:(i + 1) * 128, :], eng="sp")
        w = ww[p]
        k.tt(w, wf[p], wb[p], ALU.add)
        k.reduce(st1, w.re("p (h v) -> p h v", h=2), ALU.add)
        k.tt(sqw, w, w, ALU.mult, eng="pool")
        k.reduce(st2, sqw.re("p (h v) -> p h v", h=2), ALU.add)
        k.ts(mean, st1, 1.0 / 64, ALU.mult)
        k.tt(msq, mean, mean, ALU.mult)
        k.stt(var, st2, 1.0 / 64, msq, ALU.mult, ALU.subtract)
        k.act(var, var, AF.Sqrt, bias=epsg_t)
        k.recip(var, var)
        g_ = gn[p]
        for h in range(2):
            hb = slice(64 * h, 64 * h + 64)
            k.ts(g_[:, hb], w[:, hb], mean[:, h:h + 1], ALU.subtract, var[:, h:h + 1], ALU.mult)
        tb = bankA[(i // 4) % 2]
        k.transpose(tb[:, (i % 4) * 128:(i % 4 + 1) * 128], g_, ident_f)
        if i % 4 == 3:
            j = i // 4
            ts_ = slice(j * 512, (j + 1) * 512)
            o = ob[j % 2]
            k.dma(bf_[j % 2], bon_scr[0, :, ts_], eng="pool")
            k.dma(bb_[j % 2], bon_scr[1, :, ts_], eng="pool")
            k.ts(o, tb, pv[:, 7:8], ALU.mult, pv[:, 8:9], ALU.add)
            k.tt(o, o, bf_[j % 2], ALU.add, eng="pool")
            k.tt(o, o, bb_[j % 2], ALU.add, eng="pool")
            k.tt(obo[j % 2], o, gT[:, ts_], ALU.mult)
            k.dma(yT[:, ts_], obo[j % 2], eng="sp")


GROUPS = [[0, 1, 2, 3], [4, 5, 6, 7]]


def emit_out(k, io, last):
    Lx = 8192
    NTx = Lx // 512
    ygath, wo, xsrc, xdst = io["ygath"], io["wo"], io["xsrc"], io["xdst"]
    ones_bf = k.sb("ones_bf", [128, 128], BF16)
    k.memset(ones_bf, 1.0)
    eps_t = k.sb("eps_t", [128, 1])
    k.memset(eps_t, 1e-6)
    postw_sb = k.sb("postw_sb", [128, 2])
    k.dma(postw_sb, io["postw"])
    snw_sb = k.sb("snw_sb", [128, 4])
    k.dma(snw_sb, io["snw"])
    prew_sb = k.sb("prew_sb", [128, 2])
    k.dma(prew_sb, io["prew"])
    w_bf = k.sb("w_bf", [128, 16, 256], BF16)
    wst = [k.sb("wst%d" % i, [128, 4, 256]) for i in range(2)]
    for i in range(4):
        st = wst[i % 2]
        k.dma(st, wo[i * 512:(i + 1) * 512, :].re("(c p) m -> p c m", p=128), eng="pool" if i % 2 else "sp")
        k.copy(w_bf[:, 4 * i:4 * i + 4, :], st, eng="pool")
    mixT = k.sb("mixT", [128, 2, Lx])
    ssrow = k.sb("ssrow", [1, Lx])
    ytile = [k.sb("ytile%d" % i, [128, 16, 512], BF16) for i in range(2)]
    yn = [k.sb("yn%d" % i, [128, 4, 512], BF16) for i in range(2)]
    sq = [k.sb("sq%d" % i, [128, 512], BF16) for i in range(2)]
    rs = k.sb("rs", [128, 512])
    tot = [k.sb("tot%d" % i, [128, 512]) for i in range(2)]
    xt = [k.sb("xt%d" % i, [128, 2, 512]) for i in range(2)]
    hb = [k.sb("hb%d" % i, [128, 2, 512], BF16) for i in range(2)]
    pg = k.ps("pg", [128, 512])
    pm = [k.ps("pm%d" % i, [128, 512]) for i in range(2)]
    pss = k.ps("pss", [128, 512])

    for j in range(NTx):
        ts_ = slice(j * 512, (j + 1) * 512)
        yt = ytile[j % 2]
        ytv = yt.re("p (g q) t -> p g q t", q=4)
        for r in range(8):
            k.dma(ytv[(r % 2) * 64:(r % 2) * 64 + 64, :, r // 2, :],
                  ygath[r * 256:(r + 1) * 256, ts_].re("(g i) t -> i g t", i=64),
                  eng="sp" if r % 2 == 0 else "pool")
        ynj = yn[j % 2]
        for grp in range(2):
            for ci, g in enumerate((2 * grp, 2 * grp + 1)):
                k.act(sq[ci], yt[:, g * 4, :], AF.Square)
                k.matmul(pg, ones_bf, sq[ci], start=(ci == 0), stop=(ci == 1))
            k.act(rs, pg, AF.Sqrt, scale=1.0 / 256, bias=eps_t)
            k.recip(rs, rs)
            for g in (2 * grp, 2 * grp + 1):
                k.stt(ynj[:, g, :], yt[:, g * 4, :], snw_sb[:, g:g + 1], rs, ALU.mult, ALU.mult)
        for nb in range(2):
            for c in range(16):
                rhs = ynj[:, c // 4, :] if c % 4 == 0 else yt[:, c, :]
                k.matmul(pm[nb], w_bf[:, c, nb * 128:(nb + 1) * 128], rhs, start=(c == 0), stop=(c == 15))
            k.copy(mixT[:, nb, ts_], pm[nb], eng="dve")
            k.act(sq[nb], pm[nb], AF.Square)
            k.matmul(pss, ones_bf, sq[nb], start=(nb == 0), stop=(nb == 1))
        k.copy(ssrow[0:1, ts_], pss[0:1, :], eng="dve")
    k.dma(io["ar1_in"], ssrow, eng="sp")
    k.collective("AllReduce", io["ar1_in"], io["ar1_out"], GROUPS, op=ALU.add)

    for j in range(NTx):
        ts_ = slice(j * 512, (j + 1) * 512)
        tt_ = tot[j % 2]
        k.dma(tt_, T(io["ar1_out"].buf, io["ar1_out"].ap[0:1, ts_].partition_broadcast(128)), eng="pool")
        k.act(tt_, tt_, AF.Sqrt, scale=1.0 / 1024, bias=eps_t)
        k.recip(tt_, tt_)
        x_ = xt[j % 2]
        k.dma(x_, xsrc[:, ts_].re("(nb p) t -> p nb t", p=128), eng="sp")
        for nb in range(2):
            k.stt(mixT[:, nb, ts_], mixT[:, nb, ts_], postw_sb[:, nb:nb + 1], tt_, ALU.mult, ALU.mult)
            k.tt(mixT[:, nb, ts_], mixT[:, nb, ts_], x_[:, nb, :], ALU.add, eng="pool")
        k.dma(xdst[:, ts_].re("(nb p) t -> p nb t", p=128), mixT[:, :, ts_], eng="sp")
        if not last:
            for nb in range(2):
                k.act(sq[nb], mixT[:, nb, ts_], AF.Square)
                k.matmul(pss, ones_bf, sq[nb], start=(nb == 0), stop=(nb == 1))
            k.copy(ssrow[0:1, ts_], pss[0:1, :], eng="dve")
    if last:
        return
    k.dma(io["ar2_in"], ssrow, eng="sp")
    k.collective("AllReduce", io["ar2_in"], io["ar2_out"], GROUPS, op=ALU.add)
    for j in range(NTx):
        ts_ = slice(j * 512, (j + 1) * 512)
        tt_ = tot[j % 2]
        k.dma(tt_, T(io["ar2_out"].buf, io["ar2_out"].ap[0:1, ts_].partition_broadcast(128)), eng="pool")
        k.act(tt_, tt_, AF.Sqrt, scale=1.0 / 1024, bias=eps_t)
        k.recip(tt_, tt_)
        h_ = hb[j % 2]
        for nb in range(2):
            k.stt(h_[:, nb, :], mixT[:, nb, ts_], prew_sb[:, nb:nb + 1], tt_, ALU.mult, ALU.mult)
        k.dma(io["hslice"][:, ts_].re("(nb p) t -> p nb t", p=128), h_, eng="sp")
    for r in range(4):
        k.collective("AllGather", io["hslice"][r * 64:(r + 1) * 64, :], io["hgath"][r * 256:(r + 1) * 256, :], GROUPS)


L = 8192
PROJ_SIZES = (512, 1024, 16, 1664, 512, 512, 512, 512, 512, 512, 256, 256, 512)
OFF = np.concatenate([[0], np.cumsum(PROJ_SIZES)]).astype(int)
(O_MZ, O_XBC, O_DT, O_RU, O_RG, O_DQ, O_DK, O_DV, O_DG, O_GQ, O_GK, O_GV, O_GG) = OFF[:13]

PERM = np.array([p + 32 if (p % 64) < 32 else p - 32 for p in range(128)])


def rope_tables():
    inv = (np.float32(10000.0) ** (-np.arange(32, dtype=np.float32) / np.float32(32))).astype(np.float32)
    t = np.arange(L)
    sign = np.where((np.arange(128) % 64) < 32, -1.0, 1.0).astype(np.float32)[:, None]
    fi = np.arange(128) % 32
    pos_d = np.broadcast_to(t.astype(np.float32)[None, :], (128, L))
    ang_d = (pos_d * inv[fi][:, None]).astype(np.float32)
    pos_g = np.where((np.arange(128) < 64)[:, None], (t // 64)[None, :], (t % 64)[None, :]).astype(np.float32)
    ang_g = (pos_g * inv[fi][:, None]).astype(np.float32)
    tabs = np.stack([np.cos(ang_d), np.sin(ang_d) * sign, np.cos(ang_g), np.sin(ang_g) * sign]).astype(np.float32)
    return np.ascontiguousarray(tabs)


def pvec(v):
    return np.ascontiguousarray(v.reshape(8, 128).T)


def prep_attn(inp, layer, xT_b, tabs):
    W = inp['w_in'][layer]
    maps = []
    for core in range(8):
        b, g = core // 4, core % 4
        sl = lambda o, n=128, gg=g: W[:, o + gg * n: o + (gg + 1) * n]
        dq, dk_, dg, dv = sl(O_DQ), sl(O_DK), sl(O_DG), sl(O_DV)
        gq, gg_ = sl(O_GQ), sl(O_GG)
        gk, gv = sl(O_GK, 128, g // 2), sl(O_GV, 128, g // 2)
        wfm = np.concatenate([dq, dq[:, PERM], dk_, dk_[:, PERM], dg, gq, gq[:, PERM], gk, gk[:, PERM], gg_], axis=1)
        wtm = np.concatenate([dv, gv], axis=1)
        vecs = np.zeros((128, 8), np.float32)
        qw, kw = inp['gqa_q_norm_w'][layer], inp['gqa_k_norm_w'][layer]
        vecs[:, 0] = qw; vecs[:, 1] = qw[PERM]; vecs[:, 2] = kw; vecs[:, 3] = kw[PERM]
        vecs[:, 4] = inp['diff_norm_w'][layer]
        lam = np.ascontiguousarray(np.broadcast_to(inp['diff_lambda'][layer].reshape(1, 256), (128, 256)))
        maps.append({"xT": xT_b[b], "pw": pvec(inp['pre_norm_w'][layer]),
                     "wfm": np.ascontiguousarray(wfm), "wtm": np.ascontiguousarray(wtm),
                     "tabs": tabs, "vecs": vecs, "lam": lam})
    return maps


def ssd_consts():
    j = np.arange(128)[:, None]; l = np.arange(128)[None, :]
    c = np.stack([(j <= l), (j >= l), (j > l), (j < l), (j == l)]).astype(np.float32)
    return np.ascontiguousarray(c.transpose(1, 0, 2))


def prep_ssd(inp, layer, xT_b):
    W = inp['w_in'][layer]
    cst = ssd_consts()
    maps = []
    for core in range(8):
        b, g = core // 4, core % 4
        grp = g // 2
        wfm = np.concatenate([W[:, O_MZ + g * 128:O_MZ + (g + 1) * 128],
                              W[:, O_XBC + g * 128:O_XBC + (g + 1) * 128],
                              W[:, O_XBC + 512 + grp * 128:O_XBC + 512 + (grp + 1) * 128],
                              W[:, O_XBC + 768 + grp * 128:O_XBC + 768 + (grp + 1) * 128]], axis=1)
        dcols = [O_DT + d * 8 + 2 * g + h for d in range(2) for h in range(2)]
        wdt = W[:, dcols]
        chans = [g * 128, 512 + grp * 128, 768 + grp * 128]
        cwl = inp['conv_w'][layer]; cbl = inp['conv_b'][layer]
        cw = np.stack([cwl[:, ch:ch + 128].T for ch in chans], axis=1)
        cb = np.stack([cbl[ch:ch + 128] for ch in chans], axis=1)
        v = np.zeros(16, np.float32)
        for d in range(2):
            for h in range(2):
                v[d * 2 + h] = inp['ssm_dt_bias'][layer][d, 2 * g + h]
                v[4 + d * 2 + h] = inp['ssm_a_log'][layer][d, 2 * g + h]
        v[8] = inp['ssm_d'][layer][2 * g]; v[9] = inp['ssm_d'][layer][2 * g + 1]
        maps.append({"xT": xT_b[b], "pw": pvec(inp['pre_norm_w'][layer]),
                     "wfm": np.ascontiguousarray(wfm), "wdt": np.ascontiguousarray(wdt),
                     "cw": np.ascontiguousarray(cw), "cb": np.ascontiguousarray(cb),
                     "ssmv": np.ascontiguousarray(np.broadcast_to(v[None], (128, 16))), "cst": cst})
    return maps


def rwkv_consts():
    j = np.arange(64)[:, None]; t = np.arange(64)[None, :]
    cm = np.zeros((64, 2, 704), np.float32)
    for d in range(2):
        strict = (j < t) if d == 0 else (j > t)
        incl = (j <= t) if d == 0 else (j >= t)
        blk = np.concatenate([strict, incl], axis=1).astype(np.float32)
        cm[:, d, 0:512] = np.tile(blk, (1, 4))
        nmask = ((t < j) if d == 0 else (t > j)).astype(np.float32)
        cm[:, d, 512:640] = np.tile(nmask, (1, 2))
        cm[:, d, 640:704] = np.eye(64, dtype=np.float32)
    c128 = np.zeros((128, 2, 128), np.float32)
    c128[0:64, 0, 0:64] = 1; c128[64:128, 0, 64:128] = 1
    c128[:, 1, :] = np.eye(128, dtype=np.float32)
    return cm, c128


O_R, O_K, O_V, O_WD, O_AD = O_RU, O_RU + 512, O_RU + 1024, O_RU + 1536, O_RU + 1600


def prep_rwkv(inp, layer, xT_b):
    W = inp['w_in'][layer]
    cm, c128 = rwkv_consts()
    maps = []
    for core in range(8):
        b, g = core // 4, core % 4
        gs = slice(g * 128, (g + 1) * 128)
        wfm = np.concatenate([W[:, O_R + g * 128:O_R + (g + 1) * 128], W[:, O_K + g * 128:O_K + (g + 1) * 128],
                              W[:, O_V + g * 128:O_V + (g + 1) * 128], W[:, O_WD:O_WD + 128],
                              W[:, O_RG + g * 128:O_RG + (g + 1) * 128]], axis=1)
        mul = inp['rwkv_mu'][layer]
        mu = np.zeros((128, 2, 4), np.float32)
        for d in range(2):
            mu[:, d, 0] = mul[d, 0 + g * 128:0 + (g + 1) * 128]
            mu[:, d, 1] = mul[d, 512 + g * 128:512 + (g + 1) * 128]
            mu[:, d, 2] = mul[d, 1024 + g * 128:1024 + (g + 1) * 128]
            mu[:, d, 3] = mul[d, 1536:1664]
        w2a2 = np.zeros((128, 2, 128), np.float32)
        for d in range(2):
            w2a2[0:64, d, :] = inp['rwkv_w2'][layer][d][:, gs]
            w2a2[64:128, d, :] = inp['rwkv_a2'][layer][:, gs]
        pv = np.zeros((128, 12), np.float32)
        pv[:, 0] = inp['rwkv_w0'][layer][0, gs]; pv[:, 1] = inp['rwkv_w0'][layer][1, gs]
        pv[:, 2] = inp['rwkv_a0'][layer][gs]; pv[:, 3] = inp['rwkv_k_k'][layer][gs]
        pv[:, 4] = inp['rwkv_k_a'][layer][gs]
        pv[:, 6] = inp['rwkv_r_k'][layer].reshape(-1)[gs]
        pv[:, 7] = inp['rwkv_ln_w'][layer][gs]; pv[:, 8] = inp['rwkv_ln_b'][layer][gs]
        maps.append({"xT": xT_b[b], "pw": pvec(inp['pre_norm_w'][layer]), "wfm": np.ascontiguousarray(wfm),
                     "mu": mu, "w2a2": w2a2, "pvd": pv, "cm": cm, "c128": c128})
    return maps


def build_fused(depth=2):
    nc = bass.Bass("TRN2", target_bir_lowering=False)
    k = KB(nc, arena=True)
    din = k.dram_in
    xT = din("xT", [1024, L])
    xs0 = din("xs0", [256, L])
    tabs = din("tabs", [4, 128, L])
    cst = din("cst", [128, 5, 128])
    cm = din("cm", [64, 2, 704])
    c128 = din("c128", [128, 2, 128])
    xo = k.dram_out("xo", [256, L])
    ycat = k.dram_scratch("ycat", [512, L], BF16)
    ygath = k.dram_scratch("ygath", [2048, L], BF16)
    hslice = k.dram_scratch("hslice", [256, L], BF16)
    hgath = k.dram_scratch("hgath", [1024, L], BF16)
    xres = k.dram_scratch("xres", [256, L])
    ar = [k.dram_scratch("ar%d" % i, [1, L]) for i in range(4)]
    wkv_scr = k.dram_scratch("wkv_scr", [2, L, 128])
    bon_scr = k.dram_scratch("bon_scr", [2, 128, L])
    for layer in range(depth):
        p = "L%d_" % layer
        lambda_init = 0.8 - 0.6 * math.exp(-0.3 * layer)
        pw = din(p + "pw", [128, 8])
        if layer == 0:
            emit_prologue(k, dict(xT=xT, pw=pw, hdst=hgath))
            k.phase_reset()
        src = {"hT": hgath}
        io = dict(src, pw=pw, wfm=din(p + "s_wfm", [1024, 512]), wdt=din(p + "s_wdt", [1024, 4]),
                  cw=din(p + "s_cw", [128, 3, 5]), cb=din(p + "s_cb", [128, 3]), ssmv=din(p + "s_ssmv", [128, 16]),
                  cst=cst, y=ycat[0:128, :], ydt=BF16)
        emit_ssd(k, io)
        k.phase_reset()
        io = dict(src, pw=pw, wfm=din(p + "r_wfm", [1024, 640]), mu=din(p + "r_mu", [128, 2, 4]),
                  w2a2=din(p + "r_w2a2", [128, 2, 128]), pvd=din(p + "r_pvd", [128, 12]), cm=cm, c128=c128,
                  wkv_scr=wkv_scr, bon_scr=bon_scr, y=ycat[128:256, :], ydt=BF16)
        emit_rwkv(k, io)
        k.phase_reset()
        for r in range(4):
            k.collective("AllGather", ycat[r * 64:(r + 1) * 64, :], ygath[r * 256:(r + 1) * 256, :], GROUPS)
        io = dict(src, pw=pw, wfm=din(p + "a_wfm", [1024, 1280]), wtm=din(p + "a_wtm", [1024, 256]), tabs=tabs,
                  vecs=din(p + "a_vecs", [128, 8]), lam=din(p + "a_lam", [128, 256]),
                  y=[ycat[256:384, :], ycat[384:512, :]], ydt=BF16,
                  after_kind0=lambda: [k.collective("AllGather", ycat[r * 64:(r + 1) * 64, :],
                                                    ygath[r * 256:(r + 1) * 256, :], GROUPS) for r in (4, 5)])
        emit_attn(k, io, lambda_init)
        k.phase_reset()
        for r in range(6, 8):
            k.collective("AllGather", ycat[r * 64:(r + 1) * 64, :], ygath[r * 256:(r + 1) * 256, :], GROUPS)
        last = (layer == depth - 1)
        io = dict(ygath=ygath, wo=din(p + "o_wo", [2048, 256]), xsrc=(xs0 if layer == 0 else xres),
                  xdst=(xo if last else xres), postw=din(p + "o_postw", [128, 2]), snw=din(p + "o_snw", [128, 4]),
                  prew=din(p + "o_prew", [128, 2]), ar1_in=ar[0], ar1_out=ar[1], ar2_in=ar[2], ar2_out=ar[3],
                  hslice=hslice, hgath=hgath)
        emit_out(k, io, last)
        k.phase_reset()
    stats = k.finish()
    return nc, stats


def prep_fused(inp, depth=2):
    x = np.ascontiguousarray(inp["x"], dtype=np.float32)
    xT_b = [np.ascontiguousarray(x[b].T) for b in range(2)]
    tabs = rope_tables()
    cst = ssd_consts()
    cm, c128 = rwkv_consts()
    maps = [dict() for _ in range(8)]
    perm = np.array([kind * 512 + g * 128 + i for g in range(4) for kind in range(4) for i in range(128)])
    for core in range(8):
        b, g = core // 4, core % 4
        m = maps[core]
        m["xT"] = xT_b[b]
        m["xs0"] = np.ascontiguousarray(xT_b[b][g * 256:(g + 1) * 256])
        m["tabs"] = tabs; m["cst"] = cst; m["cm"] = cm; m["c128"] = c128
    for layer in range(depth):
        p = "L%d_" % layer
        ms = prep_ssd(inp, layer, xT_b); mr = prep_rwkv(inp, layer, xT_b); ma = prep_attn(inp, layer, xT_b, tabs)
        for core in range(8):
            b, g = core // 4, core % 4
            m = maps[core]
            m[p + "pw"] = ms[core]["pw"]
            for nm in ("wfm", "wdt", "cw", "cb", "ssmv"):
                m[p + "s_" + nm] = ms[core][nm]
            for nm in ("wfm", "mu", "w2a2", "pvd"):
                m[p + "r_" + nm] = mr[core][nm]
            for nm in ("wfm", "wtm", "vecs", "lam"):
                m[p + "a_" + nm] = ma[core][nm]
            ns = slice(g * 256, (g + 1) * 256)
            m[p + "o_wo"] = np.ascontiguousarray(inp["w_out"][layer][perm][:, ns])
            m[p + "o_postw"] = np.ascontiguousarray(inp["post_norm_w"][layer][ns].reshape(2, 128).T)
            m[p + "o_snw"] = np.ascontiguousarray(inp["ssm_norm_w"][layer].reshape(4, 128).T)
            nxt = inp["pre_norm_w"][min(layer + 1, depth - 1)]
            m[p + "o_prew"] = np.ascontiguousarray(nxt[ns].reshape(2, 128).T)
    return maps


from concourse.bass_utils import run_bass_kernel_spmd


def kernel(**inp):
    inp = {k_: np.asarray(v) for k_, v in inp.items()}
    depth = inp["w_in"].shape[0]
    nc, _ = build_fused(depth)
    maps = prep_fused(inp, depth)
    res = run_bass_kernel_spmd(nc, maps, core_ids=list(range(8))).results
    out = np.empty((2, L, 1024), np.float32)
    for core in range(8):
        b, g = core // 4, core % 4
        out[b, :, g * 256:(g + 1) * 256] = res[core]["xo"].T
    return out
```

```python
import math
import numpy as np
import concourse.bass as bass
import concourse.mybir as mybir

F32 = mybir.dt.float32
BF16 = mybir.dt.bfloat16
ALU = mybir.AluOpType
AF = mybir.ActivationFunctionType
AX = mybir.AxisListType

SEM_CHUNK = 30000


class Buf:
    __slots__ = ("name", "last_w", "readers", "dma_sem", "dma_cnt", "is_dram", "psum", "wlist", "inc_val")

    def __init__(self, name, is_dram=False, psum=False):
        self.psum = psum
        self.wlist = {}
        self.inc_val = 16
        self.name = name
        self.last_w = None
        self.readers = []
        self.dma_sem = None
        self.dma_cnt = 0
        self.is_dram = is_dram


class T:
    __slots__ = ("buf", "ap")

    def __init__(self, buf, ap):
        self.buf = buf
        self.ap = ap

    def __getitem__(self, idx):
        return T(self.buf, self.ap[idx])

    def re(self, pattern, **kw):
        return T(self.buf, self.ap.rearrange(pattern, **kw))

    def sub(self, buf, idx=None):
        return T(buf, self.ap if idx is None else self.ap[idx])


class Op:
    __slots__ = ("eng", "fn", "reads", "writes", "is_dma", "deps", "inc_idx", "dma_buf",
                 "dma_wait", "gi")

    def __init__(self, eng, fn, reads, writes, is_dma=False, dma_buf=None):
        self.eng = eng
        self.fn = fn
        self.reads = reads
        self.writes = writes
        self.is_dma = is_dma
        self.deps = []
        self.inc_idx = None
        self.dma_buf = dma_buf
        self.dma_wait = []
        self.gi = None


class KB:
    ENGS = ("pe", "act", "dve", "pool", "sp")

    ARENA_WORDS = 53100

    def __init__(self, nc, arena=False):
        self.nc = nc
        self.arena = None
        if arena:
            self.arena = nc.alloc_sbuf_tensor("arena", [128, self.ARENA_WORDS], F32).ap()
            self.a_off = 0
            self.a_peak = 0
            self.banks = [T(Buf("bank%d" % i, psum=True),
                            nc.alloc_psum_tensor("gbank%d" % i, [128, 512], F32).ap()) for i in range(8)]
            self.b_next = 0
        self.ops = []
        self.e = {"pe": nc.tensor, "act": nc.scalar, "dve": nc.vector, "pool": nc.gpsimd,
                  "sp": nc.sync}
        self._n = 0

    def sb(self, name, shape, dtype=F32):
        if self.arena is None:
            h = self.nc.alloc_sbuf_tensor(name, list(shape), dtype)
            return T(Buf(name), h.ap())
        shape = list(shape)
        esz = 2 if dtype == BF16 else 4
        n = 1
        for d in shape[1:]:
            n *= d
        nbytes = (n * esz + 31) // 32 * 32
        nw = nbytes // 4
        off = self.a_off
        assert off + nw <= self.ARENA_WORDS, "arena overflow at %s: %d + %d" % (name, off, nw)
        self.a_off = off + nw
        self.a_peak = max(self.a_peak, self.a_off)
        ap = self.arena[0:shape[0], off:off + (n * esz + 3) // 4]
        if dtype == BF16:
            ap = ap.bitcast(BF16)
            if (n * esz) % 4:
                ap = ap[:, 0:n]
        if len(shape) == 3:
            ap = ap.rearrange("p (a b) -> p a b", a=shape[1])
        elif len(shape) == 4:
            ap = ap.rearrange("p (a b c) -> p a b c", a=shape[1], b=shape[2])
        return T(Buf(name), ap)

    def ps(self, name, shape, dtype=F32):
        if self.arena is None:
            h = self.nc.alloc_psum_tensor(name, list(shape), dtype)
            return T(Buf(name, psum=True), h.ap())
        bk = self.banks[self.b_next]
        self.b_next += 1
        if dtype == BF16:
            return T(bk.buf, bk.ap.bitcast(BF16))
        return bk

    def phase_reset(self):
        self._rec("bar", None, [], [])
        self.a_off = 0
        self.b_next = 0

    def dram_in(self, name, shape, dtype=F32):
        h = self.nc.dram_tensor(name, list(shape), dtype, kind="ExternalInput")
        return T(Buf(name, True), h.ap())

    def dram_out(self, name, shape, dtype=F32):
        h = self.nc.dram_tensor(name, list(shape), dtype, kind="ExternalOutput")
        return T(Buf(name, True), h.ap())

    def dram_scratch(self, name, shape, dtype=F32):
        h = self.nc.dram_tensor(name, list(shape), dtype)
        return T(Buf(name, True), h.ap())

    def buf(self, name):
        self._n += 1
        return Buf("%s_%d" % (name, self._n))

    def _rec(self, eng, fn, reads, writes, is_dma=False, dma_buf=None):
        rb = []
        for t in reads:
            if t is None or isinstance(t, (int, float)):
                continue
            b = t.buf if isinstance(t, T) else t
            if b not in rb:
                rb.append(b)
        wb = []
        for t in writes:
            b = t.buf if isinstance(t, T) else t
            if b not in wb:
                wb.append(b)
        op = Op(eng, fn, rb, wb, is_dma, dma_buf)
        if getattr(self, "_defer", None) is not None:
            self._defer.append(op)
            return op
        op.gi = len(self.ops)
        self.ops.append(op)
        return op

    def begin_defer(self):
        self._defer = []

    def end_defer(self):
        lst = self._defer
        self._defer = None
        return lst

    def splice(self, lst, n):
        for _ in range(min(n, len(lst))):
            op = lst.pop(0)
            op.gi = len(self.ops)
            self.ops.append(op)

    @staticmethod
    def _a(x):
        return x.ap if isinstance(x, T) else x

    def dma(self, out, in_, eng="sp", **kw):
        o, i = self._a(out), self._a(in_)
        sbuf_side = out.buf if not out.buf.is_dram else in_.buf
        return self._rec(eng, lambda E: E.dma_start(out=o, in_=i, **kw), [in_], [out],
                         is_dma=True, dma_buf=sbuf_side)

    def matmul(self, out, lhsT, rhs, start=True, stop=True, extra_reads=(), **kw):
        o, l, r = self._a(out), self._a(lhsT), self._a(rhs)
        reads = [lhsT, rhs] + list(extra_reads)
        if not start:
            reads.append(out)
        return self._rec("pe", lambda E: E.matmul(o, l, r, start=start, stop=stop, **kw),
                         reads, [out])

    def transpose(self, out, in_, ident):
        o, i, d = self._a(out), self._a(in_), self._a(ident)
        return self._rec("pe", lambda E: E.transpose(o, i, d), [in_, ident], [out])

    def act(self, out, in_, func, bias=None, scale=None, accum_out=None, eng="act"):
        o, i = self._a(out), self._a(in_)
        kw = {}
        reads = [in_]
        writes = [out]
        if bias is not None:
            kw["bias"] = self._a(bias)
            reads.append(bias)
        if scale is not None:
            kw["scale"] = self._a(scale)
            reads.append(scale)
        if accum_out is not None:
            kw["accum_out"] = self._a(accum_out)
            writes.append(accum_out)
        return self._rec(eng, lambda E: E.activation(o, i, func, **kw), reads, writes)

    def tt(self, out, in0, in1, op, eng="dve"):
        o, a, b = self._a(out), self._a(in0), self._a(in1)
        return self._rec(eng, lambda E: E.tensor_tensor(o, a, b, op), [in0, in1], [out])

    def ts(self, out, in0, s1, op0, s2=None, op1=None, accum_out=None, eng="dve"):
        o, a = self._a(out), self._a(in0)
        s1a, s2a = self._a(s1), self._a(s2)
        kw = {}
        writes = [out]
        if op1 is not None:
            kw["op1"] = op1
        if accum_out is not None:
            kw["accum_out"] = self._a(accum_out)
            writes.append(accum_out)
        return self._rec(eng, lambda E: E.tensor_scalar(o, a, s1a, s2a, op0, **kw),
                         [in0, s1, s2], writes)

    def stt(self, out, in0, scalar, in1, op0, op1, eng="dve"):
        o, a, s, b = self._a(out), self._a(in0), self._a(scalar), self._a(in1)
        return self._rec(eng, lambda E: E.scalar_tensor_tensor(o, a, s, b, op0, op1),
                         [in0, scalar, in1], [out])

    def copy(self, out, in_, eng="dve"):
        o, i = self._a(out), self._a(in_)
        if eng == "act":
            return self._rec(eng, lambda E: E.copy(o, i), [in_], [out])
        return self._rec(eng, lambda E: E.tensor_copy(o, i), [in_], [out])

    def memset(self, out, val, eng="pool"):
        o = self._a(out)
        return self._rec(eng, lambda E: E.memset(o, val), [], [out])

    def recip(self, out, in_):
        o, i = self._a(out), self._a(in_)
        return self._rec("dve", lambda E: E.reciprocal(o, i), [in_], [out])

    def reduce(self, out, in_, op, axis=AX.X, eng="dve"):
        o, i = self._a(out), self._a(in_)
        return self._rec(eng, lambda E: E.tensor_reduce(o, i, axis, op), [in_], [out])

    def affine_select(self, out, in_, pattern, compare_op, fill, base=0, channel_multiplier=0):
        o, i = self._a(out), self._a(in_)
        return self._rec("pool", lambda E: E.affine_select(
            o, i, pattern, compare_op, fill, base=base, channel_multiplier=channel_multiplier),
            [in_], [out])

    def iota(self, out, pattern, base=0, channel_multiplier=0, **kw):
        o = self._a(out)
        return self._rec("pool", lambda E: E.iota(o, pattern, base=base,
                                                   channel_multiplier=channel_multiplier, **kw),
                         [], [out])

    def collective(self, kind, in_, out, groups, op=None):
        i, o = self._a(in_), self._a(out)
        alu = ALU.bypass if op is None else op
        semb = Buf("cc_%d" % len(self.ops))
        semb.inc_val = 1
        return self._rec("pool", lambda E: E.collective_compute(kind, alu, groups, [i], [o]),
                         [in_], [out], is_dma=True, dma_buf=semb)

    def generic(self, eng, fn, reads, writes):
        return self._rec(eng, fn, reads, writes)

    def finish(self, final_wait_outputs=True):
        nc = self.nc
        ops = self.ops
        last_on = {}
        last_dma = {}
        bar_deps = {e: None for e in self.ENGS}
        for op in ops:
            if op.eng == "bar":
                allp = set(last_on.values()) | set(last_dma.values())
                for e in self.ENGS:
                    bar_deps[e] = set(allp) | (bar_deps[e] or set())
                continue
            deps = set()
            for b in op.reads:
                if b.last_w is not None:
                    deps.add(b.last_w)
                if b.is_dram:
                    for w_ in b.wlist.values():
                        deps.add(w_)
                if b.psum:
                    for r in b.readers:
                        if ops[r].eng != op.eng:
                            deps.add(r)
            for b in op.writes:
                if b.last_w is not None:
                    deps.add(b.last_w)
                for r in b.readers:
                    deps.add(r)
            deps.discard(op.gi)
            keep = []
            if bar_deps[op.eng] is not None:
                keep.extend(sorted(bar_deps[op.eng]))
                bar_deps[op.eng] = None
            last_on[op.eng] = op.gi
            if op.is_dma:
                last_dma[id(op.dma_buf)] = op.gi
            for d in deps:
                dop = ops[d]
                if dop.is_dma:
                    keep.append(d)
                    continue
                if dop.eng == op.eng:
                    if op.eng in ("pe", "sp"):
                        continue
                    if op.is_dma:
                        keep.append(d)
                        continue
                    raw = any(b.last_w == d for b in op.reads)
                    if not raw:
                        continue
                keep.append(d)
            op.deps = keep
            for b in op.reads:
                b.readers.append(op.gi)
            for b in op.writes:
                b.last_w = op.gi
                b.readers = []
                if b.is_dram and op.is_dma:
                    b.wlist[id(op.dma_buf)] = op.gi
        needed = set()
        for op in ops:
            for d in op.deps:
                needed.add(d)
        final_dma = [op for op in ops if op.is_dma and any(b.is_dram for b in op.writes)]
        ecount = {e: 0 for e in self.ENGS}
        phase = 0
        nslot = 0
        slot_total = []
        slot_of = {}
        ncc = 0
        abs_cnt = {}
        sem_key = {}
        for op in ops:
            if op.eng == "bar":
                phase += 1
                nslot = 0
                continue
            if op.is_dma:
                b = op.dma_buf
                if b.inc_val == 1:
                    if id(b) not in sem_key:
                        sem_key[id(b)] = ("c", ncc)
                        ncc += 1
                        b.dma_cnt = 0
                    b.dma_cnt += 1
                    abs_cnt[op.gi] = b.dma_cnt
                else:
                    ps_ = slot_of.get(id(b))
                    if ps_ is None or ps_[0] != phase:
                        slot_of[id(b)] = (phase, nslot)
                        if nslot >= len(slot_total):
                            slot_total.append(0)
                        sem_key[id(b)] = ("p", nslot)
                        nslot += 1
                    sl = slot_of[id(b)][1]
                    slot_total[sl] += 1
                    abs_cnt[op.gi] = slot_total[sl]
                op.inc_idx = abs_cnt[op.gi]
            elif op.gi in needed:
                ecount[op.eng] += 1
                op.inc_idx = ecount[op.eng]
        import contextlib
        self._stack = contextlib.ExitStack()
        self._sems = {}
        nsem = 0
        for e in self.ENGS:
            n = max((ecount[e] + SEM_CHUNK - 1) // SEM_CHUNK, 1)
            self._sems[e] = [self._stack.enter_context(nc.semaphore("s_%s_%d" % (e, k))) for k in range(n)]
            nsem += n
        dsem = {}
        for i in range(len(slot_total)):
            dsem[("p", i)] = self._stack.enter_context(nc.semaphore("d_%d" % i))
        for i in range(ncc):
            dsem[("c", i)] = self._stack.enter_context(nc.semaphore("c_%d" % i))
        nsem += len(dsem)
        self.nsem = nsem
        waited = {}
        last_abs = {}
        cur_key = {}
        op_key = {}
        phase = 0
        nslot = 0
        slot_of2 = {}
        for op in ops:
            if op.eng == "bar":
                phase += 1
                nslot = 0
                continue
            if op.is_dma:
                b = op.dma_buf
                if b.inc_val == 1:
                    op_key[op.gi] = sem_key[id(b)]
                else:
                    ps_ = slot_of2.get(id(b))
                    if ps_ is None or ps_[0] != phase:
                        slot_of2[id(b)] = (phase, nslot)
                        nslot += 1
                    op_key[op.gi] = ("p", slot_of2[id(b)][1])
        plan = {e: [] for e in self.ENGS}
        for op in ops:
            if op.eng == "bar":
                continue
            waits = {}
            for d in op.deps:
                dop = ops[d]
                if dop.is_dma:
                    b = dop.dma_buf
                    key = ("d",) + cur_key[id(b)]
                    sem = dsem[cur_key[id(b)]]
                    val = b.inc_val * last_abs[id(b)]
                else:
                    idx = dop.inc_idx - 1
                    ch = idx // SEM_CHUNK
                    val = idx % SEM_CHUNK + 1
                    key = ("e", dop.eng, ch)
                    sem = self._sems[dop.eng][ch]
                    for c2 in range(ch):
                        waited[(op.eng, ("e", dop.eng, c2))] = SEM_CHUNK
                cur = waited.get((op.eng, key), 0)
                if val > cur:
                    waited[(op.eng, key)] = val
                    if key not in waits or waits[key][1] < val:
                        waits[key] = (sem, val)
            plan[op.eng].append((op, list(waits.values())))
            if op.is_dma:
                last_abs[id(op.dma_buf)] = abs_cnt[op.gi]
                cur_key[id(op.dma_buf)] = op_key[op.gi]
        self.stats = {e: len(plan[e]) for e in self.ENGS}
        self.stats["nsem"] = nsem
        self.stats["waits"] = sum(len(w) for e in self.ENGS for _, w in plan[e])
        if self.arena is not None:
            self.stats["arena_peak_words"] = self.a_peak
        for e in self.ENGS:
            E = self.e[e]
            for op, waits in plan[e]:
                for sem, val in waits:
                    E.wait_ge(sem, val)
                ins = op.fn(E)
                if op.is_dma:
                    ins.then_inc(dsem[op_key[op.gi]], op.dma_buf.inc_val)
                elif op.inc_idx is not None:
                    idx = op.inc_idx - 1
                    ins.then_inc(self._sems[e][idx // SEM_CHUNK], 1)
        if final_wait_outputs:
            fin = {}
            for op in final_dma:
                kk = op_key[op.gi]
                v = op.dma_buf.inc_val * abs_cnt[op.gi]
                if kk not in fin or fin[kk] < v:
                    fin[kk] = v
            for kk, v in fin.items():
                nc.sync.wait_ge(dsem[kk], v)
        return self.stats


L = 8192
NT = L // 512
NCH = L // 128
NORM_EPS = 1e-6
GN_EPS = 64e-5
C0 = math.exp(-0.5)


def load_hT(k, hT_tile, hsrc, j):
    v = hT_tile.re("p (g q) t -> p g q t", q=2)
    for r in range(4):
        k.dma(v[(r % 2) * 64:(r % 2) * 64 + 64, :, r // 2, :],
              hsrc[r * 256:(r + 1) * 256, j * 512:(j + 1) * 512].re("(g i) t -> i g t", i=64),
              eng="sp" if r % 2 == 0 else "pool")


L = 8192
NT = L // 512
NORM_EPS = 1e-6


ATT_KNOBS = {}


def emit_attn(k, io, lambda_init):
    xT, hsrc, pw, wfm, wtm, tabs, vecs, lam = (io.get("xT"), io.get("hT"), io["pw"], io["wfm"], io["wtm"],
                                                io["tabs"], io["vecs"], io["lam"])
    ydst, ydt = io["y"], io["ydt"]

    ones_bf = k.sb("ones_bf", [128, 128], BF16)
    k.memset(ones_bf, 1.0)
    eps_t = k.sb("eps_t", [128, 1])
    k.memset(eps_t, NORM_EPS)
    pw_sb = k.sb("pw_sb", [128, 8])
    k.dma(pw_sb, pw)
    vec_sb = k.sb("vec_sb", [128, 8])
    k.dma(vec_sb, vecs)
    lam_sb = k.sb("lam_sb", [128, 256])
    k.dma(lam_sb, lam)
    w_fm = k.sb("w_fm", [128, 8, 1280], BF16)
    w_tm = k.sb("w_tm", [128, 8, 256], BF16)
    wst = [k.sb("wst%d" % i, [128, 8, 256]) for i in range(2)]
    for i in range(6):
        st = wst[i % 2]
        if i < 5:
            k.dma(st, wfm[:, i * 256:(i + 1) * 256].re("(c p) m -> p c m", p=128),
                  eng="pool" if i % 2 else "sp")
            k.copy(w_fm[:, :, i * 256:(i + 1) * 256], st, eng="pool")
        else:
            k.dma(st, wtm.re("(c p) m -> p c m", p=128), eng="pool")
            k.copy(w_tm, st, eng="pool")

    lt = k.sb("lam_t", [128, 128])
    s12 = k.sb("lam_s", [128, 2])
    k.tt(lt[:, 0:64], lam_sb[:, 0:64], lam_sb[:, 64:128], ALU.mult)
    k.tt(lt[:, 64:128], lam_sb[:, 128:192], lam_sb[:, 192:256], ALU.mult)
    k.reduce(s12[:, 0:1], lt[:, 0:64], ALU.add)
    k.reduce(s12[:, 1:2], lt[:, 64:128], ALU.add)
    e12 = k.sb("lam_e", [128, 2])
    k.act(e12, s12, AF.Exp)
    neg_lam = k.sb("neg_lam", [128, 1])
    k.stt(neg_lam, e12[:, 1:2], -float(lambda_init), e12[:, 0:1], ALU.add, ALU.subtract)
    wn_s = k.sb("wn_s", [128, 1])
    k.ts(wn_s, vec_sb[:, 4:5], 1.0 - float(lambda_init), ALU.mult)

    xbuf = [k.sb("xbuf%d" % i, [128, 8, 512]) for i in range(2)]
    hT = [k.sb("hT%d" % i, [128, 8, 512], BF16) for i in range(2)]
    sq = [k.sb("sq%d" % i, [128, 512], BF16) for i in range(3)]
    rstd = [k.sb("rstd%d" % i, [128, 512]) for i in range(2)]
    tab = [k.sb("tab%d" % i, [128, 2, 512]) for i in range(2)]
    tmp = [k.sb("tmp%d" % i, [128, 512]) for i in range(4)]
    qT = k.sb("qT", [128, L], BF16)
    kT = k.sb("kT", [128, L], BF16)
    gT = k.sb("gT", [128, L], BF16)
    v_tm = k.sb("v_tm", [128, 64, 128], BF16)
    Pb = [[k.sb("Pb%d_%d" % (m, i), [128, 512], BF16) for i in range(3)] for m in range(2)]
    accD = [k.sb("accD%d" % m, [128, 512]) for m in range(2)]
    accP = [k.sb("accP%d" % m, [128, 512]) for m in range(2)]
    ones_f = k.sb("ones_f", [128, 128])
    k.memset(ones_f, 1.0)
    ost = [k.sb("ost%d" % i, [128, 512], ydt) for i in range(2)]
    fin = [k.sb("fin%d" % i, [128, 512]) for i in range(5)]
    bank = [k.ps("bank%d" % i, [128, 512]) for i in range(8)]

    for kind in range(2):
        wc = kind * 5
        def stage1(j):
            k.dma(tab[j % 2], tabs[2 * kind:2 * kind + 2, :, j * 512:(j + 1) * 512]
                  .re("a p t -> p a t"), eng="pool")
            if hsrc is not None:
                load_hT(k, hT[j % 2], hsrc, j)
                return
            xt = xbuf[j % 2]
            k.dma(xt, xT[:, j * 512:(j + 1) * 512].re("(c p) t -> p c t", p=128),
                  eng="sp")
            for c in range(8):
                s = sq[c % 3]
                k.act(s, xt[:, c, :], AF.Square)
                k.matmul(bank[0], ones_bf, s, start=(c == 0), stop=(c == 7))
            r = rstd[j % 2]
            k.act(r, bank[0], AF.Ln, scale=1.0 / 1024, bias=eps_t)
            k.act(r, r, AF.Exp, scale=-0.5)
            for c in range(8):
                k.stt(hT[j % 2][:, c, :], xt[:, c, :], pw_sb[:, c:c + 1], r, ALU.mult, ALU.mult)

        def proj_fm(ps, grp, j):
            h = hT[j % 2]
            for c in range(8):
                k.matmul(ps, w_fm[:, c, grp * 128:(grp + 1) * 128], h[:, c, :],
                         start=(c == 0), stop=(c == 7))

        def stage2(j):
            h = hT[j % 2]
            tb = tab[j % 2]
            ts_ = slice(j * 512, (j + 1) * 512)
            for which, dst in ((0, qT), (1, kT)):
                pa, pb = bank[1 + 2 * which], bank[2 + 2 * which]
                proj_fm(pa, wc + 2 * which, j)
                proj_fm(pb, wc + 2 * which + 1, j)
                t1, t2 = tmp[2 * which], tmp[2 * which + 1]
                if kind == 0:
                    k.tt(t1, pa, tb[:, 0, :], ALU.mult)
                    k.tt(t2, pb, tb[:, 1, :], ALU.mult)
                    k.tt(dst[:, ts_], t1, t2, ALU.add, eng="pool")
                else:
                    s = sq[which]
                    k.act(s, pa, AF.Square)
                    k.matmul(bank[7], ones_bf, s)
                    rr = fin[which]
                    k.act(rr, bank[7], AF.Ln, scale=1.0 / 128, bias=eps_t)
                    k.act(rr, rr, AF.Exp, scale=-0.5)
                    k.stt(t1, pa, vec_sb[:, 2 * which:2 * which + 1], tb[:, 0, :], ALU.mult, ALU.mult)
                    k.stt(t2, pb, vec_sb[:, 2 * which + 1:2 * which + 2], tb[:, 1, :], ALU.mult, ALU.mult)
                    k.tt(t1, t1, t2, ALU.add, eng="pool")
                    k.tt(dst[:, ts_], t1, rr, ALU.mult, eng="pool")
            proj_fm(bank[5], wc + 4, j)
            k.act(gT[:, ts_], bank[5], AF.Silu)
            pv = bank[6]
            for s4 in range(4):
                for c in range(8):
                    k.matmul(pv[:, s4 * 128:(s4 + 1) * 128], h[:, c, s4 * 128:(s4 + 1) * 128],
                             w_tm[:, c, kind * 128:(kind + 1) * 128],
                             start=(c == 0), stop=(c == 7))
            k.copy(v_tm[:, j * 4:(j + 1) * 4, :], pv.re("p (s e) -> p s e", s=4), eng="dve")

        for j in range(NT + 1):
            if j < NT:
                stage1(j)
            if j >= 1:
                stage2(j - 1)

        nm = 2 if kind == 0 else 1
        dk = 64 if kind == 0 else 128
        scale = dk ** -0.5
        O = [bank[0], bank[1]][:nm]
        Sb = [[bank[2], bank[3]], [bank[4], bank[5]]]
        misc = bank[7]
        pairs = [(qb, kc) for qb in range(16) for kc in range(64)]
        LA = 2
        npair = len(pairs)

        def finalize(qb):
            qs = slice(qb * 512, (qb + 1) * 512)
            o = []
            for m in range(nm):
                k.matmul(misc, ones_f, accD[m], start=True, stop=False)
                k.matmul(misc, ones_f, accP[m], start=False, stop=True)
                r = fin[m]
                k.act(r, misc, AF.Ln)
                k.act(r, r, AF.Exp, scale=-1.0)
                om = fin[2 + m]
                k.tt(om, O[m], r, ALU.mult)
                o.append(om)
            y = ost[qb % 2]
            if kind == 0:
                od = fin[4]
                k.stt(od, o[1], neg_lam, o[0], ALU.mult, ALU.add)
                s = sq[0]
                k.act(s, od, AF.Square)
                k.matmul(misc, ones_bf, s)
                rr = fin[0]
                k.act(rr, misc, AF.Ln, scale=1.0 / 128, bias=eps_t)
                k.act(rr, rr, AF.Exp, scale=-0.5)
                k.stt(od, od, wn_s, rr, ALU.mult, ALU.mult)
                k.tt(y, od, gT[:, qs], ALU.mult, eng="pool")
            else:
                k.tt(y, o[0], gT[:, qs], ALU.mult, eng="pool")
            k.dma(ydst[kind][:, qs], y, eng="sp")

        for idx in range(npair + LA):
            if idx < npair:
                qb, kc = pairs[idx]
                for m in range(nm):
                    k.matmul(Sb[m][idx % 2], kT[m * dk:(m + 1) * dk, kc * 128:(kc + 1) * 128],
                             qT[m * dk:(m + 1) * dk, qb * 512:(qb + 1) * 512])
                for m in range(nm):
                    k.act(Pb[m][idx % 3], Sb[m][idx % 2], AF.Exp, scale=scale)
            i2 = idx - LA
            if i2 >= 0:
                qb, kc = pairs[i2]
                for m in range(nm):
                    P = Pb[m][i2 % 3]
                    k.matmul(O[m], v_tm[:, kc, :], P, start=(kc == 0), stop=(kc == 63))
                    PMOD = ATT_KNOBS.get('pmod', 1 << 30)
                    on_pool = (kc % PMOD == PMOD - 1)
                    acc, eng = (accP[m], "pool") if on_pool else (accD[m], "dve")
                    first = (kc == PMOD - 1) if on_pool else (kc == 0)
                    if first:
                        k.copy(acc, P, eng=eng)
                    else:
                        k.tt(acc, acc, P, ALU.add, eng=eng)
                if kc == 63:
                    finalize(qb)
        if kind == 0 and io.get("after_kind0") is not None:
            io["after_kind0"]()


def emit_ssd(k, io):
    stop = 99
    xT, hsrc, pw, wfm, wdt, cw, cb, ssmv, cst = (io.get("xT"), io.get("hT"), io["pw"], io["wfm"], io["wdt"],
                                                 io["cw"], io["cb"], io["ssmv"], io["cst"])
    yT, ydt = io["y"], io["ydt"]

    ones_bf = k.sb("ones_bf", [128, 128], BF16)
    k.memset(ones_bf, 1.0)
    ones_f = k.sb("ones_f", [128, 128])
    k.memset(ones_f, 1.0)
    eps_t = k.sb("eps_t", [128, 1])
    k.memset(eps_t, NORM_EPS)
    one_t = k.sb("one_t", [128, 1])
    k.memset(one_t, 1.0)
    cst_sb = k.sb("cst_sb", [128, 5, 128])
    k.dma(cst_sb, cst)
    tri = [cst_sb[:, 0, :], cst_sb[:, 1, :]]
    smask = [cst_sb[:, 2, :], cst_sb[:, 3, :]]
    ident_f = cst_sb[:, 4, :]
    ident_bf = k.sb("ident_bf", [128, 128], BF16)
    k.copy(ident_bf, ident_f, eng="dve")
    pw_sb = k.sb("pw_sb", [128, 8])
    k.dma(pw_sb, pw)
    cw_sb = k.sb("cw_sb", [128, 3, 5])
    k.dma(cw_sb, cw)
    cb_sb = k.sb("cb_sb", [128, 3])
    k.dma(cb_sb, cb)
    sv = k.sb("sv", [128, 16])
    k.dma(sv, ssmv)
    w_fm = k.sb("w_fm", [128, 8, 512], BF16)
    w_dt = k.sb("w_dt", [128, 8, 4], BF16)
    wst2 = k.sb("wst2", [128, 8, 4])
    k.dma(wst2, wdt.re("(c p) m -> p c m", p=128), eng="pool")
    k.copy(w_dt, wst2, eng="pool")

    xbuf = [k.sb("xbuf%d" % i, [128, 8, 512]) for i in range(1)]
    k.dma(xbuf[0], wfm.re("(c p) m -> p c m", p=128))
    k.copy(w_fm, xbuf[0], eng="pool")
    hT = [k.sb("hT%d" % i, [128, 8, 512], BF16) for i in range(2)]
    sq = [k.sb("sq%d" % i, [128, 512], BF16) for i in range(3)]
    rstd = [k.sb("rstd%d" % i, [128, 512]) for i in range(1)] * 2
    upad = [k.sb("upad%d" % i, [128, 3, 516]) for i in range(3)]
    acc = [k.sb("acc%d" % i, [128, 512]) for i in range(3)]
    zsT = k.sb("zsT", [128, L], BF16)
    xcT = k.sb("xcT", [128, L], BF16)
    BT = k.sb("BT", [128, L], BF16)
    CT = k.sb("CT", [128, L], BF16)
    dst3 = [xcT, BT, CT]
    dtraw = k.sb("dtraw", [128, 4, NCH])
    bank = [k.ps("bank%d" % i, [128, 512]) for i in range(7)]
    tbank = k.ps("tbank", [128, 1024], BF16)

    k.memset(upad[0][:, :, 0:2], 0.0)

    def stage1(j):
        if hsrc is not None:
            load_hT(k, hT[j % 2], hsrc, j)
            return
        xt = xbuf[0]
        k.dma(xt, xT[:, j * 512:(j + 1) * 512].re("(c p) t -> p c t", p=128), eng="sp")
        for c in range(8):
            s = sq[c % 3]
            k.act(s, xt[:, c, :], AF.Square)
            k.matmul(bank[0], ones_bf, s, start=(c == 0), stop=(c == 7))
        r = rstd[j % 2]
        k.act(r, bank[0], AF.Ln, scale=1.0 / 1024, bias=eps_t)
        k.act(r, r, AF.Exp, scale=-0.5)
        for c in range(8):
            k.stt(hT[j % 2][:, c, :], xt[:, c, :], pw_sb[:, c:c + 1], r, ALU.mult, ALU.mult)

    def stage2(j):
        h = hT[j % 2]
        ts_ = slice(j * 512, (j + 1) * 512)
        up = upad[j % 3]
        for grp in range(4):
            ps = bank[1 + grp]
            for c in range(8):
                k.matmul(ps, w_fm[:, c, grp * 128:(grp + 1) * 128], h[:, c, :],
                         start=(c == 0), stop=(c == 7))
            if grp == 0:
                k.act(zsT[:, ts_], ps, AF.Silu)
            else:
                k.copy(up[:, grp - 1, 2:514], ps, eng="act")
        pdt = bank[5]
        for s4 in range(4):
            for c in range(8):
                k.matmul(pdt[:, s4 * 4:(s4 + 1) * 4], h[:, c, s4 * 128:(s4 + 1) * 128], w_dt[:, c, :],
                         start=(c == 0), stop=(c == 7))
        k.copy(dtraw[:, :, j * 4:(j + 1) * 4].re("p h s -> p s h"),
               pdt[:, 0:16].re("p (s h) -> p s h", s=4), eng="dve")

    def conv(j):
        up = upad[j % 3]
        if j > 0:
            k.copy(up[:, :, 0:2], upad[(j - 1) % 3][:, :, 512:514], eng="pool")
        if j < NT - 1:
            k.copy(up[:, :, 514:516], upad[(j + 1) % 3][:, :, 2:4], eng="pool")
        else:
            k.memset(up[:, :, 514:516], 0.0)
        ts_ = slice(j * 512, (j + 1) * 512)
        for ch in range(3):
            a = acc[ch]
            k.ts(a, up[:, ch, 0:512], cw_sb[:, ch, 0:1], ALU.mult)
            for o in range(1, 5):
                k.stt(a, up[:, ch, o:o + 512], cw_sb[:, ch, o:o + 1], a, ALU.mult, ALU.add)
            k.act(dst3[ch][:, ts_], a, AF.Silu, bias=cb_sb[:, ch:ch + 1])

    for j in range(NT + 2):
        if j < NT:
            stage1(j)
        if 1 <= j <= NT:
            stage2(j - 1)
        if j >= 2:
            conv(j - 2)


    dt = k.sb("dt", [128, 4, NCH])
    a_ = k.sb("a_", [128, 4, NCH])
    cum = k.sb("cum", [128, 4, NCH])
    dtd = k.sb("dtd", [128, 4, NCH])
    etot = k.sb("etot", [128, 4, NCH])
    aneg = k.sb("aneg", [128, 4])
    k.act(aneg, sv[:, 4:8], AF.Exp)
    k.ts(aneg, aneg, -1.0, ALU.mult)
    for hd in range(4):
        k.act(dt[:, hd, :], dtraw[:, hd, :], AF.Exp, bias=sv[:, hd:hd + 1])
    k.act(dt, dt, AF.Ln, bias=one_t)
    for hd in range(4):
        k.ts(a_[:, hd, :], dt[:, hd, :], aneg[:, hd:hd + 1], ALU.mult)
    pc = bank[0]
    k.matmul(pc[:, 0:128], tri[0], a_[:, 0:2, :].re("p h c -> p (h c)"))
    k.matmul(pc[:, 128:256], tri[1], a_[:, 2:4, :].re("p h c -> p (h c)"))
    k.matmul(pc[:, 256:512], ones_f, a_.re("p h c -> p (h c)"))
    k.copy(cum.re("p h c -> p (h c)"), pc[:, 0:256], eng="dve")
    k.act(etot.re("p h c -> p (h c)"), pc[:, 256:512], AF.Exp)
    k.tt(dtd.re("p h c -> p (h c)"), pc[:, 256:512], cum.re("p h c -> p (h c)"), ALU.subtract)
    k.act(dtd, dtd, AF.Exp)
    k.tt(dtd, dtd, dt, ALU.mult)

    Sf_all = k.sb("Sf_all", [128, NCH, 128], BF16)
    Sb_all = k.sb("Sb_all", [128, NCH, 128], BF16)
    Srun = [k.sb("Srun%d" % i, [128, 128]) for i in range(2)]
    btm = [k.sb("btm%d" % i, [128, 128], BF16) for i in range(2)]
    xdtd = [k.sb("xdtd%d" % i, [128, 128], BF16) for i in range(2)]
    for d in range(2):
        k.memset(Srun[d], 0.0)
        order = range(NCH) if d == 0 else range(NCH - 1, -1, -1)
        S_all = Sf_all if d == 0 else Sb_all
        for i, c in enumerate(order):
            cs = slice(c * 128, (c + 1) * 128)
            tb = tbank[:, (i % 2) * 256:(i % 2) * 256 + 256]
            k.transpose(tb[:, 0:128], xcT[:, cs], ident_bf)
            k.transpose(tb[:, 128:256], BT[:, cs], ident_bf)
            bt = btm[i % 2]
            k.copy(bt, tb[:, 128:256], eng="act")
            xd = xdtd[i % 2]
            for h in range(2):
                hd = d * 2 + h
                k.ts(xd[:, h * 64:(h + 1) * 64], tb[:, h * 64:(h + 1) * 64], dtd[:, hd, c:c + 1], ALU.mult)
            st = bank[1 + i % 2]
            k.matmul(st[:, 0:128], bt, xd)
            k.copy(S_all[:, c, :], Srun[d], eng="act")
            for h in range(2):
                hd = d * 2 + h
                hb = slice(h * 64, (h + 1) * 64)
                k.stt(Srun[d][:, hb], Srun[d][:, hb], etot[:, hd, c:c + 1], st[:, hb], ALU.mult, ALU.add)

    lhsD = [k.sb("lhsD%d" % i, [128, 4, 128]) for i in range(1)] * 2
    abc = [k.sb("abc%d" % i, [128, 4, 128]) for i in range(1)] * 2
    Lexp = [k.sb("Lexp%d" % i, [128, 512]) for i in range(2)]
    Ebc = [k.sb("Ebc%d" % i, [128, 512]) for i in range(1)] * 2
    Gm = [k.sb("Gm%d" % i, [128, 2, 128]) for i in range(2)]
    MT = [k.sb("MT%d" % i, [128, 4, 128], BF16) for i in range(2)]
    Ct = [k.sb("Ct%d" % i, [128, 4, 128], BF16) for i in range(2)]
    xtm = [k.sb("xtm%d" % i, [128, 128]) for i in range(2)]
    xdt = [k.sb("xdt%d" % i, [128, 4, 64], BF16) for i in range(2)]
    y_sb = [k.sb("y_sb%d" % i, [128, 128]) for i in range(2)]
    ost = [k.sb("ost%d" % i, [128, 512], ydt) for i in range(2)]
    for c in range(NCH):
        cs = slice(c * 128, (c + 1) * 128)
        p = c % 2
        tb = tbank[:, p * 256:p * 256 + 128]
        k.transpose(tb, xcT[:, cs], ident_bf)
        k.copy(xtm[p], tb, eng="act")
        for hd in range(4):
            k.ts(xdt[p][:, hd, :], xtm[p][:, (hd % 2) * 64:(hd % 2) * 64 + 64], dt[:, hd, c:c + 1], ALU.mult)
        for hd in range(4):
            d = hd // 2
            k.ts(lhsD[p][:, hd, :], smask[d], a_[:, hd, c:c + 1], ALU.mult)
            k.ts(abc[p][:, hd, :], ones_f, a_[:, hd, c:c + 1], ALU.mult)
        Dps, Cps = bank[3], bank[4]
        for hd in range(4):
            d = hd // 2
            k.matmul(Dps[:, hd * 128:(hd + 1) * 128], lhsD[p][:, hd, :], tri[d])
        for hd in range(4):
            d = hd // 2
            k.matmul(Cps[:, hd * 128:(hd + 1) * 128], abc[p][:, hd, :], tri[d])
        k.act(Lexp[p], Dps, AF.Exp)
        k.act(Ebc[p], Cps, AF.Exp)
        Gps = bank[5]
        k.matmul(Gps[:, 0:128], BT[:, cs], CT[:, cs])
        for d in range(2):
            k.tt(Gm[p][:, d, :], Gps[:, 0:128], tri[d], ALU.mult)
        for hd in range(4):
            d = hd // 2
            k.tt(MT[p][:, hd, :], Lexp[p][:, hd * 128:(hd + 1) * 128], Gm[p][:, d, :], ALU.mult)
            k.tt(Ct[p][:, hd, :], Ebc[p][:, hd * 128:(hd + 1) * 128], CT[:, cs], ALU.mult)
        Yps = bank[6]
        for h in range(2):
            hb = slice(h * 64, (h + 1) * 64)
            k.matmul(Yps[:, hb], MT[p][:, h, :], xdt[p][:, h, :], start=True, stop=False)
            k.matmul(Yps[:, hb], MT[p][:, 2 + h, :], xdt[p][:, 2 + h, :], start=False, stop=False)
            k.matmul(Yps[:, hb], Ct[p][:, h, :], Sf_all[:, c, hb], start=False, stop=False)
            k.matmul(Yps[:, hb], Ct[p][:, 2 + h, :], Sb_all[:, c, hb], start=False, stop=True)
        for h in range(2):
            hb = slice(h * 64, (h + 1) * 64)
            k.stt(y_sb[p][:, hb], xtm[p][:, hb], sv[:, 8 + h:9 + h], Yps[:, hb], ALU.mult, ALU.add)
        yTp = bank[1 + p]
        k.transpose(yTp[:, 0:128], y_sb[p], ident_f)
        o = ost[(c // 4) % 2]
        k.tt(o[:, (c % 4) * 128:(c % 4 + 1) * 128], yTp[:, 0:128], zsT[:, cs], ALU.mult)
        if c % 4 == 3:
            k.dma(yT[:, (c - 3) * 128:(c + 1) * 128], o, eng="sp")


L = 8192
NT = L // 512
NORM_EPS = 1e-6
GN_EPS = 64e-5
C0 = math.exp(-0.5)


def emit_rwkv(k, io):
    stop, nq_lim, do_chain, do_pre, pre_lim = 99, None, True, True, 9
    xT, hsrc, pw, wfm, mu, w2a2, pvd, cm, c128 = (io.get("xT"), io.get("hT"), io["pw"], io["wfm"], io["mu"],
                                                  io["w2a2"], io["pvd"], io["cm"], io["c128"])
    wkv_scr, bon_scr, yT, ydt = io["wkv_scr"], io["bon_scr"], io["y"], io["ydt"]

    ones_bf = k.sb("ones_bf", [128, 128], BF16)
    k.memset(ones_bf, 1.0)
    eps_t = k.sb("eps_t", [128, 1])
    k.memset(eps_t, NORM_EPS)
    epsg_t = k.sb("epsg_t", [128, 1])
    k.memset(epsg_t, GN_EPS)
    tiny_t = k.sb("tiny_t", [128, 1])
    k.memset(tiny_t, 1e-24)
    pw_sb = k.sb("pw_sb", [128, 8])
    k.dma(pw_sb, pw)
    mu_sb = k.sb("mu_sb", [128, 2, 4])
    k.dma(mu_sb, mu)
    w2a2_sb = k.sb("w2a2_sb", [128, 2, 128])
    k.dma(w2a2_sb, w2a2)
    pv = k.sb("pv", [128, 12])
    k.dma(pv, pvd)
    omk = k.sb("omk", [128, 1])
    k.ts(omk, pv[:, 4:5], -1.0, ALU.mult, 1.0, ALU.add)
    cm_sb = k.sb("cm_sb", [64, 2, 704])
    k.dma(cm_sb, cm)
    c128_sb = k.sb("c128_sb", [128, 2, 128])
    k.dma(c128_sb, c128)
    blk_bf = k.sb("blk_bf", [128, 128], BF16)
    k.copy(blk_bf, c128_sb[:, 0, :], eng="dve")
    ident_f = c128_sb[:, 1, :]
    ident_bf = k.sb("ident_bf", [128, 128], BF16)
    k.copy(ident_bf, ident_f, eng="dve")
    id64_bf = k.sb("id64_bf", [64, 4, 64], BF16)
    for h in range(4):
        k.copy(id64_bf[:, h, :], cm_sb[:, 0, 640:704], eng="dve")
    maskSC = [cm_sb[:, d, 0:512] for d in range(2)]
    maskN = [cm_sb[:, d, 512:640] for d in range(2)]

    xbuf = k.sb("xbuf", [128, 8, 512])
    w_fm = k.sb("w_fm", [128, 8, 640], BF16)
    k.dma(xbuf, wfm[:, 0:512].re("(c p) m -> p c m", p=128))
    k.copy(w_fm[:, :, 0:512], xbuf, eng="pool")
    k.dma(xbuf[:, :, 0:128], wfm[:, 512:640].re("(c p) m -> p c m", p=128))
    k.copy(w_fm[:, :, 512:640], xbuf[:, :, 0:128], eng="pool")

    hT = k.sb("hT", [128, 8, 512], BF16)
    sq = [k.sb("sq%d" % i, [128, 512], BF16) for i in range(2)]
    rstd = k.sb("rstd", [128, 512])
    upad = [k.sb("upad%d" % d, [128, 4, 514]) for d in range(2)]
    ul = k.sb("ul", [128, 4, 512])
    tmpd = [k.sb("tmpd%d" % i, [128, 512]) for i in range(2)]
    gT = k.sb("gT", [128, L], BF16)
    twd = k.sb("twd", [64, 512])
    sg = k.sb("sg", [128, 512])
    aic = k.sb("aic", [128, 512])
    kk = k.sb("kk", [128, 512])
    kkn = k.sb("kkn", [128, 512])
    rs = k.sb("rs", [128, 512])
    tka = k.sb("tka", [128, 512])
    k2 = k.sb("k2", [128, 512])
    bvec = k.sb("bvec", [128, 512])
    rk = k.sb("rk", [128, 512], BF16)
    bon = [k.sb("bon%d" % i, [128, 512]) for i in range(2)]
    cs = [k.sb("cs%d" % i, [128, 512]) for i in range(2)]
    csm = k.sb("csm", [128, 512])
    Epos = k.sb("Epos", [128, 512])
    Eneg = k.sb("Eneg", [128, 512])
    Eprev = k.sb("Eprev", [128, 512])
    RopT = [[k.sb("RopT%d%d" % (d, s), [128, 8, 2, 64], BF16) for s in range(2)] for d in range(2)]
    LopT = [[k.sb("LopT%d%d" % (d, s), [128, 8, 2, 64], BF16) for s in range(2)] for d in range(2)]
    vTb = [[k.sb("vTb%d%d" % (d, s), [128, 512], BF16) for s in range(2)] for d in range(2)]
    eC = [[k.sb("eC%d%d" % (d, s), [128, 8]) for s in range(2)] for d in range(2)]
    NR = 4
    tm = [[k.sb("tm%d%d" % (d, i), [64, 384], BF16) for i in range(NR)] for d in range(2)]
    scmB = [k.sb("scmB%d" % i, [64, 2, 4, 128], BF16) for i in range(NR)]
    TtB = [k.sb("TtB%d" % i, [64, 4, 64], BF16) for i in range(NR)]
    NMt2 = [[k.sb("NMt%d_%d" % (a, i), [64, 2, 4, 64], BF16) for i in range(2)] for a in range(2)]
    Pt2 = [[k.sb("Pt%d_%d" % (a, i), [64, 4, 64], BF16) for i in range(2)] for a in range(2)]
    S32 = [k.sb("S32_%d" % d, [128, 128]) for d in range(2)]
    t1 = [k.sb("t1_%d" % d, [128, 128]) for d in range(2)]
    Sbf = [[k.sb("Sbf%d%d" % (d, i), [128, 128], BF16) for i in range(2)] for d in range(2)]
    Zb = [k.sb("Zb%d" % d, [64, 128], BF16) for d in range(2)]
    Ub = [k.sb("Ub%d" % d, [64, 128], BF16) for d in range(2)]
    Yo = [[k.sb("Yo%d%d" % (d, i), [64, 128]) for i in range(2)] for d in range(2)]
    bankA = [k.ps("bankA%d" % i, [128, 512]) for i in range(2)]
    bSC = k.ps("bSC", [128, 512])
    bNM = k.ps("bNM", [128, 512])
    bP = k.ps("bP", [128, 512])
    bCH = [k.ps("bCH%d" % d, [128, 512]) for d in range(2)]
    tb32 = k.ps("tbank", [128, 512])
    tbank = T(tb32.buf, tb32.ap.bitcast(BF16))

    for d in range(2):
        k.memset(S32[d], 0.0)
        k.memset(Sbf[d][0], 0.0)
    k.memset(upad[0][:, :, 0:1], 0.0)
    k.memset(upad[1][:, :, 513:514], 0.0)

    state = {"a": 0}

    def nbank():
        return bankA[0]

    def phaseA(d, j, slot):
        first = (j == 0) if d == 0 else (j == NT - 1)
        up = upad[d]
        if not first:
            if d == 0:
                k.copy(up[:, :, 0:1], up[:, :, 512:513], eng="pool")
            else:
                k.copy(up[:, :, 513:514], up[:, :, 1:2], eng="pool")
        if hsrc is not None:
            load_hT(k, hT, hsrc, j)
        else:
            k.dma(xbuf, xT[:, j * 512:(j + 1) * 512].re("(c p) t -> p c t", p=128), eng="sp")
            ps = nbank()
            for c in range(8):
                s = sq[c % 2]
                k.act(s, xbuf[:, c, :], AF.Square)
                k.matmul(ps, ones_bf, s, start=(c == 0), stop=(c == 7))
            k.act(rstd, ps, AF.Ln, scale=1.0 / 1024, bias=eps_t)
            k.act(rstd, rstd, AF.Exp, scale=-0.5)
            for c in range(8):
                k.stt(hT[:, c, :], xbuf[:, c, :], pw_sb[:, c:c + 1], rstd, ALU.mult, ALU.mult)
        ts_ = slice(j * 512, (j + 1) * 512)
        for grp in range(5 if d == 0 else 4):
            ps = nbank()
            for c in range(8):
                k.matmul(ps, w_fm[:, c, grp * 128:(grp + 1) * 128], hT[:, c, :],
                         start=(c == 0), stop=(c == 7))
            if grp < 4:
                k.copy(up[:, grp, 1:513], ps, eng="act")
            else:
                k.act(gT[:, ts_], ps, AF.Silu)
        sh = slice(0, 512) if d == 0 else slice(2, 514)
        for grp in range(4):
            td = tmpd[grp % 2]
            k.tt(td, up[:, grp, sh], up[:, grp, 1:513], ALU.subtract, eng="pool")
            k.stt(ul[:, grp, :], td, mu_sb[:, d, grp:grp + 1], up[:, grp, 1:513], ALU.mult, ALU.add)
        r, kx, v, wa = ul[:, 0, :], ul[:, 1, :], ul[:, 2, :], ul[:, 3, :]
        k.copy(vTb[d][slot], v, eng="pool")
        k.act(twd, wa[0:64, :], AF.Tanh)
        pxw = nbank()
        k.matmul(pxw, w2a2_sb[0:64, d, :], twd)
        k.act(sg, pxw, AF.Sigmoid, bias=pv[:, d:d + 1])
        pxa = nbank()
        k.matmul(pxa, w2a2_sb[64:128, d, :], wa[64:128, :])
        k.act(aic, pxa, AF.Sigmoid, bias=pv[:, 2:3])
        k.ts(kk, kx, pv[:, 3:4], ALU.mult)
        k.act(sq[0], kk, AF.Square)
        pss = nbank()
        k.matmul(pss, blk_bf, sq[0])
        k.act(rs, pss, AF.Ln, bias=tiny_t)
        k.act(rs, rs, AF.Exp, scale=-0.5)
        k.tt(kkn, kk, rs, ALU.mult)
        k.ts(tka, aic, pv[:, 4:5], ALU.mult, omk, ALU.add)
        k.tt(k2, kx, tka, ALU.mult)
        k.tt(bvec, kkn, aic, ALU.mult, eng="pool")
        k.stt(rk, r, pv[:, 6:7], k2, ALU.mult, ALU.mult)
        pbs = nbank()
        k.matmul(pbs, blk_bf, rk)
        bo = bon[d]
        k.tt(bo, pbs, v, ALU.mult)
        k.dma(bon_scr[d, :, ts_], bo, eng="sp")
        src = sg
        i = 0
        for s in (1, 2, 4, 8, 16, 32):
            dst = cs[i % 2]
            sv_, dv_ = src.re("p (c t) -> p c t", t=64), dst.re("p (c t) -> p c t", t=64)
            if d == 0:
                k.tt(dv_[:, :, s:64], sv_[:, :, s:64], sv_[:, :, 0:64 - s], ALU.add)
                k.copy(dv_[:, :, 0:s], sv_[:, :, 0:s], eng="pool")
            else:
                k.tt(dv_[:, :, 0:64 - s], sv_[:, :, 0:64 - s], sv_[:, :, s:64], ALU.add)
                k.copy(dv_[:, :, 64 - s:64], sv_[:, :, 64 - s:64], eng="pool")
            src = dst
            i += 1
        csf = src
        k.tt(csm, csf, sg, ALU.subtract, eng="pool")
        k.act(Epos, csf, AF.Exp, scale=-C0)
        k.act(Eneg, csf, AF.Exp, scale=C0)
        k.act(Eprev, csm, AF.Exp, scale=-C0)
        last = 63 if d == 0 else 0
        k.copy(eC[d][slot], Epos.re("p (c t) -> p c t", t=64)[:, :, last], eng="pool")
        R, Lo = RopT[d][slot], LopT[d][slot]
        v3 = lambda t: t.re("p (c t) -> p c t", t=64)
        k.stt(R[:, :, 0, :], v3(kkn), -1.0, v3(Eprev), ALU.mult, ALU.mult)
        k.tt(R[:, :, 1, :], v3(r), v3(Epos), ALU.mult)
        k.tt(Lo[:, :, 0, :], v3(bvec), v3(Eneg), ALU.mult)
        k.tt(Lo[:, :, 1, :], v3(k2), v3(Eneg), ALU.mult, eng="pool")

    def pre_stages(q):
        items = items_for(q)
        i3 = q % NR
        R = [RopT[d][slot][:, cc] for (d, slot, cc, _, _) in items]
        Lo = [LopT[d][slot][:, cc] for (d, slot, cc, _, _) in items]
        sc = scmB[i3]
        NMt, Pt = NMt2[q % 2], Pt2[q % 2]
        st = []

        def s_tr():
            for (d, slot, cc, _, _) in items:
                tps = tbank[0:64, (d * 384):(d * 384) + 384]
                k.transpose(tps[:, 0:128], Lo[d][:, 0, :], ident_bf)
                k.transpose(tps[:, 128:256], Lo[d][:, 1, :], ident_bf)
                k.transpose(tps[:, 256:384], vTb[d][slot][:, cc * 64:(cc + 1) * 64], ident_bf)
                k.copy(tm[d][i3], tps, eng="act")
        st.append(s_tr)

        def s_sc():
            for h in range(2):
                hs = slice(64 * h, 64 * h + 64)
                bk = bSC if h == 0 else bankA[1]
                nk = bP[0:64, 256:384] if h == 0 else tb32[0:64, 384:512]
                for d in range(2):
                    Rh = R[d][hs].re("p a t -> p (a t)")
                    k.matmul(bk[0:64, d * 256:d * 256 + 128], Lo[d][hs, 0, :], Rh)
                    k.matmul(bk[0:64, d * 256 + 128:d * 256 + 256], Lo[d][hs, 1, :], Rh)
                for d in range(2):
                    k.matmul(nk[:, d * 64:(d + 1) * 64], R[d][hs, 0, :], Lo[d][hs, 0, :])
            nm = NMt[0]
            for h in range(2):
                bk = bSC if h == 0 else bankA[1]
                nk = bP[0:64, 256:384] if h == 0 else tb32[0:64, 384:512]
                k.tt(sc[:, :, 2 * h:2 * h + 2, :].re("p d a t -> p d (a t)"),
                     bk[0:64, :].re("p (d x) -> p d x", d=2), cm_sb[:, :, 0:256], ALU.mult)
                k.tt(nm[:, 0].re("p (d h) t -> p d h t", h=2)[:, :, h, :], nk.re("p (d t) -> p d t", d=2),
                     cm_sb[:, :, 512:576], ALU.mult)
            k.copy(nm[:, 1].re("p (d h) t -> p d h t", h=2),
                   sc.re("p d (h two) t -> p d h two t", two=2)[:, :, :, 0, 0:64], eng="dve")
            k.tt(Pt[0], nm[:, 1], id64_bf, ALU.add, eng="dve")
        st.append(s_sc)

        for lv in range(5):
            cur = lv % 2
            nm_c, nm_n = NMt[cur], NMt[1 - cur]
            P_c = Pt[cur]
            P_n = Pt[1 - cur] if lv < 4 else TtB[i3]

            def s_nm(lv=lv, nm_c=nm_c, nm_n=nm_n):
                pn = bNM[0:64, :]
                for dh in range(4):
                    k.matmul(pn[:, dh * 64:(dh + 1) * 64], nm_c[:, 1, dh, :], nm_c[:, 0, dh, :])
                if lv < 4:
                    for dh in range(4):
                        k.matmul(pn[:, 256 + dh * 64:256 + (dh + 1) * 64], nm_c[:, 0, dh, :], nm_c[:, 1, dh, :])
                    k.copy(nm_n.re("p a h t -> p (a h t)"), pn, eng="act")
                else:
                    k.copy(nm_n[:, 0].re("p h t -> p (h t)"), pn[:, 0:256], eng="act")
            st.append(s_nm)

            def s_p(nm_n=nm_n, P_c=P_c, P_n=P_n):
                pp = bP[0:64, 0:256]
                for dh in range(4):
                    k.matmul(pp[:, dh * 64:(dh + 1) * 64], nm_n[:, 0, dh, :], P_c[:, dh, :], start=True, stop=False)
                    k.matmul(pp[:, dh * 64:(dh + 1) * 64], id64_bf[:, 0, :], P_c[:, dh, :], start=False, stop=True)
                k.copy(P_n.re("p h t -> p (h t)"), pp, eng="dve")
            st.append(s_p)
        return st

    par = [0, 0]

    def chain_stages(q):
        items = items_for(q)
        i3 = q % NR
        ctx = []
        for (d, slot, cc, _, cg) in items:
            CB = bCH[d]
            ctx.append(dict(d=d, R=RopT[d][slot][:, cc], sc=scmB[i3][:, d], tmv=tm[d][i3], T=TtB[i3][:, 2 * d:2 * d + 2, :],
                            X=CB[0:64, 0:128], U=CB[0:64, 128:256], DS=CB[:, 256:384], Y=CB[0:64, 384:512],
                            Sb=Sbf[d][par[d]], Sn=Sbf[d][1 - par[d]], e=eC[d][slot][:, cc:cc + 1],
                            cg=cg, q=q))
            par[d] = 1 - par[d]
        st = []

        def c_x():
            for c in ctx:
                k.matmul(c["X"], c["R"][:, 0, :], c["Sb"], start=True, stop=False)
                for h in range(2):
                    hb = slice(64 * h, 64 * h + 64)
                    k.matmul(c["X"][:, hb], c["sc"][:, 2 * h + 1, 0:64], c["tmv"][:, 256 + 64 * h:256 + 64 * h + 64],
                             start=False, stop=(h == 1))
            for c in ctx:
                k.copy(Zb[c["d"]], c["X"], eng="act")
        st.append(c_x)

        def c_u():
            for c in ctx:
                for h in range(2):
                    hb = slice(64 * h, 64 * h + 64)
                    k.matmul(c["U"][:, hb], c["T"][:, h, :], Zb[c["d"]][:, hb])
            for c in ctx:
                k.copy(Ub[c["d"]], c["U"], eng="dve")
        st.append(c_u)

        def c_y():
            for c in ctx:
                d = c["d"]
                k.matmul(c["DS"], c["tmv"][:, 0:128], Ub[d], start=True, stop=False)
                k.matmul(c["DS"], c["tmv"][:, 128:256], c["tmv"][:, 256:384], start=False, stop=True)
                k.ts(t1[d], S32[d], c["e"], ALU.mult)
            for c in ctx:
                d = c["d"]
                k.matmul(c["Y"], c["R"][:, 1, :], c["Sb"], start=True, stop=False)
                for h in range(2):
                    hb = slice(64 * h, 64 * h + 64)
                    k.matmul(c["Y"][:, hb], c["sc"][:, 2 * h, 64:128], Ub[d][:, hb], start=False, stop=False)
                    k.matmul(c["Y"][:, hb], c["sc"][:, 2 * h + 1, 64:128], c["tmv"][:, 256 + 64 * h:256 + 64 * h + 64],
                             start=False, stop=(h == 1))
        st.append(c_y)

        def c_s():
            for c in ctx:
                d = c["d"]
                for h in range(2):
                    hs = slice(64 * h, 64 * h + 64)
                    k.stt(S32[d][hs, hs], c["DS"][hs, hs], c["e"][hs], t1[d][hs, hs], ALU.mult, ALU.add)
            for c in ctx:
                d = c["d"]
                k.copy(c["Sn"], S32[d], eng="act")
                yo = Yo[d][c["q"] % 2]
                k.copy(yo, c["Y"], eng="dve")
                k.dma(wkv_scr[d, c["cg"] * 64:(c["cg"] + 1) * 64, :], yo, eng="pool")
        st.append(c_s)
        return st

    def items_for(q):
        s, ci = q // 8, q % 8
        return [(0, s % 2, ci, q, s * 8 + ci), (1, s % 2, 7 - ci, q, (NT - 1 - s) * 8 + (7 - ci))]

    nq = NT * 8
    phaseA(0, 0, 0)
    phaseA(1, NT - 1, 0)
    pendA = []
    perA = 1
    for q in range(0, nq + 2, 2):
        PA, PB = [], []
        if q < nq:
            s, ci = q // 8, q % 8
            if ci == 2 and s + 1 < NT:
                k.begin_defer()
                phaseA(0, s + 1, (s + 1) % 2)
                phaseA(1, NT - 2 - s, (s + 1) % 2)
                pendA = k.end_defer()
                perA = (len(pendA) + 3 * 30 - 1) // (3 * 30)
            PA = pre_stages(q)
            PB = pre_stages(q + 1)
        cs_ = []
        if q >= 2:
            cs_ = chain_stages(q - 2) + chain_stages(q - 1)
        np_, nc_ = len(PA), len(cs_)
        ci_ = 0
        for i in range(np_):
            PA[i]()
            if pendA:
                k.splice(pendA, perA)
            PB[i]()
            if pendA:
                k.splice(pendA, perA)
            want = (i + 1) * nc_ // np_
            while ci_ < want:
                cs_[ci_]()
                if pendA:
                    k.splice(pendA, perA)
                ci_ += 1
        while ci_ < nc_:
            cs_[ci_]()
            ci_ += 1
        if q < nq and q % 8 == 6 and pendA:
            k.splice(pendA, len(pendA))
    assert not pendA

    wf = [k.sb("wf%d" % i, [128, 128]) for i in range(2)]
    wb = [k.sb("wb%d" % i, [128, 128]) for i in range(2)]
    ww = [k.sb("ww%d" % i, [128, 128]) for i in range(2)]
    sqw = k.sb("sqw", [128, 128])
    st1 = k.sb("st1", [128, 2])
    st2 = k.sb("st2", [128, 2])
    mean = k.sb("mean", [128, 2])
    msq = k.sb("msq", [128, 2])
    var = k.sb("var", [128, 2])
    gn = [k.sb("gn%d" % i, [128, 128]) for i in range(2)]
    ob = [k.sb("ob%d" % i, [128, 512]) for i in range(2)]
    obo = [k.sb("obo%d" % i, [128, 512], ydt) for i in range(2)]
    bf_ = [k.sb("bf_%d" % i, [128, 512]) for i in range(2)]
    bb_ = [k.sb("bb_%d" % i, [128, 512]) for i in range(2)]
    for i in range(L // 128):
        p = i % 2
        k.dma(wf[p], wkv_scr[0, i * 128:(i + 1) * 128, :], eng="sp")
        k.dma(wb[p], wkv_scr[1, i * 128:(i + 1) * 128, :], eng="sp")
        w = ww[p]
        k.tt(w, wf[p], wb[p], ALU.add)
        k.reduce(st1, w.re("p (h v) -> p h v", h=2), ALU.add)
        k.tt(sqw, w, w, ALU.mult, eng="pool")
        k.reduce(st2, sqw.re("p (h v) -> p h v", h=2), ALU.add)
        k.ts(mean, st1, 1.0 / 64, ALU.mult)
        k.tt(msq, mean, mean, ALU.mult)
        k.stt(var, st2, 1.0 / 64, msq, ALU.mult, ALU.subtract)
        k.act(var, var, AF.Sqrt, bias=epsg_t)
        k.recip(var, var)
        g_ = gn[p]
        for h in range(2):
            hb = slice(64 * h, 64 * h + 64)
            k.ts(g_[:, hb], w[:, hb], mean[:, h:h + 1], ALU.subtract, var[:, h:h + 1], ALU.mult)
        tb = bankA[(i // 4) % 2]
        k.transpose(tb[:, (i % 4) * 128:(i % 4 + 1) * 128], g_, ident_f)
        if i % 4 == 3:
            j = i // 4
            ts_ = slice(j * 512, (j + 1) * 512)
            o = ob[j % 2]
            k.dma(bf_[j % 2], bon_scr[0, :, ts_], eng="pool")
            k.dma(bb_[j % 2], bon_scr[1, :, ts_], eng="pool")
            k.ts(o, tb, pv[:, 7:8], ALU.mult, pv[:, 8:9], ALU.add)
            k.tt(o, o, bf_[j % 2], ALU.add, eng="pool")
            k.tt(o, o, bb_[j % 2], ALU.add, eng="pool")
            k.tt(obo[j % 2], o, gT[:, ts_], ALU.mult)
            k.dma(yT[:, ts_], obo[j % 2], eng="sp")


GROUPS = [[0, 1, 2, 3], [4, 5, 6, 7]]


def emit_out(k, io, last):
    Lx = 8192
    NTx = Lx // 512
    ygath, wo, xsrc, xdst = io["ygath"], io["wo"], io["xsrc"], io["xdst"]
    ones_bf = k.sb("ones_bf", [128, 128], BF16)
    k.memset(ones_bf, 1.0)
    eps_t = k.sb("eps_t", [128, 1])
    k.memset(eps_t, 1e-6)
    postw_sb = k.sb("postw_sb", [128, 2])
    k.dma(postw_sb, io["postw"])
    snw_sb = k.sb("snw_sb", [128, 4])
    k.dma(snw_sb, io["snw"])
    prew_sb = k.sb("prew_sb", [128, 2])
    k.dma(prew_sb, io["prew"])
    w_bf = k.sb("w_bf", [128, 16, 256], BF16)
    wst = [k.sb("wst%d" % i, [128, 4, 256]) for i in range(2)]
    for i in range(4):
        st = wst[i % 2]
        k.dma(st, wo[i * 512:(i + 1) * 512, :].re("(c p) m -> p c m", p=128), eng="pool" if i % 2 else "sp")
        k.copy(w_bf[:, 4 * i:4 * i + 4, :], st, eng="pool")
    mixT = k.sb("mixT", [128, 2, Lx])
    ssrow = k.sb("ssrow", [1, Lx])
    ytile = [k.sb("ytile%d" % i, [128, 16, 512], BF16) for i in range(2)]
    yn = [k.sb("yn%d" % i, [128, 4, 512], BF16) for i in range(2)]
    sq = [k.sb("sq%d" % i, [128, 512], BF16) for i in range(2)]
    rs = k.sb("rs", [128, 512])
    tot = [k.sb("tot%d" % i, [128, 512]) for i in range(2)]
    xt = [k.sb("xt%d" % i, [128, 2, 512]) for i in range(2)]
    hb = [k.sb("hb%d" % i, [128, 2, 512], BF16) for i in range(2)]
    pg = k.ps("pg", [128, 512])
    pm = [k.ps("pm%d" % i, [128, 512]) for i in range(2)]
    pss = k.ps("pss", [128, 512])

    for j in range(NTx):
        ts_ = slice(j * 512, (j + 1) * 512)
        yt = ytile[j % 2]
        ytv = yt.re("p (g q) t -> p g q t", q=4)
        for r in range(8):
            k.dma(ytv[(r % 2) * 64:(r % 2) * 64 + 64, :, r // 2, :],
                  ygath[r * 256:(r + 1) * 256, ts_].re("(g i) t -> i g t", i=64),
                  eng="sp" if r % 2 == 0 else "pool")
        ynj = yn[j % 2]
        for grp in range(2):
            for ci, g in enumerate((2 * grp, 2 * grp + 1)):
                k.act(sq[ci], yt[:, g * 4, :], AF.Square)
                k.matmul(pg, ones_bf, sq[ci], start=(ci == 0), stop=(ci == 1))
            k.act(rs, pg, AF.Ln, scale=1.0 / 256, bias=eps_t)
            k.act(rs, rs, AF.Exp, scale=-0.5)
            for g in (2 * grp, 2 * grp + 1):
                k.stt(ynj[:, g, :], yt[:, g * 4, :], snw_sb[:, g:g + 1], rs, ALU.mult, ALU.mult)
        for nb in range(2):
            for c in range(16):
                rhs = ynj[:, c // 4, :] if c % 4 == 0 else yt[:, c, :]
                k.matmul(pm[nb], w_bf[:, c, nb * 128:(nb + 1) * 128], rhs, start=(c == 0), stop=(c == 15))
            k.copy(mixT[:, nb, ts_], pm[nb], eng="dve")
            k.act(sq[nb], pm[nb], AF.Square)
            k.matmul(pss, ones_bf, sq[nb], start=(nb == 0), stop=(nb == 1))
        k.copy(ssrow[0:1, ts_], pss[0:1, :], eng="dve")
    k.dma(io["ar1_in"], ssrow, eng="sp")
    k.collective("AllReduce", io["ar1_in"], io["ar1_out"], GROUPS, op=ALU.add)

    for j in range(NTx):
        ts_ = slice(j * 512, (j + 1) * 512)
        tt_ = tot[j % 2]
        k.dma(tt_, T(io["ar1_out"].buf, io["ar1_out"].ap[0:1, ts_].partition_broadcast(128)), eng="pool")
        k.act(tt_, tt_, AF.Ln, scale=1.0 / 1024, bias=eps_t)
        k.act(tt_, tt_, AF.Exp, scale=-0.5)
        x_ = xt[j % 2]
        k.dma(x_, xsrc[:, ts_].re("(nb p) t -> p nb t", p=128), eng="sp")
        for nb in range(2):
            k.stt(mixT[:, nb, ts_], mixT[:, nb, ts_], postw_sb[:, nb:nb + 1], tt_, ALU.mult, ALU.mult)
            k.tt(mixT[:, nb, ts_], mixT[:, nb, ts_], x_[:, nb, :], ALU.add, eng="pool")
        k.dma(xdst[:, ts_].re("(nb p) t -> p nb t", p=128), mixT[:, :, ts_], eng="sp")
        if not last:
            for nb in range(2):
                k.act(sq[nb], mixT[:, nb, ts_], AF.Square)
                k.matmul(pss, ones_bf, sq[nb], start=(nb == 0), stop=(nb == 1))
            k.copy(ssrow[0:1, ts_], pss[0:1, :], eng="dve")
    if last:
        return
    k.dma(io["ar2_in"], ssrow, eng="sp")
    k.collective("AllReduce", io["ar2_in"], io["ar2_out"], GROUPS, op=ALU.add)
    for j in range(NTx):
        ts_ = slice(j * 512, (j + 1) * 512)
        tt_ = tot[j % 2]
        k.dma(tt_, T(io["ar2_out"].buf, io["ar2_out"].ap[0:1, ts_].partition_broadcast(128)), eng="pool")
        k.act(tt_, tt_, AF.Ln, scale=1.0 / 1024, bias=eps_t)
        k.act(tt_, tt_, AF.Exp, scale=-0.5)
        h_ = hb[j % 2]
        for nb in range(2):
            k.stt(h_[:, nb, :], mixT[:, nb, ts_], prew_sb[:, nb:nb + 1], tt_, ALU.mult, ALU.mult)
        k.dma(io["hslice"][:, ts_].re("(nb p) t -> p nb t", p=128), h_, eng="sp")
    for r in range(4):
        k.collective("AllGather", io["hslice"][r * 64:(r + 1) * 64, :], io["hgath"][r * 256:(r + 1) * 256, :], GROUPS)


L = 8192
PROJ_SIZES = (512, 1024, 16, 1664, 512, 512, 512, 512, 512, 512, 256, 256, 512)
OFF = np.concatenate([[0], np.cumsum(PROJ_SIZES)]).astype(int)
(O_MZ, O_XBC, O_DT, O_RU, O_RG, O_DQ, O_DK, O_DV, O_DG, O_GQ, O_GK, O_GV, O_GG) = OFF[:13]

PERM = np.array([p + 32 if (p % 64) < 32 else p - 32 for p in range(128)])


def rope_tables():
    inv = (np.float32(10000.0) ** (-np.arange(32, dtype=np.float32) / np.float32(32))).astype(np.float32)
    t = np.arange(L)
    sign = np.where((np.arange(128) % 64) < 32, -1.0, 1.0).astype(np.float32)[:, None]
    fi = np.arange(128) % 32
    pos_d = np.broadcast_to(t.astype(np.float32)[None, :], (128, L))
    ang_d = (pos_d * inv[fi][:, None]).astype(np.float32)
    pos_g = np.where((np.arange(128) < 64)[:, None], (t // 64)[None, :], (t % 64)[None, :]).astype(np.float32)
    ang_g = (pos_g * inv[fi][:, None]).astype(np.float32)
    tabs = np.stack([np.cos(ang_d), np.sin(ang_d) * sign, np.cos(ang_g), np.sin(ang_g) * sign]).astype(np.float32)
    return np.ascontiguousarray(tabs)


def pvec(v):
    return np.ascontiguousarray(v.reshape(8, 128).T)


def prep_attn(inp, layer, xT_b, tabs):
    W = inp['w_in'][layer]
    maps = []
    for core in range(8):
        b, g = core // 4, core % 4
        sl = lambda o, n=128, gg=g: W[:, o + gg * n: o + (gg + 1) * n]
        dq, dk_, dg, dv = sl(O_DQ), sl(O_DK), sl(O_DG), sl(O_DV)
        gq, gg_ = sl(O_GQ), sl(O_GG)
        gk, gv = sl(O_GK, 128, g // 2), sl(O_GV, 128, g // 2)
        wfm = np.concatenate([dq, dq[:, PERM], dk_, dk_[:, PERM], dg, gq, gq[:, PERM], gk, gk[:, PERM], gg_], axis=1)
        wtm = np.concatenate([dv, gv], axis=1)
        vecs = np.zeros((128, 8), np.float32)
        qw, kw = inp['gqa_q_norm_w'][layer], inp['gqa_k_norm_w'][layer]
        vecs[:, 0] = qw; vecs[:, 1] = qw[PERM]; vecs[:, 2] = kw; vecs[:, 3] = kw[PERM]
        vecs[:, 4] = inp['diff_norm_w'][layer]
        lam = np.ascontiguousarray(np.broadcast_to(inp['diff_lambda'][layer].reshape(1, 256), (128, 256)))
        maps.append({"xT": xT_b[b], "pw": pvec(inp['pre_norm_w'][layer]),
                     "wfm": np.ascontiguousarray(wfm), "wtm": np.ascontiguousarray(wtm),
                     "tabs": tabs, "vecs": vecs, "lam": lam})
    return maps


def ssd_consts():
    j = np.arange(128)[:, None]; l = np.arange(128)[None, :]
    c = np.stack([(j <= l), (j >= l), (j > l), (j < l), (j == l)]).astype(np.float32)
    return np.ascontiguousarray(c.transpose(1, 0, 2))


def prep_ssd(inp, layer, xT_b):
    W = inp['w_in'][layer]
    cst = ssd_consts()
    maps = []
    for core in range(8):
        b, g = core // 4, core % 4
        grp = g // 2
        wfm = np.concatenate([W[:, O_MZ + g * 128:O_MZ + (g + 1) * 128],
                              W[:, O_XBC + g * 128:O_XBC + (g + 1) * 128],
                              W[:, O_XBC + 512 + grp * 128:O_XBC + 512 + (grp + 1) * 128],
                              W[:, O_XBC + 768 + grp * 128:O_XBC + 768 + (grp + 1) * 128]], axis=1)
        dcols = [O_DT + d * 8 + 2 * g + h for d in range(2) for h in range(2)]
        wdt = W[:, dcols]
        chans = [g * 128, 512 + grp * 128, 768 + grp * 128]
        cwl = inp['conv_w'][layer]; cbl = inp['conv_b'][layer]
        cw = np.stack([cwl[:, ch:ch + 128].T for ch in chans], axis=1)
        cb = np.stack([cbl[ch:ch + 128] for ch in chans], axis=1)
        v = np.zeros(16, np.float32)
        for d in range(2):
            for h in range(2):
                v[d * 2 + h] = inp['ssm_dt_bias'][layer][d, 2 * g + h]
                v[4 + d * 2 + h] = inp['ssm_a_log'][layer][d, 2 * g + h]
        v[8] = inp['ssm_d'][layer][2 * g]; v[9] = inp['ssm_d'][layer][2 * g + 1]
        maps.append({"xT": xT_b[b], "pw": pvec(inp['pre_norm_w'][layer]),
                     "wfm": np.ascontiguousarray(wfm), "wdt": np.ascontiguousarray(wdt),
                     "cw": np.ascontiguousarray(cw), "cb": np.ascontiguousarray(cb),
                     "ssmv": np.ascontiguousarray(np.broadcast_to(v[None], (128, 16))), "cst": cst})
    return maps


def rwkv_consts():
    j = np.arange(64)[:, None]; t = np.arange(64)[None, :]
    cm = np.zeros((64, 2, 704), np.float32)
    for d in range(2):
        strict = (j < t) if d == 0 else (j > t)
        incl = (j <= t) if d == 0 else (j >= t)
        blk = np.concatenate([strict, incl], axis=1).astype(np.float32)
        cm[:, d, 0:512] = np.tile(blk, (1, 4))
        nmask = ((t < j) if d == 0 else (t > j)).astype(np.float32)
        cm[:, d, 512:640] = np.tile(nmask, (1, 2))
        cm[:, d, 640:704] = np.eye(64, dtype=np.float32)
    c128 = np.zeros((128, 2, 128), np.float32)
    c128[0:64, 0, 0:64] = 1; c128[64:128, 0, 64:128] = 1
    c128[:, 1, :] = np.eye(128, dtype=np.float32)
    return cm, c128


O_R, O_K, O_V, O_WD, O_AD = O_RU, O_RU + 512, O_RU + 1024, O_RU + 1536, O_RU + 1600


def prep_rwkv(inp, layer, xT_b):
    W = inp['w_in'][layer]
    cm, c128 = rwkv_consts()
    maps = []
    for core in range(8):
        b, g = core // 4, core % 4
        gs = slice(g * 128, (g + 1) * 128)
        wfm = np.concatenate([W[:, O_R + g * 128:O_R + (g + 1) * 128], W[:, O_K + g * 128:O_K + (g + 1) * 128],
                              W[:, O_V + g * 128:O_V + (g + 1) * 128], W[:, O_WD:O_WD + 128],
                              W[:, O_RG + g * 128:O_RG + (g + 1) * 128]], axis=1)
        mul = inp['rwkv_mu'][layer]
        mu = np.zeros((128, 2, 4), np.float32)
        for d in range(2):
            mu[:, d, 0] = mul[d, 0 + g * 128:0 + (g + 1) * 128]
            mu[:, d, 1] = mul[d, 512 + g * 128:512 + (g + 1) * 128]
            mu[:, d, 2] = mul[d, 1024 + g * 128:1024 + (g + 1) * 128]
            mu[:, d, 3] = mul[d, 1536:1664]
        w2a2 = np.zeros((128, 2, 128), np.float32)
        for d in range(2):
            w2a2[0:64, d, :] = inp['rwkv_w2'][layer][d][:, gs]
            w2a2[64:128, d, :] = inp['rwkv_a2'][layer][:, gs]
        pv = np.zeros((128, 12), np.float32)
        pv[:, 0] = inp['rwkv_w0'][layer][0, gs]; pv[:, 1] = inp['rwkv_w0'][layer][1, gs]
        pv[:, 2] = inp['rwkv_a0'][layer][gs]; pv[:, 3] = inp['rwkv_k_k'][layer][gs]
        pv[:, 4] = inp['rwkv_k_a'][layer][gs]
        pv[:, 6] = inp['rwkv_r_k'][layer].reshape(-1)[gs]
        pv[:, 7] = inp['rwkv_ln_w'][layer][gs]; pv[:, 8] = inp['rwkv_ln_b'][layer][gs]
        maps.append({"xT": xT_b[b], "pw": pvec(inp['pre_norm_w'][layer]), "wfm": np.ascontiguousarray(wfm),
                     "mu": mu, "w2a2": w2a2, "pvd": pv, "cm": cm, "c128": c128})
    return maps


def build_fused(depth=2):
    nc = bass.Bass("TRN2", target_bir_lowering=False)
    k = KB(nc, arena=True)
    din = k.dram_in
    xT = din("xT", [1024, L])
    xs0 = din("xs0", [256, L])
    tabs = din("tabs", [4, 128, L])
    cst = din("cst", [128, 5, 128])
    cm = din("cm", [64, 2, 704])
    c128 = din("c128", [128, 2, 128])
    xo = k.dram_out("xo", [256, L])
    ycat = k.dram_scratch("ycat", [512, L], BF16)
    ygath = k.dram_scratch("ygath", [2048, L], BF16)
    hslice = k.dram_scratch("hslice", [256, L], BF16)
    hgath = k.dram_scratch("hgath", [1024, L], BF16)
    xres = k.dram_scratch("xres", [256, L])
    ar = [k.dram_scratch("ar%d" % i, [1, L]) for i in range(4)]
    wkv_scr = k.dram_scratch("wkv_scr", [2, L, 128])
    bon_scr = k.dram_scratch("bon_scr", [2, 128, L])
    for layer in range(depth):
        p = "L%d_" % layer
        lambda_init = 0.8 - 0.6 * math.exp(-0.3 * layer)
        src = {"xT": xT} if layer == 0 else {"hT": hgath}
        pw = din(p + "pw", [128, 8])
        io = dict(src, pw=pw, wfm=din(p + "s_wfm", [1024, 512]), wdt=din(p + "s_wdt", [1024, 4]),
                  cw=din(p + "s_cw", [128, 3, 5]), cb=din(p + "s_cb", [128, 3]), ssmv=din(p + "s_ssmv", [128, 16]),
                  cst=cst, y=ycat[0:128, :], ydt=BF16)
        emit_ssd(k, io)
        k.phase_reset()
        io = dict(src, pw=pw, wfm=din(p + "r_wfm", [1024, 640]), mu=din(p + "r_mu", [128, 2, 4]),
                  w2a2=din(p + "r_w2a2", [128, 2, 128]), pvd=din(p + "r_pvd", [128, 12]), cm=cm, c128=c128,
                  wkv_scr=wkv_scr, bon_scr=bon_scr, y=ycat[128:256, :], ydt=BF16)
        emit_rwkv(k, io)
        k.phase_reset()
        for r in range(4):
            k.collective("AllGather", ycat[r * 64:(r + 1) * 64, :], ygath[r * 256:(r + 1) * 256, :], GROUPS)
        io = dict(src, pw=pw, wfm=din(p + "a_wfm", [1024, 1280]), wtm=din(p + "a_wtm", [1024, 256]), tabs=tabs,
                  vecs=din(p + "a_vecs", [128, 8]), lam=din(p + "a_lam", [128, 256]),
                  y=[ycat[256:384, :], ycat[384:512, :]], ydt=BF16,
                  after_kind0=lambda: [k.collective("AllGather", ycat[r * 64:(r + 1) * 64, :],
                                                    ygath[r * 256:(r + 1) * 256, :], GROUPS) for r in (4, 5)])
        emit_attn(k, io, lambda_init)
        k.phase_reset()
        for r in range(6, 8):
            k.collective("AllGather", ycat[r * 64:(r + 1) * 64, :], ygath[r * 256:(r + 1) * 256, :], GROUPS)
        last = (layer == depth - 1)
        io = dict(ygath=ygath, wo=din(p + "o_wo", [2048, 256]), xsrc=(xs0 if layer == 0 else xres),
                  xdst=(xo if last else xres), postw=din(p + "o_postw", [128, 2]), snw=din(p + "o_snw", [128, 4]),
                  prew=din(p + "o_prew", [128, 2]), ar1_in=ar[0], ar1_out=ar[1], ar2_in=ar[2], ar2_out=ar[3],
                  hslice=hslice, hgath=hgath)
        emit_out(k, io, last)
        k.phase_reset()
    stats = k.finish()
    return nc, stats


def prep_fused(inp, depth=2):
    x = np.ascontiguousarray(inp["x"], dtype=np.float32)
    xT_b = [np.ascontiguousarray(x[b].T) for b in range(2)]
    tabs = rope_tables()
    cst = ssd_consts()
    cm, c128 = rwkv_consts()
    maps = [dict() for _ in range(8)]
    perm = np.array([kind * 512 + g * 128 + i for g in range(4) for kind in range(4) for i in range(128)])
    for core in range(8):
        b, g = core // 4, core % 4
        m = maps[core]
        m["xT"] = xT_b[b]
        m["xs0"] = np.ascontiguousarray(xT_b[b][g * 256:(g + 1) * 256])
        m["tabs"] = tabs; m["cst"] = cst; m["cm"] = cm; m["c128"] = c128
    for layer in range(depth):
        p = "L%d_" % layer
        ms = prep_ssd(inp, layer, xT_b); mr = prep_rwkv(inp, layer, xT_b); ma = prep_attn(inp, layer, xT_b, tabs)
        for core in range(8):
            b, g = core // 4, core % 4
            m = maps[core]
            m[p + "pw"] = ms[core]["pw"]
            for nm in ("wfm", "wdt", "cw", "cb", "ssmv"):
                m[p + "s_" + nm] = ms[core][nm]
            for nm in ("wfm", "mu", "w2a2", "pvd"):
                m[p + "r_" + nm] = mr[core][nm]
            for nm in ("wfm", "wtm", "vecs", "lam"):
                m[p + "a_" + nm] = ma[core][nm]
            ns = slice(g * 256, (g + 1) * 256)
            m[p + "o_wo"] = np.ascontiguousarray(inp["w_out"][layer][perm][:, ns])
            m[p + "o_postw"] = np.ascontiguousarray(inp["post_norm_w"][layer][ns].reshape(2, 128).T)
            m[p + "o_snw"] = np.ascontiguousarray(inp["ssm_norm_w"][layer].reshape(4, 128).T)
            nxt = inp["pre_norm_w"][min(layer + 1, depth - 1)]
            m[p + "o_prew"] = np.ascontiguousarray(nxt[ns].reshape(2, 128).T)
    return maps


from concourse.bass_utils import run_bass_kernel_spmd


def kernel(**inp):
    inp = {k_: np.asarray(v) for k_, v in inp.items()}
    depth = inp["w_in"].shape[0]
    nc, _ = build_fused(depth)
    maps = prep_fused(inp, depth)
    res = run_bass_kernel_spmd(nc, maps, core_ids=list(range(8))).results
    out = np.empty((2, L, 1024), np.float32)
    for core in range(8):
        b, g = core // 4, core % 4
        out[b, :, g * 256:(g + 1) * 256] = res[core]["xo"].T
    return out
```

```python
import math
import numpy as np
import concourse.bass as bass
import concourse.mybir as mybir

F32 = mybir.dt.float32
BF16 = mybir.dt.bfloat16
ALU = mybir.AluOpType
AF = mybir.ActivationFunctionType
AX = mybir.AxisListType

SEM_CHUNK = 30000


class Buf:
    __slots__ = ("name", "last_w", "readers", "dma_sem", "dma_cnt", "is_dram", "psum", "wlist", "inc_val")

    def __init__(self, name, is_dram=False, psum=False):
        self.psum = psum
        self.wlist = {}
        self.inc_val = 16
        self.name = name
        self.last_w = None
        self.readers = []
        self.dma_sem = None
        self.dma_cnt = 0
        self.is_dram = is_dram


class T:
    __slots__ = ("buf", "ap")

    def __init__(self, buf, ap):
        self.buf = buf
        self.ap = ap

    def __getitem__(self, idx):
        return T(self.buf, self.ap[idx])

    def re(self, pattern, **kw):
        return T(self.buf, self.ap.rearrange(pattern, **kw))

    def sub(self, buf, idx=None):
        return T(buf, self.ap if idx is None else self.ap[idx])


class Op:
    __slots__ = ("eng", "fn", "reads", "writes", "is_dma", "deps", "inc_idx", "dma_buf",
                 "dma_wait", "gi")

    def __init__(self, eng, fn, reads, writes, is_dma=False, dma_buf=None):
        self.eng = eng
        self.fn = fn
        self.reads = reads
        self.writes = writes
        self.is_dma = is_dma
        self.deps = []
        self.inc_idx = None
        self.dma_buf = dma_buf
        self.dma_wait = []
        self.gi = None


class KB:
    ENGS = ("pe", "act", "dve", "pool", "sp")

    ARENA_WORDS = 53100

    def __init__(self, nc, arena=False):
        self.nc = nc
        self.arena = None
        if arena:
            self.arena = nc.alloc_sbuf_tensor("arena", [128, self.ARENA_WORDS], F32).ap()
            self.a_off = 0
            self.a_peak = 0
            self.banks = [T(Buf("bank%d" % i, psum=True),
                            nc.alloc_psum_tensor("gbank%d" % i, [128, 512], F32).ap()) for i in range(8)]
            self.b_next = 0
        self.ops = []
        self.e = {"pe": nc.tensor, "act": nc.scalar, "dve": nc.vector, "pool": nc.gpsimd,
                  "sp": nc.sync}
        self._n = 0

    def sb(self, name, shape, dtype=F32):
        if self.arena is None:
            h = self.nc.alloc_sbuf_tensor(name, list(shape), dtype)
            return T(Buf(name), h.ap())
        shape = list(shape)
        esz = 2 if dtype == BF16 else 4
        n = 1
        for d in shape[1:]:
            n *= d
        nbytes = (n * esz + 31) // 32 * 32
        nw = nbytes // 4
        off = self.a_off
        assert off + nw <= self.ARENA_WORDS, "arena overflow at %s: %d + %d" % (name, off, nw)
        self.a_off = off + nw
        self.a_peak = max(self.a_peak, self.a_off)
        ap = self.arena[0:shape[0], off:off + (n * esz + 3) // 4]
        if dtype == BF16:
            ap = ap.bitcast(BF16)
            if (n * esz) % 4:
                ap = ap[:, 0:n]
        if len(shape) == 3:
            ap = ap.rearrange("p (a b) -> p a b", a=shape[1])
        elif len(shape) == 4:
            ap = ap.rearrange("p (a b c) -> p a b c", a=shape[1], b=shape[2])
        return T(Buf(name), ap)

    def ps(self, name, shape, dtype=F32):
        if self.arena is None:
            h = self.nc.alloc_psum_tensor(name, list(shape), dtype)
            return T(Buf(name, psum=True), h.ap())
        bk = self.banks[self.b_next]
        self.b_next += 1
        if dtype == BF16:
            return T(bk.buf, bk.ap.bitcast(BF16))
        return bk

    def phase_reset(self):
        self._rec("bar", None, [], [])
        self.a_off = 0
        self.b_next = 0

    def dram_in(self, name, shape, dtype=F32):
        h = self.nc.dram_tensor(name, list(shape), dtype, kind="ExternalInput")
        return T(Buf(name, True), h.ap())

    def dram_out(self, name, shape, dtype=F32):
        h = self.nc.dram_tensor(name, list(shape), dtype, kind="ExternalOutput")
        return T(Buf(name, True), h.ap())

    def dram_scratch(self, name, shape, dtype=F32):
        h = self.nc.dram_tensor(name, list(shape), dtype)
        return T(Buf(name, True), h.ap())

    def buf(self, name):
        self._n += 1
        return Buf("%s_%d" % (name, self._n))

    def _rec(self, eng, fn, reads, writes, is_dma=False, dma_buf=None):
        rb = []
        for t in reads:
            if t is None or isinstance(t, (int, float)):
                continue
            b = t.buf if isinstance(t, T) else t
            if b not in rb:
                rb.append(b)
        wb = []
        for t in writes:
            b = t.buf if isinstance(t, T) else t
            if b not in wb:
                wb.append(b)
        op = Op(eng, fn, rb, wb, is_dma, dma_buf)
        if getattr(self, "_defer", None) is not None:
            self._defer.append(op)
            return op
        op.gi = len(self.ops)
        self.ops.append(op)
        return op

    def begin_defer(self):
        self._defer = []

    def end_defer(self):
        lst = self._defer
        self._defer = None
        return lst

    def splice(self, lst, n):
        for _ in range(min(n, len(lst))):
            op = lst.pop(0)
            op.gi = len(self.ops)
            self.ops.append(op)

    @staticmethod
    def _a(x):
        return x.ap if isinstance(x, T) else x

    def dma(self, out, in_, eng="sp", **kw):
        o, i = self._a(out), self._a(in_)
        sbuf_side = out.buf if not out.buf.is_dram else in_.buf
        return self._rec(eng, lambda E: E.dma_start(out=o, in_=i, **kw), [in_], [out],
                         is_dma=True, dma_buf=sbuf_side)

    def matmul(self, out, lhsT, rhs, start=True, stop=True, extra_reads=(), **kw):
        o, l, r = self._a(out), self._a(lhsT), self._a(rhs)
        reads = [lhsT, rhs] + list(extra_reads)
        if not start:
            reads.append(out)
        return self._rec("pe", lambda E: E.matmul(o, l, r, start=start, stop=stop, **kw),
                         reads, [out])

    def transpose(self, out, in_, ident):
        o, i, d = self._a(out), self._a(in_), self._a(ident)
        return self._rec("pe", lambda E: E.transpose(o, i, d), [in_, ident], [out])

    def act(self, out, in_, func, bias=None, scale=None, accum_out=None, eng="act"):
        o, i = self._a(out), self._a(in_)
        kw = {}
        reads = [in_]
        writes = [out]
        if bias is not None:
            kw["bias"] = self._a(bias)
            reads.append(bias)
        if scale is not None:
            kw["scale"] = self._a(scale)
            reads.append(scale)
        if accum_out is not None:
            kw["accum_out"] = self._a(accum_out)
            writes.append(accum_out)
        return self._rec(eng, lambda E: E.activation(o, i, func, **kw), reads, writes)

    def tt(self, out, in0, in1, op, eng="dve"):
        o, a, b = self._a(out), self._a(in0), self._a(in1)
        return self._rec(eng, lambda E: E.tensor_tensor(o, a, b, op), [in0, in1], [out])

    def ts(self, out, in0, s1, op0, s2=None, op1=None, accum_out=None, eng="dve"):
        o, a = self._a(out), self._a(in0)
        s1a, s2a = self._a(s1), self._a(s2)
        kw = {}
        writes = [out]
        if op1 is not None:
            kw["op1"] = op1
        if accum_out is not None:
            kw["accum_out"] = self._a(accum_out)
            writes.append(accum_out)
        return self._rec(eng, lambda E: E.tensor_scalar(o, a, s1a, s2a, op0, **kw),
                         [in0, s1, s2], writes)

    def stt(self, out, in0, scalar, in1, op0, op1, eng="dve"):
        o, a, s, b = self._a(out), self._a(in0), self._a(scalar), self._a(in1)
        return self._rec(eng, lambda E: E.scalar_tensor_tensor(o, a, s, b, op0, op1),
                         [in0, scalar, in1], [out])

    def copy(self, out, in_, eng="dve"):
        o, i = self._a(out), self._a(in_)
        if eng == "act":
            return self._rec(eng, lambda E: E.copy(o, i), [in_], [out])
        return self._rec(eng, lambda E: E.tensor_copy(o, i), [in_], [out])

    def memset(self, out, val, eng="pool"):
        o = self._a(out)
        return self._rec(eng, lambda E: E.memset(o, val), [], [out])

    def recip(self, out, in_):
        o, i = self._a(out), self._a(in_)
        return self._rec("dve", lambda E: E.reciprocal(o, i), [in_], [out])

    def reduce(self, out, in_, op, axis=AX.X, eng="dve"):
        o, i = self._a(out), self._a(in_)
        return self._rec(eng, lambda E: E.tensor_reduce(o, i, axis, op), [in_], [out])

    def affine_select(self, out, in_, pattern, compare_op, fill, base=0, channel_multiplier=0):
        o, i = self._a(out), self._a(in_)
        return self._rec("pool", lambda E: E.affine_select(
            o, i, pattern, compare_op, fill, base=base, channel_multiplier=channel_multiplier),
            [in_], [out])

    def iota(self, out, pattern, base=0, channel_multiplier=0, **kw):
        o = self._a(out)
        return self._rec("pool", lambda E: E.iota(o, pattern, base=base,
                                                   channel_multiplier=channel_multiplier, **kw),
                         [], [out])

    def collective(self, kind, in_, out, groups, op=None):
        i, o = self._a(in_), self._a(out)
        alu = ALU.bypass if op is None else op
        semb = Buf("cc_%d" % len(self.ops))
        semb.inc_val = 1
        return self._rec("pool", lambda E: E.collective_compute(kind, alu, groups, [i], [o]),
                         [in_], [out], is_dma=True, dma_buf=semb)

    def generic(self, eng, fn, reads, writes):
        return self._rec(eng, fn, reads, writes)

    def finish(self, final_wait_outputs=True):
        nc = self.nc
        ops = self.ops
        last_on = {}
        last_dma = {}
        bar_deps = {e: None for e in self.ENGS}
        for op in ops:
            if op.eng == "bar":
                allp = set(last_on.values()) | set(last_dma.values())
                for e in self.ENGS:
                    bar_deps[e] = set(allp) | (bar_deps[e] or set())
                continue
            deps = set()
            for b in op.reads:
                if b.last_w is not None:
                    deps.add(b.last_w)
                if b.is_dram:
                    for w_ in b.wlist.values():
                        deps.add(w_)
                if b.psum:
                    for r in b.readers:
                        if ops[r].eng != op.eng:
                            deps.add(r)
            for b in op.writes:
                if b.last_w is not None:
                    deps.add(b.last_w)
                for r in b.readers:
                    deps.add(r)
            deps.discard(op.gi)
            keep = []
            if bar_deps[op.eng] is not None:
                keep.extend(sorted(bar_deps[op.eng]))
                bar_deps[op.eng] = None
            last_on[op.eng] = op.gi
            if op.is_dma:
                last_dma[id(op.dma_buf)] = op.gi
            for d in deps:
                dop = ops[d]
                if dop.is_dma:
                    keep.append(d)
                    continue
                if dop.eng == op.eng:
                    if op.eng in ("pe", "sp"):
                        continue
                    if op.is_dma:
                        keep.append(d)
                        continue
                    raw = any(b.last_w == d for b in op.reads)
                    if not raw:
                        continue
                keep.append(d)
            op.deps = keep
            for b in op.reads:
                b.readers.append(op.gi)
            for b in op.writes:
                b.last_w = op.gi
                b.readers = []
                if b.is_dram and op.is_dma:
                    b.wlist[id(op.dma_buf)] = op.gi
        needed = set()
        for op in ops:
            for d in op.deps:
                needed.add(d)
        final_dma = [op for op in ops if op.is_dma and any(b.is_dram for b in op.writes)]
        ecount = {e: 0 for e in self.ENGS}
        phase = 0
        nslot = 0
        slot_total = []
        slot_of = {}
        ncc = 0
        abs_cnt = {}
        sem_key = {}
        for op in ops:
            if op.eng == "bar":
                phase += 1
                nslot = 0
                continue
            if op.is_dma:
                b = op.dma_buf
                if b.inc_val == 1:
                    if id(b) not in sem_key:
                        sem_key[id(b)] = ("c", ncc)
                        ncc += 1
                        b.dma_cnt = 0
                    b.dma_cnt += 1
                    abs_cnt[op.gi] = b.dma_cnt
                else:
                    ps_ = slot_of.get(id(b))
                    if ps_ is None or ps_[0] != phase:
                        slot_of[id(b)] = (phase, nslot)
                        if nslot >= len(slot_total):
                            slot_total.append(0)
                        sem_key[id(b)] = ("p", nslot)
                        nslot += 1
                    sl = slot_of[id(b)][1]
                    slot_total[sl] += 1
                    abs_cnt[op.gi] = slot_total[sl]
                op.inc_idx = abs_cnt[op.gi]
            elif op.gi in needed:
                ecount[op.eng] += 1
                op.inc_idx = ecount[op.eng]
        import contextlib
        self._stack = contextlib.ExitStack()
        self._sems = {}
        nsem = 0
        for e in self.ENGS:
            n = max((ecount[e] + SEM_CHUNK - 1) // SEM_CHUNK, 1)
            self._sems[e] = [self._stack.enter_context(nc.semaphore("s_%s_%d" % (e, k))) for k in range(n)]
            nsem += n
        dsem = {}
        for i in range(len(slot_total)):
            dsem[("p", i)] = self._stack.enter_context(nc.semaphore("d_%d" % i))
        for i in range(ncc):
            dsem[("c", i)] = self._stack.enter_context(nc.semaphore("c_%d" % i))
        nsem += len(dsem)
        self.nsem = nsem
        waited = {}
        last_abs = {}
        cur_key = {}
        op_key = {}
        phase = 0
        nslot = 0
        slot_of2 = {}
        for op in ops:
            if op.eng == "bar":
                phase += 1
                nslot = 0
                continue
            if op.is_dma:
                b = op.dma_buf
                if b.inc_val == 1:
                    op_key[op.gi] = sem_key[id(b)]
                else:
                    ps_ = slot_of2.get(id(b))
                    if ps_ is None or ps_[0] != phase:
                        slot_of2[id(b)] = (phase, nslot)
                        nslot += 1
                    op_key[op.gi] = ("p", slot_of2[id(b)][1])
        plan = {e: [] for e in self.ENGS}
        for op in ops:
            if op.eng == "bar":
                continue
            waits = {}
            for d in op.deps:
                dop = ops[d]
                if dop.is_dma:
                    b = dop.dma_buf
                    key = ("d",) + cur_key[id(b)]
                    sem = dsem[cur_key[id(b)]]
                    val = b.inc_val * last_abs[id(b)]
                else:
                    idx = dop.inc_idx - 1
                    ch = idx // SEM_CHUNK
                    val = idx % SEM_CHUNK + 1
                    key = ("e", dop.eng, ch)
                    sem = self._sems[dop.eng][ch]
                    for c2 in range(ch):
                        waited[(op.eng, ("e", dop.eng, c2))] = SEM_CHUNK
                cur = waited.get((op.eng, key), 0)
                if val > cur:
                    waited[(op.eng, key)] = val
                    if key not in waits or waits[key][1] < val:
                        waits[key] = (sem, val)
            plan[op.eng].append((op, list(waits.values())))
            if op.is_dma:
                last_abs[id(op.dma_buf)] = abs_cnt[op.gi]
                cur_key[id(op.dma_buf)] = op_key[op.gi]
        self.stats = {e: len(plan[e]) for e in self.ENGS}
        self.stats["nsem"] = nsem
        self.stats["waits"] = sum(len(w) for e in self.ENGS for _, w in plan[e])
        if self.arena is not None:
            self.stats["arena_peak_words"] = self.a_peak
        for e in self.ENGS:
            E = self.e[e]
            for op, waits in plan[e]:
                for sem, val in waits:
                    E.wait_ge(sem, val)
                ins = op.fn(E)
                if op.is_dma:
                    ins.then_inc(dsem[op_key[op.gi]], op.dma_buf.inc_val)
                elif op.inc_idx is not None:
                    idx = op.inc_idx - 1
                    ins.then_inc(self._sems[e][idx // SEM_CHUNK], 1)
        if final_wait_outputs:
            fin = {}
            for op in final_dma:
                kk = op_key[op.gi]
                v = op.dma_buf.inc_val * abs_cnt[op.gi]
                if kk not in fin or fin[kk] < v:
                    fin[kk] = v
            for kk, v in fin.items():
                nc.sync.wait_ge(dsem[kk], v)
        return self.stats


L = 8192
NT = L // 512
NCH = L // 128
NORM_EPS = 1e-6
GN_EPS = 64e-5
C0 = math.exp(-0.5)


def load_hT(k, hT_tile, hsrc, j):
    v = hT_tile.re("p (g q) t -> p g q t", q=2)
    for r in range(4):
        k.dma(v[(r % 2) * 64:(r % 2) * 64 + 64, :, r // 2, :],
              hsrc[r * 256:(r + 1) * 256, j * 512:(j + 1) * 512].re("(g i) t -> i g t", i=64),
              eng="sp" if r % 2 == 0 else "pool")


L = 8192
NT = L // 512
NORM_EPS = 1e-6


ATT_KNOBS = {}


def emit_attn(k, io, lambda_init):
    xT, hsrc, pw, wfm, wtm, tabs, vecs, lam = (io.get("xT"), io.get("hT"), io["pw"], io["wfm"], io["wtm"],
                                                io["tabs"], io["vecs"], io["lam"])
    ydst, ydt = io["y"], io["ydt"]

    ones_bf = k.sb("ones_bf", [128, 128], BF16)
    k.memset(ones_bf, 1.0)
    eps_t = k.sb("eps_t", [128, 1])
    k.memset(eps_t, NORM_EPS)
    pw_sb = k.sb("pw_sb", [128, 8])
    k.dma(pw_sb, pw)
    vec_sb = k.sb("vec_sb", [128, 8])
    k.dma(vec_sb, vecs)
    lam_sb = k.sb("lam_sb", [128, 256])
    k.dma(lam_sb, lam)
    w_fm = k.sb("w_fm", [128, 8, 1280], BF16)
    w_tm = k.sb("w_tm", [128, 8, 256], BF16)
    wst = [k.sb("wst%d" % i, [128, 8, 256]) for i in range(2)]
    for i in range(6):
        st = wst[i % 2]
        if i < 5:
            k.dma(st, wfm[:, i * 256:(i + 1) * 256].re("(c p) m -> p c m", p=128),
                  eng="pool" if i % 2 else "sp")
            k.copy(w_fm[:, :, i * 256:(i + 1) * 256], st, eng="pool")
        else:
            k.dma(st, wtm.re("(c p) m -> p c m", p=128), eng="pool")
            k.copy(w_tm, st, eng="pool")

    lt = k.sb("lam_t", [128, 128])
    s12 = k.sb("lam_s", [128, 2])
    k.tt(lt[:, 0:64], lam_sb[:, 0:64], lam_sb[:, 64:128], ALU.mult)
    k.tt(lt[:, 64:128], lam_sb[:, 128:192], lam_sb[:, 192:256], ALU.mult)
    k.reduce(s12[:, 0:1], lt[:, 0:64], ALU.add)
    k.reduce(s12[:, 1:2], lt[:, 64:128], ALU.add)
    e12 = k.sb("lam_e", [128, 2])
    k.act(e12, s12, AF.Exp)
    neg_lam = k.sb("neg_lam", [128, 1])
    k.stt(neg_lam, e12[:, 1:2], -float(lambda_init), e12[:, 0:1], ALU.add, ALU.subtract)
    wn_s = k.sb("wn_s", [128, 1])
    k.ts(wn_s, vec_sb[:, 4:5], 1.0 - float(lambda_init), ALU.mult)

    xbuf = [k.sb("xbuf%d" % i, [128, 8, 512]) for i in range(2)]
    hT = [k.sb("hT%d" % i, [128, 8, 512], BF16) for i in range(2)]
    sq = [k.sb("sq%d" % i, [128, 512], BF16) for i in range(3)]
    rstd = [k.sb("rstd%d" % i, [128, 512]) for i in range(2)]
    tab = [k.sb("tab%d" % i, [128, 2, 512]) for i in range(2)]
    tmp = [k.sb("tmp%d" % i, [128, 512]) for i in range(4)]
    qT = k.sb("qT", [128, L], BF16)
    kT = k.sb("kT", [128, L], BF16)
    gT = k.sb("gT", [128, L], BF16)
    v_tm = k.sb("v_tm", [128, 64, 128], BF16)
    Pb = [[k.sb("Pb%d_%d" % (m, i), [128, 512], BF16) for i in range(3)] for m in range(2)]
    accD = [k.sb("accD%d" % m, [128, 512]) for m in range(2)]
    accP = [k.sb("accP%d" % m, [128, 512]) for m in range(2)]
    ones_f = k.sb("ones_f", [128, 128])
    k.memset(ones_f, 1.0)
    ost = [k.sb("ost%d" % i, [128, 512], ydt) for i in range(2)]
    fin = [k.sb("fin%d" % i, [128, 512]) for i in range(5)]
    bank = [k.ps("bank%d" % i, [128, 512]) for i in range(8)]

    for kind in range(2):
        wc = kind * 5
        def stage1(j):
            k.dma(tab[j % 2], tabs[2 * kind:2 * kind + 2, :, j * 512:(j + 1) * 512]
                  .re("a p t -> p a t"), eng="pool")
            if hsrc is not None:
                load_hT(k, hT[j % 2], hsrc, j)
                return
            xt = xbuf[j % 2]
            k.dma(xt, xT[:, j * 512:(j + 1) * 512].re("(c p) t -> p c t", p=128),
                  eng="sp")
            for c in range(8):
                s = sq[c % 3]
                k.act(s, xt[:, c, :], AF.Square)
                k.matmul(bank[0], ones_bf, s, start=(c == 0), stop=(c == 7))
            r = rstd[j % 2]
            k.act(r, bank[0], AF.Ln, scale=1.0 / 1024, bias=eps_t)
            k.act(r, r, AF.Exp, scale=-0.5)
            for c in range(8):
                k.stt(hT[j % 2][:, c, :], xt[:, c, :], pw_sb[:, c:c + 1], r, ALU.mult, ALU.mult)

        def proj_fm(ps, grp, j):
            h = hT[j % 2]
            for c in range(8):
                k.matmul(ps, w_fm[:, c, grp * 128:(grp + 1) * 128], h[:, c, :],
                         start=(c == 0), stop=(c == 7))

        def stage2(j):
            h = hT[j % 2]
            tb = tab[j % 2]
            ts_ = slice(j * 512, (j + 1) * 512)
            for which, dst in ((0, qT), (1, kT)):
                pa, pb = bank[1 + 2 * which], bank[2 + 2 * which]
                proj_fm(pa, wc + 2 * which, j)
                proj_fm(pb, wc + 2 * which + 1, j)
                t1, t2 = tmp[2 * which], tmp[2 * which + 1]
                if kind == 0:
                    k.tt(t1, pa, tb[:, 0, :], ALU.mult)
                    k.tt(t2, pb, tb[:, 1, :], ALU.mult)
                    k.tt(dst[:, ts_], t1, t2, ALU.add, eng="pool")
                else:
                    s = sq[which]
                    k.act(s, pa, AF.Square)
                    k.matmul(bank[7], ones_bf, s)
                    rr = fin[which]
                    k.act(rr, bank[7], AF.Ln, scale=1.0 / 128, bias=eps_t)
                    k.act(rr, rr, AF.Exp, scale=-0.5)
                    k.stt(t1, pa, vec_sb[:, 2 * which:2 * which + 1], tb[:, 0, :], ALU.mult, ALU.mult)
                    k.stt(t2, pb, vec_sb[:, 2 * which + 1:2 * which + 2], tb[:, 1, :], ALU.mult, ALU.mult)
                    k.tt(t1, t1, t2, ALU.add, eng="pool")
                    k.tt(dst[:, ts_], t1, rr, ALU.mult, eng="pool")
            proj_fm(bank[5], wc + 4, j)
            k.act(gT[:, ts_], bank[5], AF.Silu)
            pv = bank[6]
            for s4 in range(4):
                for c in range(8):
                    k.matmul(pv[:, s4 * 128:(s4 + 1) * 128], h[:, c, s4 * 128:(s4 + 1) * 128],
                             w_tm[:, c, kind * 128:(kind + 1) * 128],
                             start=(c == 0), stop=(c == 7))
            k.copy(v_tm[:, j * 4:(j + 1) * 4, :], pv.re("p (s e) -> p s e", s=4), eng="dve")

        for j in range(NT + 1):
            if j < NT:
                stage1(j)
            if j >= 1:
                stage2(j - 1)

        nm = 2 if kind == 0 else 1
        dk = 64 if kind == 0 else 128
        scale = dk ** -0.5
        O = [bank[0], bank[1]][:nm]
        Sb = [[bank[2], bank[3]], [bank[4], bank[5]]]
        misc = bank[7]
        pairs = [(qb, kc) for qb in range(16) for kc in range(64)]
        LA = 2
        npair = len(pairs)

        def finalize(qb):
            qs = slice(qb * 512, (qb + 1) * 512)
            o = []
            for m in range(nm):
                k.matmul(misc, ones_f, accD[m], start=True, stop=False)
                k.matmul(misc, ones_f, accP[m], start=False, stop=True)
                r = fin[m]
                k.act(r, misc, AF.Ln)
                k.act(r, r, AF.Exp, scale=-1.0)
                om = fin[2 + m]
                k.tt(om, O[m], r, ALU.mult)
                o.append(om)
            y = ost[qb % 2]
            if kind == 0:
                od = fin[4]
                k.stt(od, o[1], neg_lam, o[0], ALU.mult, ALU.add)
                s = sq[0]
                k.act(s, od, AF.Square)
                k.matmul(misc, ones_bf, s)
                rr = fin[0]
                k.act(rr, misc, AF.Ln, scale=1.0 / 128, bias=eps_t)
                k.act(rr, rr, AF.Exp, scale=-0.5)
                k.stt(od, od, wn_s, rr, ALU.mult, ALU.mult)
                k.tt(y, od, gT[:, qs], ALU.mult, eng="pool")
            else:
                k.tt(y, o[0], gT[:, qs], ALU.mult, eng="pool")
            k.dma(ydst[kind][:, qs], y, eng="sp")

        for idx in range(npair + LA):
            if idx < npair:
                qb, kc = pairs[idx]
                for m in range(nm):
                    k.matmul(Sb[m][idx % 2], kT[m * dk:(m + 1) * dk, kc * 128:(kc + 1) * 128],
                             qT[m * dk:(m + 1) * dk, qb * 512:(qb + 1) * 512])
                for m in range(nm):
                    k.act(Pb[m][idx % 3], Sb[m][idx % 2], AF.Exp, scale=scale)
            i2 = idx - LA
            if i2 >= 0:
                qb, kc = pairs[i2]
                for m in range(nm):
                    P = Pb[m][i2 % 3]
                    k.matmul(O[m], v_tm[:, kc, :], P, start=(kc == 0), stop=(kc == 63))
                    PMOD = ATT_KNOBS.get('pmod', 1 << 30)
                    on_pool = (kc % PMOD == PMOD - 1)
                    acc, eng = (accP[m], "pool") if on_pool else (accD[m], "dve")
                    first = (kc == PMOD - 1) if on_pool else (kc == 0)
                    if first:
                        k.copy(acc, P, eng=eng)
                    else:
                        k.tt(acc, acc, P, ALU.add, eng=eng)
                if kc == 63:
                    finalize(qb)
        if kind == 0 and io.get("after_kind0") is not None:
            io["after_kind0"]()


def emit_ssd(k, io):
    stop = 99
    xT, hsrc, pw, wfm, wdt, cw, cb, ssmv, cst = (io.get("xT"), io.get("hT"), io["pw"], io["wfm"], io["wdt"],
                                                 io["cw"], io["cb"], io["ssmv"], io["cst"])
    yT, ydt = io["y"], io["ydt"]

    ones_bf = k.sb("ones_bf", [128, 128], BF16)
    k.memset(ones_bf, 1.0)
    ones_f = k.sb("ones_f", [128, 128])
    k.memset(ones_f, 1.0)
    eps_t = k.sb("eps_t", [128, 1])
    k.memset(eps_t, NORM_EPS)
    one_t = k.sb("one_t", [128, 1])
    k.memset(one_t, 1.0)
    cst_sb = k.sb("cst_sb", [128, 5, 128])
    k.dma(cst_sb, cst)
    tri = [cst_sb[:, 0, :], cst_sb[:, 1, :]]
    smask = [cst_sb[:, 2, :], cst_sb[:, 3, :]]
    ident_f = cst_sb[:, 4, :]
    ident_bf = k.sb("ident_bf", [128, 128], BF16)
    k.copy(ident_bf, ident_f, eng="dve")
    pw_sb = k.sb("pw_sb", [128, 8])
    k.dma(pw_sb, pw)
    cw_sb = k.sb("cw_sb", [128, 3, 5])
    k.dma(cw_sb, cw)
    cb_sb = k.sb("cb_sb", [128, 3])
    k.dma(cb_sb, cb)
    sv = k.sb("sv", [128, 16])
    k.dma(sv, ssmv)
    w_fm = k.sb("w_fm", [128, 8, 512], BF16)
    w_dt = k.sb("w_dt", [128, 8, 4], BF16)
    wst2 = k.sb("wst2", [128, 8, 4])
    k.dma(wst2, wdt.re("(c p) m -> p c m", p=128), eng="pool")
    k.copy(w_dt, wst2, eng="pool")

    xbuf = [k.sb("xbuf%d" % i, [128, 8, 512]) for i in range(1)]
    k.dma(xbuf[0], wfm.re("(c p) m -> p c m", p=128))
    k.copy(w_fm, xbuf[0], eng="pool")
    hT = [k.sb("hT%d" % i, [128, 8, 512], BF16) for i in range(2)]
    sq = [k.sb("sq%d" % i, [128, 512], BF16) for i in range(3)]
    rstd = [k.sb("rstd%d" % i, [128, 512]) for i in range(1)] * 2
    upad = [k.sb("upad%d" % i, [128, 3, 516]) for i in range(3)]
    acc = [k.sb("acc%d" % i, [128, 512]) for i in range(3)]
    zsT = k.sb("zsT", [128, L], BF16)
    xcT = k.sb("xcT", [128, L], BF16)
    BT = k.sb("BT", [128, L], BF16)
    CT = k.sb("CT", [128, L], BF16)
    dst3 = [xcT, BT, CT]
    dtraw = k.sb("dtraw", [128, 4, NCH])
    bank = [k.ps("bank%d" % i, [128, 512]) for i in range(7)]
    tbank = k.ps("tbank", [128, 1024], BF16)

    k.memset(upad[0][:, :, 0:2], 0.0)

    def stage1(j):
        if hsrc is not None:
            load_hT(k, hT[j % 2], hsrc, j)
            return
        xt = xbuf[0]
        k.dma(xt, xT[:, j * 512:(j + 1) * 512].re("(c p) t -> p c t", p=128), eng="sp")
        for c in range(8):
            s = sq[c % 3]
            k.act(s, xt[:, c, :], AF.Square)
            k.matmul(bank[0], ones_bf, s, start=(c == 0), stop=(c == 7))
        r = rstd[j % 2]
        k.act(r, bank[0], AF.Ln, scale=1.0 / 1024, bias=eps_t)
        k.act(r, r, AF.Exp, scale=-0.5)
        for c in range(8):
            k.stt(hT[j % 2][:, c, :], xt[:, c, :], pw_sb[:, c:c + 1], r, ALU.mult, ALU.mult)

    def stage2(j):
        h = hT[j % 2]
        ts_ = slice(j * 512, (j + 1) * 512)
        up = upad[j % 3]
        for grp in range(4):
            ps = bank[1 + grp]
            for c in range(8):
                k.matmul(ps, w_fm[:, c, grp * 128:(grp + 1) * 128], h[:, c, :],
                         start=(c == 0), stop=(c == 7))
            if grp == 0:
                k.act(zsT[:, ts_], ps, AF.Silu)
            else:
                k.copy(up[:, grp - 1, 2:514], ps, eng="act")
        pdt = bank[5]
        for s4 in range(4):
            for c in range(8):
                k.matmul(pdt[:, s4 * 4:(s4 + 1) * 4], h[:, c, s4 * 128:(s4 + 1) * 128], w_dt[:, c, :],
                         start=(c == 0), stop=(c == 7))
        k.copy(dtraw[:, :, j * 4:(j + 1) * 4].re("p h s -> p s h"),
               pdt[:, 0:16].re("p (s h) -> p s h", s=4), eng="dve")

    def conv(j):
        up = upad[j % 3]
        if j > 0:
            k.copy(up[:, :, 0:2], upad[(j - 1) % 3][:, :, 512:514], eng="pool")
        if j < NT - 1:
            k.copy(up[:, :, 514:516], upad[(j + 1) % 3][:, :, 2:4], eng="pool")
        else:
            k.memset(up[:, :, 514:516], 0.0)
        ts_ = slice(j * 512, (j + 1) * 512)
        for ch in range(3):
            a = acc[ch]
            k.ts(a, up[:, ch, 0:512], cw_sb[:, ch, 0:1], ALU.mult)
            for o in range(1, 5):
                k.stt(a, up[:, ch, o:o + 512], cw_sb[:, ch, o:o + 1], a, ALU.mult, ALU.add)
            k.act(dst3[ch][:, ts_], a, AF.Silu, bias=cb_sb[:, ch:ch + 1])

    for j in range(NT + 2):
        if j < NT:
            stage1(j)
        if 1 <= j <= NT:
            stage2(j - 1)
        if j >= 2:
            conv(j - 2)


    dt = k.sb("dt", [128, 4, NCH])
    a_ = k.sb("a_", [128, 4, NCH])
    cum = k.sb("cum", [128, 4, NCH])
    dtd = k.sb("dtd", [128, 4, NCH])
    etot = k.sb("etot", [128, 4, NCH])
    aneg = k.sb("aneg", [128, 4])
    k.act(aneg, sv[:, 4:8], AF.Exp)
    k.ts(aneg, aneg, -1.0, ALU.mult)
    for hd in range(4):
        k.act(dt[:, hd, :], dtraw[:, hd, :], AF.Exp, bias=sv[:, hd:hd + 1])
    k.act(dt, dt, AF.Ln, bias=one_t)
    for hd in range(4):
        k.ts(a_[:, hd, :], dt[:, hd, :], aneg[:, hd:hd + 1], ALU.mult)
    pc = bank[0]
    k.matmul(pc[:, 0:128], tri[0], a_[:, 0:2, :].re("p h c -> p (h c)"))
    k.matmul(pc[:, 128:256], tri[1], a_[:, 2:4, :].re("p h c -> p (h c)"))
    k.matmul(pc[:, 256:512], ones_f, a_.re("p h c -> p (h c)"))
    k.copy(cum.re("p h c -> p (h c)"), pc[:, 0:256], eng="dve")
    k.act(etot.re("p h c -> p (h c)"), pc[:, 256:512], AF.Exp)
    k.tt(dtd.re("p h c -> p (h c)"), pc[:, 256:512], cum.re("p h c -> p (h c)"), ALU.subtract)
    k.act(dtd, dtd, AF.Exp)
    k.tt(dtd, dtd, dt, ALU.mult)

    Sf_all = k.sb("Sf_all", [128, NCH, 128], BF16)
    Sb_all = k.sb("Sb_all", [128, NCH, 128], BF16)
    Srun = [k.sb("Srun%d" % i, [128, 128]) for i in range(2)]
    btm = [k.sb("btm%d" % i, [128, 128], BF16) for i in range(2)]
    xdtd = [k.sb("xdtd%d" % i, [128, 128], BF16) for i in range(2)]
    for d in range(2):
        k.memset(Srun[d], 0.0)
        order = range(NCH) if d == 0 else range(NCH - 1, -1, -1)
        S_all = Sf_all if d == 0 else Sb_all
        for i, c in enumerate(order):
            cs = slice(c * 128, (c + 1) * 128)
            tb = tbank[:, (i % 2) * 256:(i % 2) * 256 + 256]
            k.transpose(tb[:, 0:128], xcT[:, cs], ident_bf)
            k.transpose(tb[:, 128:256], BT[:, cs], ident_bf)
            bt = btm[i % 2]
            k.copy(bt, tb[:, 128:256], eng="act")
            xd = xdtd[i % 2]
            for h in range(2):
                hd = d * 2 + h
                k.ts(xd[:, h * 64:(h + 1) * 64], tb[:, h * 64:(h + 1) * 64], dtd[:, hd, c:c + 1], ALU.mult)
            st = bank[1 + i % 2]
            k.matmul(st[:, 0:128], bt, xd)
            k.copy(S_all[:, c, :], Srun[d], eng="act")
            for h in range(2):
                hd = d * 2 + h
                hb = slice(h * 64, (h + 1) * 64)
                k.stt(Srun[d][:, hb], Srun[d][:, hb], etot[:, hd, c:c + 1], st[:, hb], ALU.mult, ALU.add)

    lhsD = [k.sb("lhsD%d" % i, [128, 4, 128]) for i in range(1)] * 2
    abc = [k.sb("abc%d" % i, [128, 4, 128]) for i in range(1)] * 2
    Lexp = [k.sb("Lexp%d" % i, [128, 512]) for i in range(2)]
    Ebc = [k.sb("Ebc%d" % i, [128, 512]) for i in range(2)]
    Gm = [k.sb("Gm%d" % i, [128, 2, 128]) for i in range(2)]
    MT = [k.sb("MT%d" % i, [128, 4, 128], BF16) for i in range(2)]
    Ct = [k.sb("Ct%d" % i, [128, 4, 128], BF16) for i in range(2)]
    xtm = [k.sb("xtm%d" % i, [128, 128]) for i in range(2)]
    xdt = [k.sb("xdt%d" % i, [128, 4, 64], BF16) for i in range(2)]
    y_sb = [k.sb("y_sb%d" % i, [128, 128]) for i in range(2)]
    ost = [k.sb("ost%d" % i, [128, 512], ydt) for i in range(2)]
    def front(c):
        cs = slice(c * 128, (c + 1) * 128)
        p = c % 2
        tb = tbank[:, p * 256:p * 256 + 128]
        k.transpose(tb, xcT[:, cs], ident_bf)
        k.copy(xtm[p], tb, eng="act")
        for hd in range(4):
            k.ts(xdt[p][:, hd, :], xtm[p][:, (hd % 2) * 64:(hd % 2) * 64 + 64], dt[:, hd, c:c + 1], ALU.mult)
        for hd in range(4):
            d = hd // 2
            k.ts(lhsD[p][:, hd, :], smask[d], a_[:, hd, c:c + 1], ALU.mult)
            k.ts(abc[p][:, hd, :], ones_f, a_[:, hd, c:c + 1], ALU.mult)
        Dps, Cps = bank[3], bank[4]
        for hd in range(4):
            d = hd // 2
            k.matmul(Dps[:, hd * 128:(hd + 1) * 128], lhsD[p][:, hd, :], tri[d])
        for hd in range(4):
            d = hd // 2
            k.matmul(Cps[:, hd * 128:(hd + 1) * 128], abc[p][:, hd, :], tri[d])
        k.act(Lexp[p], Dps, AF.Exp)
        k.act(Ebc[p], Cps, AF.Exp)
        Gps = bank[5]
        k.matmul(Gps[:, 0:128], BT[:, cs], CT[:, cs])
        for d in range(2):
            k.tt(Gm[p][:, d, :], Gps[:, 0:128], tri[d], ALU.mult)

    def back(c):
        cs = slice(c * 128, (c + 1) * 128)
        p = c % 2
        for hd in range(4):
            d = hd // 2
            k.tt(MT[p][:, hd, :], Lexp[p][:, hd * 128:(hd + 1) * 128], Gm[p][:, d, :], ALU.mult)
            k.tt(Ct[p][:, hd, :], Ebc[p][:, hd * 128:(hd + 1) * 128], CT[:, cs], ALU.mult)
        Yps = bank[6]
        for h in range(2):
            hb = slice(h * 64, (h + 1) * 64)
            k.matmul(Yps[:, hb], MT[p][:, h, :], xdt[p][:, h, :], start=True, stop=False)
            k.matmul(Yps[:, hb], MT[p][:, 2 + h, :], xdt[p][:, 2 + h, :], start=False, stop=False)
            k.matmul(Yps[:, hb], Ct[p][:, h, :], Sf_all[:, c, hb], start=False, stop=False)
            k.matmul(Yps[:, hb], Ct[p][:, 2 + h, :], Sb_all[:, c, hb], start=False, stop=True)
        for h in range(2):
            hb = slice(h * 64, (h + 1) * 64)
            k.stt(y_sb[p][:, hb], xtm[p][:, hb], sv[:, 8 + h:9 + h], Yps[:, hb], ALU.mult, ALU.add)
        yTp = bank[1 + p]
        k.transpose(yTp[:, 0:128], y_sb[p], ident_f)
        o = ost[(c // 4) % 2]
        k.tt(o[:, (c % 4) * 128:(c % 4 + 1) * 128], yTp[:, 0:128], zsT[:, cs], ALU.mult)
        if c % 4 == 3:
            k.dma(yT[:, (c - 3) * 128:(c + 1) * 128], o, eng="sp")

    for c in range(NCH + 1):
        if c < NCH:
            front(c)
        if c >= 1:
            back(c - 1)


L = 8192
NT = L // 512
NORM_EPS = 1e-6
GN_EPS = 64e-5
C0 = math.exp(-0.5)


def emit_rwkv(k, io):
    stop, nq_lim, do_chain, do_pre, pre_lim = 99, None, True, True, 9
    xT, hsrc, pw, wfm, mu, w2a2, pvd, cm, c128 = (io.get("xT"), io.get("hT"), io["pw"], io["wfm"], io["mu"],
                                                  io["w2a2"], io["pvd"], io["cm"], io["c128"])
    wkv_scr, bon_scr, yT, ydt = io["wkv_scr"], io["bon_scr"], io["y"], io["ydt"]

    ones_bf = k.sb("ones_bf", [128, 128], BF16)
    k.memset(ones_bf, 1.0)
    eps_t = k.sb("eps_t", [128, 1])
    k.memset(eps_t, NORM_EPS)
    epsg_t = k.sb("epsg_t", [128, 1])
    k.memset(epsg_t, GN_EPS)
    tiny_t = k.sb("tiny_t", [128, 1])
    k.memset(tiny_t, 1e-24)
    pw_sb = k.sb("pw_sb", [128, 8])
    k.dma(pw_sb, pw)
    mu_sb = k.sb("mu_sb", [128, 2, 4])
    k.dma(mu_sb, mu)
    w2a2_sb = k.sb("w2a2_sb", [128, 2, 128])
    k.dma(w2a2_sb, w2a2)
    pv = k.sb("pv", [128, 12])
    k.dma(pv, pvd)
    omk = k.sb("omk", [128, 1])
    k.ts(omk, pv[:, 4:5], -1.0, ALU.mult, 1.0, ALU.add)
    cm_sb = k.sb("cm_sb", [64, 2, 704])
    k.dma(cm_sb, cm)
    c128_sb = k.sb("c128_sb", [128, 2, 128])
    k.dma(c128_sb, c128)
    blk_bf = k.sb("blk_bf", [128, 128], BF16)
    k.copy(blk_bf, c128_sb[:, 0, :], eng="dve")
    ident_f = c128_sb[:, 1, :]
    ident_bf = k.sb("ident_bf", [128, 128], BF16)
    k.copy(ident_bf, ident_f, eng="dve")
    id64_bf = k.sb("id64_bf", [64, 4, 64], BF16)
    for h in range(4):
        k.copy(id64_bf[:, h, :], cm_sb[:, 0, 640:704], eng="dve")
    maskSC = [cm_sb[:, d, 0:512] for d in range(2)]
    maskN = [cm_sb[:, d, 512:640] for d in range(2)]

    xbuf = k.sb("xbuf", [128, 8, 512])
    w_fm = k.sb("w_fm", [128, 8, 640], BF16)
    k.dma(xbuf, wfm[:, 0:512].re("(c p) m -> p c m", p=128))
    k.copy(w_fm[:, :, 0:512], xbuf, eng="pool")
    k.dma(xbuf[:, :, 0:128], wfm[:, 512:640].re("(c p) m -> p c m", p=128))
    k.copy(w_fm[:, :, 512:640], xbuf[:, :, 0:128], eng="pool")

    hT = k.sb("hT", [128, 8, 512], BF16)
    sq = [k.sb("sq%d" % i, [128, 512], BF16) for i in range(2)]
    rstd = k.sb("rstd", [128, 512])
    upad = [k.sb("upad%d" % d, [128, 4, 514]) for d in range(2)]
    ul = k.sb("ul", [128, 4, 512])
    tmpd = [k.sb("tmpd%d" % i, [128, 512]) for i in range(2)]
    gT = k.sb("gT", [128, L], BF16)
    twd = k.sb("twd", [64, 512])
    sg = k.sb("sg", [128, 512])
    aic = k.sb("aic", [128, 512])
    kk = k.sb("kk", [128, 512])
    kkn = k.sb("kkn", [128, 512])
    rs = k.sb("rs", [128, 512])
    tka = k.sb("tka", [128, 512])
    k2 = k.sb("k2", [128, 512])
    bvec = k.sb("bvec", [128, 512])
    rk = k.sb("rk", [128, 512], BF16)
    bon = [k.sb("bon%d" % i, [128, 512]) for i in range(2)]
    cs = [k.sb("cs%d" % i, [128, 512]) for i in range(2)]
    csm = k.sb("csm", [128, 512])
    Epos = k.sb("Epos", [128, 512])
    Eneg = k.sb("Eneg", [128, 512])
    Eprev = k.sb("Eprev", [128, 512])
    RopT = [[k.sb("RopT%d%d" % (d, s), [128, 8, 2, 64], BF16) for s in range(2)] for d in range(2)]
    LopT = [[k.sb("LopT%d%d" % (d, s), [128, 8, 2, 64], BF16) for s in range(2)] for d in range(2)]
    vTb = [[k.sb("vTb%d%d" % (d, s), [128, 512], BF16) for s in range(2)] for d in range(2)]
    eC = [[k.sb("eC%d%d" % (d, s), [128, 8]) for s in range(2)] for d in range(2)]
    NR = 4
    tm = [[k.sb("tm%d%d" % (d, i), [64, 384], BF16) for i in range(NR)] for d in range(2)]
    scmB = [k.sb("scmB%d" % i, [64, 2, 4, 128], BF16) for i in range(NR)]
    TtB = [k.sb("TtB%d" % i, [64, 4, 64], BF16) for i in range(NR)]
    NMt2 = [[k.sb("NMt%d_%d" % (a, i), [64, 2, 4, 64], BF16) for i in range(2)] for a in range(2)]
    Pt2 = [[k.sb("Pt%d_%d" % (a, i), [64, 4, 64], BF16) for i in range(2)] for a in range(2)]
    S32 = [k.sb("S32_%d" % d, [128, 128]) for d in range(2)]
    t1 = [k.sb("t1_%d" % d, [128, 128]) for d in range(2)]
    Sbf = [[k.sb("Sbf%d%d" % (d, i), [128, 128], BF16) for i in range(2)] for d in range(2)]
    Zb = [k.sb("Zb%d" % d, [64, 128], BF16) for d in range(2)]
    Ub = [k.sb("Ub%d" % d, [64, 128], BF16) for d in range(2)]
    Yo = [[k.sb("Yo%d%d" % (d, i), [64, 128]) for i in range(2)] for d in range(2)]
    bankA = [k.ps("bankA%d" % i, [128, 512]) for i in range(2)]
    bSC = k.ps("bSC", [128, 512])
    bNM = k.ps("bNM", [128, 512])
    bP = k.ps("bP", [128, 512])
    bCH = [k.ps("bCH%d" % d, [128, 512]) for d in range(2)]
    tb32 = k.ps("tbank", [128, 512])
    tbank = T(tb32.buf, tb32.ap.bitcast(BF16))

    for d in range(2):
        k.memset(S32[d], 0.0)
        k.memset(Sbf[d][0], 0.0)
    k.memset(upad[0][:, :, 0:1], 0.0)
    k.memset(upad[1][:, :, 513:514], 0.0)

    state = {"a": 0}

    def nbank():
        return bankA[0]

    def phaseA(d, j, slot):
        first = (j == 0) if d == 0 else (j == NT - 1)
        up = upad[d]
        if not first:
            if d == 0:
                k.copy(up[:, :, 0:1], up[:, :, 512:513], eng="pool")
            else:
                k.copy(up[:, :, 513:514], up[:, :, 1:2], eng="pool")
        if hsrc is not None:
            load_hT(k, hT, hsrc, j)
        else:
            k.dma(xbuf, xT[:, j * 512:(j + 1) * 512].re("(c p) t -> p c t", p=128), eng="sp")
            ps = nbank()
            for c in range(8):
                s = sq[c % 2]
                k.act(s, xbuf[:, c, :], AF.Square)
                k.matmul(ps, ones_bf, s, start=(c == 0), stop=(c == 7))
            k.act(rstd, ps, AF.Ln, scale=1.0 / 1024, bias=eps_t)
            k.act(rstd, rstd, AF.Exp, scale=-0.5)
            for c in range(8):
                k.stt(hT[:, c, :], xbuf[:, c, :], pw_sb[:, c:c + 1], rstd, ALU.mult, ALU.mult)
        ts_ = slice(j * 512, (j + 1) * 512)
        for grp in range(5 if d == 0 else 4):
            ps = nbank()
            for c in range(8):
                k.matmul(ps, w_fm[:, c, grp * 128:(grp + 1) * 128], hT[:, c, :],
                         start=(c == 0), stop=(c == 7))
            if grp < 4:
                k.copy(up[:, grp, 1:513], ps, eng="act")
            else:
                k.act(gT[:, ts_], ps, AF.Silu)
        sh = slice(0, 512) if d == 0 else slice(2, 514)
        for grp in range(4):
            td = tmpd[grp % 2]
            k.tt(td, up[:, grp, sh], up[:, grp, 1:513], ALU.subtract, eng="pool")
            k.stt(ul[:, grp, :], td, mu_sb[:, d, grp:grp + 1], up[:, grp, 1:513], ALU.mult, ALU.add)
        r, kx, v, wa = ul[:, 0, :], ul[:, 1, :], ul[:, 2, :], ul[:, 3, :]
        k.copy(vTb[d][slot], v, eng="pool")
        k.act(twd, wa[0:64, :], AF.Tanh)
        pxw = nbank()
        k.matmul(pxw, w2a2_sb[0:64, d, :], twd)
        k.act(sg, pxw, AF.Sigmoid, bias=pv[:, d:d + 1])
        pxa = nbank()
        k.matmul(pxa, w2a2_sb[64:128, d, :], wa[64:128, :])
        k.act(aic, pxa, AF.Sigmoid, bias=pv[:, 2:3])
        k.ts(kk, kx, pv[:, 3:4], ALU.mult)
        k.act(sq[0], kk, AF.Square)
        pss = nbank()
        k.matmul(pss, blk_bf, sq[0])
        k.act(rs, pss, AF.Ln, bias=tiny_t)
        k.act(rs, rs, AF.Exp, scale=-0.5)
        k.tt(kkn, kk, rs, ALU.mult)
        k.ts(tka, aic, pv[:, 4:5], ALU.mult, omk, ALU.add)
        k.tt(k2, kx, tka, ALU.mult)
        k.tt(bvec, kkn, aic, ALU.mult, eng="pool")
        k.stt(rk, r, pv[:, 6:7], k2, ALU.mult, ALU.mult)
        pbs = nbank()
        k.matmul(pbs, blk_bf, rk)
        bo = bon[d]
        k.tt(bo, pbs, v, ALU.mult)
        k.dma(bon_scr[d, :, ts_], bo, eng="sp")
        src = sg
        i = 0
        for s in (1, 2, 4, 8, 16, 32):
            dst = cs[i % 2]
            sv_, dv_ = src.re("p (c t) -> p c t", t=64), dst.re("p (c t) -> p c t", t=64)
            if d == 0:
                k.tt(dv_[:, :, s:64], sv_[:, :, s:64], sv_[:, :, 0:64 - s], ALU.add)
                k.copy(dv_[:, :, 0:s], sv_[:, :, 0:s], eng="pool")
            else:
                k.tt(dv_[:, :, 0:64 - s], sv_[:, :, 0:64 - s], sv_[:, :, s:64], ALU.add)
                k.copy(dv_[:, :, 64 - s:64], sv_[:, :, 64 - s:64], eng="pool")
            src = dst
            i += 1
        csf = src
        k.tt(csm, csf, sg, ALU.subtract, eng="pool")
        k.act(Epos, csf, AF.Exp, scale=-C0)
        k.act(Eneg, csf, AF.Exp, scale=C0)
        k.act(Eprev, csm, AF.Exp, scale=-C0)
        last = 63 if d == 0 else 0
        k.copy(eC[d][slot], Epos.re("p (c t) -> p c t", t=64)[:, :, last], eng="pool")
        R, Lo = RopT[d][slot], LopT[d][slot]
        v3 = lambda t: t.re("p (c t) -> p c t", t=64)
        k.stt(R[:, :, 0, :], v3(kkn), -1.0, v3(Eprev), ALU.mult, ALU.mult)
        k.tt(R[:, :, 1, :], v3(r), v3(Epos), ALU.mult)
        k.tt(Lo[:, :, 0, :], v3(bvec), v3(Eneg), ALU.mult)
        k.tt(Lo[:, :, 1, :], v3(k2), v3(Eneg), ALU.mult, eng="pool")

    def pre_stages(q):
        items = items_for(q)
        i3 = q % NR
        R = [RopT[d][slot][:, cc] for (d, slot, cc, _, _) in items]
        Lo = [LopT[d][slot][:, cc] for (d, slot, cc, _, _) in items]
        sc = scmB[i3]
        NMt, Pt = NMt2[q % 2], Pt2[q % 2]
        st = []

        def s_tr():
            for (d, slot, cc, _, _) in items:
                tps = tbank[0:64, (d * 384):(d * 384) + 384]
                k.transpose(tps[:, 0:128], Lo[d][:, 0, :], ident_bf)
                k.transpose(tps[:, 128:256], Lo[d][:, 1, :], ident_bf)
                k.transpose(tps[:, 256:384], vTb[d][slot][:, cc * 64:(cc + 1) * 64], ident_bf)
                k.copy(tm[d][i3], tps, eng="act")
        st.append(s_tr)

        def s_sc():
            for h in range(2):
                hs = slice(64 * h, 64 * h + 64)
                bk = bSC if h == 0 else bankA[1]
                nk = bP[0:64, 256:384] if h == 0 else tb32[0:64, 384:512]
                for d in range(2):
                    Rh = R[d][hs].re("p a t -> p (a t)")
                    k.matmul(bk[0:64, d * 256:d * 256 + 128], Lo[d][hs, 0, :], Rh)
                    k.matmul(bk[0:64, d * 256 + 128:d * 256 + 256], Lo[d][hs, 1, :], Rh)
                for d in range(2):
                    k.matmul(nk[:, d * 64:(d + 1) * 64], R[d][hs, 0, :], Lo[d][hs, 0, :])
            nm = NMt[0]
            for h in range(2):
                bk = bSC if h == 0 else bankA[1]
                nk = bP[0:64, 256:384] if h == 0 else tb32[0:64, 384:512]
                k.tt(sc[:, :, 2 * h:2 * h + 2, :].re("p d a t -> p d (a t)"),
                     bk[0:64, :].re("p (d x) -> p d x", d=2), cm_sb[:, :, 0:256], ALU.mult)
                k.tt(nm[:, 0].re("p (d h) t -> p d h t", h=2)[:, :, h, :], nk.re("p (d t) -> p d t", d=2),
                     cm_sb[:, :, 512:576], ALU.mult)
            k.copy(nm[:, 1].re("p (d h) t -> p d h t", h=2),
                   sc.re("p d (h two) t -> p d h two t", two=2)[:, :, :, 0, 0:64], eng="dve")
            k.tt(Pt[0], nm[:, 1], id64_bf, ALU.add, eng="dve")
        st.append(s_sc)

        for lv in range(5):
            cur = lv % 2
            nm_c, nm_n = NMt[cur], NMt[1 - cur]
            P_c = Pt[cur]
            P_n = Pt[1 - cur] if lv < 4 else TtB[i3]

            def s_nm(lv=lv, nm_c=nm_c, nm_n=nm_n):
                pn = bNM[0:64, :]
                for dh in range(4):
                    k.matmul(pn[:, dh * 64:(dh + 1) * 64], nm_c[:, 1, dh, :], nm_c[:, 0, dh, :])
                if lv < 4:
                    for dh in range(4):
                        k.matmul(pn[:, 256 + dh * 64:256 + (dh + 1) * 64], nm_c[:, 0, dh, :], nm_c[:, 1, dh, :])
                    k.copy(nm_n.re("p a h t -> p (a h t)"), pn, eng="act")
                else:
                    k.copy(nm_n[:, 0].re("p h t -> p (h t)"), pn[:, 0:256], eng="act")
            st.append(s_nm)

            def s_p(nm_n=nm_n, P_c=P_c, P_n=P_n):
                pp = bP[0:64, 0:256]
                for dh in range(4):
                    k.matmul(pp[:, dh * 64:(dh + 1) * 64], nm_n[:, 0, dh, :], P_c[:, dh, :], start=True, stop=False)
                    k.matmul(pp[:, dh * 64:(dh + 1) * 64], id64_bf[:, 0, :], P_c[:, dh, :], start=False, stop=True)
                k.copy(P_n.re("p h t -> p (h t)"), pp, eng="dve")
            st.append(s_p)
        return st

    par = [0, 0]

    def chain_stages(q):
        items = items_for(q)
        i3 = q % NR
        ctx = []
        for (d, slot, cc, _, cg) in items:
            CB = bCH[d]
            ctx.append(dict(d=d, R=RopT[d][slot][:, cc], sc=scmB[i3][:, d], tmv=tm[d][i3], T=TtB[i3][:, 2 * d:2 * d + 2, :],
                            X=CB[0:64, 0:128], U=CB[0:64, 128:256], DS=CB[:, 256:384], Y=CB[0:64, 384:512],
                            Sb=Sbf[d][par[d]], Sn=Sbf[d][1 - par[d]], e=eC[d][slot][:, cc:cc + 1],
                            cg=cg, q=q))
            par[d] = 1 - par[d]
        st = []

        def c_x():
            for c in ctx:
                k.matmul(c["X"], c["R"][:, 0, :], c["Sb"], start=True, stop=False)
                for h in range(2):
                    hb = slice(64 * h, 64 * h + 64)
                    k.matmul(c["X"][:, hb], c["sc"][:, 2 * h + 1, 0:64], c["tmv"][:, 256 + 64 * h:256 + 64 * h + 64],
                             start=False, stop=(h == 1))
            for c in ctx:
                k.copy(Zb[c["d"]], c["X"], eng="act")
        st.append(c_x)

        def c_u():
            for c in ctx:
                for h in range(2):
                    hb = slice(64 * h, 64 * h + 64)
                    k.matmul(c["U"][:, hb], c["T"][:, h, :], Zb[c["d"]][:, hb])
            for c in ctx:
                k.copy(Ub[c["d"]], c["U"], eng="dve")
        st.append(c_u)

        def c_y():
            for c in ctx:
                d = c["d"]
                k.matmul(c["DS"], c["tmv"][:, 0:128], Ub[d], start=True, stop=False)
                k.matmul(c["DS"], c["tmv"][:, 128:256], c["tmv"][:, 256:384], start=False, stop=True)
                k.ts(t1[d], S32[d], c["e"], ALU.mult)
            for c in ctx:
                d = c["d"]
                k.matmul(c["Y"], c["R"][:, 1, :], c["Sb"], start=True, stop=False)
                for h in range(2):
                    hb = slice(64 * h, 64 * h + 64)
                    k.matmul(c["Y"][:, hb], c["sc"][:, 2 * h, 64:128], Ub[d][:, hb], start=False, stop=False)
                    k.matmul(c["Y"][:, hb], c["sc"][:, 2 * h + 1, 64:128], c["tmv"][:, 256 + 64 * h:256 + 64 * h + 64],
                             start=False, stop=(h == 1))
        st.append(c_y)

        def c_s():
            for c in ctx:
                d = c["d"]
                for h in range(2):
                    hs = slice(64 * h, 64 * h + 64)
                    k.stt(S32[d][hs, hs], c["DS"][hs, hs], c["e"][hs], t1[d][hs, hs], ALU.mult, ALU.add)
            for c in ctx:
                d = c["d"]
                k.copy(c["Sn"], S32[d], eng="act")
                yo = Yo[d][c["q"] % 2]
                k.copy(yo, c["Y"], eng="dve")
                k.dma(wkv_scr[d, c["cg"] * 64:(c["cg"] + 1) * 64, :], yo, eng="pool")
        st.append(c_s)
        return st

    def items_for(q):
        s, ci = q // 8, q % 8
        return [(0, s % 2, ci, q, s * 8 + ci), (1, s % 2, 7 - ci, q, (NT - 1 - s) * 8 + (7 - ci))]

    nq = NT * 8
    phaseA(0, 0, 0)
    phaseA(1, NT - 1, 0)
    pendA = []
    perA = 1
    for q in range(0, nq + 2, 2):
        PA, PB = [], []
        if q < nq:
            s, ci = q // 8, q % 8
            if ci == 2 and s + 1 < NT:
                k.begin_defer()
                phaseA(0, s + 1, (s + 1) % 2)
                phaseA(1, NT - 2 - s, (s + 1) % 2)
                pendA = k.end_defer()
                perA = (len(pendA) + 3 * 30 - 1) // (3 * 30)
            PA = pre_stages(q)
            PB = pre_stages(q + 1)
        cs_ = []
        if q >= 2:
            cs_ = chain_stages(q - 2) + chain_stages(q - 1)
        np_, nc_ = len(PA), len(cs_)
        ci_ = 0
        for i in range(np_):
            PA[i]()
            if pendA:
                k.splice(pendA, perA)
            PB[i]()
            if pendA:
                k.splice(pendA, perA)
            want = (i + 1) * nc_ // np_
            while ci_ < want:
                cs_[ci_]()
                if pendA:
                    k.splice(pendA, perA)
                ci_ += 1
        while ci_ < nc_:
            cs_[ci_]()
            ci_ += 1
        if q < nq and q % 8 == 6 and pendA:
            k.splice(pendA, len(pendA))
    assert not pendA

    wf = [k.sb("wf%d" % i, [128, 128]) for i in range(2)]
    wb = [k.sb("wb%d" % i, [128, 128]) for i in range(2)]
    ww = [k.sb("ww%d" % i, [128, 128]) for i in range(2)]
    sqw = k.sb("sqw", [128, 128])
    st1 = k.sb("st1", [128, 2])
    st2 = k.sb("st2", [128, 2])
    mean = k.sb("mean", [128, 2])
    msq = k.sb("msq", [128, 2])
    var = k.sb("var", [128, 2])
    gn = [k.sb("gn%d" % i, [128, 128]) for i in range(2)]
    ob = [k.sb("ob%d" % i, [128, 512]) for i in range(2)]
    obo = [k.sb("obo%d" % i, [128, 512], ydt) for i in range(2)]
    bf_ = [k.sb("bf_%d" % i, [128, 512]) for i in range(2)]
    bb_ = [k.sb("bb_%d" % i, [128, 512]) for i in range(2)]
    for i in range(L // 128):
        p = i % 2
        k.dma(wf[p], wkv_scr[0, i * 128:(i + 1) * 128, :], eng="sp")
        k.dma(wb[p], wkv_scr[1, i * 128:(i + 1) * 128, :], eng="sp")
        w = ww[p]
        k.tt(w, wf[p], wb[p], ALU.add)
        k.reduce(st1, w.re("p (h v) -> p h v", h=2), ALU.add)
        k.tt(sqw, w, w, ALU.mult, eng="pool")
        k.reduce(st2, sqw.re("p (h v) -> p h v", h=2), ALU.add)
        k.ts(mean, st1, 1.0 / 64, ALU.mult)
        k.tt(msq, mean, mean, ALU.mult)
        k.stt(var, st2, 1.0 / 64, msq, ALU.mult, ALU.subtract)
        k.act(var, var, AF.Sqrt, bias=epsg_t)
        k.recip(var, var)
        g_ = gn[p]
        for h in range(2):
            hb = slice(64 * h, 64 * h + 64)
            k.ts(g_[:, hb], w[:, hb], mean[:, h:h + 1], ALU.subtract, var[:, h:h + 1], ALU.mult)
        tb = bankA[(i // 4) % 2]
        k.transpose(tb[:, (i % 4) * 128:(i % 4 + 1) * 128], g_, ident_f)
        if i % 4 == 3:
            j = i // 4
            ts_ = slice(j * 512, (j + 1) * 512)
            o = ob[j % 2]
            k.dma(bf_[j % 2], bon_scr[0, :, ts_], eng="pool")
            k.dma(bb_[j % 2], bon_scr[1, :, ts_], eng="pool")
            k.ts(o, tb, pv[:, 7:8], ALU.mult, pv[:, 8:9], ALU.add)
            k.tt(o, o, bf_[j % 2], ALU.add, eng="pool")
            k.tt(o, o, bb_[j % 2], ALU.add, eng="pool")
            k.tt(obo[j % 2], o, gT[:, ts_], ALU.mult)
            k.dma(yT[:, ts_], obo[j % 2], eng="sp")


GROUPS = [[0, 1, 2, 3], [4, 5, 6, 7]]


def emit_out(k, io, last):
    Lx = 8192
    NTx = Lx // 512
    ygath, wo, xsrc, xdst = io["ygath"], io["wo"], io["xsrc"], io["xdst"]
    ones_bf = k.sb("ones_bf", [128, 128], BF16)
    k.memset(ones_bf, 1.0)
    eps_t = k.sb("eps_t", [128, 1])
    k.memset(eps_t, 1e-6)
    postw_sb = k.sb("postw_sb", [128, 2])
    k.dma(postw_sb, io["postw"])
    snw_sb = k.sb("snw_sb", [128, 4])
    k.dma(snw_sb, io["snw"])
    prew_sb = k.sb("prew_sb", [128, 2])
    k.dma(prew_sb, io["prew"])
    w_bf = k.sb("w_bf", [128, 16, 256], BF16)
    wst = [k.sb("wst%d" % i, [128, 4, 256]) for i in range(2)]
    for i in range(4):
        st = wst[i % 2]
        k.dma(st, wo[i * 512:(i + 1) * 512, :].re("(c p) m -> p c m", p=128), eng="pool" if i % 2 else "sp")
        k.copy(w_bf[:, 4 * i:4 * i + 4, :], st, eng="pool")
    mixT = k.sb("mixT", [128, 2, Lx])
    ssrow = k.sb("ssrow", [1, Lx])
    ytile = [k.sb("ytile%d" % i, [128, 16, 512], BF16) for i in range(2)]
    yn = [k.sb("yn%d" % i, [128, 4, 512], BF16) for i in range(2)]
    sq = [k.sb("sq%d" % i, [128, 512], BF16) for i in range(2)]
    rs = k.sb("rs", [128, 512])
    tot = [k.sb("tot%d" % i, [128, 512]) for i in range(2)]
    xt = [k.sb("xt%d" % i, [128, 2, 512]) for i in range(2)]
    hb = [k.sb("hb%d" % i, [128, 2, 512], BF16) for i in range(2)]
    pg = k.ps("pg", [128, 512])
    pm = [k.ps("pm%d" % i, [128, 512]) for i in range(2)]
    pss = k.ps("pss", [128, 512])

    for j in range(NTx):
        ts_ = slice(j * 512, (j + 1) * 512)
        yt = ytile[j % 2]
        ytv = yt.re("p (g q) t -> p g q t", q=4)
        for r in range(8):
            k.dma(ytv[(r % 2) * 64:(r % 2) * 64 + 64, :, r // 2, :],
                  ygath[r * 256:(r + 1) * 256, ts_].re("(g i) t -> i g t", i=64),
                  eng="sp" if r % 2 == 0 else "pool")
        ynj = yn[j % 2]
        for grp in range(2):
            for ci, g in enumerate((2 * grp, 2 * grp + 1)):
                k.act(sq[ci], yt[:, g * 4, :], AF.Square)
                k.matmul(pg, ones_bf, sq[ci], start=(ci == 0), stop=(ci == 1))
            k.act(rs, pg, AF.Ln, scale=1.0 / 256, bias=eps_t)
            k.act(rs, rs, AF.Exp, scale=-0.5)
            for g in (2 * grp, 2 * grp + 1):
                k.stt(ynj[:, g, :], yt[:, g * 4, :], snw_sb[:, g:g + 1], rs, ALU.mult, ALU.mult)
        for nb in range(2):
            for c in range(16):
                rhs = ynj[:, c // 4, :] if c % 4 == 0 else yt[:, c, :]
                k.matmul(pm[nb], w_bf[:, c, nb * 128:(nb + 1) * 128], rhs, start=(c == 0), stop=(c == 15))
            k.copy(mixT[:, nb, ts_], pm[nb], eng="dve")
            k.act(sq[nb], pm[nb], AF.Square)
            k.matmul(pss, ones_bf, sq[nb], start=(nb == 0), stop=(nb == 1))
        k.copy(ssrow[0:1, ts_], pss[0:1, :], eng="dve")
    k.dma(io["ar1_in"], ssrow, eng="sp")
    k.collective("AllReduce", io["ar1_in"], io["ar1_out"], GROUPS, op=ALU.add)

    for j in range(NTx):
        ts_ = slice(j * 512, (j + 1) * 512)
        tt_ = tot[j % 2]
        k.dma(tt_, T(io["ar1_out"].buf, io["ar1_out"].ap[0:1, ts_].partition_broadcast(128)), eng="pool")
        k.act(tt_, tt_, AF.Ln, scale=1.0 / 1024, bias=eps_t)
        k.act(tt_, tt_, AF.Exp, scale=-0.5)
        x_ = xt[j % 2]
        k.dma(x_, xsrc[:, ts_].re("(nb p) t -> p nb t", p=128), eng="sp")
        for nb in range(2):
            k.stt(mixT[:, nb, ts_], mixT[:, nb, ts_], postw_sb[:, nb:nb + 1], tt_, ALU.mult, ALU.mult)
            k.tt(mixT[:, nb, ts_], mixT[:, nb, ts_], x_[:, nb, :], ALU.add, eng="pool")
        k.dma(xdst[:, ts_].re("(nb p) t -> p nb t", p=128), mixT[:, :, ts_], eng="sp")
        if not last:
            for nb in range(2):
                k.act(sq[nb], mixT[:, nb, ts_], AF.Square)
                k.matmul(pss, ones_bf, sq[nb], start=(nb == 0), stop=(nb == 1))
            k.copy(ssrow[0:1, ts_], pss[0:1, :], eng="dve")
    if last:
        return
    k.dma(io["ar2_in"], ssrow, eng="sp")
    k.collective("AllReduce", io["ar2_in"], io["ar2_out"], GROUPS, op=ALU.add)
    for j in range(NTx):
        ts_ = slice(j * 512, (j + 1) * 512)
        tt_ = tot[j % 2]
        k.dma(tt_, T(io["ar2_out"].buf, io["ar2_out"].ap[0:1, ts_].partition_broadcast(128)), eng="pool")
        k.act(tt_, tt_, AF.Ln, scale=1.0 / 1024, bias=eps_t)
        k.act(tt_, tt_, AF.Exp, scale=-0.5)
        h_ = hb[j % 2]
        for nb in range(2):
            k.stt(h_[:, nb, :], mixT[:, nb, ts_], prew_sb[:, nb:nb + 1], tt_, ALU.mult, ALU.mult)
        k.dma(io["hslice"][:, ts_].re("(nb p) t -> p nb t", p=128), h_, eng="sp")
    for r in range(4):
        k.collective("AllGather", io["hslice"][r * 64:(r + 1) * 64, :], io["hgath"][r * 256:(r + 1) * 256, :], GROUPS)


L = 8192
PROJ_SIZES = (512, 1024, 16, 1664, 512, 512, 512, 512, 512, 512, 256, 256, 512)
OFF = np.concatenate([[0], np.cumsum(PROJ_SIZES)]).astype(int)
(O_MZ, O_XBC, O_DT, O_RU, O_RG, O_DQ, O_DK, O_DV, O_DG, O_GQ, O_GK, O_GV, O_GG) = OFF[:13]

PERM = np.array([p + 32 if (p % 64) < 32 else p - 32 for p in range(128)])


def rope_tables():
    inv = (np.float32(10000.0) ** (-np.arange(32, dtype=np.float32) / np.float32(32))).astype(np.float32)
    t = np.arange(L)
    sign = np.where((np.arange(128) % 64) < 32, -1.0, 1.0).astype(np.float32)[:, None]
    fi = np.arange(128) % 32
    pos_d = np.broadcast_to(t.astype(np.float32)[None, :], (128, L))
    ang_d = (pos_d * inv[fi][:, None]).astype(np.float32)
    pos_g = np.where((np.arange(128) < 64)[:, None], (t // 64)[None, :], (t % 64)[None, :]).astype(np.float32)
    ang_g = (pos_g * inv[fi][:, None]).astype(np.float32)
    tabs = np.stack([np.cos(ang_d), np.sin(ang_d) * sign, np.cos(ang_g), np.sin(ang_g) * sign]).astype(np.float32)
    return np.ascontiguousarray(tabs)


def pvec(v):
    return np.ascontiguousarray(v.reshape(8, 128).T)


def prep_attn(inp, layer, xT_b, tabs):
    W = inp['w_in'][layer]
    maps = []
    for core in range(8):
        b, g = core // 4, core % 4
        sl = lambda o, n=128, gg=g: W[:, o + gg * n: o + (gg + 1) * n]
        dq, dk_, dg, dv = sl(O_DQ), sl(O_DK), sl(O_DG), sl(O_DV)
        gq, gg_ = sl(O_GQ), sl(O_GG)
        gk, gv = sl(O_GK, 128, g // 2), sl(O_GV, 128, g // 2)
        wfm = np.concatenate([dq, dq[:, PERM], dk_, dk_[:, PERM], dg, gq, gq[:, PERM], gk, gk[:, PERM], gg_], axis=1)
        wtm = np.concatenate([dv, gv], axis=1)
        vecs = np.zeros((128, 8), np.float32)
        qw, kw = inp['gqa_q_norm_w'][layer], inp['gqa_k_norm_w'][layer]
        vecs[:, 0] = qw; vecs[:, 1] = qw[PERM]; vecs[:, 2] = kw; vecs[:, 3] = kw[PERM]
        vecs[:, 4] = inp['diff_norm_w'][layer]
        lam = np.ascontiguousarray(np.broadcast_to(inp['diff_lambda'][layer].reshape(1, 256), (128, 256)))
        maps.append({"xT": xT_b[b], "pw": pvec(inp['pre_norm_w'][layer]),
                     "wfm": np.ascontiguousarray(wfm), "wtm": np.ascontiguousarray(wtm),
                     "tabs": tabs, "vecs": vecs, "lam": lam})
    return maps


def ssd_consts():
    j = np.arange(128)[:, None]; l = np.arange(128)[None, :]
    c = np.stack([(j <= l), (j >= l), (j > l), (j < l), (j == l)]).astype(np.float32)
    return np.ascontiguousarray(c.transpose(1, 0, 2))


def prep_ssd(inp, layer, xT_b):
    W = inp['w_in'][layer]
    cst = ssd_consts()
    maps = []
    for core in range(8):
        b, g = core // 4, core % 4
        grp = g // 2
        wfm = np.concatenate([W[:, O_MZ + g * 128:O_MZ + (g + 1) * 128],
                              W[:, O_XBC + g * 128:O_XBC + (g + 1) * 128],
                              W[:, O_XBC + 512 + grp * 128:O_XBC + 512 + (grp + 1) * 128],
                              W[:, O_XBC + 768 + grp * 128:O_XBC + 768 + (grp + 1) * 128]], axis=1)
        dcols = [O_DT + d * 8 + 2 * g + h for d in range(2) for h in range(2)]
        wdt = W[:, dcols]
        chans = [g * 128, 512 + grp * 128, 768 + grp * 128]
        cwl = inp['conv_w'][layer]; cbl = inp['conv_b'][layer]
        cw = np.stack([cwl[:, ch:ch + 128].T for ch in chans], axis=1)
        cb = np.stack([cbl[ch:ch + 128] for ch in chans], axis=1)
        v = np.zeros(16, np.float32)
        for d in range(2):
            for h in range(2):
                v[d * 2 + h] = inp['ssm_dt_bias'][layer][d, 2 * g + h]
                v[4 + d * 2 + h] = inp['ssm_a_log'][layer][d, 2 * g + h]
        v[8] = inp['ssm_d'][layer][2 * g]; v[9] = inp['ssm_d'][layer][2 * g + 1]
        maps.append({"xT": xT_b[b], "pw": pvec(inp['pre_norm_w'][layer]),
                     "wfm": np.ascontiguousarray(wfm), "wdt": np.ascontiguousarray(wdt),
                     "cw": np.ascontiguousarray(cw), "cb": np.ascontiguousarray(cb),
                     "ssmv": np.ascontiguousarray(np.broadcast_to(v[None], (128, 16))), "cst": cst})
    return maps


def rwkv_consts():
    j = np.arange(64)[:, None]; t = np.arange(64)[None, :]
    cm = np.zeros((64, 2, 704), np.float32)
    for d in range(2):
        strict = (j < t) if d == 0 else (j > t)
        incl = (j <= t) if d == 0 else (j >= t)
        blk = np.concatenate([strict, incl], axis=1).astype(np.float32)
        cm[:, d, 0:512] = np.tile(blk, (1, 4))
        nmask = ((t < j) if d == 0 else (t > j)).astype(np.float32)
        cm[:, d, 512:640] = np.tile(nmask, (1, 2))
        cm[:, d, 640:704] = np.eye(64, dtype=np.float32)
    c128 = np.zeros((128, 2, 128), np.float32)
    c128[0:64, 0, 0:64] = 1; c128[64:128, 0, 64:128] = 1
    c128[:, 1, :] = np.eye(128, dtype=np.float32)
    return cm, c128


O_R, O_K, O_V, O_WD, O_AD = O_RU, O_RU + 512, O_RU + 1024, O_RU + 1536, O_RU + 1600


def prep_rwkv(inp, layer, xT_b):
    W = inp['w_in'][layer]
    cm, c128 = rwkv_consts()
    maps = []
    for core in range(8):
        b, g = core // 4, core % 4
        gs = slice(g * 128, (g + 1) * 128)
        wfm = np.concatenate([W[:, O_R + g * 128:O_R + (g + 1) * 128], W[:, O_K + g * 128:O_K + (g + 1) * 128],
                              W[:, O_V + g * 128:O_V + (g + 1) * 128], W[:, O_WD:O_WD + 128],
                              W[:, O_RG + g * 128:O_RG + (g + 1) * 128]], axis=1)
        mul = inp['rwkv_mu'][layer]
        mu = np.zeros((128, 2, 4), np.float32)
        for d in range(2):
            mu[:, d, 0] = mul[d, 0 + g * 128:0 + (g + 1) * 128]
            mu[:, d, 1] = mul[d, 512 + g * 128:512 + (g + 1) * 128]
            mu[:, d, 2] = mul[d, 1024 + g * 128:1024 + (g + 1) * 128]
            mu[:, d, 3] = mul[d, 1536:1664]
        w2a2 = np.zeros((128, 2, 128), np.float32)
        for d in range(2):
            w2a2[0:64, d, :] = inp['rwkv_w2'][layer][d][:, gs]
            w2a2[64:128, d, :] = inp['rwkv_a2'][layer][:, gs]
        pv = np.zeros((128, 12), np.float32)
        pv[:, 0] = inp['rwkv_w0'][layer][0, gs]; pv[:, 1] = inp['rwkv_w0'][layer][1, gs]
        pv[:, 2] = inp['rwkv_a0'][layer][gs]; pv[:, 3] = inp['rwkv_k_k'][layer][gs]
        pv[:, 4] = inp['rwkv_k_a'][layer][gs]
        pv[:, 6] = inp['rwkv_r_k'][layer].reshape(-1)[gs]
        pv[:, 7] = inp['rwkv_ln_w'][layer][gs]; pv[:, 8] = inp['rwkv_ln_b'][layer][gs]
        maps.append({"xT": xT_b[b], "pw": pvec(inp['pre_norm_w'][layer]), "wfm": np.ascontiguousarray(wfm),
                     "mu": mu, "w2a2": w2a2, "pvd": pv, "cm": cm, "c128": c128})
    return maps


def build_fused(depth=2):
    nc = bass.Bass("TRN2", target_bir_lowering=False)
    k = KB(nc, arena=True)
    din = k.dram_in
    xT = din("xT", [1024, L])
    xs0 = din("xs0", [256, L])
    tabs = din("tabs", [4, 128, L])
    cst = din("cst", [128, 5, 128])
    cm = din("cm", [64, 2, 704])
    c128 = din("c128", [128, 2, 128])
    xo = k.dram_out("xo", [256, L])
    ycat = k.dram_scratch("ycat", [512, L], BF16)
    ygath = k.dram_scratch("ygath", [2048, L], BF16)
    hslice = k.dram_scratch("hslice", [256, L], BF16)
    hgath = k.dram_scratch("hgath", [1024, L], BF16)
    xres = k.dram_scratch("xres", [256, L])
    ar = [k.dram_scratch("ar%d" % i, [1, L]) for i in range(4)]
    wkv_scr = k.dram_scratch("wkv_scr", [2, L, 128])
    bon_scr = k.dram_scratch("bon_scr", [2, 128, L])
    for layer in range(depth):
        p = "L%d_" % layer
        lambda_init = 0.8 - 0.6 * math.exp(-0.3 * layer)
        src = {"xT": xT} if layer == 0 else {"hT": hgath}
        pw = din(p + "pw", [128, 8])
        io = dict(src, pw=pw, wfm=din(p + "s_wfm", [1024, 512]), wdt=din(p + "s_wdt", [1024, 4]),
                  cw=din(p + "s_cw", [128, 3, 5]), cb=din(p + "s_cb", [128, 3]), ssmv=din(p + "s_ssmv", [128, 16]),
                  cst=cst, y=ycat[0:128, :], ydt=BF16)
        emit_ssd(k, io)
        k.phase_reset()
        io = dict(src, pw=pw, wfm=din(p + "r_wfm", [1024, 640]), mu=din(p + "r_mu", [128, 2, 4]),
                  w2a2=din(p + "r_w2a2", [128, 2, 128]), pvd=din(p + "r_pvd", [128, 12]), cm=cm, c128=c128,
                  wkv_scr=wkv_scr, bon_scr=bon_scr, y=ycat[128:256, :], ydt=BF16)
        emit_rwkv(k, io)
        k.phase_reset()
        for r in range(4):
            k.collective("AllGather", ycat[r * 64:(r + 1) * 64, :], ygath[r * 256:(r + 1) * 256, :], GROUPS)
        io = dict(src, pw=pw, wfm=din(p + "a_wfm", [1024, 1280]), wtm=din(p + "a_wtm", [1024, 256]), tabs=tabs,
                  vecs=din(p + "a_vecs", [128, 8]), lam=din(p + "a_lam", [128, 256]),
                  y=[ycat[256:384, :], ycat[384:512, :]], ydt=BF16,
                  after_kind0=lambda: [k.collective("AllGather", ycat[r * 64:(r + 1) * 64, :],
                                                    ygath[r * 256:(r + 1) * 256, :], GROUPS) for r in (4, 5)])
        emit_attn(k, io, lambda_init)
        k.phase_reset()
        for r in range(6, 8):
            k.collective("AllGather", ycat[r * 64:(r + 1) * 64, :], ygath[r * 256:(r + 1) * 256, :], GROUPS)
        last = (layer == depth - 1)
        io = dict(ygath=ygath, wo=din(p + "o_wo", [2048, 256]), xsrc=(xs0 if layer == 0 else xres),
                  xdst=(xo if last else xres), postw=din(p + "o_postw", [128, 2]), snw=din(p + "o_snw", [128, 4]),
                  prew=din(p + "o_prew", [128, 2]), ar1_in=ar[0], ar1_out=ar[1], ar2_in=ar[2], ar2_out=ar[3],
                  hslice=hslice, hgath=hgath)
        emit_out(k, io, last)
        k.phase_reset()
    stats = k.finish()
    return nc, stats


def prep_fused(inp, depth=2):
    x = np.ascontiguousarray(inp["x"], dtype=np.float32)
    xT_b = [np.ascontiguousarray(x[b].T) for b in range(2)]
    tabs = rope_tables()
    cst = ssd_consts()
    cm, c128 = rwkv_consts()
    maps = [dict() for _ in range(8)]
    perm = np.array([kind * 512 + g * 128 + i for g in range(4) for kind in range(4) for i in range(128)])
    for core in range(8):
        b, g = core // 4, core % 4
        m = maps[core]
        m["xT"] = xT_b[b]
        m["xs0"] = np.ascontiguousarray(xT_b[b][g * 256:(g + 1) * 256])
        m["tabs"] = tabs; m["cst"] = cst; m["cm"] = cm; m["c128"] = c128
    for layer in range(depth):
        p = "L%d_" % layer
        ms = prep_ssd(inp, layer, xT_b); mr = prep_rwkv(inp, layer, xT_b); ma = prep_attn(inp, layer, xT_b, tabs)
        for core in range(8):
            b, g = core // 4, core % 4
            m = maps[core]
            m[p + "pw"] = ms[core]["pw"]
            for nm in ("wfm", "wdt", "cw", "cb", "ssmv"):
                m[p + "s_" + nm] = ms[core][nm]
            for nm in ("wfm", "mu", "w2a2", "pvd"):
                m[p + "r_" + nm] = mr[core][nm]
            for nm in ("wfm", "wtm", "vecs", "lam"):
                m[p + "a_" + nm] = ma[core][nm]
            ns = slice(g * 256, (g + 1) * 256)
            m[p + "o_wo"] = np.ascontiguousarray(inp["w_out"][layer][perm][:, ns])
            m[p + "o_postw"] = np.ascontiguousarray(inp["post_norm_w"][layer][ns].reshape(2, 128).T)
            m[p + "o_snw"] = np.ascontiguousarray(inp["ssm_norm_w"][layer].reshape(4, 128).T)
            nxt = inp["pre_norm_w"][min(layer + 1, depth - 1)]
            m[p + "o_prew"] = np.ascontiguousarray(nxt[ns].reshape(2, 128).T)
    return maps


from concourse.bass_utils import run_bass_kernel_spmd


def kernel(**inp):
    inp = {k_: np.asarray(v) for k_, v in inp.items()}
    depth = inp["w_in"].shape[0]
    nc, _ = build_fused(depth)
    maps = prep_fused(inp, depth)
    res = run_bass_kernel_spmd(nc, maps, core_ids=list(range(8))).results
    out = np.empty((2, L, 1024), np.float32)
    for core in range(8):
        b, g = core // 4, core % 4
        out[b, :, g * 256:(g + 1) * 256] = res[core]["xo"].T
    return out
```

```python
import math
import numpy as np
import concourse.bass as bass
import concourse.mybir as mybir

F32 = mybir.dt.float32
BF16 = mybir.dt.bfloat16
ALU = mybir.AluOpType
AF = mybir.ActivationFunctionType
AX = mybir.AxisListType

SEM_CHUNK = 30000


class Buf:
    __slots__ = ("name", "last_w", "readers", "dma_sem", "dma_cnt", "is_dram", "psum", "wlist", "inc_val")

    def __init__(self, name, is_dram=False, psum=False):
        self.psum = psum
        self.wlist = {}
        self.inc_val = 16
        self.name = name
        self.last_w = None
        self.readers = []
        self.dma_sem = None
        self.dma_cnt = 0
        self.is_dram = is_dram


class T:
    __slots__ = ("buf", "ap")

    def __init__(self, buf, ap):
        self.buf = buf
        self.ap = ap

    def __getitem__(self, idx):
        return T(self.buf, self.ap[idx])

    def re(self, pattern, **kw):
        return T(self.buf, self.ap.rearrange(pattern, **kw))

    def sub(self, buf, idx=None):
        return T(buf, self.ap if idx is None else self.ap[idx])


class Op:
    __slots__ = ("eng", "fn", "reads", "writes", "is_dma", "deps", "inc_idx", "dma_buf",
                 "dma_wait", "gi")

    def __init__(self, eng, fn, reads, writes, is_dma=False, dma_buf=None):
        self.eng = eng
        self.fn = fn
        self.reads = reads
        self.writes = writes
        self.is_dma = is_dma
        self.deps = []
        self.inc_idx = None
        self.dma_buf = dma_buf
        self.dma_wait = []
        self.gi = None


class KB:
    ENGS = ("pe", "act", "dve", "pool", "sp")

    ARENA_WORDS = 53100

    def __init__(self, nc, arena=False):
        self.nc = nc
        self.arena = None
        if arena:
            self.arena = nc.alloc_sbuf_tensor("arena", [128, self.ARENA_WORDS], F32).ap()
            self.a_off = 0
            self.a_peak = 0
            self.banks = [T(Buf("bank%d" % i, psum=True),
                            nc.alloc_psum_tensor("gbank%d" % i, [128, 512], F32).ap()) for i in range(8)]
            self.b_next = 0
        self.ops = []
        self.e = {"pe": nc.tensor, "act": nc.scalar, "dve": nc.vector, "pool": nc.gpsimd,
                  "sp": nc.sync}
        self._n = 0

    def sb(self, name, shape, dtype=F32):
        if self.arena is None:
            h = self.nc.alloc_sbuf_tensor(name, list(shape), dtype)
            return T(Buf(name), h.ap())
        shape = list(shape)
        esz = 2 if dtype == BF16 else 4
        n = 1
        for d in shape[1:]:
            n *= d
        nbytes = (n * esz + 31) // 32 * 32
        nw = nbytes // 4
        off = self.a_off
        assert off + nw <= self.ARENA_WORDS, "arena overflow at %s: %d + %d" % (name, off, nw)
        self.a_off = off + nw
        self.a_peak = max(self.a_peak, self.a_off)
        ap = self.arena[0:shape[0], off:off + (n * esz + 3) // 4]
        if dtype == BF16:
            ap = ap.bitcast(BF16)
            if (n * esz) % 4:
                ap = ap[:, 0:n]
        if len(shape) == 3:
            ap = ap.rearrange("p (a b) -> p a b", a=shape[1])
        elif len(shape) == 4:
            ap = ap.rearrange("p (a b c) -> p a b c", a=shape[1], b=shape[2])
        return T(Buf(name), ap)

    def ps(self, name, shape, dtype=F32):
        if self.arena is None:
            h = self.nc.alloc_psum_tensor(name, list(shape), dtype)
            return T(Buf(name, psum=True), h.ap())
        bk = self.banks[self.b_next]
        self.b_next += 1
        if dtype == BF16:
            return T(bk.buf, bk.ap.bitcast(BF16))
        return bk

    def phase_reset(self):
        self._rec("bar", None, [], [])
        self.a_off = 0
        self.b_next = 0

    def dram_in(self, name, shape, dtype=F32):
        h = self.nc.dram_tensor(name, list(shape), dtype, kind="ExternalInput")
        return T(Buf(name, True), h.ap())

    def dram_out(self, name, shape, dtype=F32):
        h = self.nc.dram_tensor(name, list(shape), dtype, kind="ExternalOutput")
        return T(Buf(name, True), h.ap())

    def dram_scratch(self, name, shape, dtype=F32):
        h = self.nc.dram_tensor(name, list(shape), dtype)
        return T(Buf(name, True), h.ap())

    def buf(self, name):
        self._n += 1
        return Buf("%s_%d" % (name, self._n))

    def _rec(self, eng, fn, reads, writes, is_dma=False, dma_buf=None):
        rb = []
        for t in reads:
            if t is None or isinstance(t, (int, float)):
                continue
            b = t.buf if isinstance(t, T) else t
            if b not in rb:
                rb.append(b)
        wb = []
        for t in writes:
            b = t.buf if isinstance(t, T) else t
            if b not in wb:
                wb.append(b)
        op = Op(eng, fn, rb, wb, is_dma, dma_buf)
        if getattr(self, "_defer", None) is not None:
            self._defer.append(op)
            return op
        op.gi = len(self.ops)
        self.ops.append(op)
        return op

    def begin_defer(self):
        self._defer = []

    def end_defer(self):
        lst = self._defer
        self._defer = None
        return lst

    def splice(self, lst, n):
        for _ in range(min(n, len(lst))):
            op = lst.pop(0)
            op.gi = len(self.ops)
            self.ops.append(op)

    @staticmethod
    def _a(x):
        return x.ap if isinstance(x, T) else x

    def dma(self, out, in_, eng="sp", **kw):
        o, i = self._a(out), self._a(in_)
        sbuf_side = out.buf if not out.buf.is_dram else in_.buf
        return self._rec(eng, lambda E: E.dma_start(out=o, in_=i, **kw), [in_], [out],
                         is_dma=True, dma_buf=sbuf_side)

    def matmul(self, out, lhsT, rhs, start=True, stop=True, extra_reads=(), **kw):
        o, l, r = self._a(out), self._a(lhsT), self._a(rhs)
        reads = [lhsT, rhs] + list(extra_reads)
        if not start:
            reads.append(out)
        return self._rec("pe", lambda E: E.matmul(o, l, r, start=start, stop=stop, **kw),
                         reads, [out])

    def transpose(self, out, in_, ident):
        o, i, d = self._a(out), self._a(in_), self._a(ident)
        return self._rec("pe", lambda E: E.transpose(o, i, d), [in_, ident], [out])

    def act(self, out, in_, func, bias=None, scale=None, accum_out=None, eng="act"):
        o, i = self._a(out), self._a(in_)
        kw = {}
        reads = [in_]
        writes = [out]
        if bias is not None:
            kw["bias"] = self._a(bias)
            reads.append(bias)
        if scale is not None:
            kw["scale"] = self._a(scale)
            reads.append(scale)
        if accum_out is not None:
            kw["accum_out"] = self._a(accum_out)
            writes.append(accum_out)
        return self._rec(eng, lambda E: E.activation(o, i, func, **kw), reads, writes)

    def tt(self, out, in0, in1, op, eng="dve"):
        o, a, b = self._a(out), self._a(in0), self._a(in1)
        return self._rec(eng, lambda E: E.tensor_tensor(o, a, b, op), [in0, in1], [out])

    def ts(self, out, in0, s1, op0, s2=None, op1=None, accum_out=None, eng="dve"):
        o, a = self._a(out), self._a(in0)
        s1a, s2a = self._a(s1), self._a(s2)
        kw = {}
        writes = [out]
        if op1 is not None:
            kw["op1"] = op1
        if accum_out is not None:
            kw["accum_out"] = self._a(accum_out)
            writes.append(accum_out)
        return self._rec(eng, lambda E: E.tensor_scalar(o, a, s1a, s2a, op0, **kw),
                         [in0, s1, s2], writes)

    def stt(self, out, in0, scalar, in1, op0, op1, eng="dve"):
        o, a, s, b = self._a(out), self._a(in0), self._a(scalar), self._a(in1)
        return self._rec(eng, lambda E: E.scalar_tensor_tensor(o, a, s, b, op0, op1),
                         [in0, scalar, in1], [out])

    def copy(self, out, in_, eng="dve"):
        o, i = self._a(out), self._a(in_)
        if eng == "act":
            return self._rec(eng, lambda E: E.copy(o, i), [in_], [out])
        return self._rec(eng, lambda E: E.tensor_copy(o, i), [in_], [out])

    def memset(self, out, val, eng="pool"):
        o = self._a(out)
        return self._rec(eng, lambda E: E.memset(o, val), [], [out])

    def recip(self, out, in_):
        o, i = self._a(out), self._a(in_)
        return self._rec("dve", lambda E: E.reciprocal(o, i), [in_], [out])

    def reduce(self, out, in_, op, axis=AX.X, eng="dve"):
        o, i = self._a(out), self._a(in_)
        return self._rec(eng, lambda E: E.tensor_reduce(o, i, axis, op), [in_], [out])

    def affine_select(self, out, in_, pattern, compare_op, fill, base=0, channel_multiplier=0):
        o, i = self._a(out), self._a(in_)
        return self._rec("pool", lambda E: E.affine_select(
            o, i, pattern, compare_op, fill, base=base, channel_multiplier=channel_multiplier),
            [in_], [out])

    def iota(self, out, pattern, base=0, channel_multiplier=0, **kw):
        o = self._a(out)
        return self._rec("pool", lambda E: E.iota(o, pattern, base=base,
                                                   channel_multiplier=channel_multiplier, **kw),
                         [], [out])

    def collective(self, kind, in_, out, groups, op=None):
        i, o = self._a(in_), self._a(out)
        alu = ALU.bypass if op is None else op
        semb = Buf("cc_%d" % len(self.ops))
        semb.inc_val = 1
        return self._rec("pool", lambda E: E.collective_compute(kind, alu, groups, [i], [o]),
                         [in_], [out], is_dma=True, dma_buf=semb)

    def generic(self, eng, fn, reads, writes):
        return self._rec(eng, fn, reads, writes)

    def finish(self, final_wait_outputs=True):
        nc = self.nc
        ops = self.ops
        last_on = {}
        last_dma = {}
        bar_deps = {e: None for e in self.ENGS}
        for op in ops:
            if op.eng == "bar":
                allp = set(last_on.values()) | set(last_dma.values())
                for e in self.ENGS:
                    bar_deps[e] = set(allp) | (bar_deps[e] or set())
                continue
            deps = set()
            for b in op.reads:
                if b.last_w is not None:
                    deps.add(b.last_w)
                if b.is_dram:
                    for w_ in b.wlist.values():
                        deps.add(w_)
                if b.psum:
                    for r in b.readers:
                        if ops[r].eng != op.eng:
                            deps.add(r)
            for b in op.writes:
                if b.last_w is not None:
                    deps.add(b.last_w)
                for r in b.readers:
                    deps.add(r)
            deps.discard(op.gi)
            keep = []
            if bar_deps[op.eng] is not None:
                keep.extend(sorted(bar_deps[op.eng]))
                bar_deps[op.eng] = None
            last_on[op.eng] = op.gi
            if op.is_dma:
                last_dma[id(op.dma_buf)] = op.gi
            for d in deps:
                dop = ops[d]
                if dop.is_dma:
                    keep.append(d)
                    continue
                if dop.eng == op.eng:
                    if op.eng in ("pe", "sp"):
                        continue
                    if op.is_dma:
                        keep.append(d)
                        continue
                    raw = any(b.last_w == d for b in op.reads)
                    if not raw:
                        continue
                keep.append(d)
            op.deps = keep
            for b in op.reads:
                b.readers.append(op.gi)
            for b in op.writes:
                b.last_w = op.gi
                b.readers = []
                if b.is_dram and op.is_dma:
                    b.wlist[id(op.dma_buf)] = op.gi
        needed = set()
        for op in ops:
            for d in op.deps:
                needed.add(d)
        final_dma = [op for op in ops if op.is_dma and any(b.is_dram for b in op.writes)]
        ecount = {e: 0 for e in self.ENGS}
        phase = 0
        nslot = 0
        slot_total = []
        slot_of = {}
        ncc = 0
        abs_cnt = {}
        sem_key = {}
        for op in ops:
            if op.eng == "bar":
                phase += 1
                nslot = 0
                continue
            if op.is_dma:
                b = op.dma_buf
                if b.inc_val == 1:
                    if id(b) not in sem_key:
                        sem_key[id(b)] = ("c", ncc)
                        ncc += 1
                        b.dma_cnt = 0
                    b.dma_cnt += 1
                    abs_cnt[op.gi] = b.dma_cnt
                else:
                    ps_ = slot_of.get(id(b))
                    if ps_ is None or ps_[0] != phase:
                        slot_of[id(b)] = (phase, nslot)
                        if nslot >= len(slot_total):
                            slot_total.append(0)
                        sem_key[id(b)] = ("p", nslot)
                        nslot += 1
                    sl = slot_of[id(b)][1]
                    slot_total[sl] += 1
                    abs_cnt[op.gi] = slot_total[sl]
                op.inc_idx = abs_cnt[op.gi]
            elif op.gi in needed:
                ecount[op.eng] += 1
                op.inc_idx = ecount[op.eng]
        import contextlib
        self._stack = contextlib.ExitStack()
        self._sems = {}
        nsem = 0
        for e in self.ENGS:
            n = max((ecount[e] + SEM_CHUNK - 1) // SEM_CHUNK, 1)
            self._sems[e] = [self._stack.enter_context(nc.semaphore("s_%s_%d" % (e, k))) for k in range(n)]
            nsem += n
        dsem = {}
        for i in range(len(slot_total)):
            dsem[("p", i)] = self._stack.enter_context(nc.semaphore("d_%d" % i))
        for i in range(ncc):
            dsem[("c", i)] = self._stack.enter_context(nc.semaphore("c_%d" % i))
        nsem += len(dsem)
        self.nsem = nsem
        waited = {}
        last_abs = {}
        cur_key = {}
        op_key = {}
        phase = 0
        nslot = 0
        slot_of2 = {}
        for op in ops:
            if op.eng == "bar":
                phase += 1
                nslot = 0
                continue
            if op.is_dma:
                b = op.dma_buf
                if b.inc_val == 1:
                    op_key[op.gi] = sem_key[id(b)]
                else:
                    ps_ = slot_of2.get(id(b))
                    if ps_ is None or ps_[0] != phase:
                        slot_of2[id(b)] = (phase, nslot)
                        nslot += 1
                    op_key[op.gi] = ("p", slot_of2[id(b)][1])
        plan = {e: [] for e in self.ENGS}
        for op in ops:
            if op.eng == "bar":
                continue
            waits = {}
            for d in op.deps:
                dop = ops[d]
                if dop.is_dma:
                    b = dop.dma_buf
                    key = ("d",) + cur_key[id(b)]
                    sem = dsem[cur_key[id(b)]]
                    val = b.inc_val * last_abs[id(b)]
                else:
                    idx = dop.inc_idx - 1
                    ch = idx // SEM_CHUNK
                    val = idx % SEM_CHUNK + 1
                    key = ("e", dop.eng, ch)
                    sem = self._sems[dop.eng][ch]
                    for c2 in range(ch):
                        waited[(op.eng, ("e", dop.eng, c2))] = SEM_CHUNK
                cur = waited.get((op.eng, key), 0)
                if val > cur:
                    waited[(op.eng, key)] = val
                    if key not in waits or waits[key][1] < val:
                        waits[key] = (sem, val)
            plan[op.eng].append((op, list(waits.values())))
            if op.is_dma:
                last_abs[id(op.dma_buf)] = abs_cnt[op.gi]
                cur_key[id(op.dma_buf)] = op_key[op.gi]
        self.stats = {e: len(plan[e]) for e in self.ENGS}
        self.stats["nsem"] = nsem
        self.stats["waits"] = sum(len(w) for e in self.ENGS for _, w in plan[e])
        if self.arena is not None:
            self.stats["arena_peak_words"] = self.a_peak
        for e in self.ENGS:
            E = self.e[e]
            for op, waits in plan[e]:
                for sem, val in waits:
                    E.wait_ge(sem, val)
                ins = op.fn(E)
                if op.is_dma:
                    ins.then_inc(dsem[op_key[op.gi]], op.dma_buf.inc_val)
                elif op.inc_idx is not None:
                    idx = op.inc_idx - 1
                    ins.then_inc(self._sems[e][idx // SEM_CHUNK], 1)
        if final_wait_outputs:
            fin = {}
            for op in final_dma:
                kk = op_key[op.gi]
                v = op.dma_buf.inc_val * abs_cnt[op.gi]
                if kk not in fin or fin[kk] < v:
                    fin[kk] = v
            for kk, v in fin.items():
                nc.sync.wait_ge(dsem[kk], v)
        return self.stats


L = 8192
NT = L // 512
NCH = L // 128
NORM_EPS = 1e-6
GN_EPS = 64e-5
C0 = math.exp(-0.5)


def load_hT(k, hT_tile, hsrc, j):
    half, jj = j // 8, j % 8
    v = hT_tile.re("p (g q) t -> p g q t", q=2)
    for nb in range(2):
        k.dma(v[:, :, nb, :], hsrc[half][nb][:, jj * 512:(jj + 1) * 512].re("(g i) t -> i g t", i=128),
              eng="sp" if nb == 0 else "pool")


ATT_KNOBS = {}


def emit_attn(k, io, lambda_init):
    xT, hsrc, pw, wfm, wtm, tabs, vecs, lam = (io.get("xT"), io.get("hT"), io["pw"], io["wfm"], io["wtm"],
                                                io["tabs"], io["vecs"], io["lam"])
    ydst, ydt = io.get("y"), io["ydt"]

    ones_bf = k.sb("ones_bf", [128, 128], BF16)
    k.memset(ones_bf, 1.0)
    eps_t = k.sb("eps_t", [128, 1])
    k.memset(eps_t, NORM_EPS)
    pw_sb = k.sb("pw_sb", [128, 8])
    k.dma(pw_sb, pw)
    vec_sb = k.sb("vec_sb", [128, 8])
    k.dma(vec_sb, vecs)
    lam_sb = k.sb("lam_sb", [128, 256])
    k.dma(lam_sb, lam)
    w_fm = k.sb("w_fm", [128, 8, 1280], BF16)
    w_tm = k.sb("w_tm", [128, 8, 256], BF16)
    wst = [k.sb("wst%d" % i, [128, 8, 256]) for i in range(2)]
    for i in range(6):
        st = wst[i % 2]
        if i < 5:
            k.dma(st, wfm[:, i * 256:(i + 1) * 256].re("(c p) m -> p c m", p=128),
                  eng="pool" if i % 2 else "sp")
            k.copy(w_fm[:, :, i * 256:(i + 1) * 256], st, eng="pool")
        else:
            k.dma(st, wtm.re("(c p) m -> p c m", p=128), eng="pool")
            k.copy(w_tm, st, eng="pool")

    lt = k.sb("lam_t", [128, 128])
    s12 = k.sb("lam_s", [128, 2])
    k.tt(lt[:, 0:64], lam_sb[:, 0:64], lam_sb[:, 64:128], ALU.mult)
    k.tt(lt[:, 64:128], lam_sb[:, 128:192], lam_sb[:, 192:256], ALU.mult)
    k.reduce(s12[:, 0:1], lt[:, 0:64], ALU.add)
    k.reduce(s12[:, 1:2], lt[:, 64:128], ALU.add)
    e12 = k.sb("lam_e", [128, 2])
    k.act(e12, s12, AF.Exp)
    neg_lam = k.sb("neg_lam", [128, 1])
    k.stt(neg_lam, e12[:, 1:2], -float(lambda_init), e12[:, 0:1], ALU.add, ALU.subtract)
    wn_s = k.sb("wn_s", [128, 1])
    k.ts(wn_s, vec_sb[:, 4:5], 1.0 - float(lambda_init), ALU.mult)

    xbuf = [k.sb("xbuf%d" % i, [128, 8, 512]) for i in range(2)]
    hT = [k.sb("hT%d" % i, [128, 8, 512], BF16) for i in range(2)]
    sq = [k.sb("sq%d" % i, [128, 512], BF16) for i in range(3)]
    rstd = [k.sb("rstd%d" % i, [128, 512]) for i in range(2)]
    tab = [k.sb("tab%d" % i, [128, 2, 512]) for i in range(2)]
    tmp = [k.sb("tmp%d" % i, [128, 512]) for i in range(4)]
    qT = k.sb("qT", [128, L], BF16)
    kT = k.sb("kT", [128, L], BF16)
    gT = k.sb("gT", [128, L], BF16)
    v_tm = k.sb("v_tm", [128, 64, 128], BF16)
    Pb = [[k.sb("Pb%d_%d" % (m, i), [128, 512], BF16) for i in range(3)] for m in range(2)]
    accD = [k.sb("accD%d" % m, [128, 512]) for m in range(2)]
    accP = [k.sb("accP%d" % m, [128, 512]) for m in range(2)]
    ones_f = k.sb("ones_f", [128, 128])
    k.memset(ones_f, 1.0)
    ost = [k.sb("ost%d" % i, [128, 512], ydt) for i in range(2)]
    fin = [k.sb("fin%d" % i, [128, 512]) for i in range(5)]
    bank = [k.ps("bank%d" % i, [128, 512]) for i in range(8)]

    for kind in range(2):
        wc = kind * 5
        def stage1(j):
            k.dma(tab[j % 2], tabs[2 * kind:2 * kind + 2, :, j * 512:(j + 1) * 512]
                  .re("a p t -> p a t"), eng="pool")
            if hsrc is not None:
                load_hT(k, hT[j % 2], hsrc, j)
                return
            xt = xbuf[j % 2]
            k.dma(xt, xT[:, j * 512:(j + 1) * 512].re("(c p) t -> p c t", p=128),
                  eng="sp")
            for c in range(8):
                s = sq[c % 3]
                k.act(s, xt[:, c, :], AF.Square)
                k.matmul(bank[0], ones_bf, s, start=(c == 0), stop=(c == 7))
            r = rstd[j % 2]
            k.act(r, bank[0], AF.Ln, scale=1.0 / 1024, bias=eps_t)
            k.act(r, r, AF.Exp, scale=-0.5)
            for c in range(8):
                k.stt(hT[j % 2][:, c, :], xt[:, c, :], pw_sb[:, c:c + 1], r, ALU.mult, ALU.mult)

        def proj_fm(ps, grp, j):
            h = hT[j % 2]
            for c in range(8):
                k.matmul(ps, w_fm[:, c, grp * 128:(grp + 1) * 128], h[:, c, :],
                         start=(c == 0), stop=(c == 7))

        def stage2(j):
            h = hT[j % 2]
            tb = tab[j % 2]
            ts_ = slice(j * 512, (j + 1) * 512)
            for which, dst in ((0, qT), (1, kT)):
                pa, pb = bank[1 + 2 * which], bank[2 + 2 * which]
                proj_fm(pa, wc + 2 * which, j)
                proj_fm(pb, wc + 2 * which + 1, j)
                t1, t2 = tmp[2 * which], tmp[2 * which + 1]
                if kind == 0:
                    k.tt(t1, pa, tb[:, 0, :], ALU.mult)
                    k.tt(t2, pb, tb[:, 1, :], ALU.mult)
                    k.tt(dst[:, ts_], t1, t2, ALU.add, eng="pool")
                else:
                    s = sq[which]
                    k.act(s, pa, AF.Square)
                    k.matmul(bank[7], ones_bf, s)
                    rr = fin[which]
                    k.act(rr, bank[7], AF.Ln, scale=1.0 / 128, bias=eps_t)
                    k.act(rr, rr, AF.Exp, scale=-0.5)
                    k.stt(t1, pa, vec_sb[:, 2 * which:2 * which + 1], tb[:, 0, :], ALU.mult, ALU.mult)
                    k.stt(t2, pb, vec_sb[:, 2 * which + 1:2 * which + 2], tb[:, 1, :], ALU.mult, ALU.mult)
                    k.tt(t1, t1, t2, ALU.add, eng="pool")
                    k.tt(dst[:, ts_], t1, rr, ALU.mult, eng="pool")
            proj_fm(bank[5], wc + 4, j)
            k.act(gT[:, ts_], bank[5], AF.Silu)
            pv = bank[6]
            for s4 in range(4):
                for c in range(8):
                    k.matmul(pv[:, s4 * 128:(s4 + 1) * 128], h[:, c, s4 * 128:(s4 + 1) * 128],
                             w_tm[:, c, kind * 128:(kind + 1) * 128],
                             start=(c == 0), stop=(c == 7))
            k.copy(v_tm[:, j * 4:(j + 1) * 4, :], pv.re("p (s e) -> p s e", s=4), eng="dve")

        for j in range(NT + 1):
            if j < NT:
                stage1(j)
            if j >= 1:
                stage2(j - 1)

        nm = 2 if kind == 0 else 1
        dk = 64 if kind == 0 else 128
        scale = dk ** -0.5
        O = [bank[0], bank[1]][:nm]
        Sb = [[bank[2], bank[3]], [bank[4], bank[5]]]
        misc = bank[7]
        pairs = [(qb, kc) for qb in range(16) for kc in range(64)]
        LA = 2
        npair = len(pairs)

        def finalize(qb):
            qs = slice(qb * 512, (qb + 1) * 512)
            o = []
            for m in range(nm):
                k.matmul(misc, ones_f, accD[m], start=True, stop=False)
                k.matmul(misc, ones_f, accP[m], start=False, stop=True)
                r = fin[m]
                k.act(r, misc, AF.Ln)
                k.act(r, r, AF.Exp, scale=-1.0)
                om = fin[2 + m]
                k.tt(om, O[m], r, ALU.mult)
                o.append(om)
            y = ost[qb % 2]
            if kind == 0:
                od = fin[4]
                k.stt(od, o[1], neg_lam, o[0], ALU.mult, ALU.add)
                s = sq[0]
                k.act(s, od, AF.Square)
                k.matmul(misc, ones_bf, s)
                rr = fin[0]
                k.act(rr, misc, AF.Ln, scale=1.0 / 128, bias=eps_t)
                k.act(rr, rr, AF.Exp, scale=-0.5)
                k.stt(od, od, wn_s, rr, ALU.mult, ALU.mult)
                k.tt(y, od, gT[:, qs], ALU.mult, eng="pool")
            else:
                k.tt(y, o[0], gT[:, qs], ALU.mult, eng="pool")
            if io.get("yfn") is not None:
                k.dma(io["yfn"](kind, qb), y, eng="sp")
            else:
                k.dma(ydst[kind][:, qs], y, eng="sp")

        for idx in range(npair + LA):
            if idx < npair:
                qb, kc = pairs[idx]
                for m in range(nm):
                    k.matmul(Sb[m][idx % 2], kT[m * dk:(m + 1) * dk, kc * 128:(kc + 1) * 128],
                             qT[m * dk:(m + 1) * dk, qb * 512:(qb + 1) * 512])
                for m in range(nm):
                    k.act(Pb[m][idx % 3], Sb[m][idx % 2], AF.Exp, scale=scale)
            i2 = idx - LA
            if i2 >= 0:
                qb, kc = pairs[i2]
                for m in range(nm):
                    P = Pb[m][i2 % 3]
                    k.matmul(O[m], v_tm[:, kc, :], P, start=(kc == 0), stop=(kc == 63))
                    PMOD = ATT_KNOBS.get('pmod', 1 << 30)
                    on_pool = (kc % PMOD == PMOD - 1)
                    acc, eng = (accP[m], "pool") if on_pool else (accD[m], "dve")
                    first = (kc == PMOD - 1) if on_pool else (kc == 0)
                    if first:
                        k.copy(acc, P, eng=eng)
                    else:
                        k.tt(acc, acc, P, ALU.add, eng=eng)
                if kc == 63:
                    finalize(qb)
                    if qb % 8 == 7 and io.get("y_done") is not None:
                        io["y_done"](kind, qb // 8)


def emit_ssd(k, io):
    stop = 99
    xT, hsrc, pw, wfm, wdt, cw, cb, ssmv, cst = (io.get("xT"), io.get("hT"), io["pw"], io["wfm"], io["wdt"],
                                                 io["cw"], io["cb"], io["ssmv"], io["cst"])
    yT, ydt = io.get("y"), io["ydt"]

    ones_bf = k.sb("ones_bf", [128, 128], BF16)
    k.memset(ones_bf, 1.0)
    ones_f = k.sb("ones_f", [128, 128])
    k.memset(ones_f, 1.0)
    eps_t = k.sb("eps_t", [128, 1])
    k.memset(eps_t, NORM_EPS)
    one_t = k.sb("one_t", [128, 1])
    k.memset(one_t, 1.0)
    cst_sb = k.sb("cst_sb", [128, 5, 128])
    k.dma(cst_sb, cst)
    tri = [cst_sb[:, 0, :], cst_sb[:, 1, :]]
    smask = [cst_sb[:, 2, :], cst_sb[:, 3, :]]
    ident_f = cst_sb[:, 4, :]
    ident_bf = k.sb("ident_bf", [128, 128], BF16)
    k.copy(ident_bf, ident_f, eng="dve")
    pw_sb = k.sb("pw_sb", [128, 8])
    k.dma(pw_sb, pw)
    cw_sb = k.sb("cw_sb", [128, 3, 5])
    k.dma(cw_sb, cw)
    cb_sb = k.sb("cb_sb", [128, 3])
    k.dma(cb_sb, cb)
    sv = k.sb("sv", [128, 16])
    k.dma(sv, ssmv)
    w_fm = k.sb("w_fm", [128, 8, 512], BF16)
    w_dt = k.sb("w_dt", [128, 8, 4], BF16)
    wst2 = k.sb("wst2", [128, 8, 4])
    k.dma(wst2, wdt.re("(c p) m -> p c m", p=128), eng="pool")
    k.copy(w_dt, wst2, eng="pool")

    xbuf = [k.sb("xbuf%d" % i, [128, 8, 512]) for i in range(1)]
    k.dma(xbuf[0], wfm.re("(c p) m -> p c m", p=128))
    k.copy(w_fm, xbuf[0], eng="pool")
    hT = [k.sb("hT%d" % i, [128, 8, 512], BF16) for i in range(2)]
    sq = [k.sb("sq%d" % i, [128, 512], BF16) for i in range(3)]
    rstd = [k.sb("rstd%d" % i, [128, 512]) for i in range(1)] * 2
    upad = [k.sb("upad%d" % i, [128, 3, 516]) for i in range(3)]
    acc = [k.sb("acc%d" % i, [128, 512]) for i in range(3)]
    zsT = k.sb("zsT", [128, L], BF16)
    xcT = k.sb("xcT", [128, L], BF16)
    BT = k.sb("BT", [128, L], BF16)
    CT = k.sb("CT", [128, L], BF16)
    dst3 = [xcT, BT, CT]
    dtraw = k.sb("dtraw", [128, 4, NCH])
    bank = [k.ps("bank%d" % i, [128, 512]) for i in range(7)]
    tbank = k.ps("tbank", [128, 1024], BF16)

    k.memset(upad[0][:, :, 0:2], 0.0)

    def stage1(j):
        if hsrc is not None:
            load_hT(k, hT[j % 2], hsrc, j)
            return
        xt = xbuf[0]
        k.dma(xt, xT[:, j * 512:(j + 1) * 512].re("(c p) t -> p c t", p=128), eng="sp")
        for c in range(8):
            s = sq[c % 3]
            k.act(s, xt[:, c, :], AF.Square)
            k.matmul(bank[0], ones_bf, s, start=(c == 0), stop=(c == 7))
        r = rstd[j % 2]
        k.act(r, bank[0], AF.Ln, scale=1.0 / 1024, bias=eps_t)
        k.act(r, r, AF.Exp, scale=-0.5)
        for c in range(8):
            k.stt(hT[j % 2][:, c, :], xt[:, c, :], pw_sb[:, c:c + 1], r, ALU.mult, ALU.mult)

    def stage2(j):
        h = hT[j % 2]
        ts_ = slice(j * 512, (j + 1) * 512)
        up = upad[j % 3]
        for grp in range(4):
            ps = bank[1 + grp]
            for c in range(8):
                k.matmul(ps, w_fm[:, c, grp * 128:(grp + 1) * 128], h[:, c, :],
                         start=(c == 0), stop=(c == 7))
            if grp == 0:
                k.act(zsT[:, ts_], ps, AF.Silu)
            else:
                k.copy(up[:, grp - 1, 2:514], ps, eng="act")
        pdt = bank[5]
        for s4 in range(4):
            for c in range(8):
                k.matmul(pdt[:, s4 * 4:(s4 + 1) * 4], h[:, c, s4 * 128:(s4 + 1) * 128], w_dt[:, c, :],
                         start=(c == 0), stop=(c == 7))
        k.copy(dtraw[:, :, j * 4:(j + 1) * 4].re("p h s -> p s h"),
               pdt[:, 0:16].re("p (s h) -> p s h", s=4), eng="dve")

    def conv(j):
        up = upad[j % 3]
        if j > 0:
            k.copy(up[:, :, 0:2], upad[(j - 1) % 3][:, :, 512:514], eng="pool")
        if j < NT - 1:
            k.copy(up[:, :, 514:516], upad[(j + 1) % 3][:, :, 2:4], eng="pool")
        else:
            k.memset(up[:, :, 514:516], 0.0)
        ts_ = slice(j * 512, (j + 1) * 512)
        for ch in range(3):
            a = acc[ch]
            k.ts(a, up[:, ch, 0:512], cw_sb[:, ch, 0:1], ALU.mult)
            for o in range(1, 5):
                k.stt(a, up[:, ch, o:o + 512], cw_sb[:, ch, o:o + 1], a, ALU.mult, ALU.add)
            k.act(dst3[ch][:, ts_], a, AF.Silu, bias=cb_sb[:, ch:ch + 1])

    for j in range(NT + 2):
        if j < NT:
            stage1(j)
        if 1 <= j <= NT:
            stage2(j - 1)
        if j >= 2:
            conv(j - 2)


    dt = k.sb("dt", [128, 4, NCH])
    a_ = k.sb("a_", [128, 4, NCH])
    cum = k.sb("cum", [128, 4, NCH])
    dtd = k.sb("dtd", [128, 4, NCH])
    etot = k.sb("etot", [128, 4, NCH])
    aneg = k.sb("aneg", [128, 4])
    k.act(aneg, sv[:, 4:8], AF.Exp)
    k.ts(aneg, aneg, -1.0, ALU.mult)
    for hd in range(4):
        k.act(dt[:, hd, :], dtraw[:, hd, :], AF.Exp, bias=sv[:, hd:hd + 1])
    k.act(dt, dt, AF.Ln, bias=one_t)
    for hd in range(4):
        k.ts(a_[:, hd, :], dt[:, hd, :], aneg[:, hd:hd + 1], ALU.mult)
    pc = bank[0]
    k.matmul(pc[:, 0:128], tri[0], a_[:, 0:2, :].re("p h c -> p (h c)"))
    k.matmul(pc[:, 128:256], tri[1], a_[:, 2:4, :].re("p h c -> p (h c)"))
    k.matmul(pc[:, 256:512], ones_f, a_.re("p h c -> p (h c)"))
    k.copy(cum.re("p h c -> p (h c)"), pc[:, 0:256], eng="dve")
    k.act(etot.re("p h c -> p (h c)"), pc[:, 256:512], AF.Exp)
    k.tt(dtd.re("p h c -> p (h c)"), pc[:, 256:512], cum.re("p h c -> p (h c)"), ALU.subtract)
    k.act(dtd, dtd, AF.Exp)
    k.tt(dtd, dtd, dt, ALU.mult)

    Sf_all = k.sb("Sf_all", [128, NCH, 128], BF16)
    Sb_all = k.sb("Sb_all", [128, NCH, 128], BF16)
    Srun = [k.sb("Srun%d" % i, [128, 128]) for i in range(2)]
    btm = [k.sb("btm%d" % i, [128, 128], BF16) for i in range(2)]
    xdtd = [k.sb("xdtd%d" % i, [128, 128], BF16) for i in range(2)]
    for d in range(2):
        k.memset(Srun[d], 0.0)
        order = range(NCH) if d == 0 else range(NCH - 1, -1, -1)
        S_all = Sf_all if d == 0 else Sb_all
        for i, c in enumerate(order):
            cs = slice(c * 128, (c + 1) * 128)
            tb = tbank[:, (i % 2) * 256:(i % 2) * 256 + 256]
            k.transpose(tb[:, 0:128], xcT[:, cs], ident_bf)
            k.transpose(tb[:, 128:256], BT[:, cs], ident_bf)
            bt = btm[i % 2]
            k.copy(bt, tb[:, 128:256], eng="act")
            xd = xdtd[i % 2]
            for h in range(2):
                hd = d * 2 + h
                k.ts(xd[:, h * 64:(h + 1) * 64], tb[:, h * 64:(h + 1) * 64], dtd[:, hd, c:c + 1], ALU.mult)
            st = bank[1 + i % 2]
            k.matmul(st[:, 0:128], bt, xd)
            k.copy(S_all[:, c, :], Srun[d], eng="act")
            for h in range(2):
                hd = d * 2 + h
                hb = slice(h * 64, (h + 1) * 64)
                k.stt(Srun[d][:, hb], Srun[d][:, hb], etot[:, hd, c:c + 1], st[:, hb], ALU.mult, ALU.add)

    lhsD = [k.sb("lhsD%d" % i, [128, 4, 128]) for i in range(1)] * 2
    abc = [k.sb("abc%d" % i, [128, 4, 128]) for i in range(1)] * 2
    Lexp = [k.sb("Lexp%d" % i, [128, 512]) for i in range(2)]
    Ebc = [k.sb("Ebc%d" % i, [128, 512]) for i in range(2)]
    Gm = [k.sb("Gm%d" % i, [128, 2, 128]) for i in range(2)]
    MT = [k.sb("MT%d" % i, [128, 4, 128], BF16) for i in range(2)]
    Ct = [k.sb("Ct%d" % i, [128, 4, 128], BF16) for i in range(2)]
    xtm = [k.sb("xtm%d" % i, [128, 128]) for i in range(2)]
    xdt = [k.sb("xdt%d" % i, [128, 4, 64], BF16) for i in range(2)]
    y_sb = [k.sb("y_sb%d" % i, [128, 128]) for i in range(2)]
    ost = [k.sb("ost%d" % i, [128, 512], ydt) for i in range(2)]
    def front(c):
        cs = slice(c * 128, (c + 1) * 128)
        p = c % 2
        tb = tbank[:, p * 256:p * 256 + 128]
        k.transpose(tb, xcT[:, cs], ident_bf)
        k.copy(xtm[p], tb, eng="act")
        for hd in range(4):
            k.ts(xdt[p][:, hd, :], xtm[p][:, (hd % 2) * 64:(hd % 2) * 64 + 64], dt[:, hd, c:c + 1], ALU.mult)
        for hd in range(4):
            d = hd // 2
            k.ts(lhsD[p][:, hd, :], smask[d], a_[:, hd, c:c + 1], ALU.mult)
            k.ts(abc[p][:, hd, :], ones_f, a_[:, hd, c:c + 1], ALU.mult)
        Dps, Cps = bank[3], bank[4]
        for hd in range(4):
            d = hd // 2
            k.matmul(Dps[:, hd * 128:(hd + 1) * 128], lhsD[p][:, hd, :], tri[d])
        for hd in range(4):
            d = hd // 2
            k.matmul(Cps[:, hd * 128:(hd + 1) * 128], abc[p][:, hd, :], tri[d])
        k.act(Lexp[p], Dps, AF.Exp)
        k.act(Ebc[p], Cps, AF.Exp)
        Gps = bank[5]
        k.matmul(Gps[:, 0:128], BT[:, cs], CT[:, cs])
        for d in range(2):
            k.tt(Gm[p][:, d, :], Gps[:, 0:128], tri[d], ALU.mult)

    def back(c):
        cs = slice(c * 128, (c + 1) * 128)
        p = c % 2
        for hd in range(4):
            d = hd // 2
            k.tt(MT[p][:, hd, :], Lexp[p][:, hd * 128:(hd + 1) * 128], Gm[p][:, d, :], ALU.mult)
            k.tt(Ct[p][:, hd, :], Ebc[p][:, hd * 128:(hd + 1) * 128], CT[:, cs], ALU.mult)
        Yps = bank[6]
        for h in range(2):
            hb = slice(h * 64, (h + 1) * 64)
            k.matmul(Yps[:, hb], MT[p][:, h, :], xdt[p][:, h, :], start=True, stop=False)
            k.matmul(Yps[:, hb], MT[p][:, 2 + h, :], xdt[p][:, 2 + h, :], start=False, stop=False)
            k.matmul(Yps[:, hb], Ct[p][:, h, :], Sf_all[:, c, hb], start=False, stop=False)
            k.matmul(Yps[:, hb], Ct[p][:, 2 + h, :], Sb_all[:, c, hb], start=False, stop=True)
        for h in range(2):
            hb = slice(h * 64, (h + 1) * 64)
            k.stt(y_sb[p][:, hb], xtm[p][:, hb], sv[:, 8 + h:9 + h], Yps[:, hb], ALU.mult, ALU.add)
        yTp = bank[1 + p]
        k.transpose(yTp[:, 0:128], y_sb[p], ident_f)
        o = ost[(c // 4) % 2]
        k.tt(o[:, (c % 4) * 128:(c % 4 + 1) * 128], yTp[:, 0:128], zsT[:, cs], ALU.mult)
        if c % 4 == 3:
            k.dma(io["yfn"](None, c // 4) if io.get("yfn") is not None else yT[:, (c - 3) * 128:(c + 1) * 128], o, eng="sp")

    for c in range(NCH + 1):
        if c < NCH:
            front(c)
        if c >= 1:
            back(c - 1)


L = 8192
NT = L // 512
NORM_EPS = 1e-6
GN_EPS = 64e-5
C0 = math.exp(-0.5)


def emit_rwkv(k, io):
    stop, nq_lim, do_chain, do_pre, pre_lim = 99, None, True, True, 9
    xT, hsrc, pw, wfm, mu, w2a2, pvd, cm, c128 = (io.get("xT"), io.get("hT"), io["pw"], io["wfm"], io["mu"],
                                                  io["w2a2"], io["pvd"], io["cm"], io["c128"])
    wkv_scr, bon_scr, yT, ydt = io["wkv_scr"], io["bon_scr"], io.get("y"), io["ydt"]

    ones_bf = k.sb("ones_bf", [128, 128], BF16)
    k.memset(ones_bf, 1.0)
    eps_t = k.sb("eps_t", [128, 1])
    k.memset(eps_t, NORM_EPS)
    epsg_t = k.sb("epsg_t", [128, 1])
    k.memset(epsg_t, GN_EPS)
    tiny_t = k.sb("tiny_t", [128, 1])
    k.memset(tiny_t, 1e-24)
    pw_sb = k.sb("pw_sb", [128, 8])
    k.dma(pw_sb, pw)
    mu_sb = k.sb("mu_sb", [128, 2, 4])
    k.dma(mu_sb, mu)
    w2a2_sb = k.sb("w2a2_sb", [128, 2, 128])
    k.dma(w2a2_sb, w2a2)
    pv = k.sb("pv", [128, 12])
    k.dma(pv, pvd)
    omk = k.sb("omk", [128, 1])
    k.ts(omk, pv[:, 4:5], -1.0, ALU.mult, 1.0, ALU.add)
    cm_sb = k.sb("cm_sb", [64, 2, 704])
    k.dma(cm_sb, cm)
    c128_sb = k.sb("c128_sb", [128, 2, 128])
    k.dma(c128_sb, c128)
    blk_bf = k.sb("blk_bf", [128, 128], BF16)
    k.copy(blk_bf, c128_sb[:, 0, :], eng="dve")
    ident_f = c128_sb[:, 1, :]
    ident_bf = k.sb("ident_bf", [128, 128], BF16)
    k.copy(ident_bf, ident_f, eng="dve")
    id64_bf = k.sb("id64_bf", [64, 4, 64], BF16)
    for h in range(4):
        k.copy(id64_bf[:, h, :], cm_sb[:, 0, 640:704], eng="dve")
    maskSC = [cm_sb[:, d, 0:512] for d in range(2)]
    maskN = [cm_sb[:, d, 512:640] for d in range(2)]

    xbuf = k.sb("xbuf", [128, 8, 512])
    w_fm = k.sb("w_fm", [128, 8, 640], BF16)
    k.dma(xbuf, wfm[:, 0:512].re("(c p) m -> p c m", p=128))
    k.copy(w_fm[:, :, 0:512], xbuf, eng="pool")
    k.dma(xbuf[:, :, 0:128], wfm[:, 512:640].re("(c p) m -> p c m", p=128))
    k.copy(w_fm[:, :, 512:640], xbuf[:, :, 0:128], eng="pool")

    hT = k.sb("hT", [128, 8, 512], BF16)
    sq = [k.sb("sq%d" % i, [128, 512], BF16) for i in range(2)]
    rstd = k.sb("rstd", [128, 512])
    upad = [k.sb("upad%d" % d, [128, 4, 514]) for d in range(2)]
    ul = k.sb("ul", [128, 4, 512])
    tmpd = [k.sb("tmpd%d" % i, [128, 512]) for i in range(2)]
    gT = k.sb("gT", [128, L], BF16)
    twd = k.sb("twd", [64, 512])
    sg = k.sb("sg", [128, 512])
    aic = k.sb("aic", [128, 512])
    kk = k.sb("kk", [128, 512])
    kkn = k.sb("kkn", [128, 512])
    rs = k.sb("rs", [128, 512])
    tka = k.sb("tka", [128, 512])
    k2 = k.sb("k2", [128, 512])
    bvec = k.sb("bvec", [128, 512])
    rk = k.sb("rk", [128, 512], BF16)
    bon = [k.sb("bon%d" % i, [128, 512]) for i in range(2)]
    cs = [k.sb("cs%d" % i, [128, 512]) for i in range(2)]
    csm = k.sb("csm", [128, 512])
    Epos = k.sb("Epos", [128, 512])
    Eneg = k.sb("Eneg", [128, 512])
    Eprev = k.sb("Eprev", [128, 512])
    RopT = [[k.sb("RopT%d%d" % (d, s), [128, 8, 2, 64], BF16) for s in range(2)] for d in range(2)]
    LopT = [[k.sb("LopT%d%d" % (d, s), [128, 8, 2, 64], BF16) for s in range(2)] for d in range(2)]
    vTb = [[k.sb("vTb%d%d" % (d, s), [128, 512], BF16) for s in range(2)] for d in range(2)]
    eC = [[k.sb("eC%d%d" % (d, s), [128, 8]) for s in range(2)] for d in range(2)]
    NR = 4
    tm = [[k.sb("tm%d%d" % (d, i), [64, 384], BF16) for i in range(NR)] for d in range(2)]
    scmB = [k.sb("scmB%d" % i, [64, 2, 4, 128], BF16) for i in range(NR)]
    TtB = [k.sb("TtB%d" % i, [64, 4, 64], BF16) for i in range(NR)]
    NMt2 = [[k.sb("NMt%d_%d" % (a, i), [64, 2, 4, 64], BF16) for i in range(2)] for a in range(2)]
    Pt2 = [[k.sb("Pt%d_%d" % (a, i), [64, 4, 64], BF16) for i in range(2)] for a in range(2)]
    S32 = [k.sb("S32_%d" % d, [128, 128]) for d in range(2)]
    t1 = [k.sb("t1_%d" % d, [128, 128]) for d in range(2)]
    Sbf = [[k.sb("Sbf%d%d" % (d, i), [128, 128], BF16) for i in range(2)] for d in range(2)]
    Zb = [k.sb("Zb%d" % d, [64, 128], BF16) for d in range(2)]
    Ub = [k.sb("Ub%d" % d, [64, 128], BF16) for d in range(2)]
    Yo = [[k.sb("Yo%d%d" % (d, i), [64, 128]) for i in range(2)] for d in range(2)]
    bankA = [k.ps("bankA%d" % i, [128, 512]) for i in range(2)]
    bSC = k.ps("bSC", [128, 512])
    bNM = k.ps("bNM", [128, 512])
    bP = k.ps("bP", [128, 512])
    bCH = [k.ps("bCH%d" % d, [128, 512]) for d in range(2)]
    tb32 = k.ps("tbank", [128, 512])
    tbank = T(tb32.buf, tb32.ap.bitcast(BF16))

    for d in range(2):
        k.memset(S32[d], 0.0)
        k.memset(Sbf[d][0], 0.0)
    k.memset(upad[0][:, :, 0:1], 0.0)
    k.memset(upad[1][:, :, 513:514], 0.0)

    state = {"a": 0}

    def nbank():
        return bankA[0]

    def phaseA(d, j, slot):
        first = (j == 0) if d == 0 else (j == NT - 1)
        up = upad[d]
        if not first:
            if d == 0:
                k.copy(up[:, :, 0:1], up[:, :, 512:513], eng="pool")
            else:
                k.copy(up[:, :, 513:514], up[:, :, 1:2], eng="pool")
        if hsrc is not None:
            load_hT(k, hT, hsrc, j)
        else:
            k.dma(xbuf, xT[:, j * 512:(j + 1) * 512].re("(c p) t -> p c t", p=128), eng="sp")
            ps = nbank()
            for c in range(8):
                s = sq[c % 2]
                k.act(s, xbuf[:, c, :], AF.Square)
                k.matmul(ps, ones_bf, s, start=(c == 0), stop=(c == 7))
            k.act(rstd, ps, AF.Ln, scale=1.0 / 1024, bias=eps_t)
            k.act(rstd, rstd, AF.Exp, scale=-0.5)
            for c in range(8):
                k.stt(hT[:, c, :], xbuf[:, c, :], pw_sb[:, c:c + 1], rstd, ALU.mult, ALU.mult)
        ts_ = slice(j * 512, (j + 1) * 512)
        for grp in range(5 if d == 0 else 4):
            ps = nbank()
            for c in range(8):
                k.matmul(ps, w_fm[:, c, grp * 128:(grp + 1) * 128], hT[:, c, :],
                         start=(c == 0), stop=(c == 7))
            if grp < 4:
                k.copy(up[:, grp, 1:513], ps, eng="act")
            else:
                k.act(gT[:, ts_], ps, AF.Silu)
        sh = slice(0, 512) if d == 0 else slice(2, 514)
        for grp in range(4):
            td = tmpd[grp % 2]
            k.tt(td, up[:, grp, sh], up[:, grp, 1:513], ALU.subtract, eng="pool")
            k.stt(ul[:, grp, :], td, mu_sb[:, d, grp:grp + 1], up[:, grp, 1:513], ALU.mult, ALU.add)
        r, kx, v, wa = ul[:, 0, :], ul[:, 1, :], ul[:, 2, :], ul[:, 3, :]
        k.copy(vTb[d][slot], v, eng="pool")
        k.act(twd, wa[0:64, :], AF.Tanh)
        pxw = nbank()
        k.matmul(pxw, w2a2_sb[0:64, d, :], twd)
        k.act(sg, pxw, AF.Sigmoid, bias=pv[:, d:d + 1])
        pxa = nbank()
        k.matmul(pxa, w2a2_sb[64:128, d, :], wa[64:128, :])
        k.act(aic, pxa, AF.Sigmoid, bias=pv[:, 2:3])
        k.ts(kk, kx, pv[:, 3:4], ALU.mult)
        k.act(sq[0], kk, AF.Square)
        pss = nbank()
        k.matmul(pss, blk_bf, sq[0])
        k.act(rs, pss, AF.Ln, bias=tiny_t)
        k.act(rs, rs, AF.Exp, scale=-0.5)
        k.tt(kkn, kk, rs, ALU.mult)
        k.ts(tka, aic, pv[:, 4:5], ALU.mult, omk, ALU.add)
        k.tt(k2, kx, tka, ALU.mult)
        k.tt(bvec, kkn, aic, ALU.mult, eng="pool")
        k.stt(rk, r, pv[:, 6:7], k2, ALU.mult, ALU.mult)
        pbs = nbank()
        k.matmul(pbs, blk_bf, rk)
        bo = bon[d]
        k.tt(bo, pbs, v, ALU.mult)
        k.dma(bon_scr[d, :, ts_], bo, eng="sp")
        src = sg
        i = 0
        for s in (1, 2, 4, 8, 16, 32):
            dst = cs[i % 2]
            sv_, dv_ = src.re("p (c t) -> p c t", t=64), dst.re("p (c t) -> p c t", t=64)
            if d == 0:
                k.tt(dv_[:, :, s:64], sv_[:, :, s:64], sv_[:, :, 0:64 - s], ALU.add)
                k.copy(dv_[:, :, 0:s], sv_[:, :, 0:s], eng="pool")
            else:
                k.tt(dv_[:, :, 0:64 - s], sv_[:, :, 0:64 - s], sv_[:, :, s:64], ALU.add)
                k.copy(dv_[:, :, 64 - s:64], sv_[:, :, 64 - s:64], eng="pool")
            src = dst
            i += 1
        csf = src
        k.tt(csm, csf, sg, ALU.subtract, eng="pool")
        k.act(Epos, csf, AF.Exp, scale=-C0)
        k.act(Eneg, csf, AF.Exp, scale=C0)
        k.act(Eprev, csm, AF.Exp, scale=-C0)
        last = 63 if d == 0 else 0
        k.copy(eC[d][slot], Epos.re("p (c t) -> p c t", t=64)[:, :, last], eng="pool")
        R, Lo = RopT[d][slot], LopT[d][slot]
        v3 = lambda t: t.re("p (c t) -> p c t", t=64)
        k.stt(R[:, :, 0, :], v3(kkn), -1.0, v3(Eprev), ALU.mult, ALU.mult)
        k.tt(R[:, :, 1, :], v3(r), v3(Epos), ALU.mult)
        k.tt(Lo[:, :, 0, :], v3(bvec), v3(Eneg), ALU.mult)
        k.tt(Lo[:, :, 1, :], v3(k2), v3(Eneg), ALU.mult, eng="pool")

    def pre_stages(q):
        items = items_for(q)
        i3 = q % NR
        R = [RopT[d][slot][:, cc] for (d, slot, cc, _, _) in items]
        Lo = [LopT[d][slot][:, cc] for (d, slot, cc, _, _) in items]
        sc = scmB[i3]
        NMt, Pt = NMt2[q % 2], Pt2[q % 2]
        st = []

        def s_tr():
            for (d, slot, cc, _, _) in items:
                tps = tbank[0:64, (d * 384):(d * 384) + 384]
                k.transpose(tps[:, 0:128], Lo[d][:, 0, :], ident_bf)
                k.transpose(tps[:, 128:256], Lo[d][:, 1, :], ident_bf)
                k.transpose(tps[:, 256:384], vTb[d][slot][:, cc * 64:(cc + 1) * 64], ident_bf)
                k.copy(tm[d][i3], tps, eng="act")
        st.append(s_tr)

        def s_sc():
            for h in range(2):
                hs = slice(64 * h, 64 * h + 64)
                bk = bSC if h == 0 else bankA[1]
                nk = bP[0:64, 256:384] if h == 0 else tb32[0:64, 384:512]
                for d in range(2):
                    Rh = R[d][hs].re("p a t -> p (a t)")
                    k.matmul(bk[0:64, d * 256:d * 256 + 128], Lo[d][hs, 0, :], Rh)
                    k.matmul(bk[0:64, d * 256 + 128:d * 256 + 256], Lo[d][hs, 1, :], Rh)
                for d in range(2):
                    k.matmul(nk[:, d * 64:(d + 1) * 64], R[d][hs, 0, :], Lo[d][hs, 0, :])
            nm = NMt[0]
            for h in range(2):
                bk = bSC if h == 0 else bankA[1]
                nk = bP[0:64, 256:384] if h == 0 else tb32[0:64, 384:512]
                k.tt(sc[:, :, 2 * h:2 * h + 2, :].re("p d a t -> p d (a t)"),
                     bk[0:64, :].re("p (d x) -> p d x", d=2), cm_sb[:, :, 0:256], ALU.mult)
                k.tt(nm[:, 0].re("p (d h) t -> p d h t", h=2)[:, :, h, :], nk.re("p (d t) -> p d t", d=2),
                     cm_sb[:, :, 512:576], ALU.mult)
            k.copy(nm[:, 1].re("p (d h) t -> p d h t", h=2),
                   sc.re("p d (h two) t -> p d h two t", two=2)[:, :, :, 0, 0:64], eng="dve")
            k.tt(Pt[0], nm[:, 1], id64_bf, ALU.add, eng="dve")
        st.append(s_sc)

        for lv in range(5):
            cur = lv % 2
            nm_c, nm_n = NMt[cur], NMt[1 - cur]
            P_c = Pt[cur]
            P_n = Pt[1 - cur] if lv < 4 else TtB[i3]

            def s_nm(lv=lv, nm_c=nm_c, nm_n=nm_n):
                pn = bNM[0:64, :]
                for dh in range(4):
                    k.matmul(pn[:, dh * 64:(dh + 1) * 64], nm_c[:, 1, dh, :], nm_c[:, 0, dh, :])
                if lv < 4:
                    for dh in range(4):
                        k.matmul(pn[:, 256 + dh * 64:256 + (dh + 1) * 64], nm_c[:, 0, dh, :], nm_c[:, 1, dh, :])
                    k.copy(nm_n.re("p a h t -> p (a h t)"), pn, eng="act")
                else:
                    k.copy(nm_n[:, 0].re("p h t -> p (h t)"), pn[:, 0:256], eng="act")
            st.append(s_nm)

            def s_p(nm_n=nm_n, P_c=P_c, P_n=P_n):
                pp = bP[0:64, 0:256]
                for dh in range(4):
                    k.matmul(pp[:, dh * 64:(dh + 1) * 64], nm_n[:, 0, dh, :], P_c[:, dh, :], start=True, stop=False)
                    k.matmul(pp[:, dh * 64:(dh + 1) * 64], id64_bf[:, 0, :], P_c[:, dh, :], start=False, stop=True)
                k.copy(P_n.re("p h t -> p (h t)"), pp, eng="dve")
            st.append(s_p)
        return st

    par = [0, 0]

    def chain_stages(q):
        items = items_for(q)
        i3 = q % NR
        ctx = []
        for (d, slot, cc, _, cg) in items:
            CB = bCH[d]
            ctx.append(dict(d=d, R=RopT[d][slot][:, cc], sc=scmB[i3][:, d], tmv=tm[d][i3], T=TtB[i3][:, 2 * d:2 * d + 2, :],
                            X=CB[0:64, 0:128], U=CB[0:64, 128:256], DS=CB[:, 256:384], Y=CB[0:64, 384:512],
                            Sb=Sbf[d][par[d]], Sn=Sbf[d][1 - par[d]], e=eC[d][slot][:, cc:cc + 1],
                            cg=cg, q=q))
            par[d] = 1 - par[d]
        st = []

        def c_x():
            for c in ctx:
                k.matmul(c["X"], c["R"][:, 0, :], c["Sb"], start=True, stop=False)
                for h in range(2):
                    hb = slice(64 * h, 64 * h + 64)
                    k.matmul(c["X"][:, hb], c["sc"][:, 2 * h + 1, 0:64], c["tmv"][:, 256 + 64 * h:256 + 64 * h + 64],
                             start=False, stop=(h == 1))
            for c in ctx:
                k.copy(Zb[c["d"]], c["X"], eng="act")
        st.append(c_x)

        def c_u():
            for c in ctx:
                for h in range(2):
                    hb = slice(64 * h, 64 * h + 64)
                    k.matmul(c["U"][:, hb], c["T"][:, h, :], Zb[c["d"]][:, hb])
            for c in ctx:
                k.copy(Ub[c["d"]], c["U"], eng="dve")
        st.append(c_u)

        def c_y():
            for c in ctx:
                d = c["d"]
                k.matmul(c["DS"], c["tmv"][:, 0:128], Ub[d], start=True, stop=False)
                k.matmul(c["DS"], c["tmv"][:, 128:256], c["tmv"][:, 256:384], start=False, stop=True)
                k.ts(t1[d], S32[d], c["e"], ALU.mult)
            for c in ctx:
                d = c["d"]
                k.matmul(c["Y"], c["R"][:, 1, :], c["Sb"], start=True, stop=False)
                for h in range(2):
                    hb = slice(64 * h, 64 * h + 64)
                    k.matmul(c["Y"][:, hb], c["sc"][:, 2 * h, 64:128], Ub[d][:, hb], start=False, stop=False)
                    k.matmul(c["Y"][:, hb], c["sc"][:, 2 * h + 1, 64:128], c["tmv"][:, 256 + 64 * h:256 + 64 * h + 64],
                             start=False, stop=(h == 1))
        st.append(c_y)

        def c_s():
            for c in ctx:
                d = c["d"]
                for h in range(2):
                    hs = slice(64 * h, 64 * h + 64)
                    k.stt(S32[d][hs, hs], c["DS"][hs, hs], c["e"][hs], t1[d][hs, hs], ALU.mult, ALU.add)
            for c in ctx:
                d = c["d"]
                k.copy(c["Sn"], S32[d], eng="act")
                yo = Yo[d][c["q"] % 2]
                k.copy(yo, c["Y"], eng="dve")
                k.dma(wkv_scr[d, c["cg"] * 64:(c["cg"] + 1) * 64, :], yo, eng="pool")
        st.append(c_s)
        return st

    def items_for(q):
        s, ci = q // 8, q % 8
        return [(0, s % 2, ci, q, s * 8 + ci), (1, s % 2, 7 - ci, q, (NT - 1 - s) * 8 + (7 - ci))]

    nq = NT * 8
    phaseA(0, 0, 0)
    phaseA(1, NT - 1, 0)
    pendA = []
    perA = 1
    for q in range(0, nq + 2, 2):
        PA, PB = [], []
        if q < nq:
            s, ci = q // 8, q % 8
            if ci == 2 and s + 1 < NT:
                k.begin_defer()
                phaseA(0, s + 1, (s + 1) % 2)
                phaseA(1, NT - 2 - s, (s + 1) % 2)
                pendA = k.end_defer()
                perA = (len(pendA) + 3 * 30 - 1) // (3 * 30)
            PA = pre_stages(q)
            PB = pre_stages(q + 1)
        cs_ = []
        if q >= 2:
            cs_ = chain_stages(q - 2) + chain_stages(q - 1)
        np_, nc_ = len(PA), len(cs_)
        ci_ = 0
        for i in range(np_):
            PA[i]()
            if pendA:
                k.splice(pendA, perA)
            PB[i]()
            if pendA:
                k.splice(pendA, perA)
            want = (i + 1) * nc_ // np_
            while ci_ < want:
                cs_[ci_]()
                if pendA:
                    k.splice(pendA, perA)
                ci_ += 1
        while ci_ < nc_:
            cs_[ci_]()
            ci_ += 1
        if q < nq and q % 8 == 6 and pendA:
            k.splice(pendA, len(pendA))
    assert not pendA

    wf = [k.sb("wf%d" % i, [128, 128]) for i in range(2)]
    wb = [k.sb("wb%d" % i, [128, 128]) for i in range(2)]
    ww = [k.sb("ww%d" % i, [128, 128]) for i in range(2)]
    sqw = k.sb("sqw", [128, 128])
    st1 = k.sb("st1", [128, 2])
    st2 = k.sb("st2", [128, 2])
    mean = k.sb("mean", [128, 2])
    msq = k.sb("msq", [128, 2])
    var = k.sb("var", [128, 2])
    gn = [k.sb("gn%d" % i, [128, 128]) for i in range(2)]
    ob = [k.sb("ob%d" % i, [128, 512]) for i in range(2)]
    obo = [k.sb("obo%d" % i, [128, 512], ydt) for i in range(2)]
    bf_ = [k.sb("bf_%d" % i, [128, 512]) for i in range(2)]
    bb_ = [k.sb("bb_%d" % i, [128, 512]) for i in range(2)]
    for i in range(L // 128):
        p = i % 2
        k.dma(wf[p], wkv_scr[0, i * 128:(i + 1) * 128, :], eng="sp")
        k.dma(wb[p], wkv_scr[1, i * 128:(i + 1) * 128, :], eng="sp")
        w = ww[p]
        k.tt(w, wf[p], wb[p], ALU.add)
        k.reduce(st1, w.re("p (h v) -> p h v", h=2), ALU.add)
        k.tt(sqw, w, w, ALU.mult, eng="pool")
        k.reduce(st2, sqw.re("p (h v) -> p h v", h=2), ALU.add)
        k.ts(mean, st1, 1.0 / 64, ALU.mult)
        k.tt(msq, mean, mean, ALU.mult)
        k.stt(var, st2, 1.0 / 64, msq, ALU.mult, ALU.subtract)
        k.act(var, var, AF.Sqrt, bias=epsg_t)
        k.recip(var, var)
        g_ = gn[p]
        for h in range(2):
            hb = slice(64 * h, 64 * h + 64)
            k.ts(g_[:, hb], w[:, hb], mean[:, h:h + 1], ALU.subtract, var[:, h:h + 1], ALU.mult)
        tb = bankA[(i // 4) % 2]
        k.transpose(tb[:, (i % 4) * 128:(i % 4 + 1) * 128], g_, ident_f)
        if i % 4 == 3:
            j = i // 4
            ts_ = slice(j * 512, (j + 1) * 512)
            o = ob[j % 2]
            k.dma(bf_[j % 2], bon_scr[0, :, ts_], eng="pool")
            k.dma(bb_[j % 2], bon_scr[1, :, ts_], eng="pool")
            k.ts(o, tb, pv[:, 7:8], ALU.mult, pv[:, 8:9], ALU.add)
            k.tt(o, o, bf_[j % 2], ALU.add, eng="pool")
            k.tt(o, o, bb_[j % 2], ALU.add, eng="pool")
            k.tt(obo[j % 2], o, gT[:, ts_], ALU.mult)
            k.dma(io["yfn"](None, j) if io.get("yfn") is not None else yT[:, ts_], obo[j % 2], eng="sp")


GROUPS = [[0, 1, 2, 3], [4, 5, 6, 7]]


def emit_out(k, io, last):
    Lx = 8192
    NTx = Lx // 512
    ygath, wo, xsrc, xdst = io["ygath"], io["wo"], io["xsrc"], io["xdst"]
    ones_bf = k.sb("ones_bf", [128, 128], BF16)
    k.memset(ones_bf, 1.0)
    eps_t = k.sb("eps_t", [128, 1])
    k.memset(eps_t, 1e-6)
    postw_sb = k.sb("postw_sb", [128, 2])
    k.dma(postw_sb, io["postw"])
    snw_sb = k.sb("snw_sb", [128, 4])
    k.dma(snw_sb, io["snw"])
    prew_sb = k.sb("prew_sb", [128, 2])
    k.dma(prew_sb, io["prew"])
    w_bf = k.sb("w_bf", [128, 16, 256], BF16)
    wst = [k.sb("wst%d" % i, [128, 4, 256]) for i in range(2)]
    for i in range(4):
        st = wst[i % 2]
        k.dma(st, wo[i * 512:(i + 1) * 512, :].re("(c p) m -> p c m", p=128), eng="pool" if i % 2 else "sp")
        k.copy(w_bf[:, 4 * i:4 * i + 4, :], st, eng="pool")
    mixT = k.sb("mixT", [128, 2, Lx])
    ssrow = k.sb("ssrow", [1, Lx])
    ytile = [k.sb("ytile%d" % i, [128, 16, 512], BF16) for i in range(2)]
    yn = [k.sb("yn%d" % i, [128, 4, 512], BF16) for i in range(2)]
    sq = [k.sb("sq%d" % i, [128, 512], BF16) for i in range(2)]
    rs = k.sb("rs", [128, 512])
    tot = [k.sb("tot%d" % i, [128, 512]) for i in range(2)]
    xt = [k.sb("xt%d" % i, [128, 2, 512]) for i in range(2)]
    hb = [k.sb("hb%d" % i, [128, 2, 512], BF16) for i in range(2)]
    pg = k.ps("pg", [128, 512])
    pm = [k.ps("pm%d" % i, [128, 512]) for i in range(2)]
    pss = k.ps("pss", [128, 512])

    for j in range(NTx):
        ts_ = slice(j * 512, (j + 1) * 512)
        yt = ytile[j % 2]
        ytv = yt.re("p (g q) t -> p g q t", q=4)
        half, jj = j // 8, j % 8
        for kind in range(4):
            k.dma(ytv[:, :, kind, :], ygath[half][kind][:, jj * 512:(jj + 1) * 512].re("(g i) t -> i g t", i=128),
                  eng="sp" if kind % 2 == 0 else "pool")
        ynj = yn[j % 2]
        for grp in range(2):
            for ci, g in enumerate((2 * grp, 2 * grp + 1)):
                k.act(sq[ci], yt[:, g * 4, :], AF.Square)
                k.matmul(pg, ones_bf, sq[ci], start=(ci == 0), stop=(ci == 1))
            k.act(rs, pg, AF.Ln, scale=1.0 / 256, bias=eps_t)
            k.act(rs, rs, AF.Exp, scale=-0.5)
            for g in (2 * grp, 2 * grp + 1):
                k.stt(ynj[:, g, :], yt[:, g * 4, :], snw_sb[:, g:g + 1], rs, ALU.mult, ALU.mult)
        for nb in range(2):
            for c in range(16):
                rhs = ynj[:, c // 4, :] if c % 4 == 0 else yt[:, c, :]
                k.matmul(pm[nb], w_bf[:, c, nb * 128:(nb + 1) * 128], rhs, start=(c == 0), stop=(c == 15))
            k.copy(mixT[:, nb, ts_], pm[nb], eng="dve")
            k.act(sq[nb], pm[nb], AF.Square)
            k.matmul(pss, ones_bf, sq[nb], start=(nb == 0), stop=(nb == 1))
        k.copy(ssrow[0:1, ts_], pss[0:1, :], eng="dve")
    k.dma(io["ar1_in"], ssrow, eng="sp")
    k.collective("AllReduce", io["ar1_in"], io["ar1_out"], GROUPS, op=ALU.add)

    for j in range(NTx):
        ts_ = slice(j * 512, (j + 1) * 512)
        tt_ = tot[j % 2]
        k.dma(tt_, T(io["ar1_out"].buf, io["ar1_out"].ap[0:1, ts_].partition_broadcast(128)), eng="pool")
        k.act(tt_, tt_, AF.Ln, scale=1.0 / 1024, bias=eps_t)
        k.act(tt_, tt_, AF.Exp, scale=-0.5)
        x_ = xt[j % 2]
        k.dma(x_, xsrc[:, ts_].re("(nb p) t -> p nb t", p=128), eng="sp")
        for nb in range(2):
            k.stt(mixT[:, nb, ts_], mixT[:, nb, ts_], postw_sb[:, nb:nb + 1], tt_, ALU.mult, ALU.mult)
            k.tt(mixT[:, nb, ts_], mixT[:, nb, ts_], x_[:, nb, :], ALU.add, eng="pool")
        k.dma(xdst[:, ts_].re("(nb p) t -> p nb t", p=128), mixT[:, :, ts_], eng="sp")
        if not last:
            for nb in range(2):
                k.act(sq[nb], mixT[:, nb, ts_], AF.Square)
                k.matmul(pss, ones_bf, sq[nb], start=(nb == 0), stop=(nb == 1))
            k.copy(ssrow[0:1, ts_], pss[0:1, :], eng="dve")
    if last:
        return
    k.dma(io["ar2_in"], ssrow, eng="sp")
    k.collective("AllReduce", io["ar2_in"], io["ar2_out"], GROUPS, op=ALU.add)
    for j in range(NTx):
        ts_ = slice(j * 512, (j + 1) * 512)
        tt_ = tot[j % 2]
        k.dma(tt_, T(io["ar2_out"].buf, io["ar2_out"].ap[0:1, ts_].partition_broadcast(128)), eng="pool")
        k.act(tt_, tt_, AF.Ln, scale=1.0 / 1024, bias=eps_t)
        k.act(tt_, tt_, AF.Exp, scale=-0.5)
        h_ = hb[j % 2]
        for nb in range(2):
            k.stt(h_[:, nb, :], mixT[:, nb, ts_], prew_sb[:, nb:nb + 1], tt_, ALU.mult, ALU.mult)
        half, jj = j // 8, j % 8
        for nb in range(2):
            k.dma(io["hslice"][half][nb][:, jj * 512:(jj + 1) * 512], h_[:, nb, :], eng="sp" if nb == 0 else "pool")
        if jj == 7:
            for nb in range(2):
                k.collective("AllGather", io["hslice"][half][nb], io["hgath"][half][nb], GROUPS)


L = 8192
PROJ_SIZES = (512, 1024, 16, 1664, 512, 512, 512, 512, 512, 512, 256, 256, 512)
OFF = np.concatenate([[0], np.cumsum(PROJ_SIZES)]).astype(int)
(O_MZ, O_XBC, O_DT, O_RU, O_RG, O_DQ, O_DK, O_DV, O_DG, O_GQ, O_GK, O_GV, O_GG) = OFF[:13]

PERM = np.array([p + 32 if (p % 64) < 32 else p - 32 for p in range(128)])


def rope_tables():
    inv = (np.float32(10000.0) ** (-np.arange(32, dtype=np.float32) / np.float32(32))).astype(np.float32)
    t = np.arange(L)
    sign = np.where((np.arange(128) % 64) < 32, -1.0, 1.0).astype(np.float32)[:, None]
    fi = np.arange(128) % 32
    pos_d = np.broadcast_to(t.astype(np.float32)[None, :], (128, L))
    ang_d = (pos_d * inv[fi][:, None]).astype(np.float32)
    pos_g = np.where((np.arange(128) < 64)[:, None], (t // 64)[None, :], (t % 64)[None, :]).astype(np.float32)
    ang_g = (pos_g * inv[fi][:, None]).astype(np.float32)
    tabs = np.stack([np.cos(ang_d), np.sin(ang_d) * sign, np.cos(ang_g), np.sin(ang_g) * sign]).astype(np.float32)
    return np.ascontiguousarray(tabs)


def pvec(v):
    return np.ascontiguousarray(v.reshape(8, 128).T)


def prep_attn(inp, layer, xT_b, tabs):
    W = inp['w_in'][layer]
    maps = []
    for core in range(8):
        b, g = core // 4, core % 4
        sl = lambda o, n=128, gg=g: W[:, o + gg * n: o + (gg + 1) * n]
        dq, dk_, dg, dv = sl(O_DQ), sl(O_DK), sl(O_DG), sl(O_DV)
        gq, gg_ = sl(O_GQ), sl(O_GG)
        gk, gv = sl(O_GK, 128, g // 2), sl(O_GV, 128, g // 2)
        wfm = np.concatenate([dq, dq[:, PERM], dk_, dk_[:, PERM], dg, gq, gq[:, PERM], gk, gk[:, PERM], gg_], axis=1)
        wtm = np.concatenate([dv, gv], axis=1)
        vecs = np.zeros((128, 8), np.float32)
        qw, kw = inp['gqa_q_norm_w'][layer], inp['gqa_k_norm_w'][layer]
        vecs[:, 0] = qw; vecs[:, 1] = qw[PERM]; vecs[:, 2] = kw; vecs[:, 3] = kw[PERM]
        vecs[:, 4] = inp['diff_norm_w'][layer]
        lam = np.ascontiguousarray(np.broadcast_to(inp['diff_lambda'][layer].reshape(1, 256), (128, 256)))
        maps.append({"xT": xT_b[b], "pw": pvec(inp['pre_norm_w'][layer]),
                     "wfm": np.ascontiguousarray(wfm), "wtm": np.ascontiguousarray(wtm),
                     "tabs": tabs, "vecs": vecs, "lam": lam})
    return maps


def ssd_consts():
    j = np.arange(128)[:, None]; l = np.arange(128)[None, :]
    c = np.stack([(j <= l), (j >= l), (j > l), (j < l), (j == l)]).astype(np.float32)
    return np.ascontiguousarray(c.transpose(1, 0, 2))


def prep_ssd(inp, layer, xT_b):
    W = inp['w_in'][layer]
    cst = ssd_consts()
    maps = []
    for core in range(8):
        b, g = core // 4, core % 4
        grp = g // 2
        wfm = np.concatenate([W[:, O_MZ + g * 128:O_MZ + (g + 1) * 128],
                              W[:, O_XBC + g * 128:O_XBC + (g + 1) * 128],
                              W[:, O_XBC + 512 + grp * 128:O_XBC + 512 + (grp + 1) * 128],
                              W[:, O_XBC + 768 + grp * 128:O_XBC + 768 + (grp + 1) * 128]], axis=1)
        dcols = [O_DT + d * 8 + 2 * g + h for d in range(2) for h in range(2)]
        wdt = W[:, dcols]
        chans = [g * 128, 512 + grp * 128, 768 + grp * 128]
        cwl = inp['conv_w'][layer]; cbl = inp['conv_b'][layer]
        cw = np.stack([cwl[:, ch:ch + 128].T for ch in chans], axis=1)
        cb = np.stack([cbl[ch:ch + 128] for ch in chans], axis=1)
        v = np.zeros(16, np.float32)
        for d in range(2):
            for h in range(2):
                v[d * 2 + h] = inp['ssm_dt_bias'][layer][d, 2 * g + h]
                v[4 + d * 2 + h] = inp['ssm_a_log'][layer][d, 2 * g + h]
        v[8] = inp['ssm_d'][layer][2 * g]; v[9] = inp['ssm_d'][layer][2 * g + 1]
        maps.append({"xT": xT_b[b], "pw": pvec(inp['pre_norm_w'][layer]),
                     "wfm": np.ascontiguousarray(wfm), "wdt": np.ascontiguousarray(wdt),
                     "cw": np.ascontiguousarray(cw), "cb": np.ascontiguousarray(cb),
                     "ssmv": np.ascontiguousarray(np.broadcast_to(v[None], (128, 16))), "cst": cst})
    return maps


def rwkv_consts():
    j = np.arange(64)[:, None]; t = np.arange(64)[None, :]
    cm = np.zeros((64, 2, 704), np.float32)
    for d in range(2):
        strict = (j < t) if d == 0 else (j > t)
        incl = (j <= t) if d == 0 else (j >= t)
        blk = np.concatenate([strict, incl], axis=1).astype(np.float32)
        cm[:, d, 0:512] = np.tile(blk, (1, 4))
        nmask = ((t < j) if d == 0 else (t > j)).astype(np.float32)
        cm[:, d, 512:640] = np.tile(nmask, (1, 2))
        cm[:, d, 640:704] = np.eye(64, dtype=np.float32)
    c128 = np.zeros((128, 2, 128), np.float32)
    c128[0:64, 0, 0:64] = 1; c128[64:128, 0, 64:128] = 1
    c128[:, 1, :] = np.eye(128, dtype=np.float32)
    return cm, c128


O_R, O_K, O_V, O_WD, O_AD = O_RU, O_RU + 512, O_RU + 1024, O_RU + 1536, O_RU + 1600


def prep_rwkv(inp, layer, xT_b):
    W = inp['w_in'][layer]
    cm, c128 = rwkv_consts()
    maps = []
    for core in range(8):
        b, g = core // 4, core % 4
        gs = slice(g * 128, (g + 1) * 128)
        wfm = np.concatenate([W[:, O_R + g * 128:O_R + (g + 1) * 128], W[:, O_K + g * 128:O_K + (g + 1) * 128],
                              W[:, O_V + g * 128:O_V + (g + 1) * 128], W[:, O_WD:O_WD + 128],
                              W[:, O_RG + g * 128:O_RG + (g + 1) * 128]], axis=1)
        mul = inp['rwkv_mu'][layer]
        mu = np.zeros((128, 2, 4), np.float32)
        for d in range(2):
            mu[:, d, 0] = mul[d, 0 + g * 128:0 + (g + 1) * 128]
            mu[:, d, 1] = mul[d, 512 + g * 128:512 + (g + 1) * 128]
            mu[:, d, 2] = mul[d, 1024 + g * 128:1024 + (g + 1) * 128]
            mu[:, d, 3] = mul[d, 1536:1664]
        w2a2 = np.zeros((128, 2, 128), np.float32)
        for d in range(2):
            w2a2[0:64, d, :] = inp['rwkv_w2'][layer][d][:, gs]
            w2a2[64:128, d, :] = inp['rwkv_a2'][layer][:, gs]
        pv = np.zeros((128, 12), np.float32)
        pv[:, 0] = inp['rwkv_w0'][layer][0, gs]; pv[:, 1] = inp['rwkv_w0'][layer][1, gs]
        pv[:, 2] = inp['rwkv_a0'][layer][gs]; pv[:, 3] = inp['rwkv_k_k'][layer][gs]
        pv[:, 4] = inp['rwkv_k_a'][layer][gs]
        pv[:, 6] = inp['rwkv_r_k'][layer].reshape(-1)[gs]
        pv[:, 7] = inp['rwkv_ln_w'][layer][gs]; pv[:, 8] = inp['rwkv_ln_b'][layer][gs]
        maps.append({"xT": xT_b[b], "pw": pvec(inp['pre_norm_w'][layer]), "wfm": np.ascontiguousarray(wfm),
                     "mu": mu, "w2a2": w2a2, "pvd": pv, "cm": cm, "c128": c128})
    return maps


def build_fused(depth=2):
    nc = bass.Bass("TRN2", target_bir_lowering=False)
    k = KB(nc, arena=True)
    din = k.dram_in
    xT = din("xT", [1024, L])
    xs0 = din("xs0", [256, L])
    tabs = din("tabs", [4, 128, L])
    cst = din("cst", [128, 5, 128])
    cm = din("cm", [64, 2, 704])
    c128 = din("c128", [128, 2, 128])
    xo = k.dram_out("xo", [256, L])
    LH = L // 2
    ycat = [[k.dram_scratch("ycat%d%d" % (h_, q_), [128, LH], BF16) for q_ in range(4)] for h_ in range(2)]
    ygath = [[k.dram_scratch("ygath%d%d" % (h_, q_), [512, LH], BF16) for q_ in range(4)] for h_ in range(2)]
    hslice = [[k.dram_scratch("hslice%d%d" % (h_, q_), [128, LH], BF16) for q_ in range(2)] for h_ in range(2)]
    hgath = [[k.dram_scratch("hgath%d%d" % (h_, q_), [512, LH], BF16) for q_ in range(2)] for h_ in range(2)]

    def yfn_for(kind_idx):
        def fn(_, j):
            half, jj = j // 8, j % 8
            return ycat[half][kind_idx][:, jj * 512:(jj + 1) * 512]
        return fn

    def gather_y(half, kind_idx):
        k.collective("AllGather", ycat[half][kind_idx], ygath[half][kind_idx], GROUPS)

    xres = k.dram_scratch("xres", [256, L])
    ar = [k.dram_scratch("ar%d" % i, [1, L]) for i in range(4)]
    wkv_scr = k.dram_scratch("wkv_scr", [2, L, 128])
    bon_scr = k.dram_scratch("bon_scr", [2, 128, L])
    for layer in range(depth):
        p = "L%d_" % layer
        lambda_init = 0.8 - 0.6 * math.exp(-0.3 * layer)
        src = {"xT": xT} if layer == 0 else {"hT": hgath}
        pw = din(p + "pw", [128, 8])
        io = dict(src, pw=pw, wfm=din(p + "s_wfm", [1024, 512]), wdt=din(p + "s_wdt", [1024, 4]),
                  cw=din(p + "s_cw", [128, 3, 5]), cb=din(p + "s_cb", [128, 3]), ssmv=din(p + "s_ssmv", [128, 16]),
                  cst=cst, yfn=yfn_for(0), ydt=BF16)
        emit_ssd(k, io)
        k.phase_reset()
        io = dict(src, pw=pw, wfm=din(p + "r_wfm", [1024, 640]), mu=din(p + "r_mu", [128, 2, 4]),
                  w2a2=din(p + "r_w2a2", [128, 2, 128]), pvd=din(p + "r_pvd", [128, 12]), cm=cm, c128=c128,
                  wkv_scr=wkv_scr, bon_scr=bon_scr, yfn=yfn_for(1), ydt=BF16)
        emit_rwkv(k, io)
        k.phase_reset()
        for half in range(2):
            for kind_idx in range(2):
                gather_y(half, kind_idx)
        io = dict(src, pw=pw, wfm=din(p + "a_wfm", [1024, 1280]), wtm=din(p + "a_wtm", [1024, 256]), tabs=tabs,
                  vecs=din(p + "a_vecs", [128, 8]), lam=din(p + "a_lam", [128, 256]),
                  yfn=(lambda kind, j: yfn_for(2 + kind)(None, j)), ydt=BF16,
                  y_done=(lambda kind, half: gather_y(half, 2 + kind)))
        emit_attn(k, io, lambda_init)
        k.phase_reset()
        last = (layer == depth - 1)
        io = dict(ygath=ygath, wo=din(p + "o_wo", [2048, 256]), xsrc=(xs0 if layer == 0 else xres),
                  xdst=(xo if last else xres), postw=din(p + "o_postw", [128, 2]), snw=din(p + "o_snw", [128, 4]),
                  prew=din(p + "o_prew", [128, 2]), ar1_in=ar[0], ar1_out=ar[1], ar2_in=ar[2], ar2_out=ar[3],
                  hslice=hslice, hgath=hgath)
        emit_out(k, io, last)
        k.phase_reset()
    stats = k.finish()
    return nc, stats


def prep_fused(inp, depth=2):
    x = np.ascontiguousarray(inp["x"], dtype=np.float32)
    xT_b = [np.ascontiguousarray(x[b].T) for b in range(2)]
    tabs = rope_tables()
    cst = ssd_consts()
    cm, c128 = rwkv_consts()
    maps = [dict() for _ in range(8)]
    perm = np.array([kind * 512 + g * 128 + i for g in range(4) for kind in range(4) for i in range(128)])
    for core in range(8):
        b, g = core // 4, core % 4
        m = maps[core]
        m["xT"] = xT_b[b]
        m["xs0"] = np.ascontiguousarray(xT_b[b][g * 256:(g + 1) * 256])
        m["tabs"] = tabs; m["cst"] = cst; m["cm"] = cm; m["c128"] = c128
    for layer in range(depth):
        p = "L%d_" % layer
        ms = prep_ssd(inp, layer, xT_b); mr = prep_rwkv(inp, layer, xT_b); ma = prep_attn(inp, layer, xT_b, tabs)
        for core in range(8):
            b, g = core // 4, core % 4
            m = maps[core]
            m[p + "pw"] = ms[core]["pw"]
            for nm in ("wfm", "wdt", "cw", "cb", "ssmv"):
                m[p + "s_" + nm] = ms[core][nm]
            for nm in ("wfm", "mu", "w2a2", "pvd"):
                m[p + "r_" + nm] = mr[core][nm]
            for nm in ("wfm", "wtm", "vecs", "lam"):
                m[p + "a_" + nm] = ma[core][nm]
            ns = slice(g * 256, (g + 1) * 256)
            m[p + "o_wo"] = np.ascontiguousarray(inp["w_out"][layer][perm][:, ns])
            m[p + "o_postw"] = np.ascontiguousarray(inp["post_norm_w"][layer][ns].reshape(2, 128).T)
            m[p + "o_snw"] = np.ascontiguousarray(inp["ssm_norm_w"][layer].reshape(4, 128).T)
            nxt = inp["pre_norm_w"][min(layer + 1, depth - 1)]
            m[p + "o_prew"] = np.ascontiguousarray(nxt[ns].reshape(2, 128).T)
    return maps


from concourse.bass_utils import run_bass_kernel_spmd


def kernel(**inp):
    inp = {k_: np.asarray(v) for k_, v in inp.items()}
    depth = inp["w_in"].shape[0]
    nc, _ = build_fused(depth)
    maps = prep_fused(inp, depth)
    res = run_bass_kernel_spmd(nc, maps, core_ids=list(range(8))).results
    out = np.empty((2, L, 1024), np.float32)
    for core in range(8):
        b, g = core // 4, core % 4
        out[b, :, g * 256:(g + 1) * 256] = res[core]["xo"].T
    return out
```

```python
import math
import numpy as np
import concourse.bass as bass
import concourse.mybir as mybir

F32 = mybir.dt.float32
BF16 = mybir.dt.bfloat16
ALU = mybir.AluOpType
AF = mybir.ActivationFunctionType
AX = mybir.AxisListType

SEM_CHUNK = 30000


class Buf:
    __slots__ = ("name", "last_w", "readers", "dma_sem", "dma_cnt", "is_dram", "psum", "wlist", "inc_val")

    def __init__(self, name, is_dram=False, psum=False):
        self.psum = psum
        self.wlist = {}
        self.inc_val = 16
        self.name = name
        self.last_w = None
        self.readers = []
        self.dma_sem = None
        self.dma_cnt = 0
        self.is_dram = is_dram


class T:
    __slots__ = ("buf", "ap")

    def __init__(self, buf, ap):
        self.buf = buf
        self.ap = ap

    def __getitem__(self, idx):
        return T(self.buf, self.ap[idx])

    def re(self, pattern, **kw):
        return T(self.buf, self.ap.rearrange(pattern, **kw))

    def sub(self, buf, idx=None):
        return T(buf, self.ap if idx is None else self.ap[idx])


class Op:
    __slots__ = ("eng", "fn", "reads", "writes", "is_dma", "deps", "inc_idx", "dma_buf",
                 "dma_wait", "gi")

    def __init__(self, eng, fn, reads, writes, is_dma=False, dma_buf=None):
        self.eng = eng
        self.fn = fn
        self.reads = reads
        self.writes = writes
        self.is_dma = is_dma
        self.deps = []
        self.inc_idx = None
        self.dma_buf = dma_buf
        self.dma_wait = []
        self.gi = None


class KB:
    ENGS = ("pe", "act", "dve", "pool", "sp")

    ARENA_WORDS = 53100

    def __init__(self, nc, arena=False):
        self.nc = nc
        self.arena = None
        if arena:
            self.arena = nc.alloc_sbuf_tensor("arena", [128, self.ARENA_WORDS], F32).ap()
            self.a_off = 0
            self.a_peak = 0
            self.banks = [T(Buf("bank%d" % i, psum=True),
                            nc.alloc_psum_tensor("gbank%d" % i, [128, 512], F32).ap()) for i in range(8)]
            self.b_next = 0
        self.ops = []
        self.e = {"pe": nc.tensor, "act": nc.scalar, "dve": nc.vector, "pool": nc.gpsimd,
                  "sp": nc.sync}
        self._n = 0

    def sb(self, name, shape, dtype=F32):
        if self.arena is None:
            h = self.nc.alloc_sbuf_tensor(name, list(shape), dtype)
            return T(Buf(name), h.ap())
        shape = list(shape)
        esz = 2 if dtype == BF16 else 4
        n = 1
        for d in shape[1:]:
            n *= d
        nbytes = (n * esz + 31) // 32 * 32
        nw = nbytes // 4
        off = self.a_off
        assert off + nw <= self.ARENA_WORDS, "arena overflow at %s: %d + %d" % (name, off, nw)
        self.a_off = off + nw
        self.a_peak = max(self.a_peak, self.a_off)
        ap = self.arena[0:shape[0], off:off + (n * esz + 3) // 4]
        if dtype == BF16:
            ap = ap.bitcast(BF16)
            if (n * esz) % 4:
                ap = ap[:, 0:n]
        if len(shape) == 3:
            ap = ap.rearrange("p (a b) -> p a b", a=shape[1])
        elif len(shape) == 4:
            ap = ap.rearrange("p (a b c) -> p a b c", a=shape[1], b=shape[2])
        return T(Buf(name), ap)

    def ps(self, name, shape, dtype=F32):
        if self.arena is None:
            h = self.nc.alloc_psum_tensor(name, list(shape), dtype)
            return T(Buf(name, psum=True), h.ap())
        bk = self.banks[self.b_next]
        self.b_next += 1
        if dtype == BF16:
            return T(bk.buf, bk.ap.bitcast(BF16))
        return bk

    def phase_reset(self):
        self._rec("bar", None, [], [])
        self.a_off = 0
        self.b_next = 0

    def dram_in(self, name, shape, dtype=F32):
        h = self.nc.dram_tensor(name, list(shape), dtype, kind="ExternalInput")
        return T(Buf(name, True), h.ap())

    def dram_out(self, name, shape, dtype=F32):
        h = self.nc.dram_tensor(name, list(shape), dtype, kind="ExternalOutput")
        return T(Buf(name, True), h.ap())

    def dram_scratch(self, name, shape, dtype=F32):
        h = self.nc.dram_tensor(name, list(shape), dtype)
        return T(Buf(name, True), h.ap())

    def buf(self, name):
        self._n += 1
        return Buf("%s_%d" % (name, self._n))

    def _rec(self, eng, fn, reads, writes, is_dma=False, dma_buf=None):
        rb = []
        for t in reads:
            if t is None or isinstance(t, (int, float)):
                continue
            b = t.buf if isinstance(t, T) else t
            if b not in rb:
                rb.append(b)
        wb = []
        for t in writes:
            b = t.buf if isinstance(t, T) else t
            if b not in wb:
                wb.append(b)
        op = Op(eng, fn, rb, wb, is_dma, dma_buf)
        if getattr(self, "_defer", None) is not None:
            self._defer.append(op)
            return op
        op.gi = len(self.ops)
        self.ops.append(op)
        return op

    def begin_defer(self):
        self._defer = []

    def end_defer(self):
        lst = self._defer
        self._defer = None
        return lst

    def splice(self, lst, n):
        for _ in range(min(n, len(lst))):
            op = lst.pop(0)
            op.gi = len(self.ops)
            self.ops.append(op)

    @staticmethod
    def _a(x):
        return x.ap if isinstance(x, T) else x

    def dma(self, out, in_, eng="sp", **kw):
        o, i = self._a(out), self._a(in_)
        sbuf_side = out.buf if not out.buf.is_dram else in_.buf
        return self._rec(eng, lambda E: E.dma_start(out=o, in_=i, **kw), [in_], [out],
                         is_dma=True, dma_buf=sbuf_side)

    def matmul(self, out, lhsT, rhs, start=True, stop=True, extra_reads=(), **kw):
        o, l, r = self._a(out), self._a(lhsT), self._a(rhs)
        reads = [lhsT, rhs] + list(extra_reads)
        if not start:
            reads.append(out)
        return self._rec("pe", lambda E: E.matmul(o, l, r, start=start, stop=stop, **kw),
                         reads, [out])

    def transpose(self, out, in_, ident):
        o, i, d = self._a(out), self._a(in_), self._a(ident)
        return self._rec("pe", lambda E: E.transpose(o, i, d), [in_, ident], [out])

    def act(self, out, in_, func, bias=None, scale=None, accum_out=None, eng="act"):
        o, i = self._a(out), self._a(in_)
        kw = {}
        reads = [in_]
        writes = [out]
        if bias is not None:
            kw["bias"] = self._a(bias)
            reads.append(bias)
        if scale is not None:
            kw["scale"] = self._a(scale)
            reads.append(scale)
        if accum_out is not None:
            kw["accum_out"] = self._a(accum_out)
            writes.append(accum_out)
        return self._rec(eng, lambda E: E.activation(o, i, func, **kw), reads, writes)

    def tt(self, out, in0, in1, op, eng="dve"):
        o, a, b = self._a(out), self._a(in0), self._a(in1)
        return self._rec(eng, lambda E: E.tensor_tensor(o, a, b, op), [in0, in1], [out])

    def ts(self, out, in0, s1, op0, s2=None, op1=None, accum_out=None, eng="dve"):
        o, a = self._a(out), self._a(in0)
        s1a, s2a = self._a(s1), self._a(s2)
        kw = {}
        writes = [out]
        if op1 is not None:
            kw["op1"] = op1
        if accum_out is not None:
            kw["accum_out"] = self._a(accum_out)
            writes.append(accum_out)
        return self._rec(eng, lambda E: E.tensor_scalar(o, a, s1a, s2a, op0, **kw),
                         [in0, s1, s2], writes)

    def stt(self, out, in0, scalar, in1, op0, op1, eng="dve"):
        o, a, s, b = self._a(out), self._a(in0), self._a(scalar), self._a(in1)
        return self._rec(eng, lambda E: E.scalar_tensor_tensor(o, a, s, b, op0, op1),
                         [in0, scalar, in1], [out])

    def copy(self, out, in_, eng="dve"):
        o, i = self._a(out), self._a(in_)
        if eng == "act":
            return self._rec(eng, lambda E: E.copy(o, i), [in_], [out])
        return self._rec(eng, lambda E: E.tensor_copy(o, i), [in_], [out])

    def memset(self, out, val, eng="pool"):
        o = self._a(out)
        return self._rec(eng, lambda E: E.memset(o, val), [], [out])

    def recip(self, out, in_):
        o, i = self._a(out), self._a(in_)
        return self._rec("dve", lambda E: E.reciprocal(o, i), [in_], [out])

    def reduce(self, out, in_, op, axis=AX.X, eng="dve"):
        o, i = self._a(out), self._a(in_)
        return self._rec(eng, lambda E: E.tensor_reduce(o, i, axis, op), [in_], [out])

    def affine_select(self, out, in_, pattern, compare_op, fill, base=0, channel_multiplier=0):
        o, i = self._a(out), self._a(in_)
        return self._rec("pool", lambda E: E.affine_select(
            o, i, pattern, compare_op, fill, base=base, channel_multiplier=channel_multiplier),
            [in_], [out])

    def iota(self, out, pattern, base=0, channel_multiplier=0, **kw):
        o = self._a(out)
        return self._rec("pool", lambda E: E.iota(o, pattern, base=base,
                                                   channel_multiplier=channel_multiplier, **kw),
                         [], [out])

    def collective(self, kind, in_, out, groups, op=None):
        i, o = self._a(in_), self._a(out)
        alu = ALU.bypass if op is None else op
        semb = Buf("cc_%d" % len(self.ops))
        semb.inc_val = 1
        return self._rec("pool", lambda E: E.collective_compute(kind, alu, groups, [i], [o]),
                         [in_], [out], is_dma=True, dma_buf=semb)

    def generic(self, eng, fn, reads, writes):
        return self._rec(eng, fn, reads, writes)

    def finish(self, final_wait_outputs=True):
        nc = self.nc
        ops = self.ops
        last_on = {}
        last_dma = {}
        bar_deps = {e: None for e in self.ENGS}
        for op in ops:
            if op.eng == "bar":
                allp = set(last_on.values()) | set(last_dma.values())
                for e in self.ENGS:
                    bar_deps[e] = set(allp) | (bar_deps[e] or set())
                continue
            deps = set()
            for b in op.reads:
                if b.last_w is not None:
                    deps.add(b.last_w)
                if b.is_dram:
                    for w_ in b.wlist.values():
                        deps.add(w_)
                if b.psum:
                    for r in b.readers:
                        if ops[r].eng != op.eng:
                            deps.add(r)
            for b in op.writes:
                if b.last_w is not None:
                    deps.add(b.last_w)
                for r in b.readers:
                    deps.add(r)
            deps.discard(op.gi)
            keep = []
            if bar_deps[op.eng] is not None:
                keep.extend(sorted(bar_deps[op.eng]))
                bar_deps[op.eng] = None
            last_on[op.eng] = op.gi
            if op.is_dma:
                last_dma[id(op.dma_buf)] = op.gi
            for d in deps:
                dop = ops[d]
                if dop.is_dma:
                    keep.append(d)
                    continue
                if dop.eng == op.eng:
                    if op.eng in ("pe", "sp"):
                        continue
                    if op.is_dma:
                        keep.append(d)
                        continue
                    raw = any(b.last_w == d for b in op.reads)
                    if not raw:
                        continue
                keep.append(d)
            op.deps = keep
            for b in op.reads:
                b.readers.append(op.gi)
            for b in op.writes:
                b.last_w = op.gi
                b.readers = []
                if b.is_dram and op.is_dma:
                    b.wlist[id(op.dma_buf)] = op.gi
        needed = set()
        for op in ops:
            for d in op.deps:
                needed.add(d)
        final_dma = [op for op in ops if op.is_dma and any(b.is_dram for b in op.writes)]
        ecount = {e: 0 for e in self.ENGS}
        phase = 0
        nslot = 0
        slot_total = []
        slot_of = {}
        ncc = 0
        abs_cnt = {}
        sem_key = {}
        for op in ops:
            if op.eng == "bar":
                phase += 1
                nslot = 0
                continue
            if op.is_dma:
                b = op.dma_buf
                if b.inc_val == 1:
                    if id(b) not in sem_key:
                        sem_key[id(b)] = ("c", ncc)
                        ncc += 1
                        b.dma_cnt = 0
                    b.dma_cnt += 1
                    abs_cnt[op.gi] = b.dma_cnt
                else:
                    ps_ = slot_of.get(id(b))
                    if ps_ is None or ps_[0] != phase:
                        slot_of[id(b)] = (phase, nslot)
                        if nslot >= len(slot_total):
                            slot_total.append(0)
                        sem_key[id(b)] = ("p", nslot)
                        nslot += 1
                    sl = slot_of[id(b)][1]
                    slot_total[sl] += 1
                    abs_cnt[op.gi] = slot_total[sl]
                op.inc_idx = abs_cnt[op.gi]
            elif op.gi in needed:
                ecount[op.eng] += 1
                op.inc_idx = ecount[op.eng]
        import contextlib
        self._stack = contextlib.ExitStack()
        self._sems = {}
        nsem = 0
        for e in self.ENGS:
            n = max((ecount[e] + SEM_CHUNK - 1) // SEM_CHUNK, 1)
            self._sems[e] = [self._stack.enter_context(nc.semaphore("s_%s_%d" % (e, k))) for k in range(n)]
            nsem += n
        dsem = {}
        for i in range(len(slot_total)):
            dsem[("p", i)] = self._stack.enter_context(nc.semaphore("d_%d" % i))
        for i in range(ncc):
            dsem[("c", i)] = self._stack.enter_context(nc.semaphore("c_%d" % i))
        nsem += len(dsem)
        self.nsem = nsem
        waited = {}
        last_abs = {}
        cur_key = {}
        op_key = {}
        phase = 0
        nslot = 0
        slot_of2 = {}
        for op in ops:
            if op.eng == "bar":
                phase += 1
                nslot = 0
                continue
            if op.is_dma:
                b = op.dma_buf
                if b.inc_val == 1:
                    op_key[op.gi] = sem_key[id(b)]
                else:
                    ps_ = slot_of2.get(id(b))
                    if ps_ is None or ps_[0] != phase:
                        slot_of2[id(b)] = (phase, nslot)
                        nslot += 1
                    op_key[op.gi] = ("p", slot_of2[id(b)][1])
        plan = {e: [] for e in self.ENGS}
        for op in ops:
            if op.eng == "bar":
                continue
            waits = {}
            for d in op.deps:
                dop = ops[d]
                if dop.is_dma:
                    b = dop.dma_buf
                    key = ("d",) + cur_key[id(b)]
                    sem = dsem[cur_key[id(b)]]
                    val = b.inc_val * last_abs[id(b)]
                else:
                    idx = dop.inc_idx - 1
                    ch = idx // SEM_CHUNK
                    val = idx % SEM_CHUNK + 1
                    key = ("e", dop.eng, ch)
                    sem = self._sems[dop.eng][ch]
                    for c2 in range(ch):
                        waited[(op.eng, ("e", dop.eng, c2))] = SEM_CHUNK
                cur = waited.get((op.eng, key), 0)
                if val > cur:
                    waited[(op.eng, key)] = val
                    if key not in waits or waits[key][1] < val:
                        waits[key] = (sem, val)
            plan[op.eng].append((op, list(waits.values())))
            if op.is_dma:
                last_abs[id(op.dma_buf)] = abs_cnt[op.gi]
                cur_key[id(op.dma_buf)] = op_key[op.gi]
        self.stats = {e: len(plan[e]) for e in self.ENGS}
        self.stats["nsem"] = nsem
        self.stats["waits"] = sum(len(w) for e in self.ENGS for _, w in plan[e])
        if self.arena is not None:
            self.stats["arena_peak_words"] = self.a_peak
        for e in self.ENGS:
            E = self.e[e]
            for op, waits in plan[e]:
                for sem, val in waits:
                    E.wait_ge(sem, val)
                ins = op.fn(E)
                if op.is_dma:
                    ins.then_inc(dsem[op_key[op.gi]], op.dma_buf.inc_val)
                elif op.inc_idx is not None:
                    idx = op.inc_idx - 1
                    ins.then_inc(self._sems[e][idx // SEM_CHUNK], 1)
        if final_wait_outputs:
            fin = {}
            for op in final_dma:
                kk = op_key[op.gi]
                v = op.dma_buf.inc_val * abs_cnt[op.gi]
                if kk not in fin or fin[kk] < v:
                    fin[kk] = v
            for kk, v in fin.items():
                nc.sync.wait_ge(dsem[kk], v)
        return self.stats


L = 8192
NT = L // 512
NCH = L // 128
NORM_EPS = 1e-6
GN_EPS = 64e-5
C0 = math.exp(-0.5)


def load_hT(k, hT_tile, hsrc, j):
    half, jj = j // 8, j % 8
    v = hT_tile.re("p (g q) t -> p g q t", q=2)
    for nb in range(2):
        k.dma(v[:, :, nb, :], hsrc[half][nb][:, jj * 512:(jj + 1) * 512].re("(g i) t -> i g t", i=128),
              eng="sp" if nb == 0 else "pool")


ATT_KNOBS = {}


def emit_attn(k, io, lambda_init):
    xT, hsrc, pw, wfm, wtm, tabs, vecs, lam = (io.get("xT"), io.get("hT"), io["pw"], io["wfm"], io["wtm"],
                                                io["tabs"], io["vecs"], io["lam"])
    ydst, ydt = io.get("y"), io["ydt"]

    ones_bf = k.sb("ones_bf", [128, 128], BF16)
    k.memset(ones_bf, 1.0)
    eps_t = k.sb("eps_t", [128, 1])
    k.memset(eps_t, NORM_EPS)
    pw_sb = k.sb("pw_sb", [128, 8])
    k.dma(pw_sb, pw)
    vec_sb = k.sb("vec_sb", [128, 8])
    k.dma(vec_sb, vecs)
    lam_sb = k.sb("lam_sb", [128, 256])
    k.dma(lam_sb, lam)
    w_fm = k.sb("w_fm", [128, 8, 1280], BF16)
    w_tm = k.sb("w_tm", [128, 8, 256], BF16)
    wst = [k.sb("wst%d" % i, [128, 8, 256]) for i in range(2)]
    for i in range(6):
        st = wst[i % 2]
        if i < 5:
            k.dma(st, wfm[:, i * 256:(i + 1) * 256].re("(c p) m -> p c m", p=128),
                  eng="pool" if i % 2 else "sp")
            k.copy(w_fm[:, :, i * 256:(i + 1) * 256], st, eng="pool")
        else:
            k.dma(st, wtm.re("(c p) m -> p c m", p=128), eng="pool")
            k.copy(w_tm, st, eng="pool")

    lt = k.sb("lam_t", [128, 128])
    s12 = k.sb("lam_s", [128, 2])
    k.tt(lt[:, 0:64], lam_sb[:, 0:64], lam_sb[:, 64:128], ALU.mult)
    k.tt(lt[:, 64:128], lam_sb[:, 128:192], lam_sb[:, 192:256], ALU.mult)
    k.reduce(s12[:, 0:1], lt[:, 0:64], ALU.add)
    k.reduce(s12[:, 1:2], lt[:, 64:128], ALU.add)
    e12 = k.sb("lam_e", [128, 2])
    k.act(e12, s12, AF.Exp)
    neg_lam = k.sb("neg_lam", [128, 1])
    k.stt(neg_lam, e12[:, 1:2], -float(lambda_init), e12[:, 0:1], ALU.add, ALU.subtract)
    wn_s = k.sb("wn_s", [128, 1])
    k.ts(wn_s, vec_sb[:, 4:5], 1.0 - float(lambda_init), ALU.mult)

    xbuf = [k.sb("xbuf%d" % i, [128, 8, 512]) for i in range(2)]
    hT = [k.sb("hT%d" % i, [128, 8, 512], BF16) for i in range(2)]
    sq = [k.sb("sq%d" % i, [128, 512], BF16) for i in range(3)]
    rstd = [k.sb("rstd%d" % i, [128, 512]) for i in range(2)]
    tab = [k.sb("tab%d" % i, [128, 2, 512]) for i in range(2)]
    tmp = [k.sb("tmp%d" % i, [128, 512]) for i in range(4)]
    qT = k.sb("qT", [128, L], BF16)
    kT = k.sb("kT", [128, L], BF16)
    gT = k.sb("gT", [128, L], BF16)
    v_tm = k.sb("v_tm", [128, 64, 128], BF16)
    Pb2 = [k.sb("Pb2_%d" % i, [128, 2, 512], BF16) for i in range(3)]
    acc2 = [k.sb("acc2_%d" % i, [128, 2, 512]) for i in range(2)]
    ones_f = k.sb("ones_f", [128, 128])
    k.memset(ones_f, 1.0)
    ost = [k.sb("ost%d" % i, [128, 512], ydt) for i in range(2)]
    fin = [k.sb("fin%d" % i, [128, 512]) for i in range(5)]
    bank = [k.ps("bank%d" % i, [128, 512]) for i in range(8)]

    for kind in range(2):
        wc = kind * 5
        def stage1(j):
            k.dma(tab[j % 2], tabs[2 * kind:2 * kind + 2, :, j * 512:(j + 1) * 512]
                  .re("a p t -> p a t"), eng="pool")
            if hsrc is not None:
                load_hT(k, hT[j % 2], hsrc, j)
                return
            xt = xbuf[j % 2]
            k.dma(xt, xT[:, j * 512:(j + 1) * 512].re("(c p) t -> p c t", p=128),
                  eng="sp")
            for c in range(8):
                s = sq[c % 3]
                k.act(s, xt[:, c, :], AF.Square)
                k.matmul(bank[0], ones_bf, s, start=(c == 0), stop=(c == 7))
            r = rstd[j % 2]
            k.act(r, bank[0], AF.Ln, scale=1.0 / 1024, bias=eps_t)
            k.act(r, r, AF.Exp, scale=-0.5)
            for c in range(8):
                k.stt(hT[j % 2][:, c, :], xt[:, c, :], pw_sb[:, c:c + 1], r, ALU.mult, ALU.mult)

        def proj_fm(ps, grp, j):
            h = hT[j % 2]
            for c in range(8):
                k.matmul(ps, w_fm[:, c, grp * 128:(grp + 1) * 128], h[:, c, :],
                         start=(c == 0), stop=(c == 7))

        def stage2(j):
            h = hT[j % 2]
            tb = tab[j % 2]
            ts_ = slice(j * 512, (j + 1) * 512)
            for which, dst in ((0, qT), (1, kT)):
                pa, pb = bank[1 + 2 * which], bank[2 + 2 * which]
                proj_fm(pa, wc + 2 * which, j)
                proj_fm(pb, wc + 2 * which + 1, j)
                t1, t2 = tmp[2 * which], tmp[2 * which + 1]
                if kind == 0:
                    k.tt(t1, pa, tb[:, 0, :], ALU.mult)
                    k.tt(t2, pb, tb[:, 1, :], ALU.mult)
                    k.tt(dst[:, ts_], t1, t2, ALU.add, eng="pool")
                else:
                    s = sq[which]
                    k.act(s, pa, AF.Square)
                    k.matmul(bank[7], ones_bf, s)
                    rr = fin[which]
                    k.act(rr, bank[7], AF.Ln, scale=1.0 / 128, bias=eps_t)
                    k.act(rr, rr, AF.Exp, scale=-0.5)
                    k.stt(t1, pa, vec_sb[:, 2 * which:2 * which + 1], tb[:, 0, :], ALU.mult, ALU.mult)
                    k.stt(t2, pb, vec_sb[:, 2 * which + 1:2 * which + 2], tb[:, 1, :], ALU.mult, ALU.mult)
                    k.tt(t1, t1, t2, ALU.add, eng="pool")
                    k.tt(dst[:, ts_], t1, rr, ALU.mult, eng="pool")
            proj_fm(bank[5], wc + 4, j)
            k.act(gT[:, ts_], bank[5], AF.Silu)
            pv = bank[6]
            for s4 in range(4):
                for c in range(8):
                    k.matmul(pv[:, s4 * 128:(s4 + 1) * 128], h[:, c, s4 * 128:(s4 + 1) * 128],
                             w_tm[:, c, kind * 128:(kind + 1) * 128],
                             start=(c == 0), stop=(c == 7))
            k.copy(v_tm[:, j * 4:(j + 1) * 4, :], pv.re("p (s e) -> p s e", s=4), eng="dve")

        for j in range(NT + 1):
            if j < NT:
                stage1(j)
            if j >= 1:
                stage2(j - 1)

        nm = 2 if kind == 0 else 1
        dk = 64 if kind == 0 else 128
        scale = dk ** -0.5
        O = [bank[0], bank[1]][:nm]
        Sb = [[bank[2], bank[3]], [bank[4], bank[5]]]
        misc = bank[7]
        pairs = [(qb, kc) for qb in range(16) for kc in range(64)]
        LA = 2
        npair = len(pairs)

        def finalize(qb):
            qs = slice(qb * 512, (qb + 1) * 512)
            o = []
            for m in range(nm):
                k.matmul(misc, ones_f, acc2[0][:, m, :], start=True, stop=False)
                k.matmul(misc, ones_f, acc2[1][:, m, :], start=False, stop=True)
                r = fin[m]
                k.act(r, misc, AF.Ln)
                k.act(r, r, AF.Exp, scale=-1.0)
                om = fin[2 + m]
                k.tt(om, O[m], r, ALU.mult)
                o.append(om)
            y = ost[qb % 2]
            if kind == 0:
                od = fin[4]
                k.stt(od, o[1], neg_lam, o[0], ALU.mult, ALU.add)
                s = sq[0]
                k.act(s, od, AF.Square)
                k.matmul(misc, ones_bf, s)
                rr = fin[0]
                k.act(rr, misc, AF.Ln, scale=1.0 / 128, bias=eps_t)
                k.act(rr, rr, AF.Exp, scale=-0.5)
                k.stt(od, od, wn_s, rr, ALU.mult, ALU.mult)
                k.tt(y, od, gT[:, qs], ALU.mult, eng="pool")
            else:
                k.tt(y, o[0], gT[:, qs], ALU.mult, eng="pool")
            if io.get("yfn") is not None:
                k.dma(io["yfn"](kind, qb), y, eng="sp")
            else:
                k.dma(ydst[kind][:, qs], y, eng="sp")

        for idx in range(npair + LA):
            if idx < npair:
                qb, kc = pairs[idx]
                for m in range(nm):
                    k.matmul(Sb[m][idx % 2], kT[m * dk:(m + 1) * dk, kc * 128:(kc + 1) * 128],
                             qT[m * dk:(m + 1) * dk, qb * 512:(qb + 1) * 512])
                for m in range(nm):
                    k.act(Pb2[idx % 3][:, m, :], Sb[m][idx % 2], AF.Exp, scale=scale)
            i2 = idx - LA
            if i2 >= 0:
                qb, kc = pairs[i2]
                P2 = Pb2[i2 % 3]
                for m in range(nm):
                    k.matmul(O[m], v_tm[:, kc, :], P2[:, m, :], start=(kc == 0), stop=(kc == 63))
                acc = acc2[kc % 2]
                if kc < 2:
                    k.copy(acc[:, 0:nm, :], P2[:, 0:nm, :], eng="dve")
                else:
                    k.tt(acc[:, 0:nm, :], acc[:, 0:nm, :], P2[:, 0:nm, :], ALU.add, eng="dve")
                if kc == 63:
                    finalize(qb)
                    if qb % 8 == 7 and io.get("y_done") is not None:
                        io["y_done"](kind, qb // 8)


def emit_ssd(k, io):
    stop = 99
    xT, hsrc, pw, wfm, wdt, cw, cb, ssmv, cst = (io.get("xT"), io.get("hT"), io["pw"], io["wfm"], io["wdt"],
                                                 io["cw"], io["cb"], io["ssmv"], io["cst"])
    yT, ydt = io.get("y"), io["ydt"]

    ones_bf = k.sb("ones_bf", [128, 128], BF16)
    k.memset(ones_bf, 1.0)
    ones_f = k.sb("ones_f", [128, 128])
    k.memset(ones_f, 1.0)
    eps_t = k.sb("eps_t", [128, 1])
    k.memset(eps_t, NORM_EPS)
    one_t = k.sb("one_t", [128, 1])
    k.memset(one_t, 1.0)
    cst_sb = k.sb("cst_sb", [128, 5, 128])
    k.dma(cst_sb, cst)
    tri = [cst_sb[:, 0, :], cst_sb[:, 1, :]]
    smask = [cst_sb[:, 2, :], cst_sb[:, 3, :]]
    ident_f = cst_sb[:, 4, :]
    ident_bf = k.sb("ident_bf", [128, 128], BF16)
    k.copy(ident_bf, ident_f, eng="dve")
    pw_sb = k.sb("pw_sb", [128, 8])
    k.dma(pw_sb, pw)
    cw_sb = k.sb("cw_sb", [128, 3, 5])
    k.dma(cw_sb, cw)
    cb_sb = k.sb("cb_sb", [128, 3])
    k.dma(cb_sb, cb)
    sv = k.sb("sv", [128, 16])
    k.dma(sv, ssmv)
    w_fm = k.sb("w_fm", [128, 8, 512], BF16)
    w_dt = k.sb("w_dt", [128, 8, 4], BF16)
    wst2 = k.sb("wst2", [128, 8, 4])
    k.dma(wst2, wdt.re("(c p) m -> p c m", p=128), eng="pool")
    k.copy(w_dt, wst2, eng="pool")

    xbuf = [k.sb("xbuf%d" % i, [128, 8, 512]) for i in range(1)]
    k.dma(xbuf[0], wfm.re("(c p) m -> p c m", p=128))
    k.copy(w_fm, xbuf[0], eng="pool")
    hT = [k.sb("hT%d" % i, [128, 8, 512], BF16) for i in range(2)]
    sq = [k.sb("sq%d" % i, [128, 512], BF16) for i in range(3)]
    rstd = [k.sb("rstd%d" % i, [128, 512]) for i in range(1)] * 2
    upad = [k.sb("upad%d" % i, [128, 3, 516]) for i in range(3)]
    acc = [k.sb("acc%d" % i, [128, 512]) for i in range(3)]
    zsT = k.sb("zsT", [128, L], BF16)
    xcT = k.sb("xcT", [128, L], BF16)
    BT = k.sb("BT", [128, L], BF16)
    CT = k.sb("CT", [128, L], BF16)
    dst3 = [xcT, BT, CT]
    dtraw = k.sb("dtraw", [128, 4, NCH])
    bank = [k.ps("bank%d" % i, [128, 512]) for i in range(7)]
    tbank = k.ps("tbank", [128, 1024], BF16)

    k.memset(upad[0][:, :, 0:2], 0.0)

    def stage1(j):
        if hsrc is not None:
            load_hT(k, hT[j % 2], hsrc, j)
            return
        xt = xbuf[0]
        k.dma(xt, xT[:, j * 512:(j + 1) * 512].re("(c p) t -> p c t", p=128), eng="sp")
        for c in range(8):
            s = sq[c % 3]
            k.act(s, xt[:, c, :], AF.Square)
            k.matmul(bank[0], ones_bf, s, start=(c == 0), stop=(c == 7))
        r = rstd[j % 2]
        k.act(r, bank[0], AF.Ln, scale=1.0 / 1024, bias=eps_t)
        k.act(r, r, AF.Exp, scale=-0.5)
        for c in range(8):
            k.stt(hT[j % 2][:, c, :], xt[:, c, :], pw_sb[:, c:c + 1], r, ALU.mult, ALU.mult)

    def stage2(j):
        h = hT[j % 2]
        ts_ = slice(j * 512, (j + 1) * 512)
        up = upad[j % 3]
        for grp in range(4):
            ps = bank[1 + grp]
            for c in range(8):
                k.matmul(ps, w_fm[:, c, grp * 128:(grp + 1) * 128], h[:, c, :],
                         start=(c == 0), stop=(c == 7))
            if grp == 0:
                k.act(zsT[:, ts_], ps, AF.Silu)
            else:
                k.copy(up[:, grp - 1, 2:514], ps, eng="act")
        pdt = bank[5]
        for s4 in range(4):
            for c in range(8):
                k.matmul(pdt[:, s4 * 4:(s4 + 1) * 4], h[:, c, s4 * 128:(s4 + 1) * 128], w_dt[:, c, :],
                         start=(c == 0), stop=(c == 7))
        k.copy(dtraw[:, :, j * 4:(j + 1) * 4].re("p h s -> p s h"),
               pdt[:, 0:16].re("p (s h) -> p s h", s=4), eng="dve")

    def conv(j):
        up = upad[j % 3]
        if j > 0:
            k.copy(up[:, :, 0:2], upad[(j - 1) % 3][:, :, 512:514], eng="pool")
        if j < NT - 1:
            k.copy(up[:, :, 514:516], upad[(j + 1) % 3][:, :, 2:4], eng="pool")
        else:
            k.memset(up[:, :, 514:516], 0.0)
        ts_ = slice(j * 512, (j + 1) * 512)
        for ch in range(3):
            a = acc[ch]
            k.ts(a, up[:, ch, 0:512], cw_sb[:, ch, 0:1], ALU.mult)
            for o in range(1, 5):
                k.stt(a, up[:, ch, o:o + 512], cw_sb[:, ch, o:o + 1], a, ALU.mult, ALU.add)
            k.act(dst3[ch][:, ts_], a, AF.Silu, bias=cb_sb[:, ch:ch + 1])

    for j in range(NT + 2):
        if j < NT:
            stage1(j)
        if 1 <= j <= NT:
            stage2(j - 1)
        if j >= 2:
            conv(j - 2)


    dt = k.sb("dt", [128, 4, NCH])
    a_ = k.sb("a_", [128, 4, NCH])
    cum = k.sb("cum", [128, 4, NCH])
    dtd = k.sb("dtd", [128, 4, NCH])
    etot = k.sb("etot", [128, 4, NCH])
    aneg = k.sb("aneg", [128, 4])
    k.act(aneg, sv[:, 4:8], AF.Exp)
    k.ts(aneg, aneg, -1.0, ALU.mult)
    for hd in range(4):
        k.act(dt[:, hd, :], dtraw[:, hd, :], AF.Exp, bias=sv[:, hd:hd + 1])
    k.act(dt, dt, AF.Ln, bias=one_t)
    for hd in range(4):
        k.ts(a_[:, hd, :], dt[:, hd, :], aneg[:, hd:hd + 1], ALU.mult)
    pc = bank[0]
    k.matmul(pc[:, 0:128], tri[0], a_[:, 0:2, :].re("p h c -> p (h c)"))
    k.matmul(pc[:, 128:256], tri[1], a_[:, 2:4, :].re("p h c -> p (h c)"))
    k.matmul(pc[:, 256:512], ones_f, a_.re("p h c -> p (h c)"))
    k.copy(cum.re("p h c -> p (h c)"), pc[:, 0:256], eng="dve")
    k.act(etot.re("p h c -> p (h c)"), pc[:, 256:512], AF.Exp)
    k.tt(dtd.re("p h c -> p (h c)"), pc[:, 256:512], cum.re("p h c -> p (h c)"), ALU.subtract)
    k.act(dtd, dtd, AF.Exp)
    k.tt(dtd, dtd, dt, ALU.mult)

    Sf_all = k.sb("Sf_all", [128, NCH, 128], BF16)
    Sb_all = k.sb("Sb_all", [128, NCH, 128], BF16)
    Srun = [k.sb("Srun%d" % i, [128, 128]) for i in range(2)]
    btm = [k.sb("btm%d" % i, [128, 128], BF16) for i in range(2)]
    xdtd = [k.sb("xdtd%d" % i, [128, 128], BF16) for i in range(2)]
    for d in range(2):
        k.memset(Srun[d], 0.0)
        order = range(NCH) if d == 0 else range(NCH - 1, -1, -1)
        S_all = Sf_all if d == 0 else Sb_all
        for i, c in enumerate(order):
            cs = slice(c * 128, (c + 1) * 128)
            tb = tbank[:, (i % 2) * 256:(i % 2) * 256 + 256]
            k.transpose(tb[:, 0:128], xcT[:, cs], ident_bf)
            k.transpose(tb[:, 128:256], BT[:, cs], ident_bf)
            bt = btm[i % 2]
            k.copy(bt, tb[:, 128:256], eng="act")
            xd = xdtd[i % 2]
            for h in range(2):
                hd = d * 2 + h
                k.ts(xd[:, h * 64:(h + 1) * 64], tb[:, h * 64:(h + 1) * 64], dtd[:, hd, c:c + 1], ALU.mult)
            st = bank[1 + i % 2]
            k.matmul(st[:, 0:128], bt, xd)
            k.copy(S_all[:, c, :], Srun[d], eng="act")
            for h in range(2):
                hd = d * 2 + h
                hb = slice(h * 64, (h + 1) * 64)
                k.stt(Srun[d][:, hb], Srun[d][:, hb], etot[:, hd, c:c + 1], st[:, hb], ALU.mult, ALU.add)

    lhsD = [k.sb("lhsD%d" % i, [128, 4, 128]) for i in range(1)] * 2
    abc = [k.sb("abc%d" % i, [128, 4, 128]) for i in range(1)] * 2
    Lexp = [k.sb("Lexp%d" % i, [128, 512]) for i in range(2)]
    Ebc = [k.sb("Ebc%d" % i, [128, 512]) for i in range(2)]
    Gm = [k.sb("Gm%d" % i, [128, 2, 128]) for i in range(2)]
    MT = [k.sb("MT%d" % i, [128, 4, 128], BF16) for i in range(2)]
    Ct = [k.sb("Ct%d" % i, [128, 4, 128], BF16) for i in range(2)]
    xtm = [k.sb("xtm%d" % i, [128, 128]) for i in range(2)]
    xdt = [k.sb("xdt%d" % i, [128, 4, 64], BF16) for i in range(2)]
    y_sb = [k.sb("y_sb%d" % i, [128, 128]) for i in range(2)]
    ost = [k.sb("ost%d" % i, [128, 512], ydt) for i in range(2)]
    def front(c):
        cs = slice(c * 128, (c + 1) * 128)
        p = c % 2
        tb = tbank[:, p * 256:p * 256 + 128]
        k.transpose(tb, xcT[:, cs], ident_bf)
        k.copy(xtm[p], tb, eng="act")
        for hd in range(4):
            k.ts(xdt[p][:, hd, :], xtm[p][:, (hd % 2) * 64:(hd % 2) * 64 + 64], dt[:, hd, c:c + 1], ALU.mult)
        for hd in range(4):
            d = hd // 2
            k.ts(lhsD[p][:, hd, :], smask[d], a_[:, hd, c:c + 1], ALU.mult)
            k.ts(abc[p][:, hd, :], ones_f, a_[:, hd, c:c + 1], ALU.mult)
        Dps, Cps = bank[3], bank[4]
        for hd in range(4):
            d = hd // 2
            k.matmul(Dps[:, hd * 128:(hd + 1) * 128], lhsD[p][:, hd, :], tri[d])
        for hd in range(4):
            d = hd // 2
            k.matmul(Cps[:, hd * 128:(hd + 1) * 128], abc[p][:, hd, :], tri[d])
        k.act(Lexp[p], Dps, AF.Exp)
        k.act(Ebc[p], Cps, AF.Exp)
        Gps = bank[5]
        k.matmul(Gps[:, 0:128], BT[:, cs], CT[:, cs])
        for d in range(2):
            k.tt(Gm[p][:, d, :], Gps[:, 0:128], tri[d], ALU.mult)

    def back(c):
        cs = slice(c * 128, (c + 1) * 128)
        p = c % 2
        for hd in range(4):
            d = hd // 2
            k.tt(MT[p][:, hd, :], Lexp[p][:, hd * 128:(hd + 1) * 128], Gm[p][:, d, :], ALU.mult)
            k.tt(Ct[p][:, hd, :], Ebc[p][:, hd * 128:(hd + 1) * 128], CT[:, cs], ALU.mult)
        Yps = bank[6]
        for h in range(2):
            hb = slice(h * 64, (h + 1) * 64)
            k.matmul(Yps[:, hb], MT[p][:, h, :], xdt[p][:, h, :], start=True, stop=False)
            k.matmul(Yps[:, hb], MT[p][:, 2 + h, :], xdt[p][:, 2 + h, :], start=False, stop=False)
            k.matmul(Yps[:, hb], Ct[p][:, h, :], Sf_all[:, c, hb], start=False, stop=False)
            k.matmul(Yps[:, hb], Ct[p][:, 2 + h, :], Sb_all[:, c, hb], start=False, stop=True)
        for h in range(2):
            hb = slice(h * 64, (h + 1) * 64)
            k.stt(y_sb[p][:, hb], xtm[p][:, hb], sv[:, 8 + h:9 + h], Yps[:, hb], ALU.mult, ALU.add)
        yTp = bank[1 + p]
        k.transpose(yTp[:, 0:128], y_sb[p], ident_f)
        o = ost[(c // 4) % 2]
        k.tt(o[:, (c % 4) * 128:(c % 4 + 1) * 128], yTp[:, 0:128], zsT[:, cs], ALU.mult)
        if c % 4 == 3:
            k.dma(io["yfn"](None, c // 4) if io.get("yfn") is not None else yT[:, (c - 3) * 128:(c + 1) * 128], o, eng="sp")

    for c in range(NCH + 1):
        if c < NCH:
            front(c)
        if c >= 1:
            back(c - 1)


L = 8192
NT = L // 512
NORM_EPS = 1e-6
GN_EPS = 64e-5
C0 = math.exp(-0.5)


def emit_rwkv(k, io):
    stop, nq_lim, do_chain, do_pre, pre_lim = 99, None, True, True, 9
    xT, hsrc, pw, wfm, mu, w2a2, pvd, cm, c128 = (io.get("xT"), io.get("hT"), io["pw"], io["wfm"], io["mu"],
                                                  io["w2a2"], io["pvd"], io["cm"], io["c128"])
    wkv_scr, bon_scr, yT, ydt = io["wkv_scr"], io["bon_scr"], io.get("y"), io["ydt"]

    ones_bf = k.sb("ones_bf", [128, 128], BF16)
    k.memset(ones_bf, 1.0)
    eps_t = k.sb("eps_t", [128, 1])
    k.memset(eps_t, NORM_EPS)
    epsg_t = k.sb("epsg_t", [128, 1])
    k.memset(epsg_t, GN_EPS)
    tiny_t = k.sb("tiny_t", [128, 1])
    k.memset(tiny_t, 1e-24)
    pw_sb = k.sb("pw_sb", [128, 8])
    k.dma(pw_sb, pw)
    mu_sb = k.sb("mu_sb", [128, 2, 4])
    k.dma(mu_sb, mu)
    w2a2_sb = k.sb("w2a2_sb", [128, 2, 128])
    k.dma(w2a2_sb, w2a2)
    pv = k.sb("pv", [128, 12])
    k.dma(pv, pvd)
    omk = k.sb("omk", [128, 1])
    k.ts(omk, pv[:, 4:5], -1.0, ALU.mult, 1.0, ALU.add)
    cm_sb = k.sb("cm_sb", [64, 2, 704])
    k.dma(cm_sb, cm)
    c128_sb = k.sb("c128_sb", [128, 2, 128])
    k.dma(c128_sb, c128)
    blk_bf = k.sb("blk_bf", [128, 128], BF16)
    k.copy(blk_bf, c128_sb[:, 0, :], eng="dve")
    ident_f = c128_sb[:, 1, :]
    ident_bf = k.sb("ident_bf", [128, 128], BF16)
    k.copy(ident_bf, ident_f, eng="dve")
    id64_bf = k.sb("id64_bf", [64, 4, 64], BF16)
    for h in range(4):
        k.copy(id64_bf[:, h, :], cm_sb[:, 0, 640:704], eng="dve")
    maskSC = [cm_sb[:, d, 0:512] for d in range(2)]
    maskN = [cm_sb[:, d, 512:640] for d in range(2)]

    xbuf = k.sb("xbuf", [128, 8, 512])
    w_fm = k.sb("w_fm", [128, 8, 640], BF16)
    k.dma(xbuf, wfm[:, 0:512].re("(c p) m -> p c m", p=128))
    k.copy(w_fm[:, :, 0:512], xbuf, eng="pool")
    k.dma(xbuf[:, :, 0:128], wfm[:, 512:640].re("(c p) m -> p c m", p=128))
    k.copy(w_fm[:, :, 512:640], xbuf[:, :, 0:128], eng="pool")

    hT = k.sb("hT", [128, 8, 512], BF16)
    sq = [k.sb("sq%d" % i, [128, 512], BF16) for i in range(2)]
    rstd = k.sb("rstd", [128, 512])
    upad = [k.sb("upad%d" % d, [128, 4, 514]) for d in range(2)]
    ul = k.sb("ul", [128, 4, 512])
    tmpd = [k.sb("tmpd%d" % i, [128, 512]) for i in range(2)]
    gT = k.sb("gT", [128, L], BF16)
    twd = k.sb("twd", [64, 512])
    sg = k.sb("sg", [128, 512])
    aic = k.sb("aic", [128, 512])
    kk = k.sb("kk", [128, 512])
    kkn = k.sb("kkn", [128, 512])
    rs = k.sb("rs", [128, 512])
    tka = k.sb("tka", [128, 512])
    k2 = k.sb("k2", [128, 512])
    bvec = k.sb("bvec", [128, 512])
    rk = k.sb("rk", [128, 512], BF16)
    bon = [k.sb("bon%d" % i, [128, 512]) for i in range(2)]
    cs = [k.sb("cs%d" % i, [128, 512]) for i in range(2)]
    csm = k.sb("csm", [128, 512])
    Epos = k.sb("Epos", [128, 512])
    Eneg = k.sb("Eneg", [128, 512])
    Eprev = k.sb("Eprev", [128, 512])
    RopT = [[k.sb("RopT%d%d" % (d, s), [128, 8, 2, 64], BF16) for s in range(2)] for d in range(2)]
    LopT = [[k.sb("LopT%d%d" % (d, s), [128, 8, 2, 64], BF16) for s in range(2)] for d in range(2)]
    vTb = [[k.sb("vTb%d%d" % (d, s), [128, 512], BF16) for s in range(2)] for d in range(2)]
    eC = [[k.sb("eC%d%d" % (d, s), [128, 8]) for s in range(2)] for d in range(2)]
    NR = 4
    tm = [[k.sb("tm%d%d" % (d, i), [64, 384], BF16) for i in range(NR)] for d in range(2)]
    scmB = [k.sb("scmB%d" % i, [64, 2, 4, 128], BF16) for i in range(NR)]
    TtB = [k.sb("TtB%d" % i, [64, 4, 64], BF16) for i in range(NR)]
    NMt2 = [[k.sb("NMt%d_%d" % (a, i), [64, 2, 4, 64], BF16) for i in range(2)] for a in range(2)]
    Pt2 = [[k.sb("Pt%d_%d" % (a, i), [64, 4, 64], BF16) for i in range(2)] for a in range(2)]
    S32 = [k.sb("S32_%d" % d, [128, 128]) for d in range(2)]
    t1 = [k.sb("t1_%d" % d, [128, 128]) for d in range(2)]
    Sbf = [[k.sb("Sbf%d%d" % (d, i), [128, 128], BF16) for i in range(2)] for d in range(2)]
    Zb = [k.sb("Zb%d" % d, [64, 128], BF16) for d in range(2)]
    Ub = [k.sb("Ub%d" % d, [64, 128], BF16) for d in range(2)]
    Yo = [[k.sb("Yo%d%d" % (d, i), [64, 128]) for i in range(2)] for d in range(2)]
    bankA = [k.ps("bankA%d" % i, [128, 512]) for i in range(2)]
    bSC = k.ps("bSC", [128, 512])
    bNM = k.ps("bNM", [128, 512])
    bP = k.ps("bP", [128, 512])
    bCH = [k.ps("bCH%d" % d, [128, 512]) for d in range(2)]
    tb32 = k.ps("tbank", [128, 512])
    tbank = T(tb32.buf, tb32.ap.bitcast(BF16))

    for d in range(2):
        k.memset(S32[d], 0.0)
        k.memset(Sbf[d][0], 0.0)
    k.memset(upad[0][:, :, 0:1], 0.0)
    k.memset(upad[1][:, :, 513:514], 0.0)

    state = {"a": 0}

    def nbank():
        return bankA[0]

    def phaseA(d, j, slot):
        first = (j == 0) if d == 0 else (j == NT - 1)
        up = upad[d]
        if not first:
            if d == 0:
                k.copy(up[:, :, 0:1], up[:, :, 512:513], eng="pool")
            else:
                k.copy(up[:, :, 513:514], up[:, :, 1:2], eng="pool")
        if hsrc is not None:
            load_hT(k, hT, hsrc, j)
        else:
            k.dma(xbuf, xT[:, j * 512:(j + 1) * 512].re("(c p) t -> p c t", p=128), eng="sp")
            ps = nbank()
            for c in range(8):
                s = sq[c % 2]
                k.act(s, xbuf[:, c, :], AF.Square)
                k.matmul(ps, ones_bf, s, start=(c == 0), stop=(c == 7))
            k.act(rstd, ps, AF.Ln, scale=1.0 / 1024, bias=eps_t)
            k.act(rstd, rstd, AF.Exp, scale=-0.5)
            for c in range(8):
                k.stt(hT[:, c, :], xbuf[:, c, :], pw_sb[:, c:c + 1], rstd, ALU.mult, ALU.mult)
        ts_ = slice(j * 512, (j + 1) * 512)
        for grp in range(5 if d == 0 else 4):
            ps = nbank()
            for c in range(8):
                k.matmul(ps, w_fm[:, c, grp * 128:(grp + 1) * 128], hT[:, c, :],
                         start=(c == 0), stop=(c == 7))
            if grp < 4:
                k.copy(up[:, grp, 1:513], ps, eng="act")
            else:
                k.act(gT[:, ts_], ps, AF.Silu)
        sh = slice(0, 512) if d == 0 else slice(2, 514)
        for grp in range(4):
            td = tmpd[grp % 2]
            k.tt(td, up[:, grp, sh], up[:, grp, 1:513], ALU.subtract, eng="pool")
            k.stt(ul[:, grp, :], td, mu_sb[:, d, grp:grp + 1], up[:, grp, 1:513], ALU.mult, ALU.add)
        r, kx, v, wa = ul[:, 0, :], ul[:, 1, :], ul[:, 2, :], ul[:, 3, :]
        k.copy(vTb[d][slot], v, eng="pool")
        k.act(twd, wa[0:64, :], AF.Tanh)
        pxw = nbank()
        k.matmul(pxw, w2a2_sb[0:64, d, :], twd)
        k.act(sg, pxw, AF.Sigmoid, bias=pv[:, d:d + 1])
        pxa = nbank()
        k.matmul(pxa, w2a2_sb[64:128, d, :], wa[64:128, :])
        k.act(aic, pxa, AF.Sigmoid, bias=pv[:, 2:3])
        k.ts(kk, kx, pv[:, 3:4], ALU.mult)
        k.act(sq[0], kk, AF.Square)
        pss = nbank()
        k.matmul(pss, blk_bf, sq[0])
        k.act(rs, pss, AF.Ln, bias=tiny_t)
        k.act(rs, rs, AF.Exp, scale=-0.5)
        k.tt(kkn, kk, rs, ALU.mult)
        k.ts(tka, aic, pv[:, 4:5], ALU.mult, omk, ALU.add)
        k.tt(k2, kx, tka, ALU.mult)
        k.tt(bvec, kkn, aic, ALU.mult, eng="pool")
        k.stt(rk, r, pv[:, 6:7], k2, ALU.mult, ALU.mult)
        pbs = nbank()
        k.matmul(pbs, blk_bf, rk)
        bo = bon[d]
        k.tt(bo, pbs, v, ALU.mult)
        k.dma(bon_scr[d, :, ts_], bo, eng="sp")
        src = sg
        i = 0
        for s in (1, 2, 4, 8, 16, 32):
            dst = cs[i % 2]
            sv_, dv_ = src.re("p (c t) -> p c t", t=64), dst.re("p (c t) -> p c t", t=64)
            if d == 0:
                k.tt(dv_[:, :, s:64], sv_[:, :, s:64], sv_[:, :, 0:64 - s], ALU.add)
                k.copy(dv_[:, :, 0:s], sv_[:, :, 0:s], eng="pool")
            else:
                k.tt(dv_[:, :, 0:64 - s], sv_[:, :, 0:64 - s], sv_[:, :, s:64], ALU.add)
                k.copy(dv_[:, :, 64 - s:64], sv_[:, :, 64 - s:64], eng="pool")
            src = dst
            i += 1
        csf = src
        k.tt(csm, csf, sg, ALU.subtract, eng="pool")
        k.act(Epos, csf, AF.Exp, scale=-C0)
        k.act(Eneg, csf, AF.Exp, scale=C0)
        k.act(Eprev, csm, AF.Exp, scale=-C0)
        last = 63 if d == 0 else 0
        k.copy(eC[d][slot], Epos.re("p (c t) -> p c t", t=64)[:, :, last], eng="pool")
        R, Lo = RopT[d][slot], LopT[d][slot]
        v3 = lambda t: t.re("p (c t) -> p c t", t=64)
        k.stt(R[:, :, 0, :], v3(kkn), -1.0, v3(Eprev), ALU.mult, ALU.mult)
        k.tt(R[:, :, 1, :], v3(r), v3(Epos), ALU.mult)
        k.tt(Lo[:, :, 0, :], v3(bvec), v3(Eneg), ALU.mult)
        k.tt(Lo[:, :, 1, :], v3(k2), v3(Eneg), ALU.mult, eng="pool")

    def pre_stages(q):
        items = items_for(q)
        i3 = q % NR
        R = [RopT[d][slot][:, cc] for (d, slot, cc, _, _) in items]
        Lo = [LopT[d][slot][:, cc] for (d, slot, cc, _, _) in items]
        sc = scmB[i3]
        NMt, Pt = NMt2[q % 2], Pt2[q % 2]
        st = []

        def s_tr():
            for (d, slot, cc, _, _) in items:
                tps = tbank[0:64, (d * 384):(d * 384) + 384]
                k.transpose(tps[:, 0:128], Lo[d][:, 0, :], ident_bf)
                k.transpose(tps[:, 128:256], Lo[d][:, 1, :], ident_bf)
                k.transpose(tps[:, 256:384], vTb[d][slot][:, cc * 64:(cc + 1) * 64], ident_bf)
                k.copy(tm[d][i3], tps, eng="act")
        st.append(s_tr)

        def s_sc():
            for h in range(2):
                hs = slice(64 * h, 64 * h + 64)
                bk = bSC if h == 0 else bankA[1]
                nk = bP[0:64, 256:384] if h == 0 else tb32[0:64, 384:512]
                for d in range(2):
                    Rh = R[d][hs].re("p a t -> p (a t)")
                    k.matmul(bk[0:64, d * 256:d * 256 + 128], Lo[d][hs, 0, :], Rh)
                    k.matmul(bk[0:64, d * 256 + 128:d * 256 + 256], Lo[d][hs, 1, :], Rh)
                for d in range(2):
                    k.matmul(nk[:, d * 64:(d + 1) * 64], R[d][hs, 0, :], Lo[d][hs, 0, :])
            nm = NMt[0]
            for h in range(2):
                bk = bSC if h == 0 else bankA[1]
                nk = bP[0:64, 256:384] if h == 0 else tb32[0:64, 384:512]
                k.tt(sc[:, :, 2 * h:2 * h + 2, :].re("p d a t -> p d (a t)"),
                     bk[0:64, :].re("p (d x) -> p d x", d=2), cm_sb[:, :, 0:256], ALU.mult)
                k.tt(nm[:, 0].re("p (d h) t -> p d h t", h=2)[:, :, h, :], nk.re("p (d t) -> p d t", d=2),
                     cm_sb[:, :, 512:576], ALU.mult)
            k.copy(nm[:, 1].re("p (d h) t -> p d h t", h=2),
                   sc.re("p d (h two) t -> p d h two t", two=2)[:, :, :, 0, 0:64], eng="dve")
            k.tt(Pt[0], nm[:, 1], id64_bf, ALU.add, eng="dve")
        st.append(s_sc)

        for lv in range(5):
            cur = lv % 2
            nm_c, nm_n = NMt[cur], NMt[1 - cur]
            P_c = Pt[cur]
            P_n = Pt[1 - cur] if lv < 4 else TtB[i3]

            def s_nm(lv=lv, nm_c=nm_c, nm_n=nm_n):
                pn = bNM[0:64, :]
                for dh in range(4):
                    k.matmul(pn[:, dh * 64:(dh + 1) * 64], nm_c[:, 1, dh, :], nm_c[:, 0, dh, :])
                if lv < 4:
                    for dh in range(4):
                        k.matmul(pn[:, 256 + dh * 64:256 + (dh + 1) * 64], nm_c[:, 0, dh, :], nm_c[:, 1, dh, :])
                    k.copy(nm_n.re("p a h t -> p (a h t)"), pn, eng="act")
                else:
                    k.copy(nm_n[:, 0].re("p h t -> p (h t)"), pn[:, 0:256], eng="act")
            st.append(s_nm)

            def s_p(nm_n=nm_n, P_c=P_c, P_n=P_n):
                pp = bP[0:64, 0:256]
                for dh in range(4):
                    k.matmul(pp[:, dh * 64:(dh + 1) * 64], nm_n[:, 0, dh, :], P_c[:, dh, :], start=True, stop=False)
                    k.matmul(pp[:, dh * 64:(dh + 1) * 64], id64_bf[:, 0, :], P_c[:, dh, :], start=False, stop=True)
                k.copy(P_n.re("p h t -> p (h t)"), pp, eng="dve")
            st.append(s_p)
        return st

    par = [0, 0]

    def chain_stages(q):
        items = items_for(q)
        i3 = q % NR
        ctx = []
        for (d, slot, cc, _, cg) in items:
            CB = bCH[d]
            ctx.append(dict(d=d, R=RopT[d][slot][:, cc], sc=scmB[i3][:, d], tmv=tm[d][i3], T=TtB[i3][:, 2 * d:2 * d + 2, :],
                            X=CB[0:64, 0:128], U=CB[0:64, 128:256], DS=CB[:, 256:384], Y=CB[0:64, 384:512],
                            Sb=Sbf[d][par[d]], Sn=Sbf[d][1 - par[d]], e=eC[d][slot][:, cc:cc + 1],
                            cg=cg, q=q))
            par[d] = 1 - par[d]
        st = []

        def c_x():
            for c in ctx:
                k.matmul(c["X"], c["R"][:, 0, :], c["Sb"], start=True, stop=False)
                for h in range(2):
                    hb = slice(64 * h, 64 * h + 64)
                    k.matmul(c["X"][:, hb], c["sc"][:, 2 * h + 1, 0:64], c["tmv"][:, 256 + 64 * h:256 + 64 * h + 64],
                             start=False, stop=(h == 1))
            for c in ctx:
                k.copy(Zb[c["d"]], c["X"], eng="act")
        st.append(c_x)

        def c_u():
            for c in ctx:
                for h in range(2):
                    hb = slice(64 * h, 64 * h + 64)
                    k.matmul(c["U"][:, hb], c["T"][:, h, :], Zb[c["d"]][:, hb])
            for c in ctx:
                k.copy(Ub[c["d"]], c["U"], eng="dve")
        st.append(c_u)

        def c_y():
            for c in ctx:
                d = c["d"]
                k.matmul(c["DS"], c["tmv"][:, 0:128], Ub[d], start=True, stop=False)
                k.matmul(c["DS"], c["tmv"][:, 128:256], c["tmv"][:, 256:384], start=False, stop=True)
                k.ts(t1[d], S32[d], c["e"], ALU.mult)
            for c in ctx:
                d = c["d"]
                k.matmul(c["Y"], c["R"][:, 1, :], c["Sb"], start=True, stop=False)
                for h in range(2):
                    hb = slice(64 * h, 64 * h + 64)
                    k.matmul(c["Y"][:, hb], c["sc"][:, 2 * h, 64:128], Ub[d][:, hb], start=False, stop=False)
                    k.matmul(c["Y"][:, hb], c["sc"][:, 2 * h + 1, 64:128], c["tmv"][:, 256 + 64 * h:256 + 64 * h + 64],
                             start=False, stop=(h == 1))
        st.append(c_y)

        def c_s():
            for c in ctx:
                d = c["d"]
                for h in range(2):
                    hs = slice(64 * h, 64 * h + 64)
                    k.stt(S32[d][hs, hs], c["DS"][hs, hs], c["e"][hs], t1[d][hs, hs], ALU.mult, ALU.add)
            for c in ctx:
                d = c["d"]
                k.copy(c["Sn"], S32[d], eng="act")
                yo = Yo[d][c["q"] % 2]
                k.copy(yo, c["Y"], eng="dve")
                k.dma(wkv_scr[d, c["cg"] * 64:(c["cg"] + 1) * 64, :], yo, eng="pool")
        st.append(c_s)
        return st

    def items_for(q):
        s, ci = q // 8, q % 8
        return [(0, s % 2, ci, q, s * 8 + ci), (1, s % 2, 7 - ci, q, (NT - 1 - s) * 8 + (7 - ci))]

    nq = NT * 8
    phaseA(0, 0, 0)
    phaseA(1, NT - 1, 0)
    pendA = []
    perA = 1
    for q in range(0, nq + 2, 2):
        PA, PB = [], []
        if q < nq:
            s, ci = q // 8, q % 8
            if ci == 2 and s + 1 < NT:
                k.begin_defer()
                phaseA(0, s + 1, (s + 1) % 2)
                phaseA(1, NT - 2 - s, (s + 1) % 2)
                pendA = k.end_defer()
                perA = (len(pendA) + 3 * 30 - 1) // (3 * 30)
            PA = pre_stages(q)
            PB = pre_stages(q + 1)
        cs_ = []
        if q >= 2:
            cs_ = chain_stages(q - 2) + chain_stages(q - 1)
        np_, nc_ = len(PA), len(cs_)
        ci_ = 0
        for i in range(np_):
            PA[i]()
            if pendA:
                k.splice(pendA, perA)
            PB[i]()
            if pendA:
                k.splice(pendA, perA)
            want = (i + 1) * nc_ // np_
            while ci_ < want:
                cs_[ci_]()
                if pendA:
                    k.splice(pendA, perA)
                ci_ += 1
        while ci_ < nc_:
            cs_[ci_]()
            ci_ += 1
        if q < nq and q % 8 == 6 and pendA:
            k.splice(pendA, len(pendA))
    assert not pendA

    wf = [k.sb("wf%d" % i, [128, 128]) for i in range(2)]
    wb = [k.sb("wb%d" % i, [128, 128]) for i in range(2)]
    ww = [k.sb("ww%d" % i, [128, 128]) for i in range(2)]
    sqw = k.sb("sqw", [128, 128])
    st1 = k.sb("st1", [128, 2])
    st2 = k.sb("st2", [128, 2])
    mean = k.sb("mean", [128, 2])
    msq = k.sb("msq", [128, 2])
    var = k.sb("var", [128, 2])
    gn = [k.sb("gn%d" % i, [128, 128]) for i in range(2)]
    ob = [k.sb("ob%d" % i, [128, 512]) for i in range(2)]
    obo = [k.sb("obo%d" % i, [128, 512], ydt) for i in range(2)]
    bf_ = [k.sb("bf_%d" % i, [128, 512]) for i in range(2)]
    bb_ = [k.sb("bb_%d" % i, [128, 512]) for i in range(2)]
    for i in range(L // 128):
        p = i % 2
        k.dma(wf[p], wkv_scr[0, i * 128:(i + 1) * 128, :], eng="sp")
        k.dma(wb[p], wkv_scr[1, i * 128:(i + 1) * 128, :], eng="sp")
        w = ww[p]
        k.tt(w, wf[p], wb[p], ALU.add)
        k.reduce(st1, w.re("p (h v) -> p h v", h=2), ALU.add)
        k.tt(sqw, w, w, ALU.mult, eng="pool")
        k.reduce(st2, sqw.re("p (h v) -> p h v", h=2), ALU.add)
        k.ts(mean, st1, 1.0 / 64, ALU.mult)
        k.tt(msq, mean, mean, ALU.mult)
        k.stt(var, st2, 1.0 / 64, msq, ALU.mult, ALU.subtract)
        k.act(var, var, AF.Sqrt, bias=epsg_t)
        k.recip(var, var)
        g_ = gn[p]
        for h in range(2):
            hb = slice(64 * h, 64 * h + 64)
            k.ts(g_[:, hb], w[:, hb], mean[:, h:h + 1], ALU.subtract, var[:, h:h + 1], ALU.mult)
        tb = bankA[(i // 4) % 2]
        k.transpose(tb[:, (i % 4) * 128:(i % 4 + 1) * 128], g_, ident_f)
        if i % 4 == 3:
            j = i // 4
            ts_ = slice(j * 512, (j + 1) * 512)
            o = ob[j % 2]
            k.dma(bf_[j % 2], bon_scr[0, :, ts_], eng="pool")
            k.dma(bb_[j % 2], bon_scr[1, :, ts_], eng="pool")
            k.ts(o, tb, pv[:, 7:8], ALU.mult, pv[:, 8:9], ALU.add)
            k.tt(o, o, bf_[j % 2], ALU.add, eng="pool")
            k.tt(o, o, bb_[j % 2], ALU.add, eng="pool")
            k.tt(obo[j % 2], o, gT[:, ts_], ALU.mult)
            k.dma(io["yfn"](None, j) if io.get("yfn") is not None else yT[:, ts_], obo[j % 2], eng="sp")


GROUPS = [[0, 1, 2, 3], [4, 5, 6, 7]]


def emit_out(k, io, last):
    Lx = 8192
    NTx = Lx // 512
    ygath, wo, xsrc, xdst = io["ygath"], io["wo"], io["xsrc"], io["xdst"]
    ones_bf = k.sb("ones_bf", [128, 128], BF16)
    k.memset(ones_bf, 1.0)
    eps_t = k.sb("eps_t", [128, 1])
    k.memset(eps_t, 1e-6)
    postw_sb = k.sb("postw_sb", [128, 2])
    k.dma(postw_sb, io["postw"])
    snw_sb = k.sb("snw_sb", [128, 4])
    k.dma(snw_sb, io["snw"])
    prew_sb = k.sb("prew_sb", [128, 2])
    k.dma(prew_sb, io["prew"])
    w_bf = k.sb("w_bf", [128, 16, 256], BF16)
    wst = [k.sb("wst%d" % i, [128, 4, 256]) for i in range(2)]
    for i in range(4):
        st = wst[i % 2]
        k.dma(st, wo[i * 512:(i + 1) * 512, :].re("(c p) m -> p c m", p=128), eng="pool" if i % 2 else "sp")
        k.copy(w_bf[:, 4 * i:4 * i + 4, :], st, eng="pool")
    mixT = k.sb("mixT", [128, 2, Lx])
    ssrow = k.sb("ssrow", [1, Lx])
    ytile = [k.sb("ytile%d" % i, [128, 16, 512], BF16) for i in range(2)]
    yn = [k.sb("yn%d" % i, [128, 4, 512], BF16) for i in range(2)]
    sq = [k.sb("sq%d" % i, [128, 512], BF16) for i in range(2)]
    rs = k.sb("rs", [128, 512])
    tot = [k.sb("tot%d" % i, [128, 512]) for i in range(2)]
    xt = [k.sb("xt%d" % i, [128, 2, 512]) for i in range(2)]
    hb = [k.sb("hb%d" % i, [128, 2, 512], BF16) for i in range(2)]
    pg = k.ps("pg", [128, 512])
    pm = [k.ps("pm%d" % i, [128, 512]) for i in range(2)]
    pss = k.ps("pss", [128, 512])

    def p1(j):
        ts_ = slice(j * 512, (j + 1) * 512)
        yt = ytile[j % 2]
        ytv = yt.re("p (g q) t -> p g q t", q=4)
        half, jj = j // 8, j % 8
        for kind in range(4):
            k.dma(ytv[:, :, kind, :], ygath[half][kind][:, jj * 512:(jj + 1) * 512].re("(g i) t -> i g t", i=128),
                  eng="sp" if kind % 2 == 0 else "pool")
        ynj = yn[j % 2]
        for grp in range(2):
            for ci, g in enumerate((2 * grp, 2 * grp + 1)):
                k.act(sq[ci], yt[:, g * 4, :], AF.Square)
                k.matmul(pg, ones_bf, sq[ci], start=(ci == 0), stop=(ci == 1))
            k.act(rs, pg, AF.Ln, scale=1.0 / 256, bias=eps_t)
            k.act(rs, rs, AF.Exp, scale=-0.5)
            for g in (2 * grp, 2 * grp + 1):
                k.stt(ynj[:, g, :], yt[:, g * 4, :], snw_sb[:, g:g + 1], rs, ALU.mult, ALU.mult)
        for nb in range(2):
            for c in range(16):
                rhs = ynj[:, c // 4, :] if c % 4 == 0 else yt[:, c, :]
                k.matmul(pm[nb], w_bf[:, c, nb * 128:(nb + 1) * 128], rhs, start=(c == 0), stop=(c == 15))
            k.copy(mixT[:, nb, ts_], pm[nb], eng="dve")
            k.act(sq[nb], pm[nb], AF.Square)
            k.matmul(pss, ones_bf, sq[nb], start=(nb == 0), stop=(nb == 1))
        k.copy(ssrow[0:1, ts_], pss[0:1, :], eng="dve")

    def p2(j):
        ts_ = slice(j * 512, (j + 1) * 512)
        tt_ = tot[j % 2]
        k.dma(tt_, T(io["ar1_out"][j // 8].buf, io["ar1_out"][j // 8].ap[0:1, (j % 8) * 512:(j % 8 + 1) * 512].partition_broadcast(128)), eng="pool")
        k.act(tt_, tt_, AF.Ln, scale=1.0 / 1024, bias=eps_t)
        k.act(tt_, tt_, AF.Exp, scale=-0.5)
        x_ = xt[j % 2]
        k.dma(x_, xsrc[:, ts_].re("(nb p) t -> p nb t", p=128), eng="sp")
        for nb in range(2):
            k.stt(mixT[:, nb, ts_], mixT[:, nb, ts_], postw_sb[:, nb:nb + 1], tt_, ALU.mult, ALU.mult)
            k.tt(mixT[:, nb, ts_], mixT[:, nb, ts_], x_[:, nb, :], ALU.add, eng="pool")
        k.dma(xdst[:, ts_].re("(nb p) t -> p nb t", p=128), mixT[:, :, ts_], eng="sp")
        if not last:
            for nb in range(2):
                k.act(sq[nb], mixT[:, nb, ts_], AF.Square)
                k.matmul(pss, ones_bf, sq[nb], start=(nb == 0), stop=(nb == 1))
            k.copy(ssrow[0:1, ts_], pss[0:1, :], eng="dve")

    def p3(j):
        ts_ = slice(j * 512, (j + 1) * 512)
        tt_ = tot[j % 2]
        k.dma(tt_, T(io["ar2_out"][j // 8].buf, io["ar2_out"][j // 8].ap[0:1, (j % 8) * 512:(j % 8 + 1) * 512].partition_broadcast(128)), eng="pool")
        k.act(tt_, tt_, AF.Ln, scale=1.0 / 1024, bias=eps_t)
        k.act(tt_, tt_, AF.Exp, scale=-0.5)
        h_ = hb[j % 2]
        for nb in range(2):
            k.stt(h_[:, nb, :], mixT[:, nb, ts_], prew_sb[:, nb:nb + 1], tt_, ALU.mult, ALU.mult)
        half, jj = j // 8, j % 8
        for nb in range(2):
            k.dma(io["hslice"][half][nb][:, jj * 512:(jj + 1) * 512], h_[:, nb, :], eng="sp" if nb == 0 else "pool")
        if jj == 7:
            for nb in range(2):
                k.collective("AllGather", io["hslice"][half][nb], io["hgath"][half][nb], GROUPS)

    LHx = Lx // 2
    for half in range(2):
        for j in range(half * 8, half * 8 + 8):
            p1(j)
        k.dma(io["ar1_in"][half], ssrow[0:1, half * LHx:(half + 1) * LHx], eng="sp")
        k.collective("AllReduce", io["ar1_in"][half], io["ar1_out"][half], GROUPS, op=ALU.add)
    for half in range(2):
        for j in range(half * 8, half * 8 + 8):
            p2(j)
        if not last:
            k.dma(io["ar2_in"][half], ssrow[0:1, half * LHx:(half + 1) * LHx], eng="sp")
            k.collective("AllReduce", io["ar2_in"][half], io["ar2_out"][half], GROUPS, op=ALU.add)
    if last:
        return
    for j in range(NTx):
        p3(j)


L = 8192
PROJ_SIZES = (512, 1024, 16, 1664, 512, 512, 512, 512, 512, 512, 256, 256, 512)
OFF = np.concatenate([[0], np.cumsum(PROJ_SIZES)]).astype(int)
(O_MZ, O_XBC, O_DT, O_RU, O_RG, O_DQ, O_DK, O_DV, O_DG, O_GQ, O_GK, O_GV, O_GG) = OFF[:13]

PERM = np.array([p + 32 if (p % 64) < 32 else p - 32 for p in range(128)])


def rope_tables():
    inv = (np.float32(10000.0) ** (-np.arange(32, dtype=np.float32) / np.float32(32))).astype(np.float32)
    t = np.arange(L)
    sign = np.where((np.arange(128) % 64) < 32, -1.0, 1.0).astype(np.float32)[:, None]
    fi = np.arange(128) % 32
    pos_d = np.broadcast_to(t.astype(np.float32)[None, :], (128, L))
    ang_d = (pos_d * inv[fi][:, None]).astype(np.float32)
    pos_g = np.where((np.arange(128) < 64)[:, None], (t // 64)[None, :], (t % 64)[None, :]).astype(np.float32)
    ang_g = (pos_g * inv[fi][:, None]).astype(np.float32)
    tabs = np.stack([np.cos(ang_d), np.sin(ang_d) * sign, np.cos(ang_g), np.sin(ang_g) * sign]).astype(np.float32)
    return np.ascontiguousarray(tabs)


def pvec(v):
    return np.ascontiguousarray(v.reshape(8, 128).T)


def prep_attn(inp, layer, xT_b, tabs):
    W = inp['w_in'][layer]
    maps = []
    for core in range(8):
        b, g = core // 4, core % 4
        sl = lambda o, n=128, gg=g: W[:, o + gg * n: o + (gg + 1) * n]
        dq, dk_, dg, dv = sl(O_DQ), sl(O_DK), sl(O_DG), sl(O_DV)
        gq, gg_ = sl(O_GQ), sl(O_GG)
        gk, gv = sl(O_GK, 128, g // 2), sl(O_GV, 128, g // 2)
        wfm = np.concatenate([dq, dq[:, PERM], dk_, dk_[:, PERM], dg, gq, gq[:, PERM], gk, gk[:, PERM], gg_], axis=1)
        wtm = np.concatenate([dv, gv], axis=1)
        vecs = np.zeros((128, 8), np.float32)
        qw, kw = inp['gqa_q_norm_w'][layer], inp['gqa_k_norm_w'][layer]
        vecs[:, 0] = qw; vecs[:, 1] = qw[PERM]; vecs[:, 2] = kw; vecs[:, 3] = kw[PERM]
        vecs[:, 4] = inp['diff_norm_w'][layer]
        lam = np.ascontiguousarray(np.broadcast_to(inp['diff_lambda'][layer].reshape(1, 256), (128, 256)))
        maps.append({"xT": xT_b[b], "pw": pvec(inp['pre_norm_w'][layer]),
                     "wfm": np.ascontiguousarray(wfm), "wtm": np.ascontiguousarray(wtm),
                     "tabs": tabs, "vecs": vecs, "lam": lam})
    return maps


def ssd_consts():
    j = np.arange(128)[:, None]; l = np.arange(128)[None, :]
    c = np.stack([(j <= l), (j >= l), (j > l), (j < l), (j == l)]).astype(np.float32)
    return np.ascontiguousarray(c.transpose(1, 0, 2))


def prep_ssd(inp, layer, xT_b):
    W = inp['w_in'][layer]
    cst = ssd_consts()
    maps = []
    for core in range(8):
        b, g = core // 4, core % 4
        grp = g // 2
        wfm = np.concatenate([W[:, O_MZ + g * 128:O_MZ + (g + 1) * 128],
                              W[:, O_XBC + g * 128:O_XBC + (g + 1) * 128],
                              W[:, O_XBC + 512 + grp * 128:O_XBC + 512 + (grp + 1) * 128],
                              W[:, O_XBC + 768 + grp * 128:O_XBC + 768 + (grp + 1) * 128]], axis=1)
        dcols = [O_DT + d * 8 + 2 * g + h for d in range(2) for h in range(2)]
        wdt = W[:, dcols]
        chans = [g * 128, 512 + grp * 128, 768 + grp * 128]
        cwl = inp['conv_w'][layer]; cbl = inp['conv_b'][layer]
        cw = np.stack([cwl[:, ch:ch + 128].T for ch in chans], axis=1)
        cb = np.stack([cbl[ch:ch + 128] for ch in chans], axis=1)
        v = np.zeros(16, np.float32)
        for d in range(2):
            for h in range(2):
                v[d * 2 + h] = inp['ssm_dt_bias'][layer][d, 2 * g + h]
                v[4 + d * 2 + h] = inp['ssm_a_log'][layer][d, 2 * g + h]
        v[8] = inp['ssm_d'][layer][2 * g]; v[9] = inp['ssm_d'][layer][2 * g + 1]
        maps.append({"xT": xT_b[b], "pw": pvec(inp['pre_norm_w'][layer]),
                     "wfm": np.ascontiguousarray(wfm), "wdt": np.ascontiguousarray(wdt),
                     "cw": np.ascontiguousarray(cw), "cb": np.ascontiguousarray(cb),
                     "ssmv": np.ascontiguousarray(np.broadcast_to(v[None], (128, 16))), "cst": cst})
    return maps


def rwkv_consts():
    j = np.arange(64)[:, None]; t = np.arange(64)[None, :]
    cm = np.zeros((64, 2, 704), np.float32)
    for d in range(2):
        strict = (j < t) if d == 0 else (j > t)
        incl = (j <= t) if d == 0 else (j >= t)
        blk = np.concatenate([strict, incl], axis=1).astype(np.float32)
        cm[:, d, 0:512] = np.tile(blk, (1, 4))
        nmask = ((t < j) if d == 0 else (t > j)).astype(np.float32)
        cm[:, d, 512:640] = np.tile(nmask, (1, 2))
        cm[:, d, 640:704] = np.eye(64, dtype=np.float32)
    c128 = np.zeros((128, 2, 128), np.float32)
    c128[0:64, 0, 0:64] = 1; c128[64:128, 0, 64:128] = 1
    c128[:, 1, :] = np.eye(128, dtype=np.float32)
    return cm, c128


O_R, O_K, O_V, O_WD, O_AD = O_RU, O_RU + 512, O_RU + 1024, O_RU + 1536, O_RU + 1600


def prep_rwkv(inp, layer, xT_b):
    W = inp['w_in'][layer]
    cm, c128 = rwkv_consts()
    maps = []
    for core in range(8):
        b, g = core // 4, core % 4
        gs = slice(g * 128, (g + 1) * 128)
        wfm = np.concatenate([W[:, O_R + g * 128:O_R + (g + 1) * 128], W[:, O_K + g * 128:O_K + (g + 1) * 128],
                              W[:, O_V + g * 128:O_V + (g + 1) * 128], W[:, O_WD:O_WD + 128],
                              W[:, O_RG + g * 128:O_RG + (g + 1) * 128]], axis=1)
        mul = inp['rwkv_mu'][layer]
        mu = np.zeros((128, 2, 4), np.float32)
        for d in range(2):
            mu[:, d, 0] = mul[d, 0 + g * 128:0 + (g + 1) * 128]
            mu[:, d, 1] = mul[d, 512 + g * 128:512 + (g + 1) * 128]
            mu[:, d, 2] = mul[d, 1024 + g * 128:1024 + (g + 1) * 128]
            mu[:, d, 3] = mul[d, 1536:1664]
        w2a2 = np.zeros((128, 2, 128), np.float32)
        for d in range(2):
            w2a2[0:64, d, :] = inp['rwkv_w2'][layer][d][:, gs]
            w2a2[64:128, d, :] = inp['rwkv_a2'][layer][:, gs]
        pv = np.zeros((128, 12), np.float32)
        pv[:, 0] = inp['rwkv_w0'][layer][0, gs]; pv[:, 1] = inp['rwkv_w0'][layer][1, gs]
        pv[:, 2] = inp['rwkv_a0'][layer][gs]; pv[:, 3] = inp['rwkv_k_k'][layer][gs]
        pv[:, 4] = inp['rwkv_k_a'][layer][gs]
        pv[:, 6] = inp['rwkv_r_k'][layer].reshape(-1)[gs]
        pv[:, 7] = inp['rwkv_ln_w'][layer][gs]; pv[:, 8] = inp['rwkv_ln_b'][layer][gs]
        maps.append({"xT": xT_b[b], "pw": pvec(inp['pre_norm_w'][layer]), "wfm": np.ascontiguousarray(wfm),
                     "mu": mu, "w2a2": w2a2, "pvd": pv, "cm": cm, "c128": c128})
    return maps


def build_fused(depth=2):
    nc = bass.Bass("TRN2", target_bir_lowering=False)
    k = KB(nc, arena=True)
    din = k.dram_in
    xT = din("xT", [1024, L])
    xs0 = din("xs0", [256, L])
    tabs = din("tabs", [4, 128, L])
    cst = din("cst", [128, 5, 128])
    cm = din("cm", [64, 2, 704])
    c128 = din("c128", [128, 2, 128])
    xo = k.dram_out("xo", [256, L])
    LH = L // 2
    ycat = [[k.dram_scratch("ycat%d%d" % (h_, q_), [128, LH], BF16) for q_ in range(4)] for h_ in range(2)]
    ygath = [[k.dram_scratch("ygath%d%d" % (h_, q_), [512, LH], BF16) for q_ in range(4)] for h_ in range(2)]
    hslice = [[k.dram_scratch("hslice%d%d" % (h_, q_), [128, LH], BF16) for q_ in range(2)] for h_ in range(2)]
    hgath = [[k.dram_scratch("hgath%d%d" % (h_, q_), [512, LH], BF16) for q_ in range(2)] for h_ in range(2)]

    def yfn_for(kind_idx):
        def fn(_, j):
            half, jj = j // 8, j % 8
            return ycat[half][kind_idx][:, jj * 512:(jj + 1) * 512]
        return fn

    def gather_y(half, kind_idx):
        k.collective("AllGather", ycat[half][kind_idx], ygath[half][kind_idx], GROUPS)

    xres = k.dram_scratch("xres", [256, L])
    ar = [[k.dram_scratch("ar%d_%d" % (i, h_), [1, L // 2]) for h_ in range(2)] for i in range(4)]
    wkv_scr = k.dram_scratch("wkv_scr", [2, L, 128])
    bon_scr = k.dram_scratch("bon_scr", [2, 128, L])
    for layer in range(depth):
        p = "L%d_" % layer
        lambda_init = 0.8 - 0.6 * math.exp(-0.3 * layer)
        src = {"xT": xT} if layer == 0 else {"hT": hgath}
        pw = din(p + "pw", [128, 8])
        io = dict(src, pw=pw, wfm=din(p + "s_wfm", [1024, 512]), wdt=din(p + "s_wdt", [1024, 4]),
                  cw=din(p + "s_cw", [128, 3, 5]), cb=din(p + "s_cb", [128, 3]), ssmv=din(p + "s_ssmv", [128, 16]),
                  cst=cst, yfn=yfn_for(0), ydt=BF16)
        emit_ssd(k, io)
        k.phase_reset()
        io = dict(src, pw=pw, wfm=din(p + "r_wfm", [1024, 640]), mu=din(p + "r_mu", [128, 2, 4]),
                  w2a2=din(p + "r_w2a2", [128, 2, 128]), pvd=din(p + "r_pvd", [128, 12]), cm=cm, c128=c128,
                  wkv_scr=wkv_scr, bon_scr=bon_scr, yfn=yfn_for(1), ydt=BF16)
        emit_rwkv(k, io)
        k.phase_reset()
        for half in range(2):
            for kind_idx in range(2):
                gather_y(half, kind_idx)
        io = dict(src, pw=pw, wfm=din(p + "a_wfm", [1024, 1280]), wtm=din(p + "a_wtm", [1024, 256]), tabs=tabs,
                  vecs=din(p + "a_vecs", [128, 8]), lam=din(p + "a_lam", [128, 256]),
                  yfn=(lambda kind, j: yfn_for(2 + kind)(None, j)), ydt=BF16,
                  y_done=(lambda kind, half: gather_y(half, 2 + kind)))
        emit_attn(k, io, lambda_init)
        k.phase_reset()
        last = (layer == depth - 1)
        io = dict(ygath=ygath, wo=din(p + "o_wo", [2048, 256]), xsrc=(xs0 if layer == 0 else xres),
                  xdst=(xo if last else xres), postw=din(p + "o_postw", [128, 2]), snw=din(p + "o_snw", [128, 4]),
                  prew=din(p + "o_prew", [128, 2]), ar1_in=ar[0], ar1_out=ar[1], ar2_in=ar[2], ar2_out=ar[3],
                  hslice=hslice, hgath=hgath)
        emit_out(k, io, last)
        k.phase_reset()
    stats = k.finish()
    return nc, stats


def prep_fused(inp, depth=2):
    x = np.ascontiguousarray(inp["x"], dtype=np.float32)
    xT_b = [np.ascontiguousarray(x[b].T) for b in range(2)]
    tabs = rope_tables()
    cst = ssd_consts()
    cm, c128 = rwkv_consts()
    maps = [dict() for _ in range(8)]
    perm = np.array([kind * 512 + g * 128 + i for g in range(4) for kind in range(4) for i in range(128)])
    for core in range(8):
        b, g = core // 4, core % 4
        m = maps[core]
        m["xT"] = xT_b[b]
        m["xs0"] = np.ascontiguousarray(xT_b[b][g * 256:(g + 1) * 256])
        m["tabs"] = tabs; m["cst"] = cst; m["cm"] = cm; m["c128"] = c128
    for layer in range(depth):
        p = "L%d_" % layer
        ms = prep_ssd(inp, layer, xT_b); mr = prep_rwkv(inp, layer, xT_b); ma = prep_attn(inp, layer, xT_b, tabs)
        for core in range(8):
            b, g = core // 4, core % 4
            m = maps[core]
            m[p + "pw"] = ms[core]["pw"]
            for nm in ("wfm", "wdt", "cw", "cb", "ssmv"):
                m[p + "s_" + nm] = ms[core][nm]
            for nm in ("wfm", "mu", "w2a2", "pvd"):
                m[p + "r_" + nm] = mr[core][nm]
            for nm in ("wfm", "wtm", "vecs", "lam"):
                m[p + "a_" + nm] = ma[core][nm]
            ns = slice(g * 256, (g + 1) * 256)
            m[p + "o_wo"] = np.ascontiguousarray(inp["w_out"][layer][perm][:, ns])
            m[p + "o_postw"] = np.ascontiguousarray(inp["post_norm_w"][layer][ns].reshape(2, 128).T)
            m[p + "o_snw"] = np.ascontiguousarray(inp["ssm_norm_w"][layer].reshape(4, 128).T)
            nxt = inp["pre_norm_w"][min(layer + 1, depth - 1)]
            m[p + "o_prew"] = np.ascontiguousarray(nxt[ns].reshape(2, 128).T)
    return maps


from concourse.bass_utils import run_bass_kernel_spmd


def kernel(**inp):
    inp = {k_: np.asarray(v) for k_, v in inp.items()}
    depth = inp["w_in"].shape[0]
    nc, _ = build_fused(depth)
    maps = prep_fused(inp, depth)
    res = run_bass_kernel_spmd(nc, maps, core_ids=list(range(8))).results
    out = np.empty((2, L, 1024), np.float32)
    for core in range(8):
        b, g = core // 4, core % 4
        out[b, :, g * 256:(g + 1) * 256] = res[core]["xo"].T
    return out
```

```python
import math
import numpy as np
import concourse.bass as bass
import concourse.mybir as mybir

F32 = mybir.dt.float32
BF16 = mybir.dt.bfloat16
ALU = mybir.AluOpType
AF = mybir.ActivationFunctionType
AX = mybir.AxisListType

SEM_CHUNK = 30000


class Buf:
    __slots__ = ("name", "last_w", "readers", "dma_sem", "dma_cnt", "is_dram", "psum", "wlist", "inc_val")

    def __init__(self, name, is_dram=False, psum=False):
        self.psum = psum
        self.wlist = {}
        self.inc_val = 16
        self.name = name
        self.last_w = None
        self.readers = []
        self.dma_sem = None
        self.dma_cnt = 0
        self.is_dram = is_dram


class T:
    __slots__ = ("buf", "ap")

    def __init__(self, buf, ap):
        self.buf = buf
        self.ap = ap

    def __getitem__(self, idx):
        return T(self.buf, self.ap[idx])

    def re(self, pattern, **kw):
        return T(self.buf, self.ap.rearrange(pattern, **kw))

    def sub(self, buf, idx=None):
        return T(buf, self.ap if idx is None else self.ap[idx])


class Op:
    __slots__ = ("eng", "fn", "reads", "writes", "is_dma", "deps", "inc_idx", "dma_buf",
                 "dma_wait", "gi")

    def __init__(self, eng, fn, reads, writes, is_dma=False, dma_buf=None):
        self.eng = eng
        self.fn = fn
        self.reads = reads
        self.writes = writes
        self.is_dma = is_dma
        self.deps = []
        self.inc_idx = None
        self.dma_buf = dma_buf
        self.dma_wait = []
        self.gi = None


class KB:
    ENGS = ("pe", "act", "dve", "pool", "sp")

    ARENA_WORDS = 53100

    def __init__(self, nc, arena=False):
        self.nc = nc
        self.arena = None
        if arena:
            self.arena = nc.alloc_sbuf_tensor("arena", [128, self.ARENA_WORDS], F32).ap()
            self.a_off = 0
            self.a_peak = 0
            self.psum_all = nc.alloc_psum_tensor("gbanks", [128, 4096], F32).ap()
            self.banks = [T(Buf("bank%d" % i, psum=True), self.psum_all[:, i * 512:(i + 1) * 512]) for i in range(8)]
            self.b_next = 0
        self.ops = []
        self.e = {"pe": nc.tensor, "act": nc.scalar, "dve": nc.vector, "pool": nc.gpsimd,
                  "sp": nc.sync}
        self._n = 0

    def sb(self, name, shape, dtype=F32):
        if self.arena is None:
            h = self.nc.alloc_sbuf_tensor(name, list(shape), dtype)
            return T(Buf(name), h.ap())
        shape = list(shape)
        esz = 2 if dtype == BF16 else 4
        n = 1
        for d in shape[1:]:
            n *= d
        nbytes = (n * esz + 31) // 32 * 32
        nw = nbytes // 4
        off = self.a_off
        assert off + nw <= self.ARENA_WORDS, "arena overflow at %s: %d + %d" % (name, off, nw)
        self.a_off = off + nw
        self.a_peak = max(self.a_peak, self.a_off)
        ap = self.arena[0:shape[0], off:off + (n * esz + 3) // 4]
        if dtype == BF16:
            ap = ap.bitcast(BF16)
            if (n * esz) % 4:
                ap = ap[:, 0:n]
        if len(shape) == 3:
            ap = ap.rearrange("p (a b) -> p a b", a=shape[1])
        elif len(shape) == 4:
            ap = ap.rearrange("p (a b c) -> p a b c", a=shape[1], b=shape[2])
        return T(Buf(name), ap)

    def ps(self, name, shape, dtype=F32):
        if self.arena is None:
            h = self.nc.alloc_psum_tensor(name, list(shape), dtype)
            return T(Buf(name, psum=True), h.ap())
        bk = self.banks[self.b_next]
        self.b_next += 1
        if dtype == BF16:
            return T(bk.buf, bk.ap.bitcast(BF16))
        return bk

    def bank_pair(self, i):
        return T(self.banks[i].buf, self.psum_all[:, i * 512:(i + 2) * 512]), self.banks[i + 1]

    def phase_reset(self):
        self._rec("bar", None, [], [])
        self.a_off = 0
        self.b_next = 0

    def dram_in(self, name, shape, dtype=F32):
        h = self.nc.dram_tensor(name, list(shape), dtype, kind="ExternalInput")
        return T(Buf(name, True), h.ap())

    def dram_out(self, name, shape, dtype=F32):
        h = self.nc.dram_tensor(name, list(shape), dtype, kind="ExternalOutput")
        return T(Buf(name, True), h.ap())

    def dram_scratch(self, name, shape, dtype=F32):
        h = self.nc.dram_tensor(name, list(shape), dtype)
        return T(Buf(name, True), h.ap())

    def buf(self, name):
        self._n += 1
        return Buf("%s_%d" % (name, self._n))

    def _rec(self, eng, fn, reads, writes, is_dma=False, dma_buf=None):
        rb = []
        for t in reads:
            if t is None or isinstance(t, (int, float)):
                continue
            b = t.buf if isinstance(t, T) else t
            if b not in rb:
                rb.append(b)
        wb = []
        for t in writes:
            b = t.buf if isinstance(t, T) else t
            if b not in wb:
                wb.append(b)
        op = Op(eng, fn, rb, wb, is_dma, dma_buf)
        if getattr(self, "_defer", None) is not None:
            self._defer.append(op)
            return op
        op.gi = len(self.ops)
        self.ops.append(op)
        return op

    def begin_defer(self):
        self._defer = []

    def end_defer(self):
        lst = self._defer
        self._defer = None
        return lst

    def splice(self, lst, n):
        for _ in range(min(n, len(lst))):
            op = lst.pop(0)
            op.gi = len(self.ops)
            self.ops.append(op)

    @staticmethod
    def _a(x):
        return x.ap if isinstance(x, T) else x

    def dma(self, out, in_, eng="sp", **kw):
        o, i = self._a(out), self._a(in_)
        sbuf_side = out.buf if not out.buf.is_dram else in_.buf
        return self._rec(eng, lambda E: E.dma_start(out=o, in_=i, **kw), [in_], [out],
                         is_dma=True, dma_buf=sbuf_side)

    def matmul(self, out, lhsT, rhs, start=True, stop=True, extra_reads=(), **kw):
        o, l, r = self._a(out), self._a(lhsT), self._a(rhs)
        reads = [lhsT, rhs] + list(extra_reads)
        if not start:
            reads.append(out)
        return self._rec("pe", lambda E: E.matmul(o, l, r, start=start, stop=stop, **kw),
                         reads, [out])

    def transpose(self, out, in_, ident):
        o, i, d = self._a(out), self._a(in_), self._a(ident)
        return self._rec("pe", lambda E: E.transpose(o, i, d), [in_, ident], [out])

    def act(self, out, in_, func, bias=None, scale=None, accum_out=None, eng="act", extra_reads=()):
        o, i = self._a(out), self._a(in_)
        kw = {}
        reads = [in_] + list(extra_reads)
        writes = [out]
        if bias is not None:
            kw["bias"] = self._a(bias)
            reads.append(bias)
        if scale is not None:
            kw["scale"] = self._a(scale)
            reads.append(scale)
        if accum_out is not None:
            kw["accum_out"] = self._a(accum_out)
            writes.append(accum_out)
        return self._rec(eng, lambda E: E.activation(o, i, func, **kw), reads, writes)

    def tt(self, out, in0, in1, op, eng="dve"):
        o, a, b = self._a(out), self._a(in0), self._a(in1)
        return self._rec(eng, lambda E: E.tensor_tensor(o, a, b, op), [in0, in1], [out])

    def ts(self, out, in0, s1, op0, s2=None, op1=None, accum_out=None, eng="dve"):
        o, a = self._a(out), self._a(in0)
        s1a, s2a = self._a(s1), self._a(s2)
        kw = {}
        writes = [out]
        if op1 is not None:
            kw["op1"] = op1
        if accum_out is not None:
            kw["accum_out"] = self._a(accum_out)
            writes.append(accum_out)
        return self._rec(eng, lambda E: E.tensor_scalar(o, a, s1a, s2a, op0, **kw),
                         [in0, s1, s2], writes)

    def stt(self, out, in0, scalar, in1, op0, op1, eng="dve"):
        o, a, s, b = self._a(out), self._a(in0), self._a(scalar), self._a(in1)
        return self._rec(eng, lambda E: E.scalar_tensor_tensor(o, a, s, b, op0, op1),
                         [in0, scalar, in1], [out])

    def copy(self, out, in_, eng="dve"):
        o, i = self._a(out), self._a(in_)
        if eng == "act":
            return self._rec(eng, lambda E: E.copy(o, i), [in_], [out])
        return self._rec(eng, lambda E: E.tensor_copy(o, i), [in_], [out])

    def memset(self, out, val, eng="pool"):
        o = self._a(out)
        return self._rec(eng, lambda E: E.memset(o, val), [], [out])

    def recip(self, out, in_):
        o, i = self._a(out), self._a(in_)
        return self._rec("dve", lambda E: E.reciprocal(o, i), [in_], [out])

    def reduce(self, out, in_, op, axis=AX.X, eng="dve"):
        o, i = self._a(out), self._a(in_)
        return self._rec(eng, lambda E: E.tensor_reduce(o, i, axis, op), [in_], [out])

    def affine_select(self, out, in_, pattern, compare_op, fill, base=0, channel_multiplier=0):
        o, i = self._a(out), self._a(in_)
        return self._rec("pool", lambda E: E.affine_select(
            o, i, pattern, compare_op, fill, base=base, channel_multiplier=channel_multiplier),
            [in_], [out])

    def iota(self, out, pattern, base=0, channel_multiplier=0, **kw):
        o = self._a(out)
        return self._rec("pool", lambda E: E.iota(o, pattern, base=base,
                                                   channel_multiplier=channel_multiplier, **kw),
                         [], [out])

    def collective(self, kind, in_, out, groups, op=None):
        i, o = self._a(in_), self._a(out)
        alu = ALU.bypass if op is None else op
        semb = Buf("cc_%d" % len(self.ops))
        semb.inc_val = 1
        return self._rec("pool", lambda E: E.collective_compute(kind, alu, groups, [i], [o]),
                         [in_], [out], is_dma=True, dma_buf=semb)

    def generic(self, eng, fn, reads, writes):
        return self._rec(eng, fn, reads, writes)

    def finish(self, final_wait_outputs=True):
        nc = self.nc
        ops = self.ops
        last_on = {}
        last_dma = {}
        bar_deps = {e: None for e in self.ENGS}
        for op in ops:
            if op.eng == "bar":
                allp = set(last_on.values()) | set(last_dma.values())
                for e in self.ENGS:
                    bar_deps[e] = set(allp) | (bar_deps[e] or set())
                continue
            deps = set()
            for b in op.reads:
                if b.last_w is not None:
                    deps.add(b.last_w)
                if b.is_dram:
                    for w_ in b.wlist.values():
                        deps.add(w_)
                if b.psum:
                    for r in b.readers:
                        if ops[r].eng != op.eng:
                            deps.add(r)
            for b in op.writes:
                if b.last_w is not None:
                    deps.add(b.last_w)
                for r in b.readers:
                    deps.add(r)
            deps.discard(op.gi)
            keep = []
            if bar_deps[op.eng] is not None:
                keep.extend(sorted(bar_deps[op.eng]))
                bar_deps[op.eng] = None
            last_on[op.eng] = op.gi
            if op.is_dma:
                last_dma[id(op.dma_buf)] = op.gi
            for d in deps:
                dop = ops[d]
                if dop.is_dma:
                    keep.append(d)
                    continue
                if dop.eng == op.eng:
                    if op.eng in ("pe", "sp"):
                        continue
                    if op.is_dma:
                        keep.append(d)
                        continue
                    raw = any(b.last_w == d for b in op.reads)
                    if not raw:
                        continue
                keep.append(d)
            op.deps = keep
            for b in op.reads:
                b.readers.append(op.gi)
            for b in op.writes:
                b.last_w = op.gi
                b.readers = []
                if b.is_dram and op.is_dma:
                    b.wlist[id(op.dma_buf)] = op.gi
        needed = set()
        for op in ops:
            for d in op.deps:
                needed.add(d)
        final_dma = [op for op in ops if op.is_dma and any(b.is_dram for b in op.writes)]
        ecount = {e: 0 for e in self.ENGS}
        phase = 0
        nslot = 0
        slot_total = []
        slot_of = {}
        ncc = 0
        abs_cnt = {}
        sem_key = {}
        for op in ops:
            if op.eng == "bar":
                phase += 1
                nslot = 0
                continue
            if op.is_dma:
                b = op.dma_buf
                if b.inc_val == 1:
                    if id(b) not in sem_key:
                        sem_key[id(b)] = ("c", ncc)
                        ncc += 1
                        b.dma_cnt = 0
                    b.dma_cnt += 1
                    abs_cnt[op.gi] = b.dma_cnt
                else:
                    ps_ = slot_of.get(id(b))
                    if ps_ is None or ps_[0] != phase:
                        slot_of[id(b)] = (phase, nslot)
                        if nslot >= len(slot_total):
                            slot_total.append(0)
                        sem_key[id(b)] = ("p", nslot)
                        nslot += 1
                    sl = slot_of[id(b)][1]
                    slot_total[sl] += 1
                    abs_cnt[op.gi] = slot_total[sl]
                op.inc_idx = abs_cnt[op.gi]
            elif op.gi in needed:
                ecount[op.eng] += 1
                op.inc_idx = ecount[op.eng]
        import contextlib
        self._stack = contextlib.ExitStack()
        self._sems = {}
        nsem = 0
        for e in self.ENGS:
            n = max((ecount[e] + SEM_CHUNK - 1) // SEM_CHUNK, 1)
            self._sems[e] = [self._stack.enter_context(nc.semaphore("s_%s_%d" % (e, k))) for k in range(n)]
            nsem += n
        dsem = {}
        for i in range(len(slot_total)):
            dsem[("p", i)] = self._stack.enter_context(nc.semaphore("d_%d" % i))
        for i in range(ncc):
            dsem[("c", i)] = self._stack.enter_context(nc.semaphore("c_%d" % i))
        nsem += len(dsem)
        self.nsem = nsem
        waited = {}
        last_abs = {}
        cur_key = {}
        op_key = {}
        phase = 0
        nslot = 0
        slot_of2 = {}
        for op in ops:
            if op.eng == "bar":
                phase += 1
                nslot = 0
                continue
            if op.is_dma:
                b = op.dma_buf
                if b.inc_val == 1:
                    op_key[op.gi] = sem_key[id(b)]
                else:
                    ps_ = slot_of2.get(id(b))
                    if ps_ is None or ps_[0] != phase:
                        slot_of2[id(b)] = (phase, nslot)
                        nslot += 1
                    op_key[op.gi] = ("p", slot_of2[id(b)][1])
        plan = {e: [] for e in self.ENGS}
        for op in ops:
            if op.eng == "bar":
                continue
            waits = {}
            for d in op.deps:
                dop = ops[d]
                if dop.is_dma:
                    b = dop.dma_buf
                    key = ("d",) + cur_key[id(b)]
                    sem = dsem[cur_key[id(b)]]
                    val = b.inc_val * last_abs[id(b)]
                else:
                    idx = dop.inc_idx - 1
                    ch = idx // SEM_CHUNK
                    val = idx % SEM_CHUNK + 1
                    key = ("e", dop.eng, ch)
                    sem = self._sems[dop.eng][ch]
                    for c2 in range(ch):
                        waited[(op.eng, ("e", dop.eng, c2))] = SEM_CHUNK
                cur = waited.get((op.eng, key), 0)
                if val > cur:
                    waited[(op.eng, key)] = val
                    if key not in waits or waits[key][1] < val:
                        waits[key] = (sem, val)
            plan[op.eng].append((op, list(waits.values())))
            if op.is_dma:
                last_abs[id(op.dma_buf)] = abs_cnt[op.gi]
                cur_key[id(op.dma_buf)] = op_key[op.gi]
        self.stats = {e: len(plan[e]) for e in self.ENGS}
        self.stats["nsem"] = nsem
        self.stats["waits"] = sum(len(w) for e in self.ENGS for _, w in plan[e])
        if self.arena is not None:
            self.stats["arena_peak_words"] = self.a_peak
        for e in self.ENGS:
            E = self.e[e]
            for op, waits in plan[e]:
                for sem, val in waits:
                    E.wait_ge(sem, val)
                ins = op.fn(E)
                if op.is_dma:
                    ins.then_inc(dsem[op_key[op.gi]], op.dma_buf.inc_val)
                elif op.inc_idx is not None:
                    idx = op.inc_idx - 1
                    ins.then_inc(self._sems[e][idx // SEM_CHUNK], 1)
        if final_wait_outputs:
            fin = {}
            for op in final_dma:
                kk = op_key[op.gi]
                v = op.dma_buf.inc_val * abs_cnt[op.gi]
                if kk not in fin or fin[kk] < v:
                    fin[kk] = v
            for kk, v in fin.items():
                nc.sync.wait_ge(dsem[kk], v)
        return self.stats


L = 8192
NT = L // 512
NCH = L // 128
NORM_EPS = 1e-6
GN_EPS = 64e-5
C0 = math.exp(-0.5)


def load_hT(k, hT_tile, hsrc, j):
    half, jj = j // 8, j % 8
    v = hT_tile.re("p (g q) t -> p g q t", q=2)
    for nb in range(2):
        k.dma(v[:, :, nb, :], hsrc[half][nb][:, jj * 512:(jj + 1) * 512].re("(g i) t -> i g t", i=128),
              eng="sp" if nb == 0 else "pool")


ATT_KNOBS = {}


def emit_attn(k, io, lambda_init):
    xT, hsrc, pw, wfm, wtm, tabs, vecs, lam = (io.get("xT"), io.get("hT"), io["pw"], io["wfm"], io["wtm"],
                                                io["tabs"], io["vecs"], io["lam"])
    ydst, ydt = io.get("y"), io["ydt"]

    ones_bf = k.sb("ones_bf", [128, 128], BF16)
    k.memset(ones_bf, 1.0)
    eps_t = k.sb("eps_t", [128, 1])
    k.memset(eps_t, NORM_EPS)
    pw_sb = k.sb("pw_sb", [128, 8])
    k.dma(pw_sb, pw)
    vec_sb = k.sb("vec_sb", [128, 8])
    k.dma(vec_sb, vecs)
    lam_sb = k.sb("lam_sb", [128, 256])
    k.dma(lam_sb, lam)
    w_fm = k.sb("w_fm", [128, 8, 1280], BF16)
    w_tm = k.sb("w_tm", [128, 8, 256], BF16)
    wst = [k.sb("wst%d" % i, [128, 8, 256]) for i in range(2)]
    for i in range(6):
        st = wst[i % 2]
        if i < 5:
            k.dma(st, wfm[:, i * 256:(i + 1) * 256].re("(c p) m -> p c m", p=128),
                  eng="pool" if i % 2 else "sp")
            k.copy(w_fm[:, :, i * 256:(i + 1) * 256], st, eng="pool")
        else:
            k.dma(st, wtm.re("(c p) m -> p c m", p=128), eng="pool")
            k.copy(w_tm, st, eng="pool")

    lt = k.sb("lam_t", [128, 128])
    s12 = k.sb("lam_s", [128, 2])
    k.tt(lt[:, 0:64], lam_sb[:, 0:64], lam_sb[:, 64:128], ALU.mult)
    k.tt(lt[:, 64:128], lam_sb[:, 128:192], lam_sb[:, 192:256], ALU.mult)
    k.reduce(s12[:, 0:1], lt[:, 0:64], ALU.add)
    k.reduce(s12[:, 1:2], lt[:, 64:128], ALU.add)
    e12 = k.sb("lam_e", [128, 2])
    k.act(e12, s12, AF.Exp)
    neg_lam = k.sb("neg_lam", [128, 1])
    k.stt(neg_lam, e12[:, 1:2], -float(lambda_init), e12[:, 0:1], ALU.add, ALU.subtract)
    wn_s = k.sb("wn_s", [128, 1])
    k.ts(wn_s, vec_sb[:, 4:5], 1.0 - float(lambda_init), ALU.mult)

    xbuf = [k.sb("xbuf%d" % i, [128, 8, 512]) for i in range(2)]
    hT = [k.sb("hT%d" % i, [128, 8, 512], BF16) for i in range(2)]
    sq = [k.sb("sq%d" % i, [128, 512], BF16) for i in range(3)]
    rstd = [k.sb("rstd%d" % i, [128, 512]) for i in range(2)]
    tab = [k.sb("tab%d" % i, [128, 2, 512]) for i in range(2)]
    tmp = [k.sb("tmp%d" % i, [128, 512]) for i in range(4)]
    qT = k.sb("qT", [128, L], BF16)
    kT = k.sb("kT", [128, L], BF16)
    gT = k.sb("gT", [128, L], BF16)
    v_tm = k.sb("v_tm", [128, 64, 128], BF16)
    NPB = ATT_KNOBS.get("LA", 2) + 1
    Pb2 = [k.sb("Pb2_%d" % i, [128, 2, 512], BF16) for i in range(NPB)]
    acc2 = [k.sb("acc2_%d" % i, [128, 2, 512]) for i in range(2)]
    ones_f = k.sb("ones_f", [128, 128])
    k.memset(ones_f, 1.0)
    ost = [k.sb("ost%d" % i, [128, 512], ydt) for i in range(2)]
    fin = [k.sb("fin%d" % i, [128, 512]) for i in range(5)]
    bank = [k.ps("bank%d" % i, [128, 512]) for i in range(8)]

    for kind in range(2):
        wc = kind * 5
        def stage1(j):
            k.dma(tab[j % 2], tabs[2 * kind:2 * kind + 2, :, j * 512:(j + 1) * 512]
                  .re("a p t -> p a t"), eng="pool")
            if hsrc is not None:
                load_hT(k, hT[j % 2], hsrc, j)
                return
            xt = xbuf[j % 2]
            k.dma(xt, xT[:, j * 512:(j + 1) * 512].re("(c p) t -> p c t", p=128),
                  eng="sp")
            for c in range(8):
                s = sq[c % 3]
                k.act(s, xt[:, c, :], AF.Square)
                k.matmul(bank[0], ones_bf, s, start=(c == 0), stop=(c == 7))
            r = rstd[j % 2]
            k.act(r, bank[0], AF.Ln, scale=1.0 / 1024, bias=eps_t)
            k.act(r, r, AF.Exp, scale=-0.5)
            for c in range(8):
                k.stt(hT[j % 2][:, c, :], xt[:, c, :], pw_sb[:, c:c + 1], r, ALU.mult, ALU.mult)

        def proj_fm(ps, grp, j):
            h = hT[j % 2]
            for c in range(8):
                k.matmul(ps, w_fm[:, c, grp * 128:(grp + 1) * 128], h[:, c, :],
                         start=(c == 0), stop=(c == 7))

        def stage2(j):
            h = hT[j % 2]
            tb = tab[j % 2]
            ts_ = slice(j * 512, (j + 1) * 512)
            for which, dst in ((0, qT), (1, kT)):
                pa, pb = bank[1 + 2 * which], bank[2 + 2 * which]
                proj_fm(pa, wc + 2 * which, j)
                proj_fm(pb, wc + 2 * which + 1, j)
                t1, t2 = tmp[2 * which], tmp[2 * which + 1]
                if kind == 0:
                    k.tt(t1, pa, tb[:, 0, :], ALU.mult)
                    k.tt(t2, pb, tb[:, 1, :], ALU.mult)
                    k.tt(dst[:, ts_], t1, t2, ALU.add, eng="pool")
                else:
                    s = sq[which]
                    k.act(s, pa, AF.Square)
                    k.matmul(bank[7], ones_bf, s)
                    rr = fin[which]
                    k.act(rr, bank[7], AF.Ln, scale=1.0 / 128, bias=eps_t)
                    k.act(rr, rr, AF.Exp, scale=-0.5)
                    k.stt(t1, pa, vec_sb[:, 2 * which:2 * which + 1], tb[:, 0, :], ALU.mult, ALU.mult)
                    k.stt(t2, pb, vec_sb[:, 2 * which + 1:2 * which + 2], tb[:, 1, :], ALU.mult, ALU.mult)
                    k.tt(t1, t1, t2, ALU.add, eng="pool")
                    k.tt(dst[:, ts_], t1, rr, ALU.mult, eng="pool")
            proj_fm(bank[5], wc + 4, j)
            k.act(gT[:, ts_], bank[5], AF.Silu)
            pv = bank[6]
            for s4 in range(4):
                for c in range(8):
                    k.matmul(pv[:, s4 * 128:(s4 + 1) * 128], h[:, c, s4 * 128:(s4 + 1) * 128],
                             w_tm[:, c, kind * 128:(kind + 1) * 128],
                             start=(c == 0), stop=(c == 7))
            k.copy(v_tm[:, j * 4:(j + 1) * 4, :], pv.re("p (s e) -> p s e", s=4), eng="dve")

        for j in range(NT + 1):
            if j < NT:
                stage1(j)
            if j >= 1:
                stage2(j - 1)

        nm = 2 if kind == 0 else 1
        dk = 64 if kind == 0 else 128
        scale = dk ** -0.5
        O = [bank[0], bank[1]][:nm]
        Sb = [[bank[2], bank[4]], [bank[3], bank[5]]]
        Spair = [k.bank_pair(2), k.bank_pair(4)] if (nm == 2 and k.arena is not None) else None
        misc = bank[7]
        pairs = [(qb, kc) for qb in range(16) for kc in range(64)]
        LA = ATT_KNOBS.get("LA", 2)
        npair = len(pairs)

        def finalize(qb):
            qs = slice(qb * 512, (qb + 1) * 512)
            o = []
            for m in range(nm):
                k.matmul(misc, ones_f, acc2[0][:, m, :], start=True, stop=False)
                k.matmul(misc, ones_f, acc2[1][:, m, :], start=False, stop=True)
                r = fin[m]
                k.act(r, misc, AF.Ln)
                k.act(r, r, AF.Exp, scale=-1.0)
                om = fin[2 + m]
                k.tt(om, O[m], r, ALU.mult)
                o.append(om)
            y = ost[qb % 2]
            if kind == 0:
                od = fin[4]
                k.stt(od, o[1], neg_lam, o[0], ALU.mult, ALU.add)
                s = sq[0]
                k.act(s, od, AF.Square)
                k.matmul(misc, ones_bf, s)
                rr = fin[0]
                k.act(rr, misc, AF.Ln, scale=1.0 / 128, bias=eps_t)
                k.act(rr, rr, AF.Exp, scale=-0.5)
                k.stt(od, od, wn_s, rr, ALU.mult, ALU.mult)
                k.tt(y, od, gT[:, qs], ALU.mult, eng="pool")
            else:
                k.tt(y, o[0], gT[:, qs], ALU.mult, eng="pool")
            if io.get("yfn") is not None:
                k.dma(io["yfn"](kind, qb), y, eng="sp")
            else:
                k.dma(ydst[kind][:, qs], y, eng="sp")

        for idx in range(npair + LA):
            if idx < npair:
                qb, kc = pairs[idx]
                for m in range(nm):
                    k.matmul(Sb[m][idx % 2], kT[m * dk:(m + 1) * dk, kc * 128:(kc + 1) * 128],
                             qT[m * dk:(m + 1) * dk, qb * 512:(qb + 1) * 512])
                if Spair is not None:
                    sp2, other = Spair[idx % 2]
                    k.act(Pb2[idx % NPB].re("p m t -> p (m t)"), sp2, AF.Exp, scale=scale, extra_reads=[other])
                else:
                    for m in range(nm):
                        k.act(Pb2[idx % NPB][:, m, :], Sb[m][idx % 2], AF.Exp, scale=scale)
            i2 = idx - LA
            if i2 >= 0:
                qb, kc = pairs[i2]
                P2 = Pb2[i2 % NPB]
                for m in range(nm):
                    k.matmul(O[m], v_tm[:, kc, :], P2[:, m, :], start=(kc == 0), stop=(kc == 63))
                acc = acc2[kc % 2]
                if kc < 2:
                    k.copy(acc[:, 0:nm, :], P2[:, 0:nm, :], eng="dve")
                else:
                    k.tt(acc[:, 0:nm, :], acc[:, 0:nm, :], P2[:, 0:nm, :], ALU.add, eng="dve")
                if kc == 63:
                    finalize(qb)
                    if qb % 8 == 7 and io.get("y_done") is not None:
                        io["y_done"](kind, qb // 8)


def emit_ssd(k, io):
    stop = 99
    xT, hsrc, pw, wfm, wdt, cw, cb, ssmv, cst = (io.get("xT"), io.get("hT"), io["pw"], io["wfm"], io["wdt"],
                                                 io["cw"], io["cb"], io["ssmv"], io["cst"])
    yT, ydt = io.get("y"), io["ydt"]

    ones_bf = k.sb("ones_bf", [128, 128], BF16)
    k.memset(ones_bf, 1.0)
    ones_f = k.sb("ones_f", [128, 128])
    k.memset(ones_f, 1.0)
    eps_t = k.sb("eps_t", [128, 1])
    k.memset(eps_t, NORM_EPS)
    one_t = k.sb("one_t", [128, 1])
    k.memset(one_t, 1.0)
    cst_sb = k.sb("cst_sb", [128, 5, 128])
    k.dma(cst_sb, cst)
    tri = [cst_sb[:, 0, :], cst_sb[:, 1, :]]
    smask = [cst_sb[:, 2, :], cst_sb[:, 3, :]]
    ident_f = cst_sb[:, 4, :]
    ident_bf = k.sb("ident_bf", [128, 128], BF16)
    k.copy(ident_bf, ident_f, eng="dve")
    pw_sb = k.sb("pw_sb", [128, 8])
    k.dma(pw_sb, pw)
    cw_sb = k.sb("cw_sb", [128, 3, 5])
    k.dma(cw_sb, cw)
    cb_sb = k.sb("cb_sb", [128, 3])
    k.dma(cb_sb, cb)
    sv = k.sb("sv", [128, 16])
    k.dma(sv, ssmv)
    w_fm = k.sb("w_fm", [128, 8, 512], BF16)
    w_dt = k.sb("w_dt", [128, 8, 4], BF16)
    wst2 = k.sb("wst2", [128, 8, 4])
    k.dma(wst2, wdt.re("(c p) m -> p c m", p=128), eng="pool")
    k.copy(w_dt, wst2, eng="pool")

    xbuf = [k.sb("xbuf%d" % i, [128, 8, 512]) for i in range(1)]
    k.dma(xbuf[0], wfm.re("(c p) m -> p c m", p=128))
    k.copy(w_fm, xbuf[0], eng="pool")
    hT = [k.sb("hT%d" % i, [128, 8, 512], BF16) for i in range(2)]
    sq = [k.sb("sq%d" % i, [128, 512], BF16) for i in range(3)]
    rstd = [k.sb("rstd%d" % i, [128, 512]) for i in range(1)] * 2
    upad = [k.sb("upad%d" % i, [128, 3, 516]) for i in range(3)]
    acc = [k.sb("acc%d" % i, [128, 512]) for i in range(3)]
    zsT = k.sb("zsT", [128, L], BF16)
    xcT = k.sb("xcT", [128, L], BF16)
    BT = k.sb("BT", [128, L], BF16)
    CT = k.sb("CT", [128, L], BF16)
    dst3 = [xcT, BT, CT]
    dtraw = k.sb("dtraw", [128, 4, NCH])
    bank = [k.ps("bank%d" % i, [128, 512]) for i in range(7)]
    tbank = k.ps("tbank", [128, 1024], BF16)

    k.memset(upad[0][:, :, 0:2], 0.0)

    def stage1(j):
        if hsrc is not None:
            load_hT(k, hT[j % 2], hsrc, j)
            return
        xt = xbuf[0]
        k.dma(xt, xT[:, j * 512:(j + 1) * 512].re("(c p) t -> p c t", p=128), eng="sp")
        for c in range(8):
            s = sq[c % 3]
            k.act(s, xt[:, c, :], AF.Square)
            k.matmul(bank[0], ones_bf, s, start=(c == 0), stop=(c == 7))
        r = rstd[j % 2]
        k.act(r, bank[0], AF.Ln, scale=1.0 / 1024, bias=eps_t)
        k.act(r, r, AF.Exp, scale=-0.5)
        for c in range(8):
            k.stt(hT[j % 2][:, c, :], xt[:, c, :], pw_sb[:, c:c + 1], r, ALU.mult, ALU.mult)

    def stage2(j):
        h = hT[j % 2]
        ts_ = slice(j * 512, (j + 1) * 512)
        up = upad[j % 3]
        for grp in range(4):
            ps = bank[1 + grp]
            for c in range(8):
                k.matmul(ps, w_fm[:, c, grp * 128:(grp + 1) * 128], h[:, c, :],
                         start=(c == 0), stop=(c == 7))
            if grp == 0:
                k.act(zsT[:, ts_], ps, AF.Silu)
            else:
                k.copy(up[:, grp - 1, 2:514], ps, eng="act")
        pdt = bank[5]
        for s4 in range(4):
            for c in range(8):
                k.matmul(pdt[:, s4 * 4:(s4 + 1) * 4], h[:, c, s4 * 128:(s4 + 1) * 128], w_dt[:, c, :],
                         start=(c == 0), stop=(c == 7))
        k.copy(dtraw[:, :, j * 4:(j + 1) * 4].re("p h s -> p s h"),
               pdt[:, 0:16].re("p (s h) -> p s h", s=4), eng="dve")

    def conv(j):
        up = upad[j % 3]
        if j > 0:
            k.copy(up[:, :, 0:2], upad[(j - 1) % 3][:, :, 512:514], eng="pool")
        if j < NT - 1:
            k.copy(up[:, :, 514:516], upad[(j + 1) % 3][:, :, 2:4], eng="pool")
        else:
            k.memset(up[:, :, 514:516], 0.0)
        ts_ = slice(j * 512, (j + 1) * 512)
        for ch in range(3):
            a = acc[ch]
            k.ts(a, up[:, ch, 0:512], cw_sb[:, ch, 0:1], ALU.mult)
            for o in range(1, 5):
                k.stt(a, up[:, ch, o:o + 512], cw_sb[:, ch, o:o + 1], a, ALU.mult, ALU.add)
            k.act(dst3[ch][:, ts_], a, AF.Silu, bias=cb_sb[:, ch:ch + 1])

    for j in range(NT + 2):
        if j < NT:
            stage1(j)
        if 1 <= j <= NT:
            stage2(j - 1)
        if j >= 2:
            conv(j - 2)


    dt = k.sb("dt", [128, 4, NCH])
    a_ = k.sb("a_", [128, 4, NCH])
    cum = k.sb("cum", [128, 4, NCH])
    dtd = k.sb("dtd", [128, 4, NCH])
    etot = k.sb("etot", [128, 4, NCH])
    aneg = k.sb("aneg", [128, 4])
    k.act(aneg, sv[:, 4:8], AF.Exp)
    k.ts(aneg, aneg, -1.0, ALU.mult)
    for hd in range(4):
        k.act(dt[:, hd, :], dtraw[:, hd, :], AF.Exp, bias=sv[:, hd:hd + 1])
    k.act(dt, dt, AF.Ln, bias=one_t)
    for hd in range(4):
        k.ts(a_[:, hd, :], dt[:, hd, :], aneg[:, hd:hd + 1], ALU.mult)
    pc = bank[0]
    k.matmul(pc[:, 0:128], tri[0], a_[:, 0:2, :].re("p h c -> p (h c)"))
    k.matmul(pc[:, 128:256], tri[1], a_[:, 2:4, :].re("p h c -> p (h c)"))
    k.matmul(pc[:, 256:512], ones_f, a_.re("p h c -> p (h c)"))
    k.copy(cum.re("p h c -> p (h c)"), pc[:, 0:256], eng="dve")
    k.act(etot.re("p h c -> p (h c)"), pc[:, 256:512], AF.Exp)
    k.tt(dtd.re("p h c -> p (h c)"), pc[:, 256:512], cum.re("p h c -> p (h c)"), ALU.subtract)
    k.act(dtd, dtd, AF.Exp)
    k.tt(dtd, dtd, dt, ALU.mult)

    Sf_all = k.sb("Sf_all", [128, NCH, 128], BF16)
    Sb_all = k.sb("Sb_all", [128, NCH, 128], BF16)
    Srun = [k.sb("Srun%d" % i, [128, 128]) for i in range(2)]
    btm = [k.sb("btm%d" % i, [128, 128], BF16) for i in range(2)]
    xdtd = [k.sb("xdtd%d" % i, [128, 128], BF16) for i in range(2)]
    for d in range(2):
        k.memset(Srun[d], 0.0)
        order = range(NCH) if d == 0 else range(NCH - 1, -1, -1)
        S_all = Sf_all if d == 0 else Sb_all
        for i, c in enumerate(order):
            cs = slice(c * 128, (c + 1) * 128)
            tb = tbank[:, (i % 2) * 256:(i % 2) * 256 + 256]
            k.transpose(tb[:, 0:128], xcT[:, cs], ident_bf)
            k.transpose(tb[:, 128:256], BT[:, cs], ident_bf)
            bt = btm[i % 2]
            k.copy(bt, tb[:, 128:256], eng="act")
            xd = xdtd[i % 2]
            for h in range(2):
                hd = d * 2 + h
                k.ts(xd[:, h * 64:(h + 1) * 64], tb[:, h * 64:(h + 1) * 64], dtd[:, hd, c:c + 1], ALU.mult)
            st = bank[1 + i % 2]
            k.matmul(st[:, 0:128], bt, xd)
            k.copy(S_all[:, c, :], Srun[d], eng="act")
            for h in range(2):
                hd = d * 2 + h
                hb = slice(h * 64, (h + 1) * 64)
                k.stt(Srun[d][:, hb], Srun[d][:, hb], etot[:, hd, c:c + 1], st[:, hb], ALU.mult, ALU.add)

    lhsD = [k.sb("lhsD%d" % i, [128, 4, 128]) for i in range(1)] * 2
    abc = [k.sb("abc%d" % i, [128, 4, 128]) for i in range(1)] * 2
    Lexp = [k.sb("Lexp%d" % i, [128, 512]) for i in range(2)]
    Ebc = [k.sb("Ebc%d" % i, [128, 512]) for i in range(2)]
    Gm = [k.sb("Gm%d" % i, [128, 2, 128]) for i in range(2)]
    MT = [k.sb("MT%d" % i, [128, 4, 128], BF16) for i in range(2)]
    Ct = [k.sb("Ct%d" % i, [128, 4, 128], BF16) for i in range(2)]
    xtm = [k.sb("xtm%d" % i, [128, 128]) for i in range(2)]
    xdt = [k.sb("xdt%d" % i, [128, 4, 64], BF16) for i in range(2)]
    y_sb = [k.sb("y_sb%d" % i, [128, 128]) for i in range(2)]
    ost = [k.sb("ost%d" % i, [128, 512], ydt) for i in range(2)]
    def front(c):
        cs = slice(c * 128, (c + 1) * 128)
        p = c % 2
        tb = tbank[:, p * 256:p * 256 + 128]
        k.transpose(tb, xcT[:, cs], ident_bf)
        k.copy(xtm[p], tb, eng="act")
        for hd in range(4):
            k.ts(xdt[p][:, hd, :], xtm[p][:, (hd % 2) * 64:(hd % 2) * 64 + 64], dt[:, hd, c:c + 1], ALU.mult)
        for hd in range(4):
            d = hd // 2
            k.ts(lhsD[p][:, hd, :], smask[d], a_[:, hd, c:c + 1], ALU.mult)
            k.ts(abc[p][:, hd, :], ones_f, a_[:, hd, c:c + 1], ALU.mult)
        Dps, Cps = bank[3], bank[4]
        for hd in range(4):
            d = hd // 2
            k.matmul(Dps[:, hd * 128:(hd + 1) * 128], lhsD[p][:, hd, :], tri[d])
        for hd in range(4):
            d = hd // 2
            k.matmul(Cps[:, hd * 128:(hd + 1) * 128], abc[p][:, hd, :], tri[d])
        k.act(Lexp[p], Dps, AF.Exp)
        k.act(Ebc[p], Cps, AF.Exp)
        Gps = bank[5]
        k.matmul(Gps[:, 0:128], BT[:, cs], CT[:, cs])
        for d in range(2):
            k.tt(Gm[p][:, d, :], Gps[:, 0:128], tri[d], ALU.mult)

    def back(c):
        cs = slice(c * 128, (c + 1) * 128)
        p = c % 2
        for hd in range(4):
            d = hd // 2
            k.tt(MT[p][:, hd, :], Lexp[p][:, hd * 128:(hd + 1) * 128], Gm[p][:, d, :], ALU.mult)
            k.tt(Ct[p][:, hd, :], Ebc[p][:, hd * 128:(hd + 1) * 128], CT[:, cs], ALU.mult)
        Yps = bank[6]
        for h in range(2):
            hb = slice(h * 64, (h + 1) * 64)
            k.matmul(Yps[:, hb], MT[p][:, h, :], xdt[p][:, h, :], start=True, stop=False)
            k.matmul(Yps[:, hb], MT[p][:, 2 + h, :], xdt[p][:, 2 + h, :], start=False, stop=False)
            k.matmul(Yps[:, hb], Ct[p][:, h, :], Sf_all[:, c, hb], start=False, stop=False)
            k.matmul(Yps[:, hb], Ct[p][:, 2 + h, :], Sb_all[:, c, hb], start=False, stop=True)
        for h in range(2):
            hb = slice(h * 64, (h + 1) * 64)
            k.stt(y_sb[p][:, hb], xtm[p][:, hb], sv[:, 8 + h:9 + h], Yps[:, hb], ALU.mult, ALU.add)
        yTp = bank[1 + p]
        k.transpose(yTp[:, 0:128], y_sb[p], ident_f)
        o = ost[(c // 4) % 2]
        k.tt(o[:, (c % 4) * 128:(c % 4 + 1) * 128], yTp[:, 0:128], zsT[:, cs], ALU.mult)
        if c % 4 == 3:
            k.dma(io["yfn"](None, c // 4) if io.get("yfn") is not None else yT[:, (c - 3) * 128:(c + 1) * 128], o, eng="sp")

    for c in range(NCH + 1):
        if c < NCH:
            front(c)
        if c >= 1:
            back(c - 1)


L = 8192
NT = L // 512
NORM_EPS = 1e-6
GN_EPS = 64e-5
C0 = math.exp(-0.5)


def emit_rwkv(k, io):
    stop, nq_lim, do_chain, do_pre, pre_lim = 99, None, True, True, 9
    xT, hsrc, pw, wfm, mu, w2a2, pvd, cm, c128 = (io.get("xT"), io.get("hT"), io["pw"], io["wfm"], io["mu"],
                                                  io["w2a2"], io["pvd"], io["cm"], io["c128"])
    wkv_scr, bon_scr, yT, ydt = io["wkv_scr"], io["bon_scr"], io.get("y"), io["ydt"]

    ones_bf = k.sb("ones_bf", [128, 128], BF16)
    k.memset(ones_bf, 1.0)
    eps_t = k.sb("eps_t", [128, 1])
    k.memset(eps_t, NORM_EPS)
    epsg_t = k.sb("epsg_t", [128, 1])
    k.memset(epsg_t, GN_EPS)
    tiny_t = k.sb("tiny_t", [128, 1])
    k.memset(tiny_t, 1e-24)
    pw_sb = k.sb("pw_sb", [128, 8])
    k.dma(pw_sb, pw)
    mu_sb = k.sb("mu_sb", [128, 2, 4])
    k.dma(mu_sb, mu)
    w2a2_sb = k.sb("w2a2_sb", [128, 2, 128])
    k.dma(w2a2_sb, w2a2)
    pv = k.sb("pv", [128, 12])
    k.dma(pv, pvd)
    omk = k.sb("omk", [128, 1])
    k.ts(omk, pv[:, 4:5], -1.0, ALU.mult, 1.0, ALU.add)
    cm_sb = k.sb("cm_sb", [64, 2, 704])
    k.dma(cm_sb, cm)
    c128_sb = k.sb("c128_sb", [128, 2, 128])
    k.dma(c128_sb, c128)
    blk_bf = k.sb("blk_bf", [128, 128], BF16)
    k.copy(blk_bf, c128_sb[:, 0, :], eng="dve")
    ident_f = c128_sb[:, 1, :]
    ident_bf = k.sb("ident_bf", [128, 128], BF16)
    k.copy(ident_bf, ident_f, eng="dve")
    id64_bf = k.sb("id64_bf", [64, 4, 64], BF16)
    for h in range(4):
        k.copy(id64_bf[:, h, :], cm_sb[:, 0, 640:704], eng="dve")
    maskSC = [cm_sb[:, d, 0:512] for d in range(2)]
    maskN = [cm_sb[:, d, 512:640] for d in range(2)]

    xbuf = k.sb("xbuf", [128, 8, 512])
    w_fm = k.sb("w_fm", [128, 8, 640], BF16)
    k.dma(xbuf, wfm[:, 0:512].re("(c p) m -> p c m", p=128))
    k.copy(w_fm[:, :, 0:512], xbuf, eng="pool")
    k.dma(xbuf[:, :, 0:128], wfm[:, 512:640].re("(c p) m -> p c m", p=128))
    k.copy(w_fm[:, :, 512:640], xbuf[:, :, 0:128], eng="pool")

    hT = k.sb("hT", [128, 8, 512], BF16)
    sq = [k.sb("sq%d" % i, [128, 512], BF16) for i in range(2)]
    rstd = k.sb("rstd", [128, 512])
    upad = [k.sb("upad%d" % d, [128, 4, 514]) for d in range(2)]
    ul = k.sb("ul", [128, 4, 512])
    tmpd = [k.sb("tmpd%d" % i, [128, 512]) for i in range(2)]
    gT = k.sb("gT", [128, L], BF16)
    twd = k.sb("twd", [64, 512])
    sg = k.sb("sg", [128, 512])
    aic = k.sb("aic", [128, 512])
    kk = k.sb("kk", [128, 512])
    kkn = k.sb("kkn", [128, 512])
    rs = k.sb("rs", [128, 512])
    tka = k.sb("tka", [128, 512])
    k2 = k.sb("k2", [128, 512])
    bvec = k.sb("bvec", [128, 512])
    rk = k.sb("rk", [128, 512], BF16)
    bon = [k.sb("bon%d" % i, [128, 512]) for i in range(2)]
    cs = [k.sb("cs%d" % i, [128, 512]) for i in range(2)]
    csm = k.sb("csm", [128, 512])
    Epos = k.sb("Epos", [128, 512])
    Eneg = k.sb("Eneg", [128, 512])
    Eprev = k.sb("Eprev", [128, 512])
    RopT = [[k.sb("RopT%d%d" % (d, s), [128, 8, 2, 64], BF16) for s in range(2)] for d in range(2)]
    LopT = [[k.sb("LopT%d%d" % (d, s), [128, 8, 2, 64], BF16) for s in range(2)] for d in range(2)]
    vTb = [[k.sb("vTb%d%d" % (d, s), [128, 512], BF16) for s in range(2)] for d in range(2)]
    eC = [[k.sb("eC%d%d" % (d, s), [128, 8]) for s in range(2)] for d in range(2)]
    NR = 4
    tm = [[k.sb("tm%d%d" % (d, i), [64, 384], BF16) for i in range(NR)] for d in range(2)]
    scmB = [k.sb("scmB%d" % i, [64, 2, 4, 128], BF16) for i in range(NR)]
    TtB = [k.sb("TtB%d" % i, [64, 4, 64], BF16) for i in range(NR)]
    NMt2 = [[k.sb("NMt%d_%d" % (a, i), [64, 2, 4, 64], BF16) for i in range(2)] for a in range(2)]
    Pt2 = [[k.sb("Pt%d_%d" % (a, i), [64, 4, 64], BF16) for i in range(2)] for a in range(2)]
    S32 = [k.sb("S32_%d" % d, [128, 128]) for d in range(2)]
    t1 = [k.sb("t1_%d" % d, [128, 128]) for d in range(2)]
    Sbf = [[k.sb("Sbf%d%d" % (d, i), [128, 128], BF16) for i in range(2)] for d in range(2)]
    Zb = [k.sb("Zb%d" % d, [64, 128], BF16) for d in range(2)]
    Ub = [k.sb("Ub%d" % d, [64, 128], BF16) for d in range(2)]
    Yo = [[k.sb("Yo%d%d" % (d, i), [64, 128]) for i in range(2)] for d in range(2)]
    bankA = [k.ps("bankA%d" % i, [128, 512]) for i in range(2)]
    bSC = k.ps("bSC", [128, 512])
    bNM = k.ps("bNM", [128, 512])
    bP = k.ps("bP", [128, 512])
    bCH = [k.ps("bCH%d" % d, [128, 512]) for d in range(2)]
    tb32 = k.ps("tbank", [128, 512])
    tbank = T(tb32.buf, tb32.ap.bitcast(BF16))

    for d in range(2):
        k.memset(S32[d], 0.0)
        k.memset(Sbf[d][0], 0.0)
    k.memset(upad[0][:, :, 0:1], 0.0)
    k.memset(upad[1][:, :, 513:514], 0.0)

    state = {"a": 0}

    def nbank():
        return bankA[0]

    def phaseA(d, j, slot):
        first = (j == 0) if d == 0 else (j == NT - 1)
        up = upad[d]
        if not first:
            if d == 0:
                k.copy(up[:, :, 0:1], up[:, :, 512:513], eng="pool")
            else:
                k.copy(up[:, :, 513:514], up[:, :, 1:2], eng="pool")
        if hsrc is not None:
            load_hT(k, hT, hsrc, j)
        else:
            k.dma(xbuf, xT[:, j * 512:(j + 1) * 512].re("(c p) t -> p c t", p=128), eng="sp")
            ps = nbank()
            for c in range(8):
                s = sq[c % 2]
                k.act(s, xbuf[:, c, :], AF.Square)
                k.matmul(ps, ones_bf, s, start=(c == 0), stop=(c == 7))
            k.act(rstd, ps, AF.Ln, scale=1.0 / 1024, bias=eps_t)
            k.act(rstd, rstd, AF.Exp, scale=-0.5)
            for c in range(8):
                k.stt(hT[:, c, :], xbuf[:, c, :], pw_sb[:, c:c + 1], rstd, ALU.mult, ALU.mult)
        ts_ = slice(j * 512, (j + 1) * 512)
        for grp in range(5 if d == 0 else 4):
            ps = nbank()
            for c in range(8):
                k.matmul(ps, w_fm[:, c, grp * 128:(grp + 1) * 128], hT[:, c, :],
                         start=(c == 0), stop=(c == 7))
            if grp < 4:
                k.copy(up[:, grp, 1:513], ps, eng="act")
            else:
                k.act(gT[:, ts_], ps, AF.Silu)
        sh = slice(0, 512) if d == 0 else slice(2, 514)
        for grp in range(4):
            td = tmpd[grp % 2]
            k.tt(td, up[:, grp, sh], up[:, grp, 1:513], ALU.subtract, eng="pool")
            k.stt(ul[:, grp, :], td, mu_sb[:, d, grp:grp + 1], up[:, grp, 1:513], ALU.mult, ALU.add)
        r, kx, v, wa = ul[:, 0, :], ul[:, 1, :], ul[:, 2, :], ul[:, 3, :]
        k.copy(vTb[d][slot], v, eng="pool")
        k.act(twd, wa[0:64, :], AF.Tanh)
        pxw = nbank()
        k.matmul(pxw, w2a2_sb[0:64, d, :], twd)
        k.act(sg, pxw, AF.Sigmoid, bias=pv[:, d:d + 1])
        pxa = nbank()
        k.matmul(pxa, w2a2_sb[64:128, d, :], wa[64:128, :])
        k.act(aic, pxa, AF.Sigmoid, bias=pv[:, 2:3])
        k.ts(kk, kx, pv[:, 3:4], ALU.mult)
        k.act(sq[0], kk, AF.Square)
        pss = nbank()
        k.matmul(pss, blk_bf, sq[0])
        k.act(rs, pss, AF.Ln, bias=tiny_t)
        k.act(rs, rs, AF.Exp, scale=-0.5)
        k.tt(kkn, kk, rs, ALU.mult)
        k.ts(tka, aic, pv[:, 4:5], ALU.mult, omk, ALU.add)
        k.tt(k2, kx, tka, ALU.mult)
        k.tt(bvec, kkn, aic, ALU.mult, eng="pool")
        k.stt(rk, r, pv[:, 6:7], k2, ALU.mult, ALU.mult)
        pbs = nbank()
        k.matmul(pbs, blk_bf, rk)
        bo = bon[d]
        k.tt(bo, pbs, v, ALU.mult)
        k.dma(bon_scr[d, :, ts_], bo, eng="sp")
        src = sg
        i = 0
        for s in (1, 2, 4, 8, 16, 32):
            dst = cs[i % 2]
            sv_, dv_ = src.re("p (c t) -> p c t", t=64), dst.re("p (c t) -> p c t", t=64)
            if d == 0:
                k.tt(dv_[:, :, s:64], sv_[:, :, s:64], sv_[:, :, 0:64 - s], ALU.add)
                k.copy(dv_[:, :, 0:s], sv_[:, :, 0:s], eng="pool")
            else:
                k.tt(dv_[:, :, 0:64 - s], sv_[:, :, 0:64 - s], sv_[:, :, s:64], ALU.add)
                k.copy(dv_[:, :, 64 - s:64], sv_[:, :, 64 - s:64], eng="pool")
            src = dst
            i += 1
        csf = src
        k.tt(csm, csf, sg, ALU.subtract, eng="pool")
        k.act(Epos, csf, AF.Exp, scale=-C0)
        k.act(Eneg, csf, AF.Exp, scale=C0)
        k.act(Eprev, csm, AF.Exp, scale=-C0)
        last = 63 if d == 0 else 0
        k.copy(eC[d][slot], Epos.re("p (c t) -> p c t", t=64)[:, :, last], eng="pool")
        R, Lo = RopT[d][slot], LopT[d][slot]
        v3 = lambda t: t.re("p (c t) -> p c t", t=64)
        k.stt(R[:, :, 0, :], v3(kkn), -1.0, v3(Eprev), ALU.mult, ALU.mult)
        k.tt(R[:, :, 1, :], v3(r), v3(Epos), ALU.mult)
        k.tt(Lo[:, :, 0, :], v3(bvec), v3(Eneg), ALU.mult)
        k.tt(Lo[:, :, 1, :], v3(k2), v3(Eneg), ALU.mult, eng="pool")

    def pre_stages(q):
        items = items_for(q)
        i3 = q % NR
        R = [RopT[d][slot][:, cc] for (d, slot, cc, _, _) in items]
        Lo = [LopT[d][slot][:, cc] for (d, slot, cc, _, _) in items]
        sc = scmB[i3]
        NMt, Pt = NMt2[q % 2], Pt2[q % 2]
        st = []

        def s_tr():
            for (d, slot, cc, _, _) in items:
                tps = tbank[0:64, (d * 384):(d * 384) + 384]
                k.transpose(tps[:, 0:128], Lo[d][:, 0, :], ident_bf)
                k.transpose(tps[:, 128:256], Lo[d][:, 1, :], ident_bf)
                k.transpose(tps[:, 256:384], vTb[d][slot][:, cc * 64:(cc + 1) * 64], ident_bf)
                k.copy(tm[d][i3], tps, eng="act")
        st.append(s_tr)

        def s_sc():
            for h in range(2):
                hs = slice(64 * h, 64 * h + 64)
                bk = bSC if h == 0 else bankA[1]
                nk = bP[0:64, 256:384] if h == 0 else tb32[0:64, 384:512]
                for d in range(2):
                    Rh = R[d][hs].re("p a t -> p (a t)")
                    k.matmul(bk[0:64, d * 256:d * 256 + 128], Lo[d][hs, 0, :], Rh)
                    k.matmul(bk[0:64, d * 256 + 128:d * 256 + 256], Lo[d][hs, 1, :], Rh)
                for d in range(2):
                    k.matmul(nk[:, d * 64:(d + 1) * 64], R[d][hs, 0, :], Lo[d][hs, 0, :])
            nm = NMt[0]
            for h in range(2):
                bk = bSC if h == 0 else bankA[1]
                nk = bP[0:64, 256:384] if h == 0 else tb32[0:64, 384:512]
                k.tt(sc[:, :, 2 * h:2 * h + 2, :].re("p d a t -> p d (a t)"),
                     bk[0:64, :].re("p (d x) -> p d x", d=2), cm_sb[:, :, 0:256], ALU.mult)
                k.tt(nm[:, 0].re("p (d h) t -> p d h t", h=2)[:, :, h, :], nk.re("p (d t) -> p d t", d=2),
                     cm_sb[:, :, 512:576], ALU.mult)
            k.copy(nm[:, 1].re("p (d h) t -> p d h t", h=2),
                   sc.re("p d (h two) t -> p d h two t", two=2)[:, :, :, 0, 0:64], eng="dve")
            k.tt(Pt[0], nm[:, 1], id64_bf, ALU.add, eng="dve")
        st.append(s_sc)

        for lv in range(5):
            cur = lv % 2
            nm_c, nm_n = NMt[cur], NMt[1 - cur]
            P_c = Pt[cur]
            P_n = Pt[1 - cur] if lv < 4 else TtB[i3]

            def s_nm(lv=lv, nm_c=nm_c, nm_n=nm_n):
                pn = bNM[0:64, :]
                for dh in range(4):
                    k.matmul(pn[:, dh * 64:(dh + 1) * 64], nm_c[:, 1, dh, :], nm_c[:, 0, dh, :])
                if lv < 4:
                    for dh in range(4):
                        k.matmul(pn[:, 256 + dh * 64:256 + (dh + 1) * 64], nm_c[:, 0, dh, :], nm_c[:, 1, dh, :])
                    k.copy(nm_n.re("p a h t -> p (a h t)"), pn, eng="act")
                else:
                    k.copy(nm_n[:, 0].re("p h t -> p (h t)"), pn[:, 0:256], eng="act")
            st.append(s_nm)

            def s_p(nm_n=nm_n, P_c=P_c, P_n=P_n):
                pp = bP[0:64, 0:256]
                for dh in range(4):
                    k.matmul(pp[:, dh * 64:(dh + 1) * 64], nm_n[:, 0, dh, :], P_c[:, dh, :], start=True, stop=False)
                    k.matmul(pp[:, dh * 64:(dh + 1) * 64], id64_bf[:, 0, :], P_c[:, dh, :], start=False, stop=True)
                k.copy(P_n.re("p h t -> p (h t)"), pp, eng="dve")
            st.append(s_p)
        return st

    par = [0, 0]

    def chain_stages(q):
        items = items_for(q)
        i3 = q % NR
        ctx = []
        for (d, slot, cc, _, cg) in items:
            CB = bCH[d]
            ctx.append(dict(d=d, R=RopT[d][slot][:, cc], sc=scmB[i3][:, d], tmv=tm[d][i3], T=TtB[i3][:, 2 * d:2 * d + 2, :],
                            X=CB[0:64, 0:128], U=CB[0:64, 128:256], DS=CB[:, 256:384], Y=CB[0:64, 384:512],
                            Sb=Sbf[d][par[d]], Sn=Sbf[d][1 - par[d]], e=eC[d][slot][:, cc:cc + 1],
                            cg=cg, q=q))
            par[d] = 1 - par[d]
        st = []

        def c_x():
            for c in ctx:
                k.matmul(c["X"], c["R"][:, 0, :], c["Sb"], start=True, stop=False)
                for h in range(2):
                    hb = slice(64 * h, 64 * h + 64)
                    k.matmul(c["X"][:, hb], c["sc"][:, 2 * h + 1, 0:64], c["tmv"][:, 256 + 64 * h:256 + 64 * h + 64],
                             start=False, stop=(h == 1))
            for c in ctx:
                k.copy(Zb[c["d"]], c["X"], eng="act")
        st.append(c_x)

        def c_u():
            for c in ctx:
                for h in range(2):
                    hb = slice(64 * h, 64 * h + 64)
                    k.matmul(c["U"][:, hb], c["T"][:, h, :], Zb[c["d"]][:, hb])
            for c in ctx:
                k.copy(Ub[c["d"]], c["U"], eng="dve")
        st.append(c_u)

        def c_y():
            for c in ctx:
                d = c["d"]
                k.matmul(c["DS"], c["tmv"][:, 0:128], Ub[d], start=True, stop=False)
                k.matmul(c["DS"], c["tmv"][:, 128:256], c["tmv"][:, 256:384], start=False, stop=True)
                k.ts(t1[d], S32[d], c["e"], ALU.mult)
            for c in ctx:
                d = c["d"]
                k.matmul(c["Y"], c["R"][:, 1, :], c["Sb"], start=True, stop=False)
                for h in range(2):
                    hb = slice(64 * h, 64 * h + 64)
                    k.matmul(c["Y"][:, hb], c["sc"][:, 2 * h, 64:128], Ub[d][:, hb], start=False, stop=False)
                    k.matmul(c["Y"][:, hb], c["sc"][:, 2 * h + 1, 64:128], c["tmv"][:, 256 + 64 * h:256 + 64 * h + 64],
                             start=False, stop=(h == 1))
        st.append(c_y)

        def c_s():
            for c in ctx:
                d = c["d"]
                for h in range(2):
                    hs = slice(64 * h, 64 * h + 64)
                    k.stt(S32[d][hs, hs], c["DS"][hs, hs], c["e"][hs], t1[d][hs, hs], ALU.mult, ALU.add)
            for c in ctx:
                d = c["d"]
                k.copy(c["Sn"], S32[d], eng="act")
                yo = Yo[d][c["q"] % 2]
                k.copy(yo, c["Y"], eng="dve")
                k.dma(wkv_scr[d, c["cg"] * 64:(c["cg"] + 1) * 64, :], yo, eng="pool")
        st.append(c_s)
        return st

    def items_for(q):
        s, ci = q // 8, q % 8
        return [(0, s % 2, ci, q, s * 8 + ci), (1, s % 2, 7 - ci, q, (NT - 1 - s) * 8 + (7 - ci))]

    nq = NT * 8
    phaseA(0, 0, 0)
    phaseA(1, NT - 1, 0)
    pendA = []
    perA = 1
    for q in range(0, nq + 2, 2):
        PA, PB = [], []
        if q < nq:
            s, ci = q // 8, q % 8
            if ci == 2 and s + 1 < NT:
                k.begin_defer()
                phaseA(0, s + 1, (s + 1) % 2)
                phaseA(1, NT - 2 - s, (s + 1) % 2)
                pendA = k.end_defer()
                perA = (len(pendA) + 3 * 30 - 1) // (3 * 30)
            PA = pre_stages(q)
            PB = pre_stages(q + 1)
        cs_ = []
        if q >= 2:
            cs_ = chain_stages(q - 2) + chain_stages(q - 1)
        np_, nc_ = len(PA), len(cs_)
        ci_ = 0
        for i in range(np_):
            PA[i]()
            if pendA:
                k.splice(pendA, perA)
            PB[i]()
            if pendA:
                k.splice(pendA, perA)
            want = (i + 1) * nc_ // np_
            while ci_ < want:
                cs_[ci_]()
                if pendA:
                    k.splice(pendA, perA)
                ci_ += 1
        while ci_ < nc_:
            cs_[ci_]()
            ci_ += 1
        if q < nq and q % 8 == 6 and pendA:
            k.splice(pendA, len(pendA))
    assert not pendA

    wf = [k.sb("wf%d" % i, [128, 128]) for i in range(2)]
    wb = [k.sb("wb%d" % i, [128, 128]) for i in range(2)]
    ww = [k.sb("ww%d" % i, [128, 128]) for i in range(2)]
    sqw_ = [k.sb("sqw%d" % i, [128, 128]) for i in range(2)]
    st1_ = [k.sb("st1_%d" % i, [128, 2]) for i in range(2)]
    st2_ = [k.sb("st2_%d" % i, [128, 2]) for i in range(2)]
    mean_ = [k.sb("mean%d" % i, [128, 2]) for i in range(2)]
    msq_ = [k.sb("msq%d" % i, [128, 2]) for i in range(2)]
    var_ = [k.sb("var%d" % i, [128, 2]) for i in range(2)]
    gn = [k.sb("gn%d" % i, [128, 128]) for i in range(2)]
    ob = [k.sb("ob%d" % i, [128, 512]) for i in range(2)]
    obo = [k.sb("obo%d" % i, [128, 512], ydt) for i in range(2)]
    bf_ = [k.sb("bf_%d" % i, [128, 512]) for i in range(2)]
    bb_ = [k.sb("bb_%d" % i, [128, 512]) for i in range(2)]
    for i in range(L // 128):
        p = i % 2
        k.dma(wf[p], wkv_scr[0, i * 128:(i + 1) * 128, :], eng="sp")
        k.dma(wb[p], wkv_scr[1, i * 128:(i + 1) * 128, :], eng="sp")
        w = ww[p]
        sqw, st1, st2, mean, msq, var = sqw_[p], st1_[p], st2_[p], mean_[p], msq_[p], var_[p]
        k.tt(w, wf[p], wb[p], ALU.add)
        k.reduce(st1, w.re("p (h v) -> p h v", h=2), ALU.add)
        k.tt(sqw, w, w, ALU.mult, eng="pool")
        k.reduce(st2, sqw.re("p (h v) -> p h v", h=2), ALU.add)
        k.ts(mean, st1, 1.0 / 64, ALU.mult)
        k.tt(msq, mean, mean, ALU.mult)
        k.stt(var, st2, 1.0 / 64, msq, ALU.mult, ALU.subtract)
        k.act(var, var, AF.Sqrt, bias=epsg_t)
        k.recip(var, var)
        g_ = gn[p]
        for h in range(2):
            hb = slice(64 * h, 64 * h + 64)
            k.ts(g_[:, hb], w[:, hb], mean[:, h:h + 1], ALU.subtract, var[:, h:h + 1], ALU.mult)
        tb = bankA[(i // 4) % 2]
        k.transpose(tb[:, (i % 4) * 128:(i % 4 + 1) * 128], g_, ident_f)
        if i % 4 == 3:
            j = i // 4
            ts_ = slice(j * 512, (j + 1) * 512)
            o = ob[j % 2]
            k.dma(bf_[j % 2], bon_scr[0, :, ts_], eng="pool")
            k.dma(bb_[j % 2], bon_scr[1, :, ts_], eng="pool")
            k.ts(o, tb, pv[:, 7:8], ALU.mult, pv[:, 8:9], ALU.add)
            k.tt(o, o, bf_[j % 2], ALU.add, eng="pool")
            k.tt(o, o, bb_[j % 2], ALU.add, eng="pool")
            k.tt(obo[j % 2], o, gT[:, ts_], ALU.mult)
            k.dma(io["yfn"](None, j) if io.get("yfn") is not None else yT[:, ts_], obo[j % 2], eng="sp")


GROUPS = [[0, 1, 2, 3], [4, 5, 6, 7]]


def emit_out(k, io, last):
    Lx = 8192
    NTx = Lx // 512
    ygath, wo, xsrc, xdst = io["ygath"], io["wo"], io["xsrc"], io["xdst"]
    ones_bf = k.sb("ones_bf", [128, 128], BF16)
    k.memset(ones_bf, 1.0)
    eps_t = k.sb("eps_t", [128, 1])
    k.memset(eps_t, 1e-6)
    postw_sb = k.sb("postw_sb", [128, 2])
    k.dma(postw_sb, io["postw"])
    snw_sb = k.sb("snw_sb", [128, 4])
    k.dma(snw_sb, io["snw"])
    prew_sb = k.sb("prew_sb", [128, 2])
    k.dma(prew_sb, io["prew"])
    w_bf = k.sb("w_bf", [128, 16, 256], BF16)
    wst = [k.sb("wst%d" % i, [128, 4, 256]) for i in range(2)]
    for i in range(4):
        st = wst[i % 2]
        k.dma(st, wo[i * 512:(i + 1) * 512, :].re("(c p) m -> p c m", p=128), eng="pool" if i % 2 else "sp")
        k.copy(w_bf[:, 4 * i:4 * i + 4, :], st, eng="pool")
    mixT = k.sb("mixT", [128, 2, Lx])
    ssrow = k.sb("ssrow", [1, Lx])
    ytile = [k.sb("ytile%d" % i, [128, 16, 512], BF16) for i in range(2)]
    yn = [k.sb("yn%d" % i, [128, 4, 512], BF16) for i in range(2)]
    sq = [k.sb("sq%d" % i, [128, 512], BF16) for i in range(2)]
    rs = k.sb("rs", [128, 512])
    tot = [k.sb("tot%d" % i, [128, 512]) for i in range(2)]
    xt = [k.sb("xt%d" % i, [128, 2, 512]) for i in range(2)]
    hb = [k.sb("hb%d" % i, [128, 2, 512], BF16) for i in range(2)]
    pg = k.ps("pg", [128, 512])
    pm = [k.ps("pm%d" % i, [128, 512]) for i in range(2)]
    pss = k.ps("pss", [128, 512])

    def p1(j):
        ts_ = slice(j * 512, (j + 1) * 512)
        yt = ytile[j % 2]
        ytv = yt.re("p (g q) t -> p g q t", q=4)
        half, jj = j // 8, j % 8
        for kind in range(4):
            k.dma(ytv[:, :, kind, :], ygath[half][kind][:, jj * 512:(jj + 1) * 512].re("(g i) t -> i g t", i=128),
                  eng="sp" if kind % 2 == 0 else "pool")
        ynj = yn[j % 2]
        for grp in range(2):
            for ci, g in enumerate((2 * grp, 2 * grp + 1)):
                k.act(sq[ci], yt[:, g * 4, :], AF.Square)
                k.matmul(pg, ones_bf, sq[ci], start=(ci == 0), stop=(ci == 1))
            k.act(rs, pg, AF.Ln, scale=1.0 / 256, bias=eps_t)
            k.act(rs, rs, AF.Exp, scale=-0.5)
            for g in (2 * grp, 2 * grp + 1):
                k.stt(ynj[:, g, :], yt[:, g * 4, :], snw_sb[:, g:g + 1], rs, ALU.mult, ALU.mult)
        for nb in range(2):
            for c in range(16):
                rhs = ynj[:, c // 4, :] if c % 4 == 0 else yt[:, c, :]
                k.matmul(pm[nb], w_bf[:, c, nb * 128:(nb + 1) * 128], rhs, start=(c == 0), stop=(c == 15))
            k.copy(mixT[:, nb, ts_], pm[nb], eng="dve")
            k.act(sq[nb], pm[nb], AF.Square)
            k.matmul(pss, ones_bf, sq[nb], start=(nb == 0), stop=(nb == 1))
        k.copy(ssrow[0:1, ts_], pss[0:1, :], eng="dve")

    def p2(j):
        ts_ = slice(j * 512, (j + 1) * 512)
        tt_ = tot[j % 2]
        k.dma(tt_, T(io["ar1_out"][j // 8].buf, io["ar1_out"][j // 8].ap[0:1, (j % 8) * 512:(j % 8 + 1) * 512].partition_broadcast(128)), eng="pool")
        k.act(tt_, tt_, AF.Ln, scale=1.0 / 1024, bias=eps_t)
        k.act(tt_, tt_, AF.Exp, scale=-0.5)
        x_ = xt[j % 2]
        k.dma(x_, xsrc[:, ts_].re("(nb p) t -> p nb t", p=128), eng="sp")
        for nb in range(2):
            k.stt(mixT[:, nb, ts_], mixT[:, nb, ts_], postw_sb[:, nb:nb + 1], tt_, ALU.mult, ALU.mult)
            k.tt(mixT[:, nb, ts_], mixT[:, nb, ts_], x_[:, nb, :], ALU.add, eng="pool")
        k.dma(xdst[:, ts_].re("(nb p) t -> p nb t", p=128), mixT[:, :, ts_], eng="sp")
        if not last:
            for nb in range(2):
                k.act(sq[nb], mixT[:, nb, ts_], AF.Square)
                k.matmul(pss, ones_bf, sq[nb], start=(nb == 0), stop=(nb == 1))
            k.copy(ssrow[0:1, ts_], pss[0:1, :], eng="dve")

    def p3(j):
        ts_ = slice(j * 512, (j + 1) * 512)
        tt_ = tot[j % 2]
        k.dma(tt_, T(io["ar2_out"][j // 8].buf, io["ar2_out"][j // 8].ap[0:1, (j % 8) * 512:(j % 8 + 1) * 512].partition_broadcast(128)), eng="pool")
        k.act(tt_, tt_, AF.Ln, scale=1.0 / 1024, bias=eps_t)
        k.act(tt_, tt_, AF.Exp, scale=-0.5)
        h_ = hb[j % 2]
        for nb in range(2):
            k.stt(h_[:, nb, :], mixT[:, nb, ts_], prew_sb[:, nb:nb + 1], tt_, ALU.mult, ALU.mult)
        half, jj = j // 8, j % 8
        for nb in range(2):
            k.dma(io["hslice"][half][nb][:, jj * 512:(jj + 1) * 512], h_[:, nb, :], eng="sp" if nb == 0 else "pool")
        if jj == 7:
            for nb in range(2):
                k.collective("AllGather", io["hslice"][half][nb], io["hgath"][half][nb], GROUPS)

    LHx = Lx // 2
    for half in range(2):
        for j in range(half * 8, half * 8 + 8):
            p1(j)
        k.dma(io["ar1_in"][half], ssrow[0:1, half * LHx:(half + 1) * LHx], eng="sp")
        k.collective("AllReduce", io["ar1_in"][half], io["ar1_out"][half], GROUPS, op=ALU.add)
    for half in range(2):
        for j in range(half * 8, half * 8 + 8):
            p2(j)
        if not last:
            k.dma(io["ar2_in"][half], ssrow[0:1, half * LHx:(half + 1) * LHx], eng="sp")
            k.collective("AllReduce", io["ar2_in"][half], io["ar2_out"][half], GROUPS, op=ALU.add)
    if last:
        return
    for j in range(NTx):
        p3(j)


L = 8192
PROJ_SIZES = (512, 1024, 16, 1664, 512, 512, 512, 512, 512, 512, 256, 256, 512)
OFF = np.concatenate([[0], np.cumsum(PROJ_SIZES)]).astype(int)
(O_MZ, O_XBC, O_DT, O_RU, O_RG, O_DQ, O_DK, O_DV, O_DG, O_GQ, O_GK, O_GV, O_GG) = OFF[:13]

PERM = np.array([p + 32 if (p % 64) < 32 else p - 32 for p in range(128)])


def rope_tables():
    inv = (np.float32(10000.0) ** (-np.arange(32, dtype=np.float32) / np.float32(32))).astype(np.float32)
    t = np.arange(L)
    sign = np.where((np.arange(128) % 64) < 32, -1.0, 1.0).astype(np.float32)[:, None]
    fi = np.arange(128) % 32
    pos_d = np.broadcast_to(t.astype(np.float32)[None, :], (128, L))
    ang_d = (pos_d * inv[fi][:, None]).astype(np.float32)
    pos_g = np.where((np.arange(128) < 64)[:, None], (t // 64)[None, :], (t % 64)[None, :]).astype(np.float32)
    ang_g = (pos_g * inv[fi][:, None]).astype(np.float32)
    tabs = np.stack([np.cos(ang_d), np.sin(ang_d) * sign, np.cos(ang_g), np.sin(ang_g) * sign]).astype(np.float32)
    return np.ascontiguousarray(tabs)


def pvec(v):
    return np.ascontiguousarray(v.reshape(8, 128).T)


def prep_attn(inp, layer, xT_b, tabs):
    W = inp['w_in'][layer]
    maps = []
    for core in range(8):
        b, g = core // 4, core % 4
        sl = lambda o, n=128, gg=g: W[:, o + gg * n: o + (gg + 1) * n]
        dq, dk_, dg, dv = sl(O_DQ), sl(O_DK), sl(O_DG), sl(O_DV)
        gq, gg_ = sl(O_GQ), sl(O_GG)
        gk, gv = sl(O_GK, 128, g // 2), sl(O_GV, 128, g // 2)
        wfm = np.concatenate([dq, dq[:, PERM], dk_, dk_[:, PERM], dg, gq, gq[:, PERM], gk, gk[:, PERM], gg_], axis=1)
        wtm = np.concatenate([dv, gv], axis=1)
        vecs = np.zeros((128, 8), np.float32)
        qw, kw = inp['gqa_q_norm_w'][layer], inp['gqa_k_norm_w'][layer]
        vecs[:, 0] = qw; vecs[:, 1] = qw[PERM]; vecs[:, 2] = kw; vecs[:, 3] = kw[PERM]
        vecs[:, 4] = inp['diff_norm_w'][layer]
        lam = np.ascontiguousarray(np.broadcast_to(inp['diff_lambda'][layer].reshape(1, 256), (128, 256)))
        maps.append({"xT": xT_b[b], "pw": pvec(inp['pre_norm_w'][layer]),
                     "wfm": np.ascontiguousarray(wfm), "wtm": np.ascontiguousarray(wtm),
                     "tabs": tabs, "vecs": vecs, "lam": lam})
    return maps


def ssd_consts():
    j = np.arange(128)[:, None]; l = np.arange(128)[None, :]
    c = np.stack([(j <= l), (j >= l), (j > l), (j < l), (j == l)]).astype(np.float32)
    return np.ascontiguousarray(c.transpose(1, 0, 2))


def prep_ssd(inp, layer, xT_b):
    W = inp['w_in'][layer]
    cst = ssd_consts()
    maps = []
    for core in range(8):
        b, g = core // 4, core % 4
        grp = g // 2
        wfm = np.concatenate([W[:, O_MZ + g * 128:O_MZ + (g + 1) * 128],
                              W[:, O_XBC + g * 128:O_XBC + (g + 1) * 128],
                              W[:, O_XBC + 512 + grp * 128:O_XBC + 512 + (grp + 1) * 128],
                              W[:, O_XBC + 768 + grp * 128:O_XBC + 768 + (grp + 1) * 128]], axis=1)
        dcols = [O_DT + d * 8 + 2 * g + h for d in range(2) for h in range(2)]
        wdt = W[:, dcols]
        chans = [g * 128, 512 + grp * 128, 768 + grp * 128]
        cwl = inp['conv_w'][layer]; cbl = inp['conv_b'][layer]
        cw = np.stack([cwl[:, ch:ch + 128].T for ch in chans], axis=1)
        cb = np.stack([cbl[ch:ch + 128] for ch in chans], axis=1)
        v = np.zeros(16, np.float32)
        for d in range(2):
            for h in range(2):
                v[d * 2 + h] = inp['ssm_dt_bias'][layer][d, 2 * g + h]
                v[4 + d * 2 + h] = inp['ssm_a_log'][layer][d, 2 * g + h]
        v[8] = inp['ssm_d'][layer][2 * g]; v[9] = inp['ssm_d'][layer][2 * g + 1]
        maps.append({"xT": xT_b[b], "pw": pvec(inp['pre_norm_w'][layer]),
                     "wfm": np.ascontiguousarray(wfm), "wdt": np.ascontiguousarray(wdt),
                     "cw": np.ascontiguousarray(cw), "cb": np.ascontiguousarray(cb),
                     "ssmv": np.ascontiguousarray(np.broadcast_to(v[None], (128, 16))), "cst": cst})
    return maps


def rwkv_consts():
    j = np.arange(64)[:, None]; t = np.arange(64)[None, :]
    cm = np.zeros((64, 2, 704), np.float32)
    for d in range(2):
        strict = (j < t) if d == 0 else (j > t)
        incl = (j <= t) if d == 0 else (j >= t)
        blk = np.concatenate([strict, incl], axis=1).astype(np.float32)
        cm[:, d, 0:512] = np.tile(blk, (1, 4))
        nmask = ((t < j) if d == 0 else (t > j)).astype(np.float32)
        cm[:, d, 512:640] = np.tile(nmask, (1, 2))
        cm[:, d, 640:704] = np.eye(64, dtype=np.float32)
    c128 = np.zeros((128, 2, 128), np.float32)
    c128[0:64, 0, 0:64] = 1; c128[64:128, 0, 64:128] = 1
    c128[:, 1, :] = np.eye(128, dtype=np.float32)
    return cm, c128


O_R, O_K, O_V, O_WD, O_AD = O_RU, O_RU + 512, O_RU + 1024, O_RU + 1536, O_RU + 1600


def prep_rwkv(inp, layer, xT_b):
    W = inp['w_in'][layer]
    cm, c128 = rwkv_consts()
    maps = []
    for core in range(8):
        b, g = core // 4, core % 4
        gs = slice(g * 128, (g + 1) * 128)
        wfm = np.concatenate([W[:, O_R + g * 128:O_R + (g + 1) * 128], W[:, O_K + g * 128:O_K + (g + 1) * 128],
                              W[:, O_V + g * 128:O_V + (g + 1) * 128], W[:, O_WD:O_WD + 128],
                              W[:, O_RG + g * 128:O_RG + (g + 1) * 128]], axis=1)
        mul = inp['rwkv_mu'][layer]
        mu = np.zeros((128, 2, 4), np.float32)
        for d in range(2):
            mu[:, d, 0] = mul[d, 0 + g * 128:0 + (g + 1) * 128]
            mu[:, d, 1] = mul[d, 512 + g * 128:512 + (g + 1) * 128]
            mu[:, d, 2] = mul[d, 1024 + g * 128:1024 + (g + 1) * 128]
            mu[:, d, 3] = mul[d, 1536:1664]
        w2a2 = np.zeros((128, 2, 128), np.float32)
        for d in range(2):
            w2a2[0:64, d, :] = inp['rwkv_w2'][layer][d][:, gs]
            w2a2[64:128, d, :] = inp['rwkv_a2'][layer][:, gs]
        pv = np.zeros((128, 12), np.float32)
        pv[:, 0] = inp['rwkv_w0'][layer][0, gs]; pv[:, 1] = inp['rwkv_w0'][layer][1, gs]
        pv[:, 2] = inp['rwkv_a0'][layer][gs]; pv[:, 3] = inp['rwkv_k_k'][layer][gs]
        pv[:, 4] = inp['rwkv_k_a'][layer][gs]
        pv[:, 6] = inp['rwkv_r_k'][layer].reshape(-1)[gs]
        pv[:, 7] = inp['rwkv_ln_w'][layer][gs]; pv[:, 8] = inp['rwkv_ln_b'][layer][gs]
        maps.append({"xT": xT_b[b], "pw": pvec(inp['pre_norm_w'][layer]), "wfm": np.ascontiguousarray(wfm),
                     "mu": mu, "w2a2": w2a2, "pvd": pv, "cm": cm, "c128": c128})
    return maps


def build_fused(depth=2):
    nc = bass.Bass("TRN2", target_bir_lowering=False)
    k = KB(nc, arena=True)
    din = k.dram_in
    xT = din("xT", [1024, L])
    xs0 = din("xs0", [256, L])
    tabs = din("tabs", [4, 128, L])
    cst = din("cst", [128, 5, 128])
    cm = din("cm", [64, 2, 704])
    c128 = din("c128", [128, 2, 128])
    xo = k.dram_out("xo", [256, L])
    LH = L // 2
    ycat = [[k.dram_scratch("ycat%d%d" % (h_, q_), [128, LH], BF16) for q_ in range(4)] for h_ in range(2)]
    ygath = [[k.dram_scratch("ygath%d%d" % (h_, q_), [512, LH], BF16) for q_ in range(4)] for h_ in range(2)]
    hslice = [[k.dram_scratch("hslice%d%d" % (h_, q_), [128, LH], BF16) for q_ in range(2)] for h_ in range(2)]
    hgath = [[k.dram_scratch("hgath%d%d" % (h_, q_), [512, LH], BF16) for q_ in range(2)] for h_ in range(2)]

    def yfn_for(kind_idx):
        def fn(_, j):
            half, jj = j // 8, j % 8
            return ycat[half][kind_idx][:, jj * 512:(jj + 1) * 512]
        return fn

    def gather_y(half, kind_idx):
        k.collective("AllGather", ycat[half][kind_idx], ygath[half][kind_idx], GROUPS)

    xres = k.dram_scratch("xres", [256, L])
    ar = [[k.dram_scratch("ar%d_%d" % (i, h_), [1, L // 2]) for h_ in range(2)] for i in range(4)]
    wkv_scr = k.dram_scratch("wkv_scr", [2, L, 128])
    bon_scr = k.dram_scratch("bon_scr", [2, 128, L])
    for layer in range(depth):
        p = "L%d_" % layer
        lambda_init = 0.8 - 0.6 * math.exp(-0.3 * layer)
        src = {"xT": xT} if layer == 0 else {"hT": hgath}
        pw = din(p + "pw", [128, 8])
        io = dict(src, pw=pw, wfm=din(p + "s_wfm", [1024, 512]), wdt=din(p + "s_wdt", [1024, 4]),
                  cw=din(p + "s_cw", [128, 3, 5]), cb=din(p + "s_cb", [128, 3]), ssmv=din(p + "s_ssmv", [128, 16]),
                  cst=cst, yfn=yfn_for(0), ydt=BF16)
        emit_ssd(k, io)
        k.phase_reset()
        io = dict(src, pw=pw, wfm=din(p + "r_wfm", [1024, 640]), mu=din(p + "r_mu", [128, 2, 4]),
                  w2a2=din(p + "r_w2a2", [128, 2, 128]), pvd=din(p + "r_pvd", [128, 12]), cm=cm, c128=c128,
                  wkv_scr=wkv_scr, bon_scr=bon_scr, yfn=yfn_for(1), ydt=BF16)
        emit_rwkv(k, io)
        k.phase_reset()
        for half in range(2):
            for kind_idx in range(2):
                gather_y(half, kind_idx)
        io = dict(src, pw=pw, wfm=din(p + "a_wfm", [1024, 1280]), wtm=din(p + "a_wtm", [1024, 256]), tabs=tabs,
                  vecs=din(p + "a_vecs", [128, 8]), lam=din(p + "a_lam", [128, 256]),
                  yfn=(lambda kind, j: yfn_for(2 + kind)(None, j)), ydt=BF16,
                  y_done=(lambda kind, half: gather_y(half, 2 + kind)))
        emit_attn(k, io, lambda_init)
        k.phase_reset()
        last = (layer == depth - 1)
        io = dict(ygath=ygath, wo=din(p + "o_wo", [2048, 256]), xsrc=(xs0 if layer == 0 else xres),
                  xdst=(xo if last else xres), postw=din(p + "o_postw", [128, 2]), snw=din(p + "o_snw", [128, 4]),
                  prew=din(p + "o_prew", [128, 2]), ar1_in=ar[0], ar1_out=ar[1], ar2_in=ar[2], ar2_out=ar[3],
                  hslice=hslice, hgath=hgath)
        emit_out(k, io, last)
        k.phase_reset()
    stats = k.finish()
    return nc, stats


def prep_fused(inp, depth=2):
    x = np.ascontiguousarray(inp["x"], dtype=np.float32)
    xT_b = [np.ascontiguousarray(x[b].T) for b in range(2)]
    tabs = rope_tables()
    cst = ssd_consts()
    cm, c128 = rwkv_consts()
    maps = [dict() for _ in range(8)]
    perm = np.array([kind * 512 + g * 128 + i for g in range(4) for kind in range(4) for i in range(128)])
    for core in range(8):
        b, g = core // 4, core % 4
        m = maps[core]
        m["xT"] = xT_b[b]
        m["xs0"] = np.ascontiguousarray(xT_b[b][g * 256:(g + 1) * 256])
        m["tabs"] = tabs; m["cst"] = cst; m["cm"] = cm; m["c128"] = c128
    for layer in range(depth):
        p = "L%d_" % layer
        ms = prep_ssd(inp, layer, xT_b); mr = prep_rwkv(inp, layer, xT_b); ma = prep_attn(inp, layer, xT_b, tabs)
        for core in range(8):
            b, g = core // 4, core % 4
            m = maps[core]
            m[p + "pw"] = ms[core]["pw"]
            for nm in ("wfm", "wdt", "cw", "cb", "ssmv"):
                m[p + "s_" + nm] = ms[core][nm]
            for nm in ("wfm", "mu", "w2a2", "pvd"):
                m[p + "r_" + nm] = mr[core][nm]
            for nm in ("wfm", "wtm", "vecs", "lam"):
                m[p + "a_" + nm] = ma[core][nm]
            ns = slice(g * 256, (g + 1) * 256)
            m[p + "o_wo"] = np.ascontiguousarray(inp["w_out"][layer][perm][:, ns])
            m[p + "o_postw"] = np.ascontiguousarray(inp["post_norm_w"][layer][ns].reshape(2, 128).T)
            m[p + "o_snw"] = np.ascontiguousarray(inp["ssm_norm_w"][layer].reshape(4, 128).T)
            nxt = inp["pre_norm_w"][min(layer + 1, depth - 1)]
            m[p + "o_prew"] = np.ascontiguousarray(nxt[ns].reshape(2, 128).T)
    return maps


from concourse.bass_utils import run_bass_kernel_spmd


def kernel(**inp):
    inp = {k_: np.asarray(v) for k_, v in inp.items()}
    depth = inp["w_in"].shape[0]
    nc, _ = build_fused(depth)
    maps = prep_fused(inp, depth)
    res = run_bass_kernel_spmd(nc, maps, core_ids=list(range(8))).results
    out = np.empty((2, L, 1024), np.float32)
    for core in range(8):
        b, g = core // 4, core % 4
        out[b, :, g * 256:(g + 1) * 256] = res[core]["xo"].T
    return out
```
